# Optimizing a Trainium2 kernel written in Bass

```python
import math
import jax, jax.numpy as jnp
from jax import lax
import numpy as np

D_MODEL = 2048
BATCH = 2
SEQ = 8192
DEPTH = 4

N_MIXERS = 2
NORM_EPS = 1e-6

SSD_EXPAND = 2
SSD_D_INNER = SSD_EXPAND * D_MODEL
SSD_HEAD_DIM = 64
SSD_N_HEADS = SSD_D_INNER // SSD_HEAD_DIM
SSD_N_GROUPS = 8
SSD_HEADS_PER_GROUP = SSD_N_HEADS // SSD_N_GROUPS
SSD_D_STATE = 128
SSD_CONV_WIDTH = 4
SSD_CHUNK = 128
SSD_CONV_DIM = SSD_D_INNER + 2 * SSD_N_GROUPS * SSD_D_STATE
SSD_IN_DIM = SSD_D_INNER + SSD_CONV_DIM + SSD_N_HEADS

DA_N_HEADS = 16
DA_HEAD_DIM = D_MODEL // DA_N_HEADS // 2
DA_V_DIM = 2 * DA_HEAD_DIM
Q_BLOCK = 128

D_FF = -(-8 * D_MODEL // (3 * 256)) * 256

kernel_name = "hybrid_ssd_diffattn_sandwich"


def rmsnorm(x, w, eps=NORM_EPS):
    xf = x.astype(jnp.float32)
    out = xf * lax.rsqrt(jnp.mean(xf * xf, axis=-1, keepdims=True) + eps)
    return (out * w.astype(jnp.float32)).astype(x.dtype)


def causal_depthwise_conv(u, w, b):
    out = lax.conv_general_dilated(
        u, w[:, None, :].astype(u.dtype), window_strides=(1,),
        padding=[(SSD_CONV_WIDTH - 1, 0)],
        dimension_numbers=("NWC", "WIO", "NWC"),
        feature_group_count=u.shape[-1])
    return out + b.astype(u.dtype)


def ssd_chunked_scan(xdt, da, bm, cm):
    bsz, s_len, h, p = xdt.shape
    g, r, n, l = SSD_N_GROUPS, SSD_HEADS_PER_GROUP, SSD_D_STATE, SSD_CHUNK
    nc = s_len // l
    xs = jnp.moveaxis(xdt.reshape(bsz, nc, l, g, r, p), 1, 0)
    das = jnp.moveaxis(da.reshape(bsz, nc, l, g, r), 1, 0)
    bs = jnp.moveaxis(bm.reshape(bsz, nc, l, g, n), 1, 0)
    cs = jnp.moveaxis(cm.reshape(bsz, nc, l, g, n), 1, 0)
    causal = jnp.tril(jnp.ones((l, l), dtype=bool))

    def step(state, inp):
        x_c, a_c, b_c, c_c = inp
        a_cum = jnp.moveaxis(jnp.cumsum(a_c, axis=1), 1, -1)
        seg = a_cum[..., :, None] - a_cum[..., None, :]
        decay = jnp.exp(jnp.where(causal, seg, -jnp.inf))
        cb = jnp.einsum('blgn,bsgn->bgls', c_c, b_c)
        y_diag = jnp.einsum('bgls,bgrls,bsgrp->blgrp', cb, decay, x_c)
        y_off = jnp.einsum('blgn,bgrpn,bgrl->blgrp', c_c, state, jnp.exp(a_cum))
        a_last = a_cum[..., -1]
        decay_to_end = jnp.exp(a_last[..., None] - a_cum)
        new_state = (state * jnp.exp(a_last)[..., None, None]
                     + jnp.einsum('bsgn,bgrs,bsgrp->bgrpn', b_c, decay_to_end, x_c))
        return new_state, y_diag + y_off

    init = jnp.zeros((bsz, g, r, p, n), jnp.float32)
    _, ys = lax.scan(step, init, (xs, das, bs, cs))
    return jnp.moveaxis(ys, 0, 1).reshape(bsz, s_len, h, p)


def gated_group_rmsnorm(y, z, w):
    gv = (y.astype(jnp.float32) * jax.nn.silu(z.astype(jnp.float32)))
    shp = gv.shape
    gv = gv.reshape(shp[:-1] + (SSD_N_GROUPS, shp[-1] // SSD_N_GROUPS))
    gv = gv * lax.rsqrt(jnp.mean(gv * gv, axis=-1, keepdims=True) + NORM_EPS)
    return (gv.reshape(shp) * w.astype(jnp.float32)).astype(z.dtype)


def ssd_mixer(u, w_in, conv_w, conv_b, dt_bias, a_log, d_skip, norm_w, w_out):
    bsz, s_len, _ = u.shape
    zxbcdt = u @ w_in
    z = zxbcdt[..., :SSD_D_INNER]
    xbc = zxbcdt[..., SSD_D_INNER:SSD_D_INNER + SSD_CONV_DIM]
    dt = zxbcdt[..., SSD_D_INNER + SSD_CONV_DIM:]
    xbc = jax.nn.silu(causal_depthwise_conv(xbc, conv_w, conv_b))
    gn = SSD_N_GROUPS * SSD_D_STATE
    xs = xbc[..., :SSD_D_INNER].reshape(bsz, s_len, SSD_N_HEADS, SSD_HEAD_DIM).astype(jnp.float32)
    bm = xbc[..., SSD_D_INNER:SSD_D_INNER + gn].reshape(bsz, s_len, SSD_N_GROUPS, SSD_D_STATE).astype(jnp.float32)
    cm = xbc[..., SSD_D_INNER + gn:].reshape(bsz, s_len, SSD_N_GROUPS, SSD_D_STATE).astype(jnp.float32)
    dt = jax.nn.softplus(dt.astype(jnp.float32) + dt_bias.astype(jnp.float32))
    a = -jnp.exp(a_log.astype(jnp.float32))
    y = ssd_chunked_scan(xs * dt[..., None], dt * a, bm, cm)
    y = y + d_skip.astype(jnp.float32)[:, None] * xs
    y = gated_group_rmsnorm(y.reshape(bsz, s_len, SSD_D_INNER), z, norm_w)
    return y @ w_out


def diff_attention(u, w_qkv, lq1, lk1, lq2, lk2, subln_w, w_out, lambda_init):
    bsz, s_len, _ = u.shape
    qkv = u @ w_qkv
    q, k, v = jnp.split(qkv, 3, axis=-1)
    q = q.reshape(bsz, s_len, DA_N_HEADS, 2, DA_HEAD_DIM) * (DA_HEAD_DIM ** -0.5)
    k = k.reshape(bsz, s_len, DA_N_HEADS, 2, DA_HEAD_DIM).transpose(0, 2, 3, 1, 4)
    v = v.reshape(bsz, s_len, DA_N_HEADS, DA_V_DIM).transpose(0, 2, 1, 3)
    lam = (jnp.exp(jnp.sum(lq1.astype(jnp.float32) * lk1.astype(jnp.float32)))
           - jnp.exp(jnp.sum(lq2.astype(jnp.float32) * lk2.astype(jnp.float32)))
           + lambda_init)
    nq = s_len // Q_BLOCK
    qb = q.reshape(bsz, nq, Q_BLOCK, DA_N_HEADS, 2, DA_HEAD_DIM).transpose(1, 0, 3, 4, 2, 5)
    key_pos = jnp.arange(s_len)

    def attend(args):
        idx, q_blk = args
        q_pos = idx * Q_BLOCK + jnp.arange(Q_BLOCK)
        s = jnp.einsum('bhcqd,bhcsd->bhcqs', q_blk, k).astype(jnp.float32)
        s = jnp.where(key_pos[None, :] <= q_pos[:, None], s, -jnp.inf)
        pr = jax.nn.softmax(s, axis=-1)
        pr = pr[:, :, 0] - lam * pr[:, :, 1]
        return jnp.einsum('bhqs,bhsd->bqhd', pr.astype(v.dtype), v)

    o = lax.map(attend, (jnp.arange(nq), qb))
    o = o.transpose(1, 0, 2, 3, 4).reshape(bsz, s_len, DA_N_HEADS, DA_V_DIM)
    o = rmsnorm(o, subln_w) * (1.0 - lambda_init)
    return o.reshape(bsz, s_len, D_MODEL) @ w_out


def swiglu(h, w_gate, w_up, w_down):
    return (jax.nn.silu(h @ w_gate) * (h @ w_up)) @ w_down


def setup_inputs(seed: int = 0) -> dict:
    key = jax.random.key(seed)
    ks = jax.random.split(key, 24)
    n_ssd = (DEPTH + 1) // 2
    n_att = DEPTH // 2
    f32 = jnp.float32

    def nrm(k, shape, scale):
        return jax.random.normal(k, shape, f32) * scale

    def gain(k, shape):
        return 1.0 + 0.02 * jax.random.normal(k, shape, f32)

    dt = jnp.exp(jax.random.uniform(ks[12], (n_ssd, SSD_N_HEADS), f32)
                 * (math.log(0.1) - math.log(0.001)) + math.log(0.001))
    return {
        "x": jax.random.normal(ks[0], (BATCH, SEQ, D_MODEL), f32),
        "norm_mix_pre": gain(ks[1], (DEPTH, D_MODEL)),
        "norm_mix_post": gain(ks[2], (DEPTH, D_MODEL)),
        "norm_ffn_pre": gain(ks[3], (DEPTH, D_MODEL)),
        "norm_ffn_post": gain(ks[4], (DEPTH, D_MODEL)),
        "ffn_w_gate": nrm(ks[5], (DEPTH, D_MODEL, D_FF), D_MODEL ** -0.5),
        "ffn_w_up": nrm(ks[6], (DEPTH, D_MODEL, D_FF), D_MODEL ** -0.5),
        "ffn_w_down": nrm(ks[7], (DEPTH, D_FF, D_MODEL), D_FF ** -0.5),
        "ssd_w_in": nrm(ks[8], (n_ssd, D_MODEL, SSD_IN_DIM), D_MODEL ** -0.5),
        "ssd_conv_w": nrm(ks[9], (n_ssd, SSD_CONV_WIDTH, SSD_CONV_DIM), SSD_CONV_WIDTH ** -0.5),
        "ssd_conv_b": nrm(ks[10], (n_ssd, SSD_CONV_DIM), 0.02),
        "ssd_dt_bias": dt + jnp.log(-jnp.expm1(-dt)),
        "ssd_a_log": jnp.log(jax.random.uniform(ks[11], (n_ssd, SSD_N_HEADS), f32, 1.0, 16.0)),
        "ssd_d": gain(ks[13], (n_ssd, SSD_N_HEADS)),
        "ssd_norm": gain(ks[14], (n_ssd, SSD_D_INNER)),
        "ssd_w_out": nrm(ks[15], (n_ssd, SSD_D_INNER, D_MODEL), SSD_D_INNER ** -0.5),
        "da_w_qkv": nrm(ks[16], (n_att, D_MODEL, 3 * D_MODEL), D_MODEL ** -0.5),
        "da_lambda_q1": nrm(ks[17], (n_att, DA_HEAD_DIM), 0.1),
        "da_lambda_k1": nrm(ks[18], (n_att, DA_HEAD_DIM), 0.1),
        "da_lambda_q2": nrm(ks[19], (n_att, DA_HEAD_DIM), 0.1),
        "da_lambda_k2": nrm(ks[20], (n_att, DA_HEAD_DIM), 0.1),
        "da_subln": gain(ks[21], (n_att, DA_V_DIM)),
        "da_w_out": nrm(ks[22], (n_att, D_MODEL, D_MODEL), D_MODEL ** -0.5),
    }


def reference(x, norm_mix_pre, norm_mix_post, norm_ffn_pre, norm_ffn_post,
              ffn_w_gate, ffn_w_up, ffn_w_down,
              ssd_w_in, ssd_conv_w, ssd_conv_b, ssd_dt_bias, ssd_a_log, ssd_d,
              ssd_norm, ssd_w_out,
              da_w_qkv, da_lambda_q1, da_lambda_k1, da_lambda_q2, da_lambda_k2,
              da_subln, da_w_out):
    for i in range(DEPTH):
        j = i // N_MIXERS
        h = rmsnorm(x, norm_mix_pre[i])
        if i % N_MIXERS == 0:
            m = ssd_mixer(h, ssd_w_in[j], ssd_conv_w[j], ssd_conv_b[j], ssd_dt_bias[j],
                          ssd_a_log[j], ssd_d[j], ssd_norm[j], ssd_w_out[j])
        else:
            lambda_init = 0.8 - 0.6 * math.exp(-0.3 * i)
            m = diff_attention(h, da_w_qkv[j], da_lambda_q1[j], da_lambda_k1[j],
                               da_lambda_q2[j], da_lambda_k2[j], da_subln[j], da_w_out[j],
                               lambda_init)
        x = x + rmsnorm(m, norm_mix_post[i])
        h = rmsnorm(x, norm_ffn_pre[i])
        x = x + rmsnorm(swiglu(h, ffn_w_gate[i], ffn_w_up[i], ffn_w_down[i]), norm_ffn_post[i])
    return x
```

```python
import math
import numpy as np
import ml_dtypes
import concourse.bass as bass
import concourse.mybir as mybir
from concourse.bass_utils import run_bass_kernel_spmd

F32 = mybir.dt.float32
BF16 = mybir.dt.bfloat16
AF = mybir.ActivationFunctionType
ALU = mybir.AluOpType
AX = mybir.AxisListType

NDMA_SLOTS = 6


class Tok:
    __slots__ = ("w", "r", "ra", "name")

    def __init__(self, name=""):
        self.w = {}
        self.r = {}
        self.ra = []
        self.name = name


class Prog:
    COMPUTE = ("pe", "act", "dve", "pool", "cc")
    QUEUES = ("q_sp", "q_pool", "q_act")
    ISSUE = {"pe": "pe", "act": "act", "dve": "dve", "pool": "pool", "cc": "pool",
             "q_sp": "sp", "q_pool": "pool", "q_act": "act"}

    def __init__(self, nc, stack):
        self.nc = nc
        self.ops = []
        self.base = 0
        self.cnt = {e: 0 for e in self.COMPUTE}
        self.dcnt = {q: 0 for q in self.QUEUES}
        self.waited = {e: {} for e in ("pe", "act", "dve", "pool", "sp")}
        self.sems = {}
        for e in self.COMPUTE:
            self.sems[e] = stack.enter_context(nc.semaphore(f"s_{e}"))
        for q in self.QUEUES:
            for s in range(NDMA_SLOTS):
                self.sems[(q, s)] = stack.enter_context(nc.semaphore(f"s_{q}{s}"))

    def op(self, eng, fn, rd=(), wr=()):
        self.ops.append((eng, fn, tuple(rd), tuple(wr)))

    def pe(self, fn, rd=(), wr=()):
        self.op("pe", fn, rd, wr)

    def act(self, fn, rd=(), wr=()):
        self.op("act", fn, rd, wr)

    def dve(self, fn, rd=(), wr=()):
        self.op("dve", fn, rd, wr)

    def pool(self, fn, rd=(), wr=()):
        self.op("pool", fn, rd, wr)

    def dma(self, fn, rd=(), wr=(), q="q_sp"):
        self.op(q, fn, rd, wr)

    def cc(self, fn, rd=(), wr=()):
        self.op("cc", fn, rd, wr)

    COST = {"pe": 0.27, "act": 0.5, "dve": 0.55, "pool": 1.2, "cc": 0.1, "q_sp": 0.06, "q_pool": 0.3, "q_act": 0.06}
    DONE_LAT = {"q_sp": 2.5, "q_pool": 3.0, "q_act": 2.5, "cc": 100.0}

    def _schedule(self, ops, order_deps, W=16):
        n = len(ops)
        issue = [self.ISSUE[o[0]] for o in ops]
        per = {e: [] for e in ("pe", "act", "dve", "pool", "sp")}
        for i in range(n):
            per[issue[i]].append(i)
        head = {e: 0 for e in per}
        done = [False] * n
        fin = [0.0] * n
        tfree = {e: 0.0 for e in per}
        order = []
        ready_t = [None] * n
        while len(order) < n:
            best = None
            for e, lst in per.items():
                h = head[e]
                while h < len(lst) and done[lst[h]]:
                    h += 1
                head[e] = h
                cnt = 0
                k = h
                while k < len(lst) and cnt < W:
                    i = lst[k]
                    k += 1
                    if done[i]:
                        continue
                    cnt += 1
                    rt = ready_t[i]
                    if rt is None:
                        ok = True
                        rt = 0.0
                        for d in order_deps[i]:
                            if not done[d]:
                                ok = False
                                break
                            f = fin[d] + (0.0 if issue[d] == e and not ops[d][0].startswith("q_") else 0.25)
                            if f > rt:
                                rt = f
                        if not ok:
                            continue
                        ready_t[i] = rt
                    st = rt if rt > tfree[e] else tfree[e]
                    key = (st, i)
                    if best is None or key < best[0]:
                        best = (key, i, e)
            (st, _), i, e = best
            c = self.COST[ops[i][0]]
            tfree[e] = st + c
            fin[i] = st + c + self.DONE_LAT.get(ops[i][0], 0.0)
            done[i] = True
            order.append(i)
        return order

    def emit(self, reorder=False):
        nc = self.nc
        ops = self.ops
        n = len(ops)
        base = self.base
        deps = [None] * n
        odeps = [None] * n
        signals = [False] * n
        for i, (eng, fn, rd, wr) in enumerate(ops):
            gi = base + i
            is_dma = eng.startswith("q_")
            key = ("dma", gi) if is_dma else eng
            d = set()
            od = set()
            for t in rd:
                for k, j in t.w.items():
                    if j >= base:
                        od.add(j - base)
                    if k == key and key == "pe":
                        continue
                    if j >= base:
                        d.add(j - base)
            for t in wr:
                for k, j in t.w.items():
                    if j >= base:
                        od.add(j - base)
                    if k == key:
                        continue
                    if j >= base:
                        d.add(j - base)
                for k, j in t.ra:
                    if j >= base:
                        od.add(j - base)
                    if k == key:
                        continue
                    if j >= base:
                        d.add(j - base)
            for t in rd:
                t.r[key] = gi
                t.ra.append((key, gi))
            for t in wr:
                t.w = {key: gi}
                t.r = {}
                t.ra = []
            od.discard(i)
            deps[i] = d
            odeps[i] = od
            for j in d:
                signals[j] = True
            if is_dma or eng == "cc":
                signals[i] = True
        if reorder and n > 2:
            order = self._schedule(ops, odeps)
            pos = [0] * n
            for p_, i in enumerate(order):
                pos[i] = p_
            ops = [ops[i] for i in order]
            deps = [{pos[j] for j in deps[i]} for i in order]
            signals = [signals[i] for i in order]
            self.ops = ops
        sig = [None] * n
        dma_idx = [None] * n
        for i, (eng, fn, rd, wr) in enumerate(ops):
            if eng.startswith("q_"):
                k = self.dcnt[eng]
                self.dcnt[eng] += 1
                dma_idx[i] = k
                sig[i] = ((eng, k % NDMA_SLOTS), 16 * (k // NDMA_SLOTS + 1))
            elif signals[i]:
                self.cnt[eng] += 1
                sig[i] = (eng, self.cnt[eng])
        waited = self.waited
        plan = {e: [] for e in ("pe", "act", "dve", "pool", "sp")}
        for i, (eng, fn, rd, wr) in enumerate(ops):
            ie = self.ISSUE[eng]
            ws = []
            need = {}
            for j in deps[i]:
                sk, v = sig[j]
                if need.get(sk, 0) < v:
                    need[sk] = v
            if eng.startswith("q_"):
                k = dma_idx[i]
                if k >= NDMA_SLOTS:
                    sk = (eng, k % NDMA_SLOTS)
                    v = 16 * (k // NDMA_SLOTS)
                    if need.get(sk, 0) < v:
                        need[sk] = v
            for sk, v in need.items():
                if waited[ie].get(sk, 0) < v:
                    waited[ie][sk] = v
                    ws.append((sk, v))
            plan[ie].append((i, ws))
        final = []
        for q, c in self.dcnt.items():
            for s in range(min(c, NDMA_SLOTS)):
                last = 16 * ((c - 1 - s) // NDMA_SLOTS + 1)
                final.append(((q, s), last))
        if self.cnt["cc"]:
            final.append(("cc", self.cnt["cc"]))
        sems = self.sems
        with nc.Block() as block:
            engs = {"pe": block.tensor, "act": block.scalar, "dve": block.vector,
                    "pool": block.gpsimd, "sp": block.sync}

            def mk(ie):
                def body(e):
                    for (i, ws) in plan[ie]:
                        for sk, v in ws:
                            e.wait_ge(sems[sk], v)
                        ins = ops[i][1](e)
                        if sig[i] is not None:
                            sk, v = sig[i]
                            ins.then_inc(sems[sk], 1 if isinstance(sk, str) else 16)
                    if ie == "sp":
                        for sk, v in final:
                            if waited["sp"].get(sk, 0) < v:
                                waited["sp"][sk] = v
                                e.wait_ge(sems[sk], v)
                return body

            for ie in ("sp", "pe", "act", "dve", "pool"):
                if plan[ie] or ie == "sp":
                    engs[ie](mk(ie))
        self.base += n
        self.ops = []


_UID = [0]


def U(name):
    _UID[0] += 1
    return f"{name}_{_UID[0]}"


class Ring:
    def __init__(self, stack, alloc, name, shape, dtype, n):
        self.bufs = []
        for i in range(n):
            t = stack.enter_context(alloc(U(f"{name}{i}"), list(shape), dtype))
            self.bufs.append((t, Tok(f"{name}{i}")))
        self.i = 0

    def next(self):
        b = self.bufs[self.i % len(self.bufs)]
        self.i += 1
        return b


D = 2048
KC_D = D // 128
EPS = 1e-6


def bcast_rows(ap_1d, nparts):
    return ap_1d.partition_broadcast(nparts)


def emit_rstd(P, ss, rstd, tok_ss, tok_rstd, n, width):
    P.act(lambda e: e.activation(out=rstd, in_=ss, func=AF.Ln, bias=EPS, scale=1.0 / width),
          rd=[tok_ss], wr=[tok_rstd])
    P.act(lambda e: e.activation(out=rstd, in_=rstd, func=AF.Exp, scale=-0.5),
          rd=[tok_rstd], wr=[tok_rstd])


def phase_norm(nc, P, stack, T, x_in, y_in, wpost, wpre, x_out, hT_out, ident_bf, sel4=None):
    nt = T // 128
    sb = nc.sbuf_tensor
    xr = Ring(stack, sb, "n_x", [128, D], F32, 2)
    yr = Ring(stack, sb, "n_y", [128, D], F32, 2) if y_in is not None else None
    tr = Ring(stack, sb, "n_t", [128, D], F32, 2)
    sqr = Ring(stack, sb, "n_sq", [128, D], BF16, 1)
    hr = Ring(stack, sb, "n_h", [128, D], BF16, 2)
    htr = Ring(stack, sb, "n_hT", [128, KC_D, 512], BF16, 2)
    str_ = Ring(stack, sb, "n_st", [128, 4], F32, 4)
    ptr = Ring(stack, nc.psum_tensor, "n_pt", [128, 1024], BF16, 4)
    if sel4 is not None:
        scr4 = Ring(stack, sb, "n_sc4", [128, KC_D, 512], BF16, 2)
        s4 = stack.enter_context(sb(U("n_sel4"), [128, 4], F32))
        s4k = Tok("n_sel4")
        P.dma(lambda e: e.dma_start(out=s4[:], in_=sel4.partition_broadcast(128)), wr=[s4k])
    consts = []
    for nm, w in (("n_wpost", wpost), ("n_wpre", wpre)):
        if w is None:
            consts.append((None, None))
            continue
        t = stack.enter_context(sb(U(nm), [128, D], F32))
        tk = Tok(nm)
        P.dma(lambda e, t=t, w=w: e.dma_start(out=t[:], in_=bcast_rows(w, 128)), wr=[tk])
        consts.append((t, tk))
    (wpost_t, wpost_k), (wpre_t, wpre_k) = consts
    for i in range(nt):
        rows = slice(i * 128, (i + 1) * 128)
        xt, xk = xr.next()
        P.dma(lambda e, xt=xt, rows=rows: e.dma_start(out=xt[:], in_=x_in[rows, :]), wr=[xk])
        cur, curk = xt, xk
        if y_in is not None:
            yt, yk = yr.next()
            if isinstance(y_in, tuple):
                P.dma(lambda e, yt=yt, rows=rows: e.dma_start(out=yt[:, 0:1024], in_=y_in[0][rows, :]), wr=[yk],
                      q="q_pool")
                P.dma(lambda e, yt=yt, rows=rows: e.dma_start(out=yt[:, 1024:2048], in_=y_in[1][rows, :]), wr=[yk],
                      q="q_pool")
            else:
                P.dma(lambda e, yt=yt, rows=rows: e.dma_start(out=yt[:], in_=y_in[rows, :]), wr=[yk], q="q_pool")
            sq, sqk = sqr.next()
            st, stk = str_.next()
            P.act(lambda e, sq=sq, yt=yt, st=st: e.activation(out=sq[:], in_=yt[:], func=AF.Square,
                                                               accum_out=st[:, 0:1]),
                  rd=[yk], wr=[sqk, stk])
            emit_rstd(P, st[:, 0:1], st[:, 1:2], stk, stk, 128, D)
            tt, tk = tr.next()
            P.dve(lambda e, tt=tt, yt=yt, st=st: e.scalar_tensor_tensor(
                out=tt[:], in0=yt[:], scalar=st[:, 1:2], in1=wpost_t[:], op0=ALU.mult, op1=ALU.mult),
                rd=[yk, stk, wpost_k], wr=[tk])
            P.dve(lambda e, tt=tt, xt=xt: e.tensor_tensor(out=tt[:], in0=tt[:], in1=xt[:], op=ALU.add),
                  rd=[tk, xk], wr=[tk])
            cur, curk = tt, tk
        if x_out is not None:
            P.dma(lambda e, cur=cur, rows=rows: e.dma_start(out=x_out[rows, :], in_=cur[:]), rd=[curk])
        if wpre is not None:
            sq, sqk = sqr.next()
            st, stk = str_.next()
            P.act(lambda e, sq=sq, cur=cur, st=st: e.activation(out=sq[:], in_=cur[:], func=AF.Square,
                                                                accum_out=st[:, 0:1]),
                  rd=[curk], wr=[sqk, stk])
            emit_rstd(P, st[:, 0:1], st[:, 1:2], stk, stk, 128, D)
            ht, hk = hr.next()
            P.dve(lambda e, ht=ht, cur=cur, st=st: e.scalar_tensor_tensor(
                out=ht[:], in0=cur[:], scalar=st[:, 1:2], in1=wpre_t[:], op0=ALU.mult, op1=ALU.mult),
                rd=[curk, stk, wpre_k], wr=[hk])
            if i % 4 == 0:
                hT, hTk = htr.next()
            tsl = slice((i % 4) * 128, (i % 4) * 128 + 128)
            for half in range(2):
                pt, ptk = ptr.next()
                for j in range(8):
                    kc = half * 8 + j
                    P.pe(lambda e, pt=pt, ht=ht, j=j, kc=kc: e.transpose(
                        out=pt[:, j * 128:(j + 1) * 128], in_=ht[:, kc * 128:(kc + 1) * 128],
                        identity=ident_bf[:]), rd=[hk], wr=[ptk])
                src = lambda pt: pt[:].rearrange("p (k t) -> p k t", t=128)
                if half == 0:
                    P.act(lambda e, hT=hT, pt=pt, tsl=tsl: e.copy(out=hT[:, 0:8, tsl], in_=src(pt)),
                          rd=[ptk], wr=[hTk])
                else:
                    P.dve(lambda e, hT=hT, pt=pt, tsl=tsl: e.tensor_copy(out=hT[:, 8:16, tsl], in_=src(pt)),
                          rd=[ptk], wr=[hTk])
            if i % 4 == 3 and sel4 is None:
                blk = i // 4
                P.dma(lambda e, hT=hT, blk=blk: e.dma_start(
                    out=hT_out.rearrange("(k p) t -> p k t", p=128)[:, :, blk * 512:(blk + 1) * 512],
                    in_=hT[:]), rd=[hTk], q="q_pool")
            elif i % 4 == 3:
                blk = i // 4
                for s_ in range(4):
                    sc, sck = scr4.next()
                    if s_ % 2 == 0:
                        P.dve(lambda e, sc=sc, hT=hT, s_=s_: e.tensor_scalar(
                            out=sc[:], in0=hT[:], scalar1=s4[:, s_:s_ + 1], scalar2=None, op0=ALU.mult),
                            rd=[hTk, s4k], wr=[sck])
                    else:
                        P.act(lambda e, sc=sc, hT=hT, s_=s_: e.activation(
                            out=sc[:], in_=hT[:], func=AF.Copy, scale=s4[:, s_:s_ + 1]),
                            rd=[hTk, s4k], wr=[sck])
                    P.dma(lambda e, sc=sc, blk=blk, s_=s_: e.dma_start(
                        out=hT_out.rearrange("(s b p) f -> p s b f", s=4, p=128)[:, s_, blk, :],
                        in_=sc[:].rearrange("p k t -> p (k t)")), rd=[sck], q="q_pool" if s_ % 2 else "q_sp")


def make_ident(nc, P, stack):
    idf = stack.enter_context(nc.sbuf_tensor(U("ident_f"), [128, 128], F32))
    idb = stack.enter_context(nc.sbuf_tensor(U("ident_b"), [128, 128], BF16))
    k = Tok("ident")
    P.pool(lambda e: e.memset(idf[:], 1.0), wr=[k])
    P.pool(lambda e: e.affine_select(out=idf[:], in_=idf[:], pattern=[[-1, 128]], compare_op=ALU.is_equal,
                                     fill=0.0, base=0, channel_multiplier=1), rd=[k], wr=[k])
    P.dve(lambda e: e.tensor_copy(out=idb[:], in_=idf[:]), rd=[k], wr=[k])
    return idf, idb, k


D_FF = 5632
KC_F = D_FF // 128


def cast_op(P, idx, out, in_, rd, wr):
    if idx % 2 == 0:
        P.dve(lambda e: e.tensor_copy(out=out, in_=in_), rd=rd, wr=wr)
    else:
        P.act(lambda e: e.copy(out=out, in_=in_), rd=rd, wr=wr)


def phase_ffn_gu(nc, P, stack, T, hT_in, wg_l, wu_l, actT_out):
    sb = nc.sbuf_tensor
    hT = stack.enter_context(sb(U("gu_hT"), [128, KC_D, T], BF16))
    hTk = [Tok(f"gu_hT{k}") for k in range(KC_D // 4)]
    hv = hT_in.rearrange("(k p) t -> p k t", p=128)
    for g in range(KC_D // 4):
        P.dma(lambda e, g=g: e.dma_start(out=hT[:, g * 4:(g + 1) * 4, :], in_=hv[:, g * 4:(g + 1) * 4, :]),
              wr=[hTk[g]], q="q_pool")
    wst = [Ring(stack, sb, f"gu_wst{m}", [128, D], F32, 2) for m in range(2)]
    wbf = [Ring(stack, sb, f"gu_wbf{m}", [128, KC_D, 128], BF16, 2) for m in range(2)]
    ps = [Ring(stack, nc.psum_tensor, f"gu_ps{m}", [128, 512], F32, 3) for m in range(2)]
    sgr = Ring(stack, sb, "gu_sg", [128, 512], F32, 2)
    ar = Ring(stack, sb, "gu_a", [128, 512], BF16, 4)
    nsl = T // 512
    ci = 0
    for fc in range(KC_F):
        wb = []
        for m, wl in enumerate((wg_l, wu_l)):
            st_, stk = wst[m].next()
            P.dma(lambda e, st_=st_, wl=wl, fc=fc: e.dma_start(out=st_[:], in_=wl[fc]), wr=[stk])
            b, bk = wbf[m].next()
            cast_op(P, ci, b[:].rearrange("p k j -> p (k j)"), st_[:], [stk], [bk])
            ci += 1
            wb.append((b, bk))
        for sl in range(nsl):
            tsl = slice(sl * 512, (sl + 1) * 512)
            pp = []
            for m in range(2):
                p_, pk = ps[m].next()
                b, bk = wb[m]
                for kc in range(KC_D):
                    P.pe(lambda e, p_=p_, b=b, kc=kc, tsl=tsl: e.matmul(
                        p_[:], lhsT=b[:, kc, :], rhs=hT[:, kc, tsl], start=(kc == 0), stop=(kc == KC_D - 1)),
                        rd=[bk, hTk[kc // 4]], wr=[pk])
                pp.append((p_, pk))
            sg, sgk = sgr.next()
            P.act(lambda e, sg=sg, p_=pp[0][0]: e.activation(out=sg[:], in_=p_[:], func=AF.Silu),
                  rd=[pp[0][1]], wr=[sgk])
            a, ak = ar.next()
            P.dve(lambda e, a=a, sg=sg, p_=pp[1][0]: e.tensor_tensor(out=a[:], in0=sg[:], in1=p_[:], op=ALU.mult),
                  rd=[sgk, pp[1][1]], wr=[ak])
            P.dma(lambda e, a=a, fc=fc, tsl=tsl: e.dma_start(out=actT_out[fc * 128:(fc + 1) * 128, tsl], in_=a[:]),
                  rd=[ak], q="q_pool")


def phase_mmT(nc, P, stack, T, K, AT_in, w_l, y_out, Th=1024, pfx="mt", halves=None, htoks=None):
    KC = K // 128
    G = 4
    NG = KC // G
    Th = min(Th, T, 1024)
    NT = Th // 128
    sb = nc.sbuf_tensor
    AT = stack.enter_context(sb(U(f"{pfx}_AT"), [128, KC, Th], BF16))
    ATk = [Tok(f"{pfx}_AT{g}") for g in range(NG)]
    wst = Ring(stack, sb, f"{pfx}_wst", [128, G, 512], F32, 3)
    wbf = Ring(stack, sb, f"{pfx}_wbf", [128, G, 512], BF16, 4)
    ps = Ring(stack, nc.psum_tensor, f"{pfx}_ps", [128, 512], F32, 8)
    ys = Ring(stack, sb, f"{pfx}_ys", [128, 512], F32, 4)
    av = AT_in.rearrange("(k p) t -> p k t", p=128)
    ci = 0
    ei = 0
    if halves is None:
        order = [(th, s) for th in range(T // Th) for s in range(4)]
    else:
        order = [(th, s) for s in range(4) for th in range(T // Th)]
    last_th = None
    for (th, s) in order:
        if th != last_th:
            last_th = th
            for g in range(NG):
                P.dma(lambda e, g=g, th=th: e.dma_start(out=AT[:, g * G:(g + 1) * G, :],
                                                        in_=av[:, g * G:(g + 1) * G, th * Th:(th + 1) * Th]),
                      wr=[ATk[g]], q="q_pool")
        if True:
            banks = [ps.next() for _ in range(NT)]
            for g in range(NG):
                st_, stk = wst.next()
                P.dma(lambda e, st_=st_, s=s, g=g: e.dma_start(out=st_[:], in_=w_l[s, :, g * G:(g + 1) * G, :]),
                      wr=[stk])
                wb, wbk = wbf.next()
                cast_op(P, ci, wb[:], st_[:], [stk], [wbk])
                ci += 1
                for tl in range(NT):
                    p_, pk = banks[tl]
                    for kk in range(G):
                        kc = g * G + kk
                        P.pe(lambda e, p_=p_, kc=kc, kk=kk, tl=tl, wb=wb: e.matmul(
                            p_[:], lhsT=AT[:, kc, tl * 128:(tl + 1) * 128], rhs=wb[:, kk, :],
                            start=(kc == 0), stop=(kc == KC - 1)),
                            rd=[ATk[g], wbk], wr=[pk])
            for tl in range(NT):
                p_, pk = banks[tl]
                y_, yk = ys.next()
                if ei % 2 == 0:
                    P.act(lambda e, y_=y_, p_=p_: e.copy(out=y_[:], in_=p_[:]), rd=[pk], wr=[yk])
                else:
                    P.dve(lambda e, y_=y_, p_=p_: e.tensor_copy(out=y_[:], in_=p_[:]), rd=[pk], wr=[yk])
                ei += 1
                r0 = th * Th + tl * 128
                if halves is None:
                    P.dma(lambda e, y_=y_, r0=r0, s=s: e.dma_start(out=y_out[r0:r0 + 128, s * 512:(s + 1) * 512],
                                                                   in_=y_[:]), rd=[yk], q="q_pool")
                else:
                    dst = halves[s // 2]
                    tk_ = Tok("yhalf")
                    htoks[s // 2].append(tk_)
                    P.dma(lambda e, y_=y_, r0=r0, s=s, dst=dst: e.dma_start(
                        out=dst[r0:r0 + 128, (s % 2) * 512:(s % 2 + 1) * 512], in_=y_[:]), rd=[yk], wr=[tk_],
                        q="q_pool")


SEQ = 8192
NBLK = SEQ // 512


def make_consts(nc, P, stack):
    sb = nc.sbuf_tensor
    c = {}
    c["ones_bf"] = stack.enter_context(sb(U("ones_bf"), [128, 128], BF16))
    c["ones_f"] = stack.enter_context(sb(U("ones_f"), [128, 128], F32))
    c["sel2"] = stack.enter_context(sb(U("sel2"), [128, 2], BF16))
    k = Tok("consts")
    c["tok"] = k
    P.pool(lambda e: e.memset(c["ones_bf"][:], 1.0), wr=[k])
    P.pool(lambda e: e.memset(c["ones_f"][:], 1.0), wr=[k])
    P.pool(lambda e: e.memset(c["sel2"][:], 0.0), wr=[k])
    P.pool(lambda e: e.memset(c["sel2"][0:64, 0:1], 1.0), wr=[k])
    P.pool(lambda e: e.memset(c["sel2"][64:128, 1:2], 1.0), wr=[k])
    return c


def phase_attn_proj(nc, P, stack, hT_blk, w_l, qTd, kTd, Vd, negMd, idf, idb, C, S=SEQ, hT_tok=None):
    sb = nc.sbuf_tensor
    ps = nc.psum_tensor
    NB = S // 512
    ones_f, sel2, ck = C["ones_f"], C["sel2"], C["tok"]
    W = stack.enter_context(sb(U("ap_w"), [128, KC_D, 12 * 128], BF16))
    Wk = [Tok(f"ap_w{f}") for f in range(12)]
    wst = Ring(stack, sb, "ap_wst", [128, D], F32, 2)
    for fc in range(12):
        st_, stk = wst.next()
        P.dma(lambda e, st_=st_, fc=fc: e.dma_start(out=st_[:], in_=w_l[fc]), wr=[stk], q="q_pool")
        cast_op(P, fc, W[:, :, fc * 128:(fc + 1) * 128], st_[:].rearrange("p (k j) -> p k j", j=128), [stk], [Wk[fc]])
    hr = Ring(stack, sb, "ap_h", [128, KC_D, 512], BF16, 2)
    pw = Ring(stack, ps, "ap_pw", [128, 512], F32, 5)
    pm = Ring(stack, ps, "ap_pm", [128, 512], F32, 2)
    osr = Ring(stack, sb, "ap_os", [128, 512], BF16, 6)
    sqr = Ring(stack, sb, "ap_sq", [128, 512], BF16, 3)
    vor = Ring(stack, sb, "ap_vo", [128, 4, 128], BF16, 3)
    nmx = stack.enter_context(sb(U("ap_nmx"), [2, 8, NB], F32))
    nmxk = Tok("ap_nmx")
    ei = 0
    for b in range(NB):
        h_, hk = hr.next()
        P.dma(lambda e, h_=h_, b=b: e.dma_start(out=h_[:], in_=hT_blk(b)), rd=([hT_tok(b)] if hT_tok else []),
              wr=[hk])
        tsl = slice(b * 512, (b + 1) * 512)
        for ch in range(12):
            which, hl = ch // 4, ch % 4
            p_, pk = pw.next()
            for kc in range(KC_D):
                P.pe(lambda e, p_=p_, ch=ch, kc=kc, h_=h_: e.matmul(
                    p_[:], lhsT=W[:, kc, ch * 128:(ch + 1) * 128], rhs=h_[:, kc, :],
                    start=(kc == 0), stop=(kc == KC_D - 1)), rd=[Wk[ch], hk], wr=[pk])
            o_, ok = osr.next()
            sc = 0.125 if which == 0 else 1.0
            if ei % 2 == 0:
                P.act(lambda e, o_=o_, p_=p_, sc=sc: e.activation(out=o_[:], in_=p_[:], func=AF.Copy, scale=sc),
                      rd=[pk], wr=[ok])
            else:
                P.dve(lambda e, o_=o_, p_=p_, sc=sc: e.tensor_scalar(out=o_[:], in0=p_[:], scalar1=sc, scalar2=None,
                                                                    op0=ALU.mult), rd=[pk], wr=[ok])
            ei += 1
            if which < 2:
                dst = qTd if which == 0 else kTd
                P.dma(lambda e, o_=o_, dst=dst, hl=hl, tsl=tsl: e.dma_start(
                    out=dst[hl * 128:(hl + 1) * 128, tsl], in_=o_[:]), rd=[ok], q="q_pool")
                sq, sqk = sqr.next()
                P.dve(lambda e, sq=sq, o_=o_: e.tensor_tensor(out=sq[:], in0=o_[:], in1=o_[:], op=ALU.mult),
                      rd=[ok], wr=[sqk])
                p2, p2k = pm.next()
                P.pe(lambda e, p2=p2, sq=sq: e.matmul(p2[0:2, :], lhsT=sel2[:], rhs=sq[:], start=True, stop=True),
                     rd=[sqk, ck], wr=[p2k])
                P.dve(lambda e, p2=p2, ch=ch, b=b: e.tensor_reduce(
                    out=nmx[:, ch, b:b + 1], in_=p2[0:2, :], axis=AX.X, op=ALU.max), rd=[p2k], wr=[nmxk])
            else:
                p2, p2k = pm.next()
                p2b = p2[:].bitcast(BF16)
                for j in range(4):
                    P.pe(lambda e, p2b=p2b, o_=o_, j=j: e.transpose(
                        out=p2b[:, j * 128:(j + 1) * 128], in_=o_[:, j * 128:(j + 1) * 128], identity=idb[:]),
                        rd=[ok], wr=[p2k])
                vo, vok = vor.next()
                P.act(lambda e, vo=vo, p2b=p2b: e.copy(out=vo[:], in_=p2b[:, 0:512].rearrange("p (j d) -> p j d", d=128)),
                      rd=[p2k], wr=[vok])
                P.dma(lambda e, vo=vo, hl=hl, b=b: e.dma_start(out=Vd[hl, :, b * 4:(b + 1) * 4, :], in_=vo[:]),
                      rd=[vok], q="q_pool")
    msc = stack.enter_context(sb(U("ap_msc"), [2, 32], F32))
    P.dve(lambda e: e.tensor_reduce(out=msc[:, 0:8], in_=nmx[:], axis=AX.X, op=ALU.max), rd=[nmxk], wr=[nmxk])
    P.dve(lambda e: e.tensor_tensor(out=msc[:, 8:12], in0=msc[:, 0:4], in1=msc[:, 4:8], op=ALU.mult),
          rd=[nmxk], wr=[nmxk])
    P.act(lambda e: e.activation(out=msc[:, 12:16], in_=msc[:, 8:12], func=AF.Ln), rd=[nmxk], wr=[nmxk])
    P.act(lambda e: e.activation(out=msc[:, 12:16], in_=msc[:, 12:16], func=AF.Exp, scale=0.5), rd=[nmxk], wr=[nmxk])
    for hl in range(4):
        P.dve(lambda e, hl=hl: e.tensor_scalar(out=msc[:, 16 + hl * 2:18 + hl * 2], in0=idf[0:2, 0:2],
                                               scalar1=msc[:, 12 + hl:13 + hl], scalar2=-1.02,
                                               op0=ALU.mult, op1=ALU.mult), rd=[nmxk], wr=[nmxk])
    p2, p2k = pm.next()
    P.pe(lambda e: e.matmul(p2[:, 0:8], lhsT=ones_f[0:2, :], rhs=msc[:, 16:24], start=True, stop=True),
         rd=[nmxk, ck], wr=[p2k])
    nm = stack.enter_context(sb(U("ap_nm"), [128, 8], F32))
    nmk = Tok("ap_nm")
    P.dve(lambda e: e.tensor_copy(out=nm[:], in_=p2[:, 0:8]), rd=[p2k], wr=[nmk])
    P.dma(lambda e: e.dma_start(out=negMd, in_=nm[:]), rd=[nmk])


def phase_attn_core(nc, P, stack, qTd, kTd, Vd, negMd, lam_in, subln_in, lambda_init, oT_out, idf, idb, C, S=SEQ):
    sb = nc.sbuf_tensor
    ps = nc.psum_tensor
    NB = S // 512
    ones_bf, ones_f, sel2, ck = C["ones_bf"], C["ones_f"], C["sel2"], C["tok"]
    lam4 = stack.enter_context(sb(U("at_lam4"), [128, 4, 64], F32))
    lamk = Tok("lam")
    P.dma(lambda e: e.dma_start(out=lam4[:].rearrange("p a d -> p (a d)"),
                                in_=lam_in.rearrange("a d -> (a d)").partition_broadcast(128)), wr=[lamk])
    lsc = stack.enter_context(sb(U("at_lsc"), [128, 8], F32))
    lpr = stack.enter_context(sb(U("at_lpr"), [128, 2, 64], F32))
    P.dve(lambda e: e.tensor_tensor(out=lpr[:, 0, :], in0=lam4[:, 0, :], in1=lam4[:, 1, :], op=ALU.mult),
          rd=[lamk], wr=[lamk])
    P.dve(lambda e: e.tensor_tensor(out=lpr[:, 1, :], in0=lam4[:, 2, :], in1=lam4[:, 3, :], op=ALU.mult),
          rd=[lamk], wr=[lamk])
    P.dve(lambda e: e.tensor_reduce(out=lsc[:, 0:2], in_=lpr[:], axis=AX.X, op=ALU.add), rd=[lamk], wr=[lamk])
    P.act(lambda e: e.activation(out=lsc[:, 2:4], in_=lsc[:, 0:2], func=AF.Exp), rd=[lamk], wr=[lamk])
    P.dve(lambda e: e.scalar_tensor_tensor(out=lsc[:, 4:5], in0=lsc[:, 3:4], scalar=-float(lambda_init),
                                           in1=lsc[:, 2:3], op0=ALU.add, op1=ALU.subtract), rd=[lamk], wr=[lamk])
    P.dma(lambda e: e.dma_start(out=lsc[:, 5:6], in_=subln_in.rearrange("(p o) -> p o", o=1)), wr=[lamk])
    P.dve(lambda e: e.tensor_scalar(out=lsc[:, 6:7], in0=lsc[:, 5:6], scalar1=1.0 - float(lambda_init),
                                    scalar2=None, op0=ALU.mult), rd=[lamk], wr=[lamk])
    neglam = lsc[:, 4:5]
    sublnw = lsc[:, 6:7]

    qTc = [stack.enter_context(sb(U(f"at_qT{c}"), [128, S], BF16)) for c in range(2)]
    qzk = Tok("at_qz")
    P.dve(lambda e: e.memset(qTc[0][64:128, :], 0.0), wr=[qzk])
    P.dve(lambda e: e.memset(qTc[1][0:64, :], 0.0), wr=[qzk])
    kT = stack.enter_context(sb(U("at_kT"), [128, S], BF16))
    V = stack.enter_context(sb(U("at_V"), [128, S // 128, 128], BF16))
    qk1, kk1, vk1 = Tok("at_q"), Tok("at_k"), Tok("at_v")
    qk_ = [qk1] * NB
    kk_ = [kk1] * NB
    vk_ = [vk1] * NB
    negM = stack.enter_context(sb(U("at_negM"), [128, 8], F32))
    negMk = Tok("negM")
    P.dma(lambda e: e.dma_start(out=negM[:], in_=negMd), wr=[negMk])
    pw = Ring(stack, ps, "at_pw", [128, 512], F32, 3)
    po = [Ring(stack, ps, f"at_po{c}", [128, 512], F32, 1) for c in range(2)]
    pl = [Ring(stack, ps, f"at_pl{c}", [128, 512], F32, 1) for c in range(2)]
    lacc = Ring(stack, sb, "at_la", [128, 512], F32, 4)
    pm = Ring(stack, ps, "at_pm", [128, 512], F32, 1)
    ptr = Ring(stack, sb, "at_pt", [128, 512], BF16, 6)
    e32 = Ring(stack, sb, "at_e32", [128, 512], F32, 8)
    obf = Ring(stack, sb, "at_obf", [128, 512], BF16, 2)
    for hl in range(4):
        hs = slice(hl * 128, (hl + 1) * 128)
        P.dma(lambda e, hl=hl: e.dma_start(out=qTc[0][0:64, :], in_=qTd[hl * 128:hl * 128 + 64, :]),
              rd=[qzk], wr=[qk1])
        P.dma(lambda e, hl=hl: e.dma_start(out=qTc[1][64:128, :], in_=qTd[hl * 128 + 64:hl * 128 + 128, :]),
              rd=[qzk], wr=[qk1], q="q_pool")
        P.dma(lambda e, hs=hs: e.dma_start(out=kT[:], in_=kTd[hs, :]), wr=[kk1])
        P.dma(lambda e, hl=hl: e.dma_start(out=V[:], in_=Vd[hl]), wr=[vk1], q="q_pool")
        steps = [(qt, c, sbk) for qt in range(NB) for c in range(2) for sbk in range(qt * 4 + 4)]
        LA = 2
        inflight = {}
        accs = {}
        tparts = {}
        deferred = []

        def stepA(k):
            qt, c, sbk = steps[k]
            q0 = qt * 512
            rows = slice(c * 64, (c + 1) * 64)
            d = max(0, sbk - qt * 4)
            cs = slice(d * 128, 512)
            w_, wk_ = pw.next()
            P.pe(lambda e: e.matmul(w_[:, cs], lhsT=kT[:, sbk * 128:(sbk + 1) * 128],
                                    rhs=qTc[c][:, q0 + cs.start:q0 + 512], start=True, stop=True),
                 rd=[kk_[sbk // 4], qk_[qt], qzk], wr=[wk_])
            pt, ptk = ptr.next()
            bcol = hl * 2 + c
            P.act(lambda e: e.activation(out=pt[:, cs], in_=w_[:, cs], func=AF.Exp, bias=negM[:, bcol:bcol + 1],
                                         scale=1.0), rd=[wk_, negMk], wr=[ptk])
            if sbk >= qt * 4:
                base = q0 + cs.start - sbk * 128
                P.pool(lambda e: e.affine_select(out=pt[:, cs], in_=pt[:, cs], pattern=[[1, 512 - cs.start]],
                                                 compare_op=ALU.is_ge, fill=0.0, base=base,
                                                 channel_multiplier=-1), rd=[ptk], wr=[ptk])
            inflight[k] = (pt, ptk, cs)

        def epi_c(qt, c, k):
            o_, ok, l_, lk, la, lak = accs.pop((qt, c))

            def part2():
                P.pe(lambda e: e.matmul(l_[:], lhsT=ones_f[:], rhs=la[:], start=False, stop=True),
                     rd=[lak, ck], wr=[lk])
                r_, rk = e32.next()
                P.dve(lambda e: e.reciprocal(out=r_[:], in_=l_[:]), rd=[lk], wr=[rk])
                t_, tk = e32.next()
                P.dve(lambda e: e.tensor_tensor(out=t_[:], in0=o_[:], in1=r_[:], op=ALU.mult), rd=[ok, rk], wr=[tk])
                tparts[(qt, c)] = (t_, tk)
                if c == 1:
                    epi_1(qt, k + 2)
            deferred.append((k + 2, part2))

        def epi_1(qt, k):
            t0, t0k = tparts.pop((qt, 0))
            t1, t1k = tparts.pop((qt, 1))
            of, ofk = e32.next()
            P.dve(lambda e: e.scalar_tensor_tensor(out=of[:], in0=t1[:], scalar=neglam, in1=t0[:],
                                                   op0=ALU.mult, op1=ALU.add), rd=[t0k, t1k, lamk], wr=[ofk])
            sq, sqk2 = e32.next()
            P.act(lambda e: e.activation(out=sq[:], in_=of[:], func=AF.Square), rd=[ofk], wr=[sqk2])

            def epi_2():
                m_, mk = pm.next()
                P.pe(lambda e: e.matmul(m_[:], lhsT=ones_f[:], rhs=sq[:], start=True, stop=True),
                     rd=[sqk2, ck], wr=[mk])
                rs, rsk = e32.next()
                P.act(lambda e: e.activation(out=rs[:], in_=m_[:], func=AF.Ln, bias=EPS, scale=1.0 / 128),
                      rd=[mk], wr=[rsk])
                P.act(lambda e: e.activation(out=rs[:], in_=rs[:], func=AF.Exp, scale=-0.5), rd=[rsk], wr=[rsk])
                ob, obk = obf.next()
                P.dve(lambda e: e.scalar_tensor_tensor(out=ob[:], in0=of[:], scalar=sublnw, in1=rs[:],
                                                       op0=ALU.mult, op1=ALU.mult), rd=[ofk, rsk, lamk], wr=[obk])
                P.dma(lambda e, hl=hl: e.dma_start(out=oT_out[hl * 128:(hl + 1) * 128, qt * 512:(qt + 1) * 512],
                                                   in_=ob[:]), rd=[obk])
            deferred.append((k + 6, epi_2))

        def stepB(k):
            qt, c, sbk = steps[k]
            nsb = qt * 4 + 4
            pt, ptk, cs = inflight.pop(k)
            if sbk == 0:
                o_, ok = po[c].next()
                l_, lk = pl[c].next()
                la, lak = lacc.next()
                accs[(qt, c)] = (o_, ok, l_, lk, la, lak)
            o_, ok, l_, lk, la, lak = accs[(qt, c)]
            P.pe(lambda e: e.matmul(o_[:, cs], lhsT=V[:, sbk, :], rhs=pt[:, cs], start=(sbk == 0),
                                    stop=(sbk == nsb - 1)), rd=[vk_[sbk // 4], ptk], wr=[ok])
            if sbk % 2 == 0:
                P.pe(lambda e: e.matmul(l_[:, cs], lhsT=ones_bf[:], rhs=pt[:, cs], start=(sbk == 0), stop=False),
                     rd=[ck, ptk], wr=[lk])
            elif sbk == 1:
                if cs.start > 0:
                    P.dve(lambda e: e.memset(la[:, 0:cs.start], 0.0), wr=[lak])
                P.dve(lambda e: e.tensor_copy(out=la[:, cs], in_=pt[:, cs]), rd=[ptk], wr=[lak])
            else:
                P.dve(lambda e: e.tensor_tensor(out=la[:, cs], in0=la[:, cs], in1=pt[:, cs], op=ALU.add),
                      rd=[ptk, lak], wr=[lak])
            if sbk == nsb - 1:
                epi_c(qt, c, k)

        ns = len(steps)
        for k in range(ns + LA):
            if k < ns:
                stepA(k)
            if k - LA >= 0:
                stepB(k - LA)
            for item in [d_ for d_ in deferred if d_[0] <= k]:
                deferred.remove(item)
                item[1]()
        for item in deferred:
            item[1]()


NZX = 20


def phase_ssd_in(nc, P, stack, hT_blk, w_l, wdt_l, dtb_in, zxT_out, dt_out, S=SEQ, hT_tok=None):
    sb = nc.sbuf_tensor
    NB = S // 512
    W = stack.enter_context(sb(U("si_w"), [128, KC_D, NZX * 128], BF16))
    Wk = [Tok(f"si_w{f}") for f in range(NZX)]
    wst = Ring(stack, sb, "si_wst", [128, D], F32, 2)
    for fc in range(NZX):
        st_, stk = wst.next()
        P.dma(lambda e, st_=st_, fc=fc: e.dma_start(out=st_[:], in_=w_l[fc]), wr=[stk])
        cast_op(P, fc, W[:, :, fc * 128:(fc + 1) * 128], st_[:].rearrange("p (k j) -> p k j", j=128), [stk], [Wk[fc]])
    wdtf = stack.enter_context(sb(U("si_wdtf"), [128, KC_D, 16], F32))
    wdt = stack.enter_context(sb(U("si_wdt"), [128, KC_D, 16], BF16))
    wdk = Tok("si_wdt")
    P.dma(lambda e: e.dma_start(out=wdtf[:], in_=wdt_l), wr=[wdk])
    P.dve(lambda e: e.tensor_copy(out=wdt[:], in_=wdtf[:]), rd=[wdk], wr=[wdk])
    dtb = stack.enter_context(sb(U("si_dtb"), [128, 16], F32))
    dtbk = Tok("si_dtb")
    P.dma(lambda e: e.dma_start(out=dtb[:], in_=dtb_in.partition_broadcast(128)), wr=[dtbk])
    hr = Ring(stack, sb, "si_h", [128, KC_D, 512], BF16, 2)
    pw = Ring(stack, nc.psum_tensor, "si_pw", [128, 512], F32, 4)
    pd = Ring(stack, nc.psum_tensor, "si_pd", [128, 16], F32, 2)
    osr = Ring(stack, sb, "si_os", [128, 512], F32, 4)
    dr = Ring(stack, sb, "si_d", [128, 4, 16], F32, 6)
    dto = Ring(stack, sb, "si_dto", [128, 4, 16], F32, 2)
    ei = 0
    for b in range(NB):
        h_, hk = hr.next()
        P.dma(lambda e, h_=h_, b=b: e.dma_start(out=h_[:], in_=hT_blk(b)), rd=([hT_tok(b)] if hT_tok else []),
              wr=[hk])
        tsl = slice(b * 512, (b + 1) * 512)
        for fc in range(NZX):
            p_, pk = pw.next()
            for kc in range(KC_D):
                P.pe(lambda e, p_=p_, fc=fc, kc=kc, h_=h_: e.matmul(
                    p_[:], lhsT=W[:, kc, fc * 128:(fc + 1) * 128], rhs=h_[:, kc, :],
                    start=(kc == 0), stop=(kc == KC_D - 1)), rd=[Wk[fc], hk], wr=[pk])
            o_, ok = osr.next()
            if ei % 2 == 0:
                P.act(lambda e, o_=o_, p_=p_: e.copy(out=o_[:], in_=p_[:]), rd=[pk], wr=[ok])
            else:
                P.dve(lambda e, o_=o_, p_=p_: e.tensor_copy(out=o_[:], in_=p_[:]), rd=[pk], wr=[ok])
            ei += 1
            P.dma(lambda e, o_=o_, fc=fc, tsl=tsl: e.dma_start(out=zxT_out[fc * 128:(fc + 1) * 128, tsl], in_=o_[:]),
                  rd=[ok])
        x_, xk = dr.next()
        for j in range(4):
            p_, pk = pd.next()
            for kc in range(KC_D):
                P.pe(lambda e, p_=p_, kc=kc, h_=h_, j=j: e.matmul(
                    p_[:], lhsT=h_[:, kc, j * 128:(j + 1) * 128], rhs=wdt[:, kc, :],
                    start=(kc == 0), stop=(kc == KC_D - 1)), rd=[wdk, hk], wr=[pk])
            P.dve(lambda e, x_=x_, p_=p_, j=j: e.tensor_tensor(out=x_[:, j, :], in0=p_[:], in1=dtb[:], op=ALU.add),
                  rd=[pk, dtbk], wr=[xk])
        a_, ak = dr.next()
        P.dve(lambda e, a_=a_, x_=x_: e.scalar_tensor_tensor(out=a_[:], in0=x_[:], scalar=-1.0, in1=x_[:],
                                                             op0=ALU.mult, op1=ALU.max), rd=[xk], wr=[ak])
        P.act(lambda e, a_=a_: e.activation(out=a_[:], in_=a_[:], func=AF.Exp, scale=-1.0), rd=[ak], wr=[ak])
        P.act(lambda e, a_=a_: e.activation(out=a_[:], in_=a_[:], func=AF.Ln, bias=1.0, scale=1.0), rd=[ak], wr=[ak])
        d_, dk = dto.next()
        P.dve(lambda e, d_=d_, x_=x_, a_=a_: e.scalar_tensor_tensor(
            out=d_[:], in0=x_[:], scalar=0.0, in1=a_[:], op0=ALU.max, op1=ALU.add), rd=[xk, ak], wr=[dk])
        P.dma(lambda e, d_=d_, b=b: e.dma_start(
            out=dt_out[b * 512:(b + 1) * 512, :].rearrange("(j p) h -> p j h", p=128), in_=d_[:]), rd=[dk])


def phase_ssd_scan(nc, P, stack, zxT_in, dt_in, convw_in, convb_in, alog_in, dsk_in, normw_in, yT_out,
                   idf, idb, C, S=SEQ):
    sb = nc.sbuf_tensor
    ps = nc.psum_tensor
    ones_f, ck = C["ones_f"], C["tok"]
    TB = 256
    NB = S // TB
    zv = zxT_in.rearrange("(k p) t -> p k t", p=128)
    tri = stack.enter_context(sb(U("ss_tri"), [128, 128], F32))
    cst = Tok("ss_const")
    P.pool(lambda e: e.memset(tri[:], 1.0), wr=[cst])
    P.pool(lambda e: e.affine_select(out=tri[:], in_=tri[:], pattern=[[1, 128]], compare_op=ALU.is_ge,
                                     fill=0.0, base=0, channel_multiplier=-1), rd=[cst], wr=[cst])
    cw = stack.enter_context(sb(U("ss_cw"), [128, 12, 4], F32))
    cb = stack.enter_context(sb(U("ss_cb"), [128, 12], F32))
    nw = stack.enter_context(sb(U("ss_nw"), [128, 8], F32))
    abc = stack.enter_context(sb(U("ss_abc"), [128, 16], F32))
    d16 = stack.enter_context(sb(U("ss_d16"), [128, 16], F32))
    Dbc = stack.enter_context(sb(U("ss_Dbc"), [128, 16, 64], F32))
    P.dma(lambda e: e.dma_start(out=cw[:], in_=convw_in), wr=[cst])
    P.dma(lambda e: e.dma_start(out=cb[:], in_=convb_in), wr=[cst])
    P.dma(lambda e: e.dma_start(out=nw[:], in_=normw_in), wr=[cst])
    P.dma(lambda e: e.dma_start(out=abc[:], in_=alog_in.partition_broadcast(128)), wr=[cst])
    P.dma(lambda e: e.dma_start(out=d16[:], in_=dsk_in.partition_broadcast(128)), wr=[cst])
    P.act(lambda e: e.activation(out=abc[:], in_=abc[:], func=AF.Exp), rd=[cst], wr=[cst])
    P.dve(lambda e: e.tensor_scalar(out=abc[:], in0=abc[:], scalar1=-1.0, scalar2=None, op0=ALU.mult),
          rd=[cst], wr=[cst])
    P.dve(lambda e: e.tensor_copy(out=Dbc[:], in_=d16[:].unsqueeze(2).to_broadcast([128, 16, 64])),
          rd=[cst], wr=[cst])
    S32 = [stack.enter_context(sb(U(f"ss_S32{g}"), [128, 512], F32)) for g in range(2)]
    Sbf = [stack.enter_context(sb(U(f"ss_Sbf{g}"), [128, 512], BF16)) for g in range(2)]
    Sk = [Tok(f"ss_S{g}") for g in range(2)]
    Sbk = [Tok(f"ss_Sb{g}") for g in range(2)]
    for g in range(2):
        P.pool(lambda e, g=g: e.memset(S32[g][:], 0.0), wr=[Sk[g]])
        P.pool(lambda e, g=g: e.memset(Sbf[g][:], 0.0), wr=[Sbk[g]])
    rawr = Ring(stack, sb, "ss_raw", [128, 12, TB + 3], F32, 2)
    zr = Ring(stack, sb, "ss_z", [128, 8, TB], F32, 3)
    accr = Ring(stack, sb, "ss_acc", [128, 12, TB], F32, 1)
    xTr = Ring(stack, sb, "ss_xT", [128, 8, TB], F32, 2)
    bcTr = Ring(stack, sb, "ss_bcT", [128, 4, TB], BF16, 3)
    dtr = Ring(stack, sb, "ss_dt", [128, TB // 128, 16], F32, 3)
    oTr = Ring(stack, sb, "ss_oT", [128, 8, TB], BF16, 3)
    xsr = Ring(stack, sb, "ss_xs", [128, 512], F32, 2)
    Btr = Ring(stack, sb, "ss_Bt", [128, 128], BF16, 4)
    smr = Ring(stack, sb, "ss_sm", [128, 48], F32, 5)
    rbr = Ring(stack, sb, "ss_rb", [128, 8, 128], F32, 2)
    segr = Ring(stack, sb, "ss_seg", [128, 8, 128], F32, 2)
    cbmr = Ring(stack, sb, "ss_cbm", [128, 128], BF16, 3)
    ebr = Ring(stack, sb, "ss_eb", [128, 8, 128], BF16, 3)
    Gr = Ring(stack, sb, "ss_G", [128, 8, 128], BF16, 4)
    x32r = Ring(stack, sb, "ss_x32", [128, 512], F32, 2)
    xbr = Ring(stack, sb, "ss_xb", [128, 512], BF16, 4)
    xer = Ring(stack, sb, "ss_xe", [128, 512], BF16, 4)
    y1r = Ring(stack, sb, "ss_y1", [128, 512], F32, 2)
    xdr = Ring(stack, sb, "ss_xd", [128, 512], F32, 4)
    gvr = Ring(stack, sb, "ss_gv", [128, 4, 128], F32, 2)
    sqr = Ring(stack, sb, "ss_sq", [128, 4, 128], F32, 2)
    rsr = Ring(stack, sb, "ss_rs", [128, 128], F32, 2)
    pbc = Ring(stack, ps, "ss_pbc", [128, 1024], F32, 1)
    pm = Ring(stack, ps, "ss_pm", [128, 512], F32, 3)
    pyd = Ring(stack, ps, "ss_pyd", [128, 512], F32, 1)
    pyo = Ring(stack, ps, "ss_pyo", [128, 512], F32, 1)
    pst = Ring(stack, ps, "ss_pst", [128, 512], F32, 1)

    def block_prologue(b):
        t0 = b * TB
        raw, rk = rawr.next()
        if b == 0:
            P.pool(lambda e, raw=raw: e.memset(raw[:, :, 0:3], 0.0), wr=[rk])
            P.dma(lambda e, raw=raw: e.dma_start(out=raw[:, :, 3:], in_=zv[:, 8:20, 0:TB]), wr=[rk])
        else:
            P.dma(lambda e, raw=raw, t0=t0: e.dma_start(out=raw[:], in_=zv[:, 8:20, t0 - 3:t0 + TB]), wr=[rk])
        z_, zk = zr.next()
        P.dma(lambda e, z_=z_, t0=t0: e.dma_start(out=z_[:], in_=zv[:, 0:8, t0:t0 + TB]), wr=[zk], q="q_pool")
        dt_, dtk = dtr.next()
        P.dma(lambda e, dt_=dt_, t0=t0: e.dma_start(
            out=dt_[:], in_=dt_in[t0:t0 + TB, :].rearrange("(j p) h -> p j h", p=128)), wr=[dtk], q="q_pool")
        P.act(lambda e, z_=z_: e.activation(out=z_[:], in_=z_[:], func=AF.Silu), rd=[zk], wr=[zk])
        acc, acck = accr.next()
        acks = [Tok(f"acc{k}") for k in range(12)]
        for w in range(4):
            for k in range(12):
                if w == 0:
                    P.dve(lambda e, k=k, w=w, raw=raw, acc=acc: e.tensor_scalar(
                        out=acc[:, k, :], in0=raw[:, k, w:w + TB], scalar1=cw[:, k, w:w + 1], scalar2=None,
                        op0=ALU.mult), rd=[rk, cst], wr=[acks[k]])
                else:
                    P.dve(lambda e, k=k, w=w, raw=raw, acc=acc: e.scalar_tensor_tensor(
                        out=acc[:, k, :], in0=raw[:, k, w:w + TB], scalar=cw[:, k, w:w + 1], in1=acc[:, k, :],
                        op0=ALU.mult, op1=ALU.add), rd=[rk, cst, acks[k]], wr=[acks[k]])
        xT, xTk = xTr.next()
        bcT, bcTk = bcTr.next()
        for k in range(12):
            if k < 8:
                P.act(lambda e, k=k, xT=xT, acc=acc: e.activation(out=xT[:, k, :], in_=acc[:, k, :], func=AF.Silu,
                                                                  bias=cb[:, k:k + 1], scale=1.0),
                      rd=[acks[k], cst], wr=[xTk])
            else:
                P.act(lambda e, k=k, bcT=bcT, acc=acc: e.activation(out=bcT[:, k - 8, :], in_=acc[:, k, :],
                                                                    func=AF.Silu, bias=cb[:, k:k + 1], scale=1.0),
                      rd=[acks[k], cst], wr=[bcTk])
        oT, oTk = oTr.next()
        return dict(t0=t0, z_=z_, zk=zk, dt_=dt_, dtk=dtk, xT=xT, xTk=xTk, bcT=bcT, bcTk=bcTk, oT=oT, oTk=oTk)

    def front(B_, j, g):
        z_, zk, dt_, dtk, xT, xTk, bcT, bcTk = (B_[k_] for k_ in ('z_', 'zk', 'dt_', 'dtk', 'xT', 'xTk', 'bcT', 'bcTk'))
        cs = slice(j * 128, (j + 1) * 128)
        px, pxk = pm.next()
        for f in range(4):
            P.pe(lambda e, px=px, f=f, g=g, xT=xT, cs=cs: e.transpose(
                out=px[:, f * 128:(f + 1) * 128], in_=xT[:, g * 4 + f, cs], identity=idf[:]),
                rd=[xTk], wr=[pxk])
        xs, xsk = xsr.next()
        P.act(lambda e, xs=xs, px=px: e.copy(out=xs[:], in_=px[:]), rd=[pxk], wr=[xsk])
        pb, pbk = pm.next()
        pbb = pb[:].bitcast(BF16)
        P.pe(lambda e, pbb=pbb, bcT=bcT, g=g, cs=cs: e.transpose(out=pbb[:, 0:128], in_=bcT[:, g, cs],
                                                                 identity=idb[:]), rd=[bcTk], wr=[pbk])
        Bt, Btk = Btr.next()
        P.dve(lambda e, Bt=Bt, pbb=pbb: e.tensor_copy(out=Bt[:], in_=pbb[:, 0:128]), rd=[pbk], wr=[Btk])
        sm, smk = smr.next()
        kdA, kacol, keacol, keal, kdte = (Tok(n_) for n_ in ('dA', 'acol', 'eacol', 'eal', 'dte'))
        dtg = dt_[:, j, g * 8:(g + 1) * 8]
        P.dve(lambda e, sm=sm, dtg=dtg, g=g: e.tensor_tensor(out=sm[:, 0:8], in0=dtg,
                                                             in1=abc[:, g * 8:(g + 1) * 8], op=ALU.mult),
              rd=[dtk, cst], wr=[smk, kdA])
        pa, pak = pm.next()
        P.pe(lambda e, pa=pa, sm=sm: e.matmul(pa[:, 0:8], lhsT=tri[:], rhs=sm[:, 0:8], start=True, stop=True),
             rd=[kdA, cst], wr=[pak])
        P.act(lambda e, sm=sm, pa=pa: e.copy(out=sm[:, 8:16], in_=pa[:, 0:8]), rd=[pak, smk], wr=[kacol])
        P.act(lambda e, sm=sm, pa=pa: e.activation(out=sm[:, 16:24], in_=pa[:, 0:8], func=AF.Exp),
              rd=[pak, smk], wr=[keacol])
        rb, rbk = rbr.next()
        P.dve(lambda e, rb=rb, sm=sm: e.tensor_tensor(
            out=rb[:], in0=tri[:].unsqueeze(1).to_broadcast([128, 8, 128]),
            in1=sm[:, 0:8].unsqueeze(2).to_broadcast([128, 8, 128]), op=ALU.mult),
            rd=[kdA, cst], wr=[rbk])
        bc, bck = pbc.next()
        for hh in range(2):
            P.pe(lambda e, bc=bc, rb=rb, hh=hh: e.matmul(
                bc[:, hh * 512:(hh + 1) * 512], lhsT=ones_f[:],
                rhs=rb[:, hh * 4:(hh + 1) * 4, :].rearrange("p h l -> p (h l)"), start=True, stop=True),
                rd=[rbk, ck], wr=[bck])
        bc3 = bc[:].rearrange("p (h l) -> p h l", l=128)
        seg, segk = segr.next()
        for h in range(8):
            P.dve(lambda e, seg=seg, bc3=bc3, sm=sm, h=h: e.tensor_scalar(
                out=seg[:, h, :], in0=bc3[:, h, :], scalar1=sm[:, 8 + h:9 + h], scalar2=0.0,
                op0=ALU.subtract, op1=ALU.min), rd=[bck, kacol], wr=[segk])
        eb, ebk = ebr.next()
        P.act(lambda e, seg=seg, eb=eb: e.activation(out=eb[:], in_=seg[:], func=AF.Exp), rd=[segk], wr=[ebk])
        P.act(lambda e, sm=sm, bc3=bc3: e.activation(out=sm[:, 24:32], in_=bc3[:, :, 127], func=AF.Exp),
              rd=[bck, smk], wr=[keal])
        P.dve(lambda e, sm=sm, bc3=bc3: e.tensor_tensor(out=sm[:, 32:40], in0=bc3[:, :, 127],
                                                        in1=sm[:, 8:16], op=ALU.subtract),
              rd=[bck, kacol, smk], wr=[kdte])
        P.act(lambda e, sm=sm: e.activation(out=sm[:, 32:40], in_=sm[:, 32:40], func=AF.Exp),
              rd=[kdte], wr=[kdte])
        pc, pck = pm.next()
        P.pe(lambda e, pc=pc, bcT=bcT, g=g, cs=cs: e.matmul(
            pc[:, 0:128], lhsT=bcT[:, g, cs], rhs=bcT[:, 2 + g, cs], start=True, stop=True),
            rd=[bcTk], wr=[pck])
        cbm, cbmk = cbmr.next()
        P.dve(lambda e, cbm=cbm, pc=pc: e.tensor_tensor(out=cbm[:], in0=pc[:, 0:128], in1=tri[:], op=ALU.mult),
              rd=[pck, cst], wr=[cbmk])
        G, Gk = Gr.next()
        P.dve(lambda e, G=G, eb=eb, cbm=cbm: e.tensor_tensor(
            out=G[:], in0=eb[:], in1=cbm[:].unsqueeze(1).to_broadcast([128, 8, 128]), op=ALU.mult),
            rd=[ebk, cbmk], wr=[Gk])
        x32, x32k = x32r.next()
        xs3 = lambda t: t[:].rearrange("p (h d) -> p h d", d=64)
        P.pool(lambda e, x32=x32, xs=xs, dtg=dtg: e.tensor_tensor(
            out=xs3(x32), in0=xs3(xs), in1=dtg.unsqueeze(2).to_broadcast([128, 8, 64]), op=ALU.mult),
            rd=[xsk, dtk], wr=[x32k])
        xb, xbk = xbr.next()
        P.act(lambda e, xb=xb, x32=x32: e.copy(out=xb[:], in_=x32[:]), rd=[x32k], wr=[xbk])
        xe, xek = xer.next()
        P.dve(lambda e, xe=xe, x32=x32, sm=sm: e.tensor_tensor(
            out=xs3(xe), in0=xs3(x32), in1=sm[:, 32:40].unsqueeze(2).to_broadcast([128, 8, 64]),
            op=ALU.mult), rd=[x32k, kdte], wr=[xek])
        xd, xdk = xdr.next()
        P.pool(lambda e, xd=xd, xs=xs, g=g: e.tensor_tensor(
            out=xs3(xd), in0=xs3(xs), in1=Dbc[:, g * 8:(g + 1) * 8, :], op=ALU.mult),
            rd=[xsk, cst], wr=[xdk])
        return dict(B_=B_, j=j, g=g, cs=cs, sm=sm, smk=smk, keacol=keacol, keal=keal, Bt=Bt, Btk=Btk, G=G, Gk=Gk, xb=xb, xbk=xbk, xe=xe, xek=xek,
                    xd=xd, xdk=xdk, xs3=xs3)

    def back(F_):
        keacol, keal = F_['keacol'], F_['keal']
        B_, j, g, cs, sm, smk, Bt, Btk, G, Gk, xb, xbk, xe, xek, xd, xdk, xs3 = (F_[k_] for k_ in (
            'B_', 'j', 'g', 'cs', 'sm', 'smk', 'Bt', 'Btk', 'G', 'Gk', 'xb', 'xbk', 'xe', 'xek', 'xd', 'xdk', 'xs3'))
        z_, zk, bcT, bcTk, oT, oTk = (B_[k_] for k_ in ('z_', 'zk', 'bcT', 'bcTk', 'oT', 'oTk'))
        yd, ydk = pyd.next()
        for h in range(8):
            P.pe(lambda e, yd=yd, G=G, xb=xb, h=h: e.matmul(
                yd[:, h * 64:(h + 1) * 64], lhsT=G[:, h, :], rhs=xb[:, h * 64:(h + 1) * 64],
                start=True, stop=True), rd=[Gk, xbk], wr=[ydk])
        yo, yok = pyo.next()
        P.pe(lambda e, yo=yo, bcT=bcT, g=g, cs=cs: e.matmul(
            yo[:], lhsT=bcT[:, 2 + g, cs], rhs=Sbf[g][:], start=True, stop=True),
            rd=[bcTk, Sbk[g]], wr=[yok])
        y1, y1k = y1r.next()
        P.dve(lambda e, y1=y1, yo=yo, sm=sm: e.tensor_tensor(
            out=xs3(y1), in0=yo[:].rearrange("p (h d) -> p h d", d=64),
            in1=sm[:, 16:24].unsqueeze(2).to_broadcast([128, 8, 64]), op=ALU.mult),
            rd=[yok, keacol], wr=[y1k])
        P.dve(lambda e, y1=y1, yd=yd: e.tensor_tensor(out=y1[:], in0=y1[:], in1=yd[:], op=ALU.add),
              rd=[y1k, ydk], wr=[y1k])
        P.dve(lambda e, y1=y1, xd=xd: e.tensor_tensor(out=y1[:], in0=y1[:], in1=xd[:], op=ALU.add),
              rd=[y1k, xdk], wr=[y1k])
        st_, stk = pst.next()
        P.pe(lambda e, st_=st_, Bt=Bt, xe=xe: e.matmul(st_[:], lhsT=Bt[:], rhs=xe[:], start=True, stop=True),
             rd=[Btk, xek], wr=[stk])
        P.dve(lambda e, g=g, sm=sm: e.tensor_tensor(
            out=xs3(S32[g]), in0=xs3(S32[g]), in1=sm[:, 24:32].unsqueeze(2).to_broadcast([128, 8, 64]),
            op=ALU.mult), rd=[Sk[g], keal], wr=[Sk[g]])
        P.dve(lambda e, g=g, st_=st_: e.tensor_tensor(out=S32[g][:], in0=S32[g][:], in1=st_[:], op=ALU.add),
              rd=[Sk[g], stk], wr=[Sk[g]])
        P.act(lambda e, g=g: e.copy(out=Sbf[g][:], in_=S32[g][:]), rd=[Sk[g]], wr=[Sbk[g]])
        py, pyk = pm.next()
        for f in range(4):
            P.pe(lambda e, py=py, y1=y1, f=f: e.transpose(
                out=py[:, f * 128:(f + 1) * 128], in_=y1[:, f * 128:(f + 1) * 128], identity=idf[:]),
                rd=[y1k], wr=[pyk])
        gv, gvk = gvr.next()
        P.dve(lambda e, gv=gv, py=py, z_=z_, g=g, cs=cs: e.tensor_tensor(
            out=gv[:], in0=py[:].rearrange("p (f t) -> p f t", t=128), in1=z_[:, g * 4:(g + 1) * 4, cs],
            op=ALU.mult), rd=[pyk, zk], wr=[gvk])
        sq, sqk = sqr.next()
        P.act(lambda e, sq=sq, gv=gv: e.activation(out=sq[:], in_=gv[:], func=AF.Square), rd=[gvk], wr=[sqk])
        pq, pqk = pm.next()
        for f in range(4):
            P.pe(lambda e, pq=pq, sq=sq, f=f: e.matmul(pq[:, 0:128], lhsT=ones_f[:], rhs=sq[:, f, :],
                                                       start=(f == 0), stop=(f == 3)),
                 rd=[sqk, ck], wr=[pqk])
        rs, rsk = rsr.next()
        P.act(lambda e, rs=rs, pq=pq: e.activation(out=rs[:], in_=pq[:, 0:128], func=AF.Ln, bias=EPS,
                                                   scale=1.0 / 512), rd=[pqk], wr=[rsk])
        P.act(lambda e, rs=rs: e.activation(out=rs[:], in_=rs[:], func=AF.Exp, scale=-0.5), rd=[rsk], wr=[rsk])
        for f in range(4):
            P.dve(lambda e, oT=oT, gv=gv, rs=rs, f=f, g=g, cs=cs: e.scalar_tensor_tensor(
                out=oT[:, g * 4 + f, cs], in0=gv[:, f, :], scalar=nw[:, g * 4 + f:g * 4 + f + 1], in1=rs[:],
                op0=ALU.mult, op1=ALU.mult), rd=[gvk, rsk, cst], wr=[oTk])

    def block_epilogue(B_):
        oT, oTk, t0 = B_['oT'], B_['oTk'], B_['t0']
        P.dma(lambda e, oT=oT, t0=t0: e.dma_start(
            out=yT_out.rearrange("(k p) t -> p k t", p=128)[:, :, t0:t0 + TB], in_=oT[:]), rd=[oTk])

    passes = [(b, j, g) for b in range(NB) for j in range(TB // 128) for g in range(2)]
    blocks = {}
    pend = []
    DEPTH_F = 1

    def retire():
        F0 = pend.pop(0)
        back(F0)
        pb, pj, pg = F0["key"]
        if (pj, pg) == (TB // 128 - 1, 1):
            block_epilogue(blocks.pop(pb))

    for (b, j, g) in passes:
        if b not in blocks:
            blocks[b] = block_prologue(b)
        F_ = front(blocks[b], j, g)
        F_["key"] = (b, j, g)
        pend.append(F_)
        if len(pend) > DEPTH_F:
            retire()
    while pend:
        retire()


NCORES = 8
BATCH = 2
TC = BATCH * SEQ // NCORES
DEPTH = 4


def lay_colchunk(W):
    K, N = W.shape
    return np.ascontiguousarray(
        W.reshape(K // 128, 128, N // 128, 128).transpose(2, 1, 0, 3).reshape(N // 128, 128, K))


def lay_slab(W):
    K, N = W.shape
    return np.ascontiguousarray(W.reshape(K // 128, 128, N // 512, 512).transpose(2, 1, 0, 3))


def ssd_core_inputs(inp, j, gl):
    w_in = inp["ssd_w_in"][j]
    cols = np.concatenate([
        np.arange(gl * 1024, (gl + 1) * 1024),
        4096 + np.arange(gl * 1024, (gl + 1) * 1024),
        8192 + np.arange(gl * 256, (gl + 1) * 256),
        9216 + np.arange(gl * 256, (gl + 1) * 256)])
    ch = cols[1024:] - 4096
    dtc = 10240 + np.arange(gl * 16, (gl + 1) * 16)
    return {
        "s_w": lay_colchunk(w_in[:, cols]),
        "s_wdt": np.ascontiguousarray(w_in[:, dtc].reshape(KC_D, 128, 16).transpose(1, 0, 2)),
        "s_dtb": np.ascontiguousarray(inp["ssd_dt_bias"][j, gl * 16:(gl + 1) * 16]),
        "s_cw": np.ascontiguousarray(inp["ssd_conv_w"][j][:, ch].reshape(4, 12, 128).transpose(2, 1, 0)),
        "s_cb": np.ascontiguousarray(inp["ssd_conv_b"][j][ch].reshape(12, 128).T),
        "s_alog": np.ascontiguousarray(inp["ssd_a_log"][j, gl * 16:(gl + 1) * 16]),
        "s_dsk": np.ascontiguousarray(inp["ssd_d"][j, gl * 16:(gl + 1) * 16]),
        "s_nw": np.ascontiguousarray(inp["ssd_norm"][j, gl * 1024:(gl + 1) * 1024].reshape(8, 128).T),
    }


def attn_core_inputs(inp, j, gl):
    w = inp["da_w_qkv"][j]
    cols = np.concatenate([which * D + np.arange(gl * 512, (gl + 1) * 512) for which in range(3)])
    return {
        "a_w": lay_colchunk(w[:, cols]),
        "a_lam": np.ascontiguousarray(np.stack([inp["da_lambda_q1"][j], inp["da_lambda_k1"][j],
                                                inp["da_lambda_q2"][j], inp["da_lambda_k2"][j]])),
        "a_sub": np.ascontiguousarray(inp["da_subln"][j]),
    }


def lambda_init(i):
    return 0.8 - 0.6 * math.exp(-0.3 * i)


def _new_nc():
    _UID[0] = 0
    return bass.Bass("TRN2", target_bir_lowering=False)


def build_first():
    import contextlib
    nc = _new_nc()
    x = nc.dram_tensor("x", [TC, D], F32, kind="ExternalInput").ap()
    wpre = nc.dram_tensor("wpre", [D], F32, kind="ExternalInput").ap()
    xo = nc.dram_tensor("xo", [TC, D], F32, kind="ExternalOutput").ap()
    hT = nc.dram_tensor("hT", [D, TC], BF16, kind="ExternalOutput").ap()
    with contextlib.ExitStack() as st0:
        P = Prog(nc, st0)
        with contextlib.ExitStack() as st:
            idf, idb, _ = make_ident(nc, P, st)
            phase_norm(nc, P, st, TC, x, None, None, wpre, xo, hT, idb)
            P.emit()
    return nc


def hT_blk_fn(hTg):
    v = hTg.rearrange("(r k p) t -> p r k t", r=4, p=128)
    return lambda b: v[:, b // 4, :, (b % 4) * 512:(b % 4 + 1) * 512]


def hT_blk_fn_bm(hTg):
    v = hTg.rearrange("(b p) (k t) -> p b k t", p=128, t=512)
    return lambda b: v[:, b, :, :]


def build_ssd():
    import contextlib
    nc = _new_nc()
    hTg = nc.dram_tensor("hTg", [4 * D, TC], BF16, kind="ExternalInput").ap()
    w = nc.dram_tensor("s_w", [NZX, 128, D], F32, kind="ExternalInput").ap()
    wdt = nc.dram_tensor("s_wdt", [128, KC_D, 16], F32, kind="ExternalInput").ap()
    dtb = nc.dram_tensor("s_dtb", [16], F32, kind="ExternalInput").ap()
    cw = nc.dram_tensor("s_cw", [128, 12, 4], F32, kind="ExternalInput").ap()
    cb = nc.dram_tensor("s_cb", [128, 12], F32, kind="ExternalInput").ap()
    alog = nc.dram_tensor("s_alog", [16], F32, kind="ExternalInput").ap()
    dsk = nc.dram_tensor("s_dsk", [16], F32, kind="ExternalInput").ap()
    nw = nc.dram_tensor("s_nw", [128, 8], F32, kind="ExternalInput").ap()
    zx = nc.dram_tensor("zx", [NZX * 128, SEQ], F32).ap()
    dt = nc.dram_tensor("dt", [SEQ, 16], F32).ap()
    yT = nc.dram_tensor("yT", [1024, SEQ], BF16, kind="ExternalOutput").ap()
    with contextlib.ExitStack() as st0:
        P = Prog(nc, st0)
        with contextlib.ExitStack() as st:
            phase_ssd_in(nc, P, st, hT_blk_fn(hTg), w, wdt, dtb, zx, dt)
            P.emit()
        with contextlib.ExitStack() as st:
            idf, idb, _ = make_ident(nc, P, st)
            C = make_consts(nc, P, st)
            phase_ssd_scan(nc, P, st, zx, dt, cw, cb, alog, dsk, nw, yT, idf, idb, C)
            P.emit()
    return nc


def build_attn(lam_init):
    import contextlib
    nc = _new_nc()
    hTg = nc.dram_tensor("hTg", [4 * D, TC], BF16, kind="ExternalInput").ap()
    w = nc.dram_tensor("a_w", [12, 128, D], F32, kind="ExternalInput").ap()
    lam = nc.dram_tensor("a_lam", [4, 64], F32, kind="ExternalInput").ap()
    sub = nc.dram_tensor("a_sub", [128], F32, kind="ExternalInput").ap()
    oT = nc.dram_tensor("yT", [512, SEQ], BF16, kind="ExternalOutput").ap()
    with contextlib.ExitStack() as st0:
        P = Prog(nc, st0)
        with contextlib.ExitStack() as st:
            idf, idb, _ = make_ident(nc, P, st)
            C = make_consts(nc, P, st)
            qTd = nc.dram_tensor("sc_qT", [512, SEQ], BF16).ap()
            kTd = nc.dram_tensor("sc_kT", [512, SEQ], BF16).ap()
            Vd = nc.dram_tensor("sc_V", [4, 128, SEQ // 128, 128], BF16).ap()
            negMd = nc.dram_tensor("sc_negM", [128, 8], F32).ap()
            phase_attn_proj(nc, P, st, hT_blk_fn(hTg), w, qTd, kTd, Vd, negMd, idf, idb, C)
            P.emit()
        with contextlib.ExitStack() as st:
            idf, idb, _ = make_ident(nc, P, st)
            C = make_consts(nc, P, st)
            phase_attn_core(nc, P, st, qTd, kTd, Vd, negMd, lam, sub, lam_init, oT, idf, idb, C)
            P.emit()
    return nc


def emit_token_phases(nc, P, K, AT, wo, x, npost, nfpre, nfpost, npre_next, wg, wu, wd, xo, hTn, scr):
    import contextlib
    with contextlib.ExitStack() as st:
        phase_mmT(nc, P, st, TC, K, AT, wo, scr["m"], pfx="mo")
        P.emit()
    with contextlib.ExitStack() as st:
        idf, idb, _ = make_ident(nc, P, st)
        phase_norm(nc, P, st, TC, x, scr["m"], npost, nfpre, scr["x1"], scr["h2T"], idb)
        P.emit()
    with contextlib.ExitStack() as st:
        phase_ffn_gu(nc, P, st, TC, scr["h2T"], wg, wu, scr["aT"])
        P.emit()
    with contextlib.ExitStack() as st:
        phase_mmT(nc, P, st, TC, D_FF, scr["aT"], wd, scr["y2"], pfx="md")
        P.emit()
    with contextlib.ExitStack() as st:
        idf, idb, _ = make_ident(nc, P, st)
        phase_norm(nc, P, st, TC, scr["x1"], scr["y2"], nfpost, npre_next, xo, hTn, idb)
        P.emit()


def build_tok(K, last):
    import contextlib
    nc = _new_nc()
    AT = nc.dram_tensor("AT", [K, TC], BF16, kind="ExternalInput").ap()
    wo = nc.dram_tensor("wo", [4, 128, K // 128, 512], F32, kind="ExternalInput").ap()
    x = nc.dram_tensor("x", [TC, D], F32, kind="ExternalInput").ap()
    npost = nc.dram_tensor("npost", [D], F32, kind="ExternalInput").ap()
    nfpre = nc.dram_tensor("nfpre", [D], F32, kind="ExternalInput").ap()
    nfpost = nc.dram_tensor("nfpost", [D], F32, kind="ExternalInput").ap()
    npre_next = None if last else nc.dram_tensor("npre_next", [D], F32, kind="ExternalInput").ap()
    wg = nc.dram_tensor("wg", [KC_F, 128, D], F32, kind="ExternalInput").ap()
    wu = nc.dram_tensor("wu", [KC_F, 128, D], F32, kind="ExternalInput").ap()
    wd = nc.dram_tensor("wd", [4, 128, KC_F, 512], F32, kind="ExternalInput").ap()
    xo = nc.dram_tensor("xo", [TC, D], F32, kind="ExternalOutput").ap()
    hTn = None if last else nc.dram_tensor("hT", [D, TC], BF16, kind="ExternalOutput").ap()
    scr = {"m": nc.dram_tensor("sc_m", [TC, D], F32).ap(), "x1": nc.dram_tensor("sc_x1", [TC, D], F32).ap(),
           "h2T": nc.dram_tensor("sc_h2T", [D, TC], BF16).ap(), "aT": nc.dram_tensor("sc_aT", [D_FF, TC], BF16).ap(),
           "y2": nc.dram_tensor("sc_y2", [TC, D], F32).ap()}
    with contextlib.ExitStack() as st0:
        P = Prog(nc, st0)
        emit_token_phases(nc, P, K, AT, wo, x, npost, nfpre, nfpost, npre_next, wg, wu, wd, xo, hTn, scr)
    return nc


def kernel_multilaunch(**inp):
    inp = {k: np.asarray(v) for k, v in inp.items()}
    cores = list(range(NCORES))
    xs = np.ascontiguousarray(inp["x"].reshape(NCORES, TC, D))
    res = run_bass_kernel_spmd(build_first(), [{"x": xs[c], "wpre": inp["norm_mix_pre"][0]} for c in cores],
                               core_ids=cores)
    xcur = [r["xo"] for r in res.results]
    hT = [np.asarray(r["hT"]) for r in res.results]
    for i in range(DEPTH):
        j = i // 2
        hTg = [np.concatenate(hT[4 * b:4 * b + 4], axis=0) for b in range(BATCH)]
        if i % 2 == 0:
            ncm = build_ssd()
            maps = [dict(ssd_core_inputs(inp, j, c % 4), hTg=hTg[c // 4]) for c in cores]
            K = 4096
            wo = lay_slab(inp["ssd_w_out"][j])
        else:
            ncm = build_attn(lambda_init(i))
            maps = [dict(attn_core_inputs(inp, j, c % 4), hTg=hTg[c // 4]) for c in cores]
            K = 2048
            wo = lay_slab(inp["da_w_out"][j])
        res = run_bass_kernel_spmd(ncm, maps, core_ids=cores)
        yT = [np.asarray(r["yT"]) for r in res.results]
        yall = [np.concatenate(yT[4 * b:4 * b + 4], axis=0) for b in range(BATCH)]
        last = i == DEPTH - 1
        wg, wu, wd = lay_colchunk(inp["ffn_w_gate"][i]), lay_colchunk(inp["ffn_w_up"][i]), lay_slab(inp["ffn_w_down"][i])
        maps = []
        for c in cores:
            m = {"AT": np.ascontiguousarray(yall[c // 4][:, (c % 4) * TC:(c % 4 + 1) * TC]), "wo": wo, "x": xcur[c],
                 "npost": inp["norm_mix_post"][i], "nfpre": inp["norm_ffn_pre"][i], "nfpost": inp["norm_ffn_post"][i],
                 "wg": wg, "wu": wu, "wd": wd}
            if not last:
                m["npre_next"] = inp["norm_mix_pre"][i + 1]
            maps.append(m)
        res = run_bass_kernel_spmd(build_tok(K, last), maps, core_ids=cores)
        xcur = [r["xo"] for r in res.results]
        if not last:
            hT = [np.asarray(r["hT"]) for r in res.results]
    out = np.stack([np.asarray(a) for a in xcur]).reshape(BATCH, SEQ, D).astype(np.float32)
    return out


RG4 = [[0, 1, 2, 3], [4, 5, 6, 7]]
RG8 = [list(range(NCORES))]


def phase_select(nc, P, stack, g8, bsel_in, hTg):
    sb = nc.sbuf_tensor
    w = stack.enter_context(sb(U("sel_w"), [128, 2], F32))
    wk = Tok("sel_w")
    P.dma(lambda e: e.dma_start(out=w[:], in_=bsel_in.partition_broadcast(128)), wr=[wk])
    ar = Ring(stack, sb, "sel_a", [128, KC_D, 512], BF16, 2)
    br = Ring(stack, sb, "sel_b", [128, KC_D, 512], BF16, 2)
    orr = Ring(stack, sb, "sel_o", [128, KC_D, 512], BF16, 2)
    v8 = g8.rearrange("(r k p) t -> p r k t", r=8, p=128)
    vo = hTg.rearrange("(r k p) t -> p r k t", r=4, p=128)
    for r in range(4):
        for tb in range(TC // 512):
            ts = slice(tb * 512, (tb + 1) * 512)
            a, ak = ar.next()
            b, bk = br.next()
            P.dma(lambda e, a=a, r=r, ts=ts: e.dma_start(out=a[:], in_=v8[:, r, :, ts]), wr=[ak])
            P.dma(lambda e, b=b, r=r, ts=ts: e.dma_start(out=b[:], in_=v8[:, 4 + r, :, ts]), wr=[bk], q="q_pool")
            o, ok = orr.next()
            P.dve(lambda e, o=o, a=a: e.tensor_scalar(out=o[:], in0=a[:], scalar1=w[:, 0:1], scalar2=None,
                                                      op0=ALU.mult), rd=[ak, wk], wr=[ok])
            P.dve(lambda e, o=o, b=b: e.scalar_tensor_tensor(out=o[:], in0=b[:], scalar=w[:, 1:2], in1=o[:],
                                                             op0=ALU.mult, op1=ALU.add), rd=[bk, wk, ok], wr=[ok])
            P.dma(lambda e, o=o, r=r, ts=ts: e.dma_start(out=vo[:, r, :, ts], in_=o[:]), rd=[ok])


def build_fused():
    import contextlib
    nc = _new_nc()
    dt_ = nc.dram_tensor
    ext = lambda n, s, d=F32: dt_(n, s, d, kind="ExternalInput").ap()
    x = ext("x", [TC, D])
    sel4 = ext("sel4", [4])
    nmpre, nmpost = ext("nmpre", [DEPTH, D]), ext("nmpost", [DEPTH, D])
    nfpre, nfpost = ext("nfpre", [DEPTH, D]), ext("nfpost", [DEPTH, D])
    ffn = [(ext(f"wg{i}", [KC_F, 128, D]), ext(f"wu{i}", [KC_F, 128, D]), ext(f"wd{i}", [4, 128, KC_F, 512]))
           for i in range(DEPTH)]
    ssd = [dict(w=ext(f"s_w{j}", [NZX, 128, D]), wdt=ext(f"s_wdt{j}", [128, KC_D, 16]), dtb=ext(f"s_dtb{j}", [16]),
                cw=ext(f"s_cw{j}", [128, 12, 4]), cb=ext(f"s_cb{j}", [128, 12]), alog=ext(f"s_alog{j}", [16]),
                dsk=ext(f"s_dsk{j}", [16]), nw=ext(f"s_nw{j}", [128, 8]), wo=ext(f"s_wo{j}", [4, 128, 8, 512]))
           for j in range(2)]
    att = [dict(w=ext(f"a_w{j}", [12, 128, D]), lam=ext(f"a_lam{j}", [4, 64]), sub=ext(f"a_sub{j}", [128]),
                wo=ext(f"a_wo{j}", [4, 128, 4, 512])) for j in range(2)]
    xo = dt_("xo", [TC, D], F32, kind="ExternalOutput").ap()
    scr = lambda n, s, d=F32: dt_(n, s, d).ap()
    hT4 = scr("sc_hT4", [16 * 128, KC_D * 512], BF16)
    hTg = scr("sc_hTg", [16 * 128, KC_D * 512], BF16)
    zx = scr("sc_zx", [NZX * 128, SEQ])
    dtt = scr("sc_dt", [SEQ, 16])
    yT = scr("sc_yT", [1024, SEQ], BF16)
    mpA, mpB = scr("sc_mpA", [SEQ, 1024]), scr("sc_mpB", [SEQ, 1024])
    mA, mB = scr("sc_mA", [TC, 1024]), scr("sc_mB", [TC, 1024])
    qTd = scr("sc_qT", [512, SEQ], BF16)
    kTd = scr("sc_kT", [512, SEQ], BF16)
    Vd = scr("sc_V", [4, 128, SEQ // 128, 128], BF16)
    negMd = scr("sc_negM", [128, 8])
    xa = scr("sc_xa", [TC, D])
    x1 = scr("sc_x1", [TC, D])
    h2T = scr("sc_h2T", [D, TC], BF16)
    aT = scr("sc_aT", [D_FF, TC], BF16)
    y2 = scr("sc_y2", [TC, D])
    with contextlib.ExitStack() as st0:
        P = Prog(nc, st0)
        with contextlib.ExitStack() as st:
            idf, idb, _ = make_ident(nc, P, st)
            phase_norm(nc, P, st, TC, x, None, None, nmpre[0], None, hT4, idb, sel4=sel4)
            P.emit(reorder=True)
        xcur = x
        for i in range(DEPTH):
            j = i // 2
            last = i == DEPTH - 1
            ptoks = [Tok(f"hTg{pc}") for pc in range(8)]
            for pc in range(8):
                rs_ = slice(pc * 256, (pc + 1) * 256)
                P.cc(lambda e, rs_=rs_: e.collective_compute("AllReduce", ALU.add, replica_groups=RG4,
                                                             ins=[hT4[rs_, :]], outs=[hTg[rs_, :]]),
                     wr=[ptoks[pc]])
            hT_tok = lambda b, ptoks=ptoks: ptoks[b // 2]
            if i % 2 == 0:
                s = ssd[j]
                with contextlib.ExitStack() as st:
                    phase_ssd_in(nc, P, st, hT_blk_fn_bm(hTg), s["w"], s["wdt"], s["dtb"], zx, dtt, hT_tok=hT_tok)
                    P.emit(reorder=True)
                with contextlib.ExitStack() as st:
                    idf, idb, _ = make_ident(nc, P, st)
                    C = make_consts(nc, P, st)
                    phase_ssd_scan(nc, P, st, zx, dtt, s["cw"], s["cb"], s["alog"], s["dsk"], s["nw"], yT,
                                   idf, idb, C)
                    P.emit(reorder=True)
                K, yT_use, wo = 1024, yT, s["wo"]
            else:
                a = att[j]
                with contextlib.ExitStack() as st:
                    idf, idb, _ = make_ident(nc, P, st)
                    C = make_consts(nc, P, st)
                    phase_attn_proj(nc, P, st, hT_blk_fn_bm(hTg), a["w"], qTd, kTd, Vd, negMd, idf, idb, C,
                                    hT_tok=hT_tok)
                    P.emit(reorder=True)
                with contextlib.ExitStack() as st:
                    idf, idb, _ = make_ident(nc, P, st)
                    C = make_consts(nc, P, st)
                    phase_attn_core(nc, P, st, qTd, kTd, Vd, negMd, a["lam"], a["sub"], lambda_init(i),
                                    yT[0:512, :], idf, idb, C)
                    P.emit(reorder=True)
                K, yT_use, wo = 512, yT[0:512, :], a["wo"]
            with contextlib.ExitStack() as st:
                htoks = ([], [])
                phase_mmT(nc, P, st, SEQ, K, yT_use, wo, None, Th=2048, pfx="mo", halves=(mpA, mpB), htoks=htoks)
                P.cc(lambda e: e.collective_compute("ReduceScatter", ALU.add, replica_groups=RG4,
                                                    ins=[mpA], outs=[mA]), rd=htoks[0])
                P.cc(lambda e: e.collective_compute("ReduceScatter", ALU.add, replica_groups=RG4,
                                                    ins=[mpB], outs=[mB]), rd=htoks[1])
                P.emit(reorder=True)
            with contextlib.ExitStack() as st:
                idf, idb, _ = make_ident(nc, P, st)
                phase_norm(nc, P, st, TC, xcur, (mA, mB), nmpost[i], nfpre[i], x1, h2T, idb)
                P.emit(reorder=True)
            wg, wu, wd = ffn[i]
            with contextlib.ExitStack() as st:
                phase_ffn_gu(nc, P, st, TC, h2T, wg, wu, aT)
                P.emit(reorder=True)
            with contextlib.ExitStack() as st:
                phase_mmT(nc, P, st, TC, D_FF, aT, wd, y2, pfx="md")
                P.emit(reorder=True)
            with contextlib.ExitStack() as st:
                idf, idb, _ = make_ident(nc, P, st)
                phase_norm(nc, P, st, TC, x1, y2, nfpost[i], None if last else nmpre[i + 1],
                           xo if last else xa, None if last else hT4, idb, sel4=None if last else sel4)
                P.emit(reorder=True)
            xcur = xa
    return nc


def fused_inputs(inp):
    inp = {k: np.asarray(v) for k, v in inp.items()}
    xs = np.ascontiguousarray(inp["x"].reshape(NCORES, TC, D))
    shared = {"nmpre": inp["norm_mix_pre"], "nmpost": inp["norm_mix_post"],
              "nfpre": inp["norm_ffn_pre"], "nfpost": inp["norm_ffn_post"]}
    for i in range(DEPTH):
        shared[f"wg{i}"] = lay_colchunk(inp["ffn_w_gate"][i])
        shared[f"wu{i}"] = lay_colchunk(inp["ffn_w_up"][i])
        shared[f"wd{i}"] = lay_slab(inp["ffn_w_down"][i])
    maps = []
    for c in range(NCORES):
        gl = c % 4
        m = dict(shared)
        m["x"] = xs[c]
        m["sel4"] = np.eye(4, dtype=np.float32)[c % 4]
        for j in range(2):
            for k, v in ssd_core_inputs(inp, j, gl).items():
                m[f"{k}{j}"] = v
            m[f"s_wo{j}"] = lay_slab(inp["ssd_w_out"][j][gl * 1024:(gl + 1) * 1024])
            for k, v in attn_core_inputs(inp, j, gl).items():
                m[f"{k}{j}"] = v
            m[f"a_wo{j}"] = lay_slab(inp["da_w_out"][j][gl * 512:(gl + 1) * 512])
        maps.append(m)
    return maps


def kernel_fused(**inp):
    maps = fused_inputs(inp)
    res = run_bass_kernel_spmd(build_fused(), maps, core_ids=list(range(NCORES)))
    return np.stack([np.asarray(r["xo"]) for r in res.results]).reshape(BATCH, SEQ, D).astype(np.float32)


def kernel(**inputs):
    return kernel_fused(**inputs)
```

```python
import math
import numpy as np
import ml_dtypes
import concourse.bass as bass
import concourse.mybir as mybir
from concourse.bass_utils import run_bass_kernel_spmd

F32 = mybir.dt.float32
BF16 = mybir.dt.bfloat16
AF = mybir.ActivationFunctionType
ALU = mybir.AluOpType
AX = mybir.AxisListType

NDMA_SLOTS = 6


class Tok:
    __slots__ = ("w", "r", "ra", "name")

    def __init__(self, name=""):
        self.w = {}
        self.r = {}
        self.ra = []
        self.name = name


class Prog:
    COMPUTE = ("pe", "act", "dve", "pool", "cc")
    QUEUES = ("q_sp", "q_pool", "q_act")
    ISSUE = {"pe": "pe", "act": "act", "dve": "dve", "pool": "pool", "cc": "pool",
             "q_sp": "sp", "q_pool": "pool", "q_act": "act"}

    def __init__(self, nc, stack):
        self.nc = nc
        self.ops = []
        self.base = 0
        self.cnt = {e: 0 for e in self.COMPUTE}
        self.dcnt = {q: 0 for q in self.QUEUES}
        self.waited = {e: {} for e in ("pe", "act", "dve", "pool", "sp")}
        self.sems = {}
        for e in self.COMPUTE:
            self.sems[e] = stack.enter_context(nc.semaphore(f"s_{e}"))
        for q in self.QUEUES:
            for s in range(NDMA_SLOTS):
                self.sems[(q, s)] = stack.enter_context(nc.semaphore(f"s_{q}{s}"))

    def op(self, eng, fn, rd=(), wr=()):
        self.ops.append((eng, fn, tuple(rd), tuple(wr)))

    def pe(self, fn, rd=(), wr=()):
        self.op("pe", fn, rd, wr)

    def act(self, fn, rd=(), wr=()):
        self.op("act", fn, rd, wr)

    def dve(self, fn, rd=(), wr=()):
        self.op("dve", fn, rd, wr)

    def pool(self, fn, rd=(), wr=()):
        self.op("pool", fn, rd, wr)

    def dma(self, fn, rd=(), wr=(), q="q_sp"):
        self.op(q, fn, rd, wr)

    def cc(self, fn, rd=(), wr=()):
        self.op("cc", fn, rd, wr)

    COST = {"pe": 0.27, "act": 0.5, "dve": 0.55, "pool": 1.2, "cc": 0.1, "q_sp": 0.06, "q_pool": 0.3, "q_act": 0.06}
    DONE_LAT = {"q_sp": 2.5, "q_pool": 3.0, "q_act": 2.5, "cc": 100.0}

    def _schedule(self, ops, order_deps, W=16):
        n = len(ops)
        issue = [self.ISSUE[o[0]] for o in ops]
        per = {e: [] for e in ("pe", "act", "dve", "pool", "sp")}
        for i in range(n):
            per[issue[i]].append(i)
        head = {e: 0 for e in per}
        done = [False] * n
        fin = [0.0] * n
        tfree = {e: 0.0 for e in per}
        order = []
        ready_t = [None] * n
        while len(order) < n:
            best = None
            for e, lst in per.items():
                h = head[e]
                while h < len(lst) and done[lst[h]]:
                    h += 1
                head[e] = h
                cnt = 0
                k = h
                while k < len(lst) and cnt < W:
                    i = lst[k]
                    k += 1
                    if done[i]:
                        continue
                    cnt += 1
                    rt = ready_t[i]
                    if rt is None:
                        ok = True
                        rt = 0.0
                        for d in order_deps[i]:
                            if not done[d]:
                                ok = False
                                break
                            f = fin[d] + (0.0 if issue[d] == e and not ops[d][0].startswith("q_") else 0.25)
                            if f > rt:
                                rt = f
                        if not ok:
                            continue
                        ready_t[i] = rt
                    st = rt if rt > tfree[e] else tfree[e]
                    key = (st, i)
                    if best is None or key < best[0]:
                        best = (key, i, e)
            (st, _), i, e = best
            c = self.COST[ops[i][0]]
            tfree[e] = st + c
            fin[i] = st + c + self.DONE_LAT.get(ops[i][0], 0.0)
            done[i] = True
            order.append(i)
        return order

    def emit(self, reorder=False):
        nc = self.nc
        ops = self.ops
        n = len(ops)
        base = self.base
        deps = [None] * n
        odeps = [None] * n
        signals = [False] * n
        for i, (eng, fn, rd, wr) in enumerate(ops):
            gi = base + i
            is_dma = eng.startswith("q_")
            key = ("dma", gi) if is_dma else eng
            d = set()
            od = set()
            for t in rd:
                for k, j in t.w.items():
                    if j >= base:
                        od.add(j - base)
                    if k == key and key == "pe":
                        continue
                    if j >= base:
                        d.add(j - base)
            for t in wr:
                for k, j in t.w.items():
                    if j >= base:
                        od.add(j - base)
                    if k == key:
                        continue
                    if j >= base:
                        d.add(j - base)
                for k, j in t.ra:
                    if j >= base:
                        od.add(j - base)
                    if k == key:
                        continue
                    if j >= base:
                        d.add(j - base)
            for t in rd:
                t.r[key] = gi
                t.ra.append((key, gi))
            for t in wr:
                t.w = {key: gi}
                t.r = {}
                t.ra = []
            od.discard(i)
            deps[i] = d
            odeps[i] = od
            for j in d:
                signals[j] = True
            if is_dma or eng == "cc":
                signals[i] = True
        if reorder and n > 2:
            order = self._schedule(ops, odeps)
            pos = [0] * n
            for p_, i in enumerate(order):
                pos[i] = p_
            ops = [ops[i] for i in order]
            deps = [{pos[j] for j in deps[i]} for i in order]
            signals = [signals[i] for i in order]
            self.ops = ops
        sig = [None] * n
        dma_idx = [None] * n
        for i, (eng, fn, rd, wr) in enumerate(ops):
            if eng.startswith("q_"):
                k = self.dcnt[eng]
                self.dcnt[eng] += 1
                dma_idx[i] = k
                sig[i] = ((eng, k % NDMA_SLOTS), 16 * (k // NDMA_SLOTS + 1))
            elif signals[i]:
                self.cnt[eng] += 1
                sig[i] = (eng, self.cnt[eng])
        waited = self.waited
        plan = {e: [] for e in ("pe", "act", "dve", "pool", "sp")}
        for i, (eng, fn, rd, wr) in enumerate(ops):
            ie = self.ISSUE[eng]
            ws = []
            need = {}
            for j in deps[i]:
                sk, v = sig[j]
                if need.get(sk, 0) < v:
                    need[sk] = v
            if eng.startswith("q_"):
                k = dma_idx[i]
                if k >= NDMA_SLOTS:
                    sk = (eng, k % NDMA_SLOTS)
                    v = 16 * (k // NDMA_SLOTS)
                    if need.get(sk, 0) < v:
                        need[sk] = v
            for sk, v in need.items():
                if waited[ie].get(sk, 0) < v:
                    waited[ie][sk] = v
                    ws.append((sk, v))
            plan[ie].append((i, ws))
        final = []
        for q, c in self.dcnt.items():
            for s in range(min(c, NDMA_SLOTS)):
                last = 16 * ((c - 1 - s) // NDMA_SLOTS + 1)
                final.append(((q, s), last))
        if self.cnt["cc"]:
            final.append(("cc", self.cnt["cc"]))
        sems = self.sems
        with nc.Block() as block:
            engs = {"pe": block.tensor, "act": block.scalar, "dve": block.vector,
                    "pool": block.gpsimd, "sp": block.sync}

            def mk(ie):
                def body(e):
                    for (i, ws) in plan[ie]:
                        for sk, v in ws:
                            e.wait_ge(sems[sk], v)
                        ins = ops[i][1](e)
                        if sig[i] is not None:
                            sk, v = sig[i]
                            ins.then_inc(sems[sk], 1 if isinstance(sk, str) else 16)
                    if ie == "sp":
                        for sk, v in final:
                            if waited["sp"].get(sk, 0) < v:
                                waited["sp"][sk] = v
                                e.wait_ge(sems[sk], v)
                return body

            for ie in ("sp", "pe", "act", "dve", "pool"):
                if plan[ie] or ie == "sp":
                    engs[ie](mk(ie))
        self.base += n
        self.ops = []


_UID = [0]


def U(name):
    _UID[0] += 1
    return f"{name}_{_UID[0]}"


class Ring:
    def __init__(self, stack, alloc, name, shape, dtype, n):
        self.bufs = []
        for i in range(n):
            t = stack.enter_context(alloc(U(f"{name}{i}"), list(shape), dtype))
            self.bufs.append((t, Tok(f"{name}{i}")))
        self.i = 0

    def next(self):
        b = self.bufs[self.i % len(self.bufs)]
        self.i += 1
        return b


D = 2048
KC_D = D // 128
EPS = 1e-6


def bcast_rows(ap_1d, nparts):
    return ap_1d.partition_broadcast(nparts)


def emit_rstd(P, ss, rstd, tok_ss, tok_rstd, n, width):
    P.act(lambda e: e.activation(out=rstd, in_=ss, func=AF.Ln, bias=EPS, scale=1.0 / width),
          rd=[tok_ss], wr=[tok_rstd])
    P.act(lambda e: e.activation(out=rstd, in_=rstd, func=AF.Exp, scale=-0.5),
          rd=[tok_rstd], wr=[tok_rstd])


def phase_norm(nc, P, stack, T, x_in, y_in, wpost, wpre, x_out, hT_out, ident_bf, sel4=None):
    nt = T // 128
    sb = nc.sbuf_tensor
    xr = Ring(stack, sb, "n_x", [128, D], F32, 2)
    yr = Ring(stack, sb, "n_y", [128, D], F32, 2) if y_in is not None else None
    tr = Ring(stack, sb, "n_t", [128, D], F32, 2)
    sqr = Ring(stack, sb, "n_sq", [128, D], BF16, 1)
    hr = Ring(stack, sb, "n_h", [128, D], BF16, 2)
    htr = Ring(stack, sb, "n_hT", [128, KC_D, 512], BF16, 2)
    str_ = Ring(stack, sb, "n_st", [128, 4], F32, 4)
    ptr = Ring(stack, nc.psum_tensor, "n_pt", [128, 1024], BF16, 4)
    if sel4 is not None:
        scr4 = Ring(stack, sb, "n_sc4", [128, KC_D, 512], BF16, 2)
        s4 = stack.enter_context(sb(U("n_sel4"), [128, 4], F32))
        s4k = Tok("n_sel4")
        P.dma(lambda e: e.dma_start(out=s4[:], in_=sel4.partition_broadcast(128)), wr=[s4k])
    consts = []
    for nm, w in (("n_wpost", wpost), ("n_wpre", wpre)):
        if w is None:
            consts.append((None, None))
            continue
        t = stack.enter_context(sb(U(nm), [128, D], F32))
        tk = Tok(nm)
        P.dma(lambda e, t=t, w=w: e.dma_start(out=t[:], in_=bcast_rows(w, 128)), wr=[tk])
        consts.append((t, tk))
    (wpost_t, wpost_k), (wpre_t, wpre_k) = consts
    for i in range(nt):
        rows = slice(i * 128, (i + 1) * 128)
        xt, xk = xr.next()
        P.dma(lambda e, xt=xt, rows=rows: e.dma_start(out=xt[:], in_=x_in[rows, :]), wr=[xk])
        cur, curk = xt, xk
        if y_in is not None:
            yt, yk = yr.next()
            if isinstance(y_in, tuple):
                P.dma(lambda e, yt=yt, rows=rows: e.dma_start(out=yt[:, 0:1024], in_=y_in[0][rows, :]), wr=[yk],
                      q="q_pool")
                P.dma(lambda e, yt=yt, rows=rows: e.dma_start(out=yt[:, 1024:2048], in_=y_in[1][rows, :]), wr=[yk],
                      q="q_pool")
            else:
                P.dma(lambda e, yt=yt, rows=rows: e.dma_start(out=yt[:], in_=y_in[rows, :]), wr=[yk], q="q_pool")
            sq, sqk = sqr.next()
            st, stk = str_.next()
            P.act(lambda e, sq=sq, yt=yt, st=st: e.activation(out=sq[:], in_=yt[:], func=AF.Square,
                                                               accum_out=st[:, 0:1]),
                  rd=[yk], wr=[sqk, stk])
            emit_rstd(P, st[:, 0:1], st[:, 1:2], stk, stk, 128, D)
            tt, tk = tr.next()
            P.dve(lambda e, tt=tt, yt=yt, st=st: e.scalar_tensor_tensor(
                out=tt[:], in0=yt[:], scalar=st[:, 1:2], in1=wpost_t[:], op0=ALU.mult, op1=ALU.mult),
                rd=[yk, stk, wpost_k], wr=[tk])
            P.dve(lambda e, tt=tt, xt=xt: e.tensor_tensor(out=tt[:], in0=tt[:], in1=xt[:], op=ALU.add),
                  rd=[tk, xk], wr=[tk])
            cur, curk = tt, tk
        if x_out is not None:
            P.dma(lambda e, cur=cur, rows=rows: e.dma_start(out=x_out[rows, :], in_=cur[:]), rd=[curk])
        if wpre is not None:
            sq, sqk = sqr.next()
            st, stk = str_.next()
            P.act(lambda e, sq=sq, cur=cur, st=st: e.activation(out=sq[:], in_=cur[:], func=AF.Square,
                                                                accum_out=st[:, 0:1]),
                  rd=[curk], wr=[sqk, stk])
            emit_rstd(P, st[:, 0:1], st[:, 1:2], stk, stk, 128, D)
            ht, hk = hr.next()
            P.dve(lambda e, ht=ht, cur=cur, st=st: e.scalar_tensor_tensor(
                out=ht[:], in0=cur[:], scalar=st[:, 1:2], in1=wpre_t[:], op0=ALU.mult, op1=ALU.mult),
                rd=[curk, stk, wpre_k], wr=[hk])
            if i % 4 == 0:
                hT, hTk = htr.next()
            tsl = slice((i % 4) * 128, (i % 4) * 128 + 128)
            for half in range(2):
                pt, ptk = ptr.next()
                for j in range(8):
                    kc = half * 8 + j
                    P.pe(lambda e, pt=pt, ht=ht, j=j, kc=kc: e.transpose(
                        out=pt[:, j * 128:(j + 1) * 128], in_=ht[:, kc * 128:(kc + 1) * 128],
                        identity=ident_bf[:]), rd=[hk], wr=[ptk])
                src = lambda pt: pt[:].rearrange("p (k t) -> p k t", t=128)
                if half == 0:
                    P.act(lambda e, hT=hT, pt=pt, tsl=tsl: e.copy(out=hT[:, 0:8, tsl], in_=src(pt)),
                          rd=[ptk], wr=[hTk])
                else:
                    P.dve(lambda e, hT=hT, pt=pt, tsl=tsl: e.tensor_copy(out=hT[:, 8:16, tsl], in_=src(pt)),
                          rd=[ptk], wr=[hTk])
            if i % 4 == 3 and sel4 is None:
                blk = i // 4
                P.dma(lambda e, hT=hT, blk=blk: e.dma_start(
                    out=hT_out.rearrange("(k p) t -> p k t", p=128)[:, :, blk * 512:(blk + 1) * 512],
                    in_=hT[:]), rd=[hTk], q="q_pool")
            elif i % 4 == 3:
                blk = i // 4
                for s_ in range(4):
                    sc, sck = scr4.next()
                    if s_ % 2 == 0:
                        P.dve(lambda e, sc=sc, hT=hT, s_=s_: e.tensor_scalar(
                            out=sc[:], in0=hT[:], scalar1=s4[:, s_:s_ + 1], scalar2=None, op0=ALU.mult),
                            rd=[hTk, s4k], wr=[sck])
                    else:
                        P.act(lambda e, sc=sc, hT=hT, s_=s_: e.activation(
                            out=sc[:], in_=hT[:], func=AF.Copy, scale=s4[:, s_:s_ + 1]),
                            rd=[hTk, s4k], wr=[sck])
                    P.dma(lambda e, sc=sc, blk=blk, s_=s_: e.dma_start(
                        out=hT_out.rearrange("(s b p) f -> p s b f", s=4, p=128)[:, s_, blk, :],
                        in_=sc[:].rearrange("p k t -> p (k t)")), rd=[sck], q="q_pool" if s_ % 2 else "q_sp")


def make_ident(nc, P, stack):
    idf = stack.enter_context(nc.sbuf_tensor(U("ident_f"), [128, 128], F32))
    idb = stack.enter_context(nc.sbuf_tensor(U("ident_b"), [128, 128], BF16))
    k = Tok("ident")
    P.pool(lambda e: e.memset(idf[:], 1.0), wr=[k])
    P.pool(lambda e: e.affine_select(out=idf[:], in_=idf[:], pattern=[[-1, 128]], compare_op=ALU.is_equal,
                                     fill=0.0, base=0, channel_multiplier=1), rd=[k], wr=[k])
    P.dve(lambda e: e.tensor_copy(out=idb[:], in_=idf[:]), rd=[k], wr=[k])
    return idf, idb, k


D_FF = 5632
KC_F = D_FF // 128


def cast_op(P, idx, out, in_, rd, wr):
    if idx % 2 == 0:
        P.dve(lambda e: e.tensor_copy(out=out, in_=in_), rd=rd, wr=wr)
    else:
        P.act(lambda e: e.copy(out=out, in_=in_), rd=rd, wr=wr)


def phase_ffn_gu(nc, P, stack, T, hT_in, wg_l, wu_l, actT_out):
    sb = nc.sbuf_tensor
    hT = stack.enter_context(sb(U("gu_hT"), [128, KC_D, T], BF16))
    hTk = [Tok(f"gu_hT{k}") for k in range(KC_D // 4)]
    hv = hT_in.rearrange("(k p) t -> p k t", p=128)
    for g in range(KC_D // 4):
        P.dma(lambda e, g=g: e.dma_start(out=hT[:, g * 4:(g + 1) * 4, :], in_=hv[:, g * 4:(g + 1) * 4, :]),
              wr=[hTk[g]], q="q_pool")
    wst = [Ring(stack, sb, f"gu_wst{m}", [128, D], F32, 2) for m in range(2)]
    wbf = [Ring(stack, sb, f"gu_wbf{m}", [128, KC_D, 128], BF16, 2) for m in range(2)]
    ps = [Ring(stack, nc.psum_tensor, f"gu_ps{m}", [128, 512], F32, 3) for m in range(2)]
    sgr = Ring(stack, sb, "gu_sg", [128, 512], F32, 2)
    ar = Ring(stack, sb, "gu_a", [128, 512], BF16, 4)
    nsl = T // 512
    ci = 0
    for fc in range(KC_F):
        wb = []
        for m, wl in enumerate((wg_l, wu_l)):
            st_, stk = wst[m].next()
            P.dma(lambda e, st_=st_, wl=wl, fc=fc: e.dma_start(out=st_[:], in_=wl[fc]), wr=[stk])
            b, bk = wbf[m].next()
            cast_op(P, ci, b[:].rearrange("p k j -> p (k j)"), st_[:], [stk], [bk])
            ci += 1
            wb.append((b, bk))
        for sl in range(nsl):
            tsl = slice(sl * 512, (sl + 1) * 512)
            pp = []
            for m in range(2):
                p_, pk = ps[m].next()
                b, bk = wb[m]
                for kc in range(KC_D):
                    P.pe(lambda e, p_=p_, b=b, kc=kc, tsl=tsl: e.matmul(
                        p_[:], lhsT=b[:, kc, :], rhs=hT[:, kc, tsl], start=(kc == 0), stop=(kc == KC_D - 1)),
                        rd=[bk, hTk[kc // 4]], wr=[pk])
                pp.append((p_, pk))
            sg, sgk = sgr.next()
            P.act(lambda e, sg=sg, p_=pp[0][0]: e.activation(out=sg[:], in_=p_[:], func=AF.Silu),
                  rd=[pp[0][1]], wr=[sgk])
            a, ak = ar.next()
            P.dve(lambda e, a=a, sg=sg, p_=pp[1][0]: e.tensor_tensor(out=a[:], in0=sg[:], in1=p_[:], op=ALU.mult),
                  rd=[sgk, pp[1][1]], wr=[ak])
            P.dma(lambda e, a=a, fc=fc, tsl=tsl: e.dma_start(out=actT_out[fc * 128:(fc + 1) * 128, tsl], in_=a[:]),
                  rd=[ak], q="q_pool")


def phase_mmT(nc, P, stack, T, K, AT_in, w_l, y_out, Th=1024, pfx="mt", halves=None, htoks=None, half_done=None):
    KC = K // 128
    G = 4
    NG = KC // G
    Th = min(Th, T, 1024)
    NT = Th // 128
    sb = nc.sbuf_tensor
    AT = stack.enter_context(sb(U(f"{pfx}_AT"), [128, KC, Th], BF16))
    ATk = [Tok(f"{pfx}_AT{g}") for g in range(NG)]
    wst = Ring(stack, sb, f"{pfx}_wst", [128, G, 512], F32, 3)
    wbf = Ring(stack, sb, f"{pfx}_wbf", [128, G, 512], BF16, 4)
    ps = Ring(stack, nc.psum_tensor, f"{pfx}_ps", [128, 512], F32, 8)
    ys = Ring(stack, sb, f"{pfx}_ys", [128, 512], F32, 4)
    av = AT_in.rearrange("(k p) t -> p k t", p=128)
    ci = 0
    ei = 0
    if halves is None:
        order = [(th, s) for th in range(T // Th) for s in range(4)]
    else:
        order = [(th, s) for s in range(4) for th in range(T // Th)]
    last_th = None
    for (th, s) in order:
        if th != last_th:
            last_th = th
            for g in range(NG):
                P.dma(lambda e, g=g, th=th: e.dma_start(out=AT[:, g * G:(g + 1) * G, :],
                                                        in_=av[:, g * G:(g + 1) * G, th * Th:(th + 1) * Th]),
                      wr=[ATk[g]], q="q_pool" if halves is None else "q_sp")
        if True:
            banks = [ps.next() for _ in range(NT)]
            for g in range(NG):
                st_, stk = wst.next()
                P.dma(lambda e, st_=st_, s=s, g=g: e.dma_start(out=st_[:], in_=w_l[s, :, g * G:(g + 1) * G, :]),
                      wr=[stk])
                wb, wbk = wbf.next()
                cast_op(P, ci, wb[:], st_[:], [stk], [wbk])
                ci += 1
                for tl in range(NT):
                    p_, pk = banks[tl]
                    for kk in range(G):
                        kc = g * G + kk
                        P.pe(lambda e, p_=p_, kc=kc, kk=kk, tl=tl, wb=wb: e.matmul(
                            p_[:], lhsT=AT[:, kc, tl * 128:(tl + 1) * 128], rhs=wb[:, kk, :],
                            start=(kc == 0), stop=(kc == KC - 1)),
                            rd=[ATk[g], wbk], wr=[pk])
            for tl in range(NT):
                p_, pk = banks[tl]
                y_, yk = ys.next()
                if ei % 2 == 0:
                    P.act(lambda e, y_=y_, p_=p_: e.copy(out=y_[:], in_=p_[:]), rd=[pk], wr=[yk])
                else:
                    P.dve(lambda e, y_=y_, p_=p_: e.tensor_copy(out=y_[:], in_=p_[:]), rd=[pk], wr=[yk])
                ei += 1
                r0 = th * Th + tl * 128
                if halves is None:
                    P.dma(lambda e, y_=y_, r0=r0, s=s: e.dma_start(out=y_out[r0:r0 + 128, s * 512:(s + 1) * 512],
                                                                   in_=y_[:]), rd=[yk], q="q_pool")
                else:
                    dst = halves[s // 2]
                    tk_ = Tok("yhalf")
                    htoks[s // 2].append(tk_)
                    P.dma(lambda e, y_=y_, r0=r0, s=s, dst=dst: e.dma_start(
                        out=dst[r0:r0 + 128, (s % 2) * 512:(s % 2 + 1) * 512], in_=y_[:]), rd=[yk], wr=[tk_],
                        q="q_act")
            if halves is not None and half_done is not None and s % 2 == 1 and th == T // Th - 1:
                half_done(s // 2)


SEQ = 8192
NBLK = SEQ // 512


def make_consts(nc, P, stack):
    sb = nc.sbuf_tensor
    c = {}
    c["ones_bf"] = stack.enter_context(sb(U("ones_bf"), [128, 128], BF16))
    c["ones_f"] = stack.enter_context(sb(U("ones_f"), [128, 128], F32))
    c["sel2"] = stack.enter_context(sb(U("sel2"), [128, 2], BF16))
    k = Tok("consts")
    c["tok"] = k
    P.pool(lambda e: e.memset(c["ones_bf"][:], 1.0), wr=[k])
    P.pool(lambda e: e.memset(c["ones_f"][:], 1.0), wr=[k])
    P.pool(lambda e: e.memset(c["sel2"][:], 0.0), wr=[k])
    P.pool(lambda e: e.memset(c["sel2"][0:64, 0:1], 1.0), wr=[k])
    P.pool(lambda e: e.memset(c["sel2"][64:128, 1:2], 1.0), wr=[k])
    return c


def phase_attn_proj(nc, P, stack, hT_blk, w_l, qTd, kTd, Vd, negMd, idf, idb, C, S=SEQ, hT_tok=None):
    sb = nc.sbuf_tensor
    ps = nc.psum_tensor
    NB = S // 512
    ones_f, sel2, ck = C["ones_f"], C["sel2"], C["tok"]
    W = stack.enter_context(sb(U("ap_w"), [128, KC_D, 12 * 128], BF16))
    Wk = [Tok(f"ap_w{f}") for f in range(12)]
    wst = Ring(stack, sb, "ap_wst", [128, D], F32, 2)
    for fc in range(12):
        st_, stk = wst.next()
        P.dma(lambda e, st_=st_, fc=fc: e.dma_start(out=st_[:], in_=w_l[fc]), wr=[stk], q="q_pool")
        cast_op(P, fc, W[:, :, fc * 128:(fc + 1) * 128], st_[:].rearrange("p (k j) -> p k j", j=128), [stk], [Wk[fc]])
    hr = Ring(stack, sb, "ap_h", [128, KC_D, 512], BF16, 2)
    pw = Ring(stack, ps, "ap_pw", [128, 512], F32, 5)
    pm = Ring(stack, ps, "ap_pm", [128, 512], F32, 2)
    osr = Ring(stack, sb, "ap_os", [128, 512], BF16, 6)
    sqr = Ring(stack, sb, "ap_sq", [128, 512], BF16, 3)
    vor = Ring(stack, sb, "ap_vo", [128, 4, 128], BF16, 3)
    nmx = stack.enter_context(sb(U("ap_nmx"), [2, 8, NB], F32))
    nmxk = Tok("ap_nmx")
    ei = 0
    for b in range(NB):
        h_, hk = hr.next()
        P.dma(lambda e, h_=h_, b=b: e.dma_start(out=h_[:], in_=hT_blk(b)), rd=([hT_tok(b)] if hT_tok else []),
              wr=[hk])
        tsl = slice(b * 512, (b + 1) * 512)
        for ch in range(12):
            which, hl = ch // 4, ch % 4
            p_, pk = pw.next()
            for kc in range(KC_D):
                P.pe(lambda e, p_=p_, ch=ch, kc=kc, h_=h_: e.matmul(
                    p_[:], lhsT=W[:, kc, ch * 128:(ch + 1) * 128], rhs=h_[:, kc, :],
                    start=(kc == 0), stop=(kc == KC_D - 1)), rd=[Wk[ch], hk], wr=[pk])
            o_, ok = osr.next()
            sc = 0.125 if which == 0 else 1.0
            if ei % 2 == 0:
                P.act(lambda e, o_=o_, p_=p_, sc=sc: e.activation(out=o_[:], in_=p_[:], func=AF.Copy, scale=sc),
                      rd=[pk], wr=[ok])
            else:
                P.dve(lambda e, o_=o_, p_=p_, sc=sc: e.tensor_scalar(out=o_[:], in0=p_[:], scalar1=sc, scalar2=None,
                                                                    op0=ALU.mult), rd=[pk], wr=[ok])
            ei += 1
            if which < 2:
                dst = qTd if which == 0 else kTd
                P.dma(lambda e, o_=o_, dst=dst, hl=hl, tsl=tsl: e.dma_start(
                    out=dst[hl * 128:(hl + 1) * 128, tsl], in_=o_[:]), rd=[ok], q="q_pool")
                sq, sqk = sqr.next()
                P.dve(lambda e, sq=sq, o_=o_: e.tensor_tensor(out=sq[:], in0=o_[:], in1=o_[:], op=ALU.mult),
                      rd=[ok], wr=[sqk])
                p2, p2k = pm.next()
                P.pe(lambda e, p2=p2, sq=sq: e.matmul(p2[0:2, :], lhsT=sel2[:], rhs=sq[:], start=True, stop=True),
                     rd=[sqk, ck], wr=[p2k])
                P.dve(lambda e, p2=p2, ch=ch, b=b: e.tensor_reduce(
                    out=nmx[:, ch, b:b + 1], in_=p2[0:2, :], axis=AX.X, op=ALU.max), rd=[p2k], wr=[nmxk])
            else:
                p2, p2k = pm.next()
                p2b = p2[:].bitcast(BF16)
                for j in range(4):
                    P.pe(lambda e, p2b=p2b, o_=o_, j=j: e.transpose(
                        out=p2b[:, j * 128:(j + 1) * 128], in_=o_[:, j * 128:(j + 1) * 128], identity=idb[:]),
                        rd=[ok], wr=[p2k])
                vo, vok = vor.next()
                P.act(lambda e, vo=vo, p2b=p2b: e.copy(out=vo[:], in_=p2b[:, 0:512].rearrange("p (j d) -> p j d", d=128)),
                      rd=[p2k], wr=[vok])
                P.dma(lambda e, vo=vo, hl=hl, b=b: e.dma_start(out=Vd[hl, :, b * 4:(b + 1) * 4, :], in_=vo[:]),
                      rd=[vok], q="q_pool")
    msc = stack.enter_context(sb(U("ap_msc"), [2, 32], F32))
    P.dve(lambda e: e.tensor_reduce(out=msc[:, 0:8], in_=nmx[:], axis=AX.X, op=ALU.max), rd=[nmxk], wr=[nmxk])
    P.dve(lambda e: e.tensor_tensor(out=msc[:, 8:12], in0=msc[:, 0:4], in1=msc[:, 4:8], op=ALU.mult),
          rd=[nmxk], wr=[nmxk])
    P.act(lambda e: e.activation(out=msc[:, 12:16], in_=msc[:, 8:12], func=AF.Ln), rd=[nmxk], wr=[nmxk])
    P.act(lambda e: e.activation(out=msc[:, 12:16], in_=msc[:, 12:16], func=AF.Exp, scale=0.5), rd=[nmxk], wr=[nmxk])
    for hl in range(4):
        P.dve(lambda e, hl=hl: e.tensor_scalar(out=msc[:, 16 + hl * 2:18 + hl * 2], in0=idf[0:2, 0:2],
                                               scalar1=msc[:, 12 + hl:13 + hl], scalar2=-1.02,
                                               op0=ALU.mult, op1=ALU.mult), rd=[nmxk], wr=[nmxk])
    p2, p2k = pm.next()
    P.pe(lambda e: e.matmul(p2[:, 0:8], lhsT=ones_f[0:2, :], rhs=msc[:, 16:24], start=True, stop=True),
         rd=[nmxk, ck], wr=[p2k])
    nm = stack.enter_context(sb(U("ap_nm"), [128, 8], F32))
    nmk = Tok("ap_nm")
    P.dve(lambda e: e.tensor_copy(out=nm[:], in_=p2[:, 0:8]), rd=[p2k], wr=[nmk])
    P.dma(lambda e: e.dma_start(out=negMd, in_=nm[:]), rd=[nmk])


def phase_attn_core(nc, P, stack, qTd, kTd, Vd, negMd, lam_in, subln_in, lambda_init, oT_out, idf, idb, C, S=SEQ):
    sb = nc.sbuf_tensor
    ps = nc.psum_tensor
    NB = S // 512
    ones_bf, ones_f, sel2, ck = C["ones_bf"], C["ones_f"], C["sel2"], C["tok"]
    lam4 = stack.enter_context(sb(U("at_lam4"), [128, 4, 64], F32))
    lamk = Tok("lam")
    P.dma(lambda e: e.dma_start(out=lam4[:].rearrange("p a d -> p (a d)"),
                                in_=lam_in.rearrange("a d -> (a d)").partition_broadcast(128)), wr=[lamk])
    lsc = stack.enter_context(sb(U("at_lsc"), [128, 8], F32))
    lpr = stack.enter_context(sb(U("at_lpr"), [128, 2, 64], F32))
    P.dve(lambda e: e.tensor_tensor(out=lpr[:, 0, :], in0=lam4[:, 0, :], in1=lam4[:, 1, :], op=ALU.mult),
          rd=[lamk], wr=[lamk])
    P.dve(lambda e: e.tensor_tensor(out=lpr[:, 1, :], in0=lam4[:, 2, :], in1=lam4[:, 3, :], op=ALU.mult),
          rd=[lamk], wr=[lamk])
    P.dve(lambda e: e.tensor_reduce(out=lsc[:, 0:2], in_=lpr[:], axis=AX.X, op=ALU.add), rd=[lamk], wr=[lamk])
    P.act(lambda e: e.activation(out=lsc[:, 2:4], in_=lsc[:, 0:2], func=AF.Exp), rd=[lamk], wr=[lamk])
    P.dve(lambda e: e.scalar_tensor_tensor(out=lsc[:, 4:5], in0=lsc[:, 3:4], scalar=-float(lambda_init),
                                           in1=lsc[:, 2:3], op0=ALU.add, op1=ALU.subtract), rd=[lamk], wr=[lamk])
    P.dma(lambda e: e.dma_start(out=lsc[:, 5:6], in_=subln_in.rearrange("(p o) -> p o", o=1)), wr=[lamk])
    P.dve(lambda e: e.tensor_scalar(out=lsc[:, 6:7], in0=lsc[:, 5:6], scalar1=1.0 - float(lambda_init),
                                    scalar2=None, op0=ALU.mult), rd=[lamk], wr=[lamk])
    neglam = lsc[:, 4:5]
    sublnw = lsc[:, 6:7]

    qTc = [stack.enter_context(sb(U(f"at_qT{c}"), [128, S], BF16)) for c in range(2)]
    qzk = Tok("at_qz")
    P.dve(lambda e: e.memset(qTc[0][64:128, :], 0.0), wr=[qzk])
    P.dve(lambda e: e.memset(qTc[1][0:64, :], 0.0), wr=[qzk])
    kT = stack.enter_context(sb(U("at_kT"), [128, S], BF16))
    V = stack.enter_context(sb(U("at_V"), [128, S // 128, 128], BF16))
    qk1, kk1, vk1 = Tok("at_q"), Tok("at_k"), Tok("at_v")
    qk_ = [qk1] * NB
    kk_ = [kk1] * NB
    vk_ = [vk1] * NB
    negM = stack.enter_context(sb(U("at_negM"), [128, 8], F32))
    negMk = Tok("negM")
    P.dma(lambda e: e.dma_start(out=negM[:], in_=negMd), wr=[negMk])
    pw = Ring(stack, ps, "at_pw", [128, 512], F32, 3)
    po = [Ring(stack, ps, f"at_po{c}", [128, 512], F32, 1) for c in range(2)]
    pl = [Ring(stack, ps, f"at_pl{c}", [128, 512], F32, 1) for c in range(2)]
    lacc = Ring(stack, sb, "at_la", [128, 512], F32, 4)
    pm = Ring(stack, ps, "at_pm", [128, 512], F32, 1)
    ptr = Ring(stack, sb, "at_pt", [128, 512], BF16, 6)
    e32 = Ring(stack, sb, "at_e32", [128, 512], F32, 8)
    obf = Ring(stack, sb, "at_obf", [128, 512], BF16, 2)
    for hl in range(4):
        hs = slice(hl * 128, (hl + 1) * 128)
        P.dma(lambda e, hl=hl: e.dma_start(out=qTc[0][0:64, :], in_=qTd[hl * 128:hl * 128 + 64, :]),
              rd=[qzk], wr=[qk1])
        P.dma(lambda e, hl=hl: e.dma_start(out=qTc[1][64:128, :], in_=qTd[hl * 128 + 64:hl * 128 + 128, :]),
              rd=[qzk], wr=[qk1], q="q_pool")
        P.dma(lambda e, hs=hs: e.dma_start(out=kT[:], in_=kTd[hs, :]), wr=[kk1])
        P.dma(lambda e, hl=hl: e.dma_start(out=V[:], in_=Vd[hl]), wr=[vk1], q="q_pool")
        steps = [(qt, c, sbk) for qt in range(NB) for c in range(2) for sbk in range(qt * 4 + 4)]
        LA = 2
        inflight = {}
        accs = {}
        tparts = {}
        deferred = []

        def stepA(k):
            qt, c, sbk = steps[k]
            q0 = qt * 512
            rows = slice(c * 64, (c + 1) * 64)
            d = max(0, sbk - qt * 4)
            cs = slice(d * 128, 512)
            w_, wk_ = pw.next()
            P.pe(lambda e: e.matmul(w_[:, cs], lhsT=kT[:, sbk * 128:(sbk + 1) * 128],
                                    rhs=qTc[c][:, q0 + cs.start:q0 + 512], start=True, stop=True),
                 rd=[kk_[sbk // 4], qk_[qt], qzk], wr=[wk_])
            pt, ptk = ptr.next()
            bcol = hl * 2 + c
            P.act(lambda e: e.activation(out=pt[:, cs], in_=w_[:, cs], func=AF.Exp, bias=negM[:, bcol:bcol + 1],
                                         scale=1.0), rd=[wk_, negMk], wr=[ptk])
            if sbk >= qt * 4:
                base = q0 + cs.start - sbk * 128
                P.pool(lambda e: e.affine_select(out=pt[:, cs], in_=pt[:, cs], pattern=[[1, 512 - cs.start]],
                                                 compare_op=ALU.is_ge, fill=0.0, base=base,
                                                 channel_multiplier=-1), rd=[ptk], wr=[ptk])
            inflight[k] = (pt, ptk, cs)

        def epi_c(qt, c, k):
            o_, ok, l_, lk, la, lak = accs.pop((qt, c))

            def part2():
                P.pe(lambda e: e.matmul(l_[:], lhsT=ones_f[:], rhs=la[:], start=False, stop=True),
                     rd=[lak, ck], wr=[lk])
                r_, rk = e32.next()
                P.dve(lambda e: e.reciprocal(out=r_[:], in_=l_[:]), rd=[lk], wr=[rk])
                t_, tk = e32.next()
                P.dve(lambda e: e.tensor_tensor(out=t_[:], in0=o_[:], in1=r_[:], op=ALU.mult), rd=[ok, rk], wr=[tk])
                tparts[(qt, c)] = (t_, tk)
                if c == 1:
                    epi_1(qt, k + 2)
            deferred.append((k + 2, part2))

        def epi_1(qt, k):
            t0, t0k = tparts.pop((qt, 0))
            t1, t1k = tparts.pop((qt, 1))
            of, ofk = e32.next()
            P.dve(lambda e: e.scalar_tensor_tensor(out=of[:], in0=t1[:], scalar=neglam, in1=t0[:],
                                                   op0=ALU.mult, op1=ALU.add), rd=[t0k, t1k, lamk], wr=[ofk])
            sq, sqk2 = e32.next()
            P.act(lambda e: e.activation(out=sq[:], in_=of[:], func=AF.Square), rd=[ofk], wr=[sqk2])

            def epi_2():
                m_, mk = pm.next()
                P.pe(lambda e: e.matmul(m_[:], lhsT=ones_f[:], rhs=sq[:], start=True, stop=True),
                     rd=[sqk2, ck], wr=[mk])
                rs, rsk = e32.next()
                P.act(lambda e: e.activation(out=rs[:], in_=m_[:], func=AF.Ln, bias=EPS, scale=1.0 / 128),
                      rd=[mk], wr=[rsk])
                P.act(lambda e: e.activation(out=rs[:], in_=rs[:], func=AF.Exp, scale=-0.5), rd=[rsk], wr=[rsk])
                ob, obk = obf.next()
                P.dve(lambda e: e.scalar_tensor_tensor(out=ob[:], in0=of[:], scalar=sublnw, in1=rs[:],
                                                       op0=ALU.mult, op1=ALU.mult), rd=[ofk, rsk, lamk], wr=[obk])
                P.dma(lambda e, hl=hl: e.dma_start(out=oT_out[hl * 128:(hl + 1) * 128, qt * 512:(qt + 1) * 512],
                                                   in_=ob[:]), rd=[obk])
            deferred.append((k + 6, epi_2))

        def stepB(k):
            qt, c, sbk = steps[k]
            nsb = qt * 4 + 4
            pt, ptk, cs = inflight.pop(k)
            if sbk == 0:
                o_, ok = po[c].next()
                l_, lk = pl[c].next()
                la, lak = lacc.next()
                accs[(qt, c)] = (o_, ok, l_, lk, la, lak)
            o_, ok, l_, lk, la, lak = accs[(qt, c)]
            P.pe(lambda e: e.matmul(o_[:, cs], lhsT=V[:, sbk, :], rhs=pt[:, cs], start=(sbk == 0),
                                    stop=(sbk == nsb - 1)), rd=[vk_[sbk // 4], ptk], wr=[ok])
            if sbk % 2 == 0:
                P.pe(lambda e: e.matmul(l_[:, cs], lhsT=ones_bf[:], rhs=pt[:, cs], start=(sbk == 0), stop=False),
                     rd=[ck, ptk], wr=[lk])
            elif sbk == 1:
                if cs.start > 0:
                    P.dve(lambda e: e.memset(la[:, 0:cs.start], 0.0), wr=[lak])
                P.dve(lambda e: e.tensor_copy(out=la[:, cs], in_=pt[:, cs]), rd=[ptk], wr=[lak])
            else:
                P.dve(lambda e: e.tensor_tensor(out=la[:, cs], in0=la[:, cs], in1=pt[:, cs], op=ALU.add),
                      rd=[ptk, lak], wr=[lak])
            if sbk == nsb - 1:
                epi_c(qt, c, k)

        ns = len(steps)
        for k in range(ns + LA):
            if k < ns:
                stepA(k)
            if k - LA >= 0:
                stepB(k - LA)
            for item in [d_ for d_ in deferred if d_[0] <= k]:
                deferred.remove(item)
                item[1]()
        for item in deferred:
            item[1]()


NZX = 20


def phase_ssd_in(nc, P, stack, hT_blk, w_l, wdt_l, dtb_in, zxT_out, dt_out, S=SEQ, hT_tok=None):
    sb = nc.sbuf_tensor
    NB = S // 512
    W = stack.enter_context(sb(U("si_w"), [128, KC_D, NZX * 128], BF16))
    Wk = [Tok(f"si_w{f}") for f in range(NZX)]
    wst = Ring(stack, sb, "si_wst", [128, D], F32, 2)
    for fc in range(NZX):
        st_, stk = wst.next()
        P.dma(lambda e, st_=st_, fc=fc: e.dma_start(out=st_[:], in_=w_l[fc]), wr=[stk])
        cast_op(P, fc, W[:, :, fc * 128:(fc + 1) * 128], st_[:].rearrange("p (k j) -> p k j", j=128), [stk], [Wk[fc]])
    wdtf = stack.enter_context(sb(U("si_wdtf"), [128, KC_D, 16], F32))
    wdt = stack.enter_context(sb(U("si_wdt"), [128, KC_D, 16], BF16))
    wdk = Tok("si_wdt")
    P.dma(lambda e: e.dma_start(out=wdtf[:], in_=wdt_l), wr=[wdk])
    P.dve(lambda e: e.tensor_copy(out=wdt[:], in_=wdtf[:]), rd=[wdk], wr=[wdk])
    dtb = stack.enter_context(sb(U("si_dtb"), [128, 16], F32))
    dtbk = Tok("si_dtb")
    P.dma(lambda e: e.dma_start(out=dtb[:], in_=dtb_in.partition_broadcast(128)), wr=[dtbk])
    hr = Ring(stack, sb, "si_h", [128, KC_D, 512], BF16, 2)
    pw = Ring(stack, nc.psum_tensor, "si_pw", [128, 512], F32, 4)
    pd = Ring(stack, nc.psum_tensor, "si_pd", [128, 16], F32, 2)
    osr = Ring(stack, sb, "si_os", [128, 512], F32, 4)
    dr = Ring(stack, sb, "si_d", [128, 4, 16], F32, 6)
    dto = Ring(stack, sb, "si_dto", [128, 4, 16], F32, 2)
    ei = 0
    for b in range(NB):
        h_, hk = hr.next()
        P.dma(lambda e, h_=h_, b=b: e.dma_start(out=h_[:], in_=hT_blk(b)), rd=([hT_tok(b)] if hT_tok else []),
              wr=[hk])
        tsl = slice(b * 512, (b + 1) * 512)
        for fc in range(NZX):
            p_, pk = pw.next()
            for kc in range(KC_D):
                P.pe(lambda e, p_=p_, fc=fc, kc=kc, h_=h_: e.matmul(
                    p_[:], lhsT=W[:, kc, fc * 128:(fc + 1) * 128], rhs=h_[:, kc, :],
                    start=(kc == 0), stop=(kc == KC_D - 1)), rd=[Wk[fc], hk], wr=[pk])
            o_, ok = osr.next()
            if ei % 2 == 0:
                P.act(lambda e, o_=o_, p_=p_: e.copy(out=o_[:], in_=p_[:]), rd=[pk], wr=[ok])
            else:
                P.dve(lambda e, o_=o_, p_=p_: e.tensor_copy(out=o_[:], in_=p_[:]), rd=[pk], wr=[ok])
            ei += 1
            P.dma(lambda e, o_=o_, fc=fc, tsl=tsl: e.dma_start(out=zxT_out[fc * 128:(fc + 1) * 128, tsl], in_=o_[:]),
                  rd=[ok])
        x_, xk = dr.next()
        for j in range(4):
            p_, pk = pd.next()
            for kc in range(KC_D):
                P.pe(lambda e, p_=p_, kc=kc, h_=h_, j=j: e.matmul(
                    p_[:], lhsT=h_[:, kc, j * 128:(j + 1) * 128], rhs=wdt[:, kc, :],
                    start=(kc == 0), stop=(kc == KC_D - 1)), rd=[wdk, hk], wr=[pk])
            P.dve(lambda e, x_=x_, p_=p_, j=j: e.tensor_tensor(out=x_[:, j, :], in0=p_[:], in1=dtb[:], op=ALU.add),
                  rd=[pk, dtbk], wr=[xk])
        a_, ak = dr.next()
        P.dve(lambda e, a_=a_, x_=x_: e.scalar_tensor_tensor(out=a_[:], in0=x_[:], scalar=-1.0, in1=x_[:],
                                                             op0=ALU.mult, op1=ALU.max), rd=[xk], wr=[ak])
        P.act(lambda e, a_=a_: e.activation(out=a_[:], in_=a_[:], func=AF.Exp, scale=-1.0), rd=[ak], wr=[ak])
        P.act(lambda e, a_=a_: e.activation(out=a_[:], in_=a_[:], func=AF.Ln, bias=1.0, scale=1.0), rd=[ak], wr=[ak])
        d_, dk = dto.next()
        P.dve(lambda e, d_=d_, x_=x_, a_=a_: e.scalar_tensor_tensor(
            out=d_[:], in0=x_[:], scalar=0.0, in1=a_[:], op0=ALU.max, op1=ALU.add), rd=[xk, ak], wr=[dk])
        P.dma(lambda e, d_=d_, b=b: e.dma_start(
            out=dt_out[b * 512:(b + 1) * 512, :].rearrange("(j p) h -> p j h", p=128), in_=d_[:]), rd=[dk])


def phase_ssd_scan(nc, P, stack, zxT_in, dt_in, convw_in, convb_in, alog_in, dsk_in, normw_in, yT_out,
                   idf, idb, C, S=SEQ):
    sb = nc.sbuf_tensor
    ps = nc.psum_tensor
    ones_f, ck = C["ones_f"], C["tok"]
    TB = 256
    NB = S // TB
    zv = zxT_in.rearrange("(k p) t -> p k t", p=128)
    tri = stack.enter_context(sb(U("ss_tri"), [128, 128], F32))
    cst = Tok("ss_const")
    P.pool(lambda e: e.memset(tri[:], 1.0), wr=[cst])
    P.pool(lambda e: e.affine_select(out=tri[:], in_=tri[:], pattern=[[1, 128]], compare_op=ALU.is_ge,
                                     fill=0.0, base=0, channel_multiplier=-1), rd=[cst], wr=[cst])
    cw = stack.enter_context(sb(U("ss_cw"), [128, 12, 4], F32))
    cb = stack.enter_context(sb(U("ss_cb"), [128, 12], F32))
    nw = stack.enter_context(sb(U("ss_nw"), [128, 8], F32))
    abc = stack.enter_context(sb(U("ss_abc"), [128, 16], F32))
    d16 = stack.enter_context(sb(U("ss_d16"), [128, 16], F32))
    Dbc = stack.enter_context(sb(U("ss_Dbc"), [128, 16, 64], F32))
    P.dma(lambda e: e.dma_start(out=cw[:], in_=convw_in), wr=[cst])
    P.dma(lambda e: e.dma_start(out=cb[:], in_=convb_in), wr=[cst])
    P.dma(lambda e: e.dma_start(out=nw[:], in_=normw_in), wr=[cst])
    P.dma(lambda e: e.dma_start(out=abc[:], in_=alog_in.partition_broadcast(128)), wr=[cst])
    P.dma(lambda e: e.dma_start(out=d16[:], in_=dsk_in.partition_broadcast(128)), wr=[cst])
    P.act(lambda e: e.activation(out=abc[:], in_=abc[:], func=AF.Exp), rd=[cst], wr=[cst])
    P.dve(lambda e: e.tensor_scalar(out=abc[:], in0=abc[:], scalar1=-1.0, scalar2=None, op0=ALU.mult),
          rd=[cst], wr=[cst])
    P.dve(lambda e: e.tensor_copy(out=Dbc[:], in_=d16[:].unsqueeze(2).to_broadcast([128, 16, 64])),
          rd=[cst], wr=[cst])
    S32 = [stack.enter_context(sb(U(f"ss_S32{g}"), [128, 512], F32)) for g in range(2)]
    Sbf = [stack.enter_context(sb(U(f"ss_Sbf{g}"), [128, 512], BF16)) for g in range(2)]
    Sk = [Tok(f"ss_S{g}") for g in range(2)]
    Sbk = [Tok(f"ss_Sb{g}") for g in range(2)]
    for g in range(2):
        P.pool(lambda e, g=g: e.memset(S32[g][:], 0.0), wr=[Sk[g]])
        P.pool(lambda e, g=g: e.memset(Sbf[g][:], 0.0), wr=[Sbk[g]])
    rawr = Ring(stack, sb, "ss_raw", [128, 12, TB + 3], F32, 2)
    zr = Ring(stack, sb, "ss_z", [128, 8, TB], F32, 3)
    accr = Ring(stack, sb, "ss_acc", [128, 12, TB], F32, 1)
    xTr = Ring(stack, sb, "ss_xT", [128, 8, TB], F32, 2)
    bcTr = Ring(stack, sb, "ss_bcT", [128, 4, TB], BF16, 3)
    dtr = Ring(stack, sb, "ss_dt", [128, TB // 128, 16], F32, 3)
    oTr = Ring(stack, sb, "ss_oT", [128, 8, TB], BF16, 3)
    xsr = Ring(stack, sb, "ss_xs", [128, 512], F32, 2)
    Btr = Ring(stack, sb, "ss_Bt", [128, 128], BF16, 4)
    smr = Ring(stack, sb, "ss_sm", [128, 48], F32, 5)
    rbr = Ring(stack, sb, "ss_rb", [128, 8, 128], F32, 2)
    segr = Ring(stack, sb, "ss_seg", [128, 8, 128], F32, 2)
    cbmr = Ring(stack, sb, "ss_cbm", [128, 128], BF16, 3)
    ebr = Ring(stack, sb, "ss_eb", [128, 8, 128], BF16, 3)
    Gr = Ring(stack, sb, "ss_G", [128, 8, 128], BF16, 4)
    x32r = Ring(stack, sb, "ss_x32", [128, 512], F32, 2)
    xbr = Ring(stack, sb, "ss_xb", [128, 512], BF16, 4)
    xer = Ring(stack, sb, "ss_xe", [128, 512], BF16, 4)
    y1r = Ring(stack, sb, "ss_y1", [128, 512], F32, 2)
    xdr = Ring(stack, sb, "ss_xd", [128, 512], F32, 4)
    gvr = Ring(stack, sb, "ss_gv", [128, 4, 128], F32, 2)
    sqr = Ring(stack, sb, "ss_sq", [128, 4, 128], F32, 2)
    rsr = Ring(stack, sb, "ss_rs", [128, 128], F32, 2)
    pbc = Ring(stack, ps, "ss_pbc", [128, 1024], F32, 1)
    pm = Ring(stack, ps, "ss_pm", [128, 512], F32, 3)
    pyd = Ring(stack, ps, "ss_pyd", [128, 512], F32, 1)
    pyo = Ring(stack, ps, "ss_pyo", [128, 512], F32, 1)
    pst = Ring(stack, ps, "ss_pst", [128, 512], F32, 1)

    def block_prologue(b):
        t0 = b * TB
        raw, rk = rawr.next()
        if b == 0:
            P.pool(lambda e, raw=raw: e.memset(raw[:, :, 0:3], 0.0), wr=[rk])
            P.dma(lambda e, raw=raw: e.dma_start(out=raw[:, :, 3:], in_=zv[:, 8:20, 0:TB]), wr=[rk])
        else:
            P.dma(lambda e, raw=raw, t0=t0: e.dma_start(out=raw[:], in_=zv[:, 8:20, t0 - 3:t0 + TB]), wr=[rk])
        z_, zk = zr.next()
        P.dma(lambda e, z_=z_, t0=t0: e.dma_start(out=z_[:], in_=zv[:, 0:8, t0:t0 + TB]), wr=[zk], q="q_pool")
        dt_, dtk = dtr.next()
        P.dma(lambda e, dt_=dt_, t0=t0: e.dma_start(
            out=dt_[:], in_=dt_in[t0:t0 + TB, :].rearrange("(j p) h -> p j h", p=128)), wr=[dtk], q="q_pool")
        P.act(lambda e, z_=z_: e.activation(out=z_[:], in_=z_[:], func=AF.Silu), rd=[zk], wr=[zk])
        acc, acck = accr.next()
        acks = [Tok(f"acc{k}") for k in range(12)]
        for w in range(4):
            for k in range(12):
                if w == 0:
                    P.dve(lambda e, k=k, w=w, raw=raw, acc=acc: e.tensor_scalar(
                        out=acc[:, k, :], in0=raw[:, k, w:w + TB], scalar1=cw[:, k, w:w + 1], scalar2=None,
                        op0=ALU.mult), rd=[rk, cst], wr=[acks[k]])
                else:
                    P.dve(lambda e, k=k, w=w, raw=raw, acc=acc: e.scalar_tensor_tensor(
                        out=acc[:, k, :], in0=raw[:, k, w:w + TB], scalar=cw[:, k, w:w + 1], in1=acc[:, k, :],
                        op0=ALU.mult, op1=ALU.add), rd=[rk, cst, acks[k]], wr=[acks[k]])
        xT, xTk = xTr.next()
        bcT, bcTk = bcTr.next()
        for k in range(12):
            if k < 8:
                P.act(lambda e, k=k, xT=xT, acc=acc: e.activation(out=xT[:, k, :], in_=acc[:, k, :], func=AF.Silu,
                                                                  bias=cb[:, k:k + 1], scale=1.0),
                      rd=[acks[k], cst], wr=[xTk])
            else:
                P.act(lambda e, k=k, bcT=bcT, acc=acc: e.activation(out=bcT[:, k - 8, :], in_=acc[:, k, :],
                                                                    func=AF.Silu, bias=cb[:, k:k + 1], scale=1.0),
                      rd=[acks[k], cst], wr=[bcTk])
        oT, oTk = oTr.next()
        return dict(t0=t0, z_=z_, zk=zk, dt_=dt_, dtk=dtk, xT=xT, xTk=xTk, bcT=bcT, bcTk=bcTk, oT=oT, oTk=oTk)

    def front(B_, j, g):
        z_, zk, dt_, dtk, xT, xTk, bcT, bcTk = (B_[k_] for k_ in ('z_', 'zk', 'dt_', 'dtk', 'xT', 'xTk', 'bcT', 'bcTk'))
        cs = slice(j * 128, (j + 1) * 128)
        px, pxk = pm.next()
        for f in range(4):
            P.pe(lambda e, px=px, f=f, g=g, xT=xT, cs=cs: e.transpose(
                out=px[:, f * 128:(f + 1) * 128], in_=xT[:, g * 4 + f, cs], identity=idf[:]),
                rd=[xTk], wr=[pxk])
        xs, xsk = xsr.next()
        P.act(lambda e, xs=xs, px=px: e.copy(out=xs[:], in_=px[:]), rd=[pxk], wr=[xsk])
        pb, pbk = pm.next()
        pbb = pb[:].bitcast(BF16)
        P.pe(lambda e, pbb=pbb, bcT=bcT, g=g, cs=cs: e.transpose(out=pbb[:, 0:128], in_=bcT[:, g, cs],
                                                                 identity=idb[:]), rd=[bcTk], wr=[pbk])
        Bt, Btk = Btr.next()
        P.dve(lambda e, Bt=Bt, pbb=pbb: e.tensor_copy(out=Bt[:], in_=pbb[:, 0:128]), rd=[pbk], wr=[Btk])
        sm, smk = smr.next()
        kdA, kacol, keacol, keal, kdte = (Tok(n_) for n_ in ('dA', 'acol', 'eacol', 'eal', 'dte'))
        dtg = dt_[:, j, g * 8:(g + 1) * 8]
        P.dve(lambda e, sm=sm, dtg=dtg, g=g: e.tensor_tensor(out=sm[:, 0:8], in0=dtg,
                                                             in1=abc[:, g * 8:(g + 1) * 8], op=ALU.mult),
              rd=[dtk, cst], wr=[smk, kdA])
        pa, pak = pm.next()
        P.pe(lambda e, pa=pa, sm=sm: e.matmul(pa[:, 0:8], lhsT=tri[:], rhs=sm[:, 0:8], start=True, stop=True),
             rd=[kdA, cst], wr=[pak])
        P.act(lambda e, sm=sm, pa=pa: e.copy(out=sm[:, 8:16], in_=pa[:, 0:8]), rd=[pak, smk], wr=[kacol])
        P.act(lambda e, sm=sm, pa=pa: e.activation(out=sm[:, 16:24], in_=pa[:, 0:8], func=AF.Exp),
              rd=[pak, smk], wr=[keacol])
        rb, rbk = rbr.next()
        P.dve(lambda e, rb=rb, sm=sm: e.tensor_tensor(
            out=rb[:], in0=tri[:].unsqueeze(1).to_broadcast([128, 8, 128]),
            in1=sm[:, 0:8].unsqueeze(2).to_broadcast([128, 8, 128]), op=ALU.mult),
            rd=[kdA, cst], wr=[rbk])
        bc, bck = pbc.next()
        for hh in range(2):
            P.pe(lambda e, bc=bc, rb=rb, hh=hh: e.matmul(
                bc[:, hh * 512:(hh + 1) * 512], lhsT=ones_f[:],
                rhs=rb[:, hh * 4:(hh + 1) * 4, :].rearrange("p h l -> p (h l)"), start=True, stop=True),
                rd=[rbk, ck], wr=[bck])
        bc3 = bc[:].rearrange("p (h l) -> p h l", l=128)
        seg, segk = segr.next()
        for h in range(8):
            P.dve(lambda e, seg=seg, bc3=bc3, sm=sm, h=h: e.tensor_scalar(
                out=seg[:, h, :], in0=bc3[:, h, :], scalar1=sm[:, 8 + h:9 + h], scalar2=0.0,
                op0=ALU.subtract, op1=ALU.min), rd=[bck, kacol], wr=[segk])
        eb, ebk = ebr.next()
        P.act(lambda e, seg=seg, eb=eb: e.activation(out=eb[:], in_=seg[:], func=AF.Exp), rd=[segk], wr=[ebk])
        P.act(lambda e, sm=sm, bc3=bc3: e.activation(out=sm[:, 24:32], in_=bc3[:, :, 127], func=AF.Exp),
              rd=[bck, smk], wr=[keal])
        P.dve(lambda e, sm=sm, bc3=bc3: e.tensor_tensor(out=sm[:, 32:40], in0=bc3[:, :, 127],
                                                        in1=sm[:, 8:16], op=ALU.subtract),
              rd=[bck, kacol, smk], wr=[kdte])
        P.act(lambda e, sm=sm: e.activation(out=sm[:, 32:40], in_=sm[:, 32:40], func=AF.Exp),
              rd=[kdte], wr=[kdte])
        pc, pck = pm.next()
        P.pe(lambda e, pc=pc, bcT=bcT, g=g, cs=cs: e.matmul(
            pc[:, 0:128], lhsT=bcT[:, g, cs], rhs=bcT[:, 2 + g, cs], start=True, stop=True),
            rd=[bcTk], wr=[pck])
        cbm, cbmk = cbmr.next()
        P.dve(lambda e, cbm=cbm, pc=pc: e.tensor_tensor(out=cbm[:], in0=pc[:, 0:128], in1=tri[:], op=ALU.mult),
              rd=[pck, cst], wr=[cbmk])
        G, Gk = Gr.next()
        P.dve(lambda e, G=G, eb=eb, cbm=cbm: e.tensor_tensor(
            out=G[:], in0=eb[:], in1=cbm[:].unsqueeze(1).to_broadcast([128, 8, 128]), op=ALU.mult),
            rd=[ebk, cbmk], wr=[Gk])
        x32, x32k = x32r.next()
        xs3 = lambda t: t[:].rearrange("p (h d) -> p h d", d=64)
        P.pool(lambda e, x32=x32, xs=xs, dtg=dtg: e.tensor_tensor(
            out=xs3(x32), in0=xs3(xs), in1=dtg.unsqueeze(2).to_broadcast([128, 8, 64]), op=ALU.mult),
            rd=[xsk, dtk], wr=[x32k])
        xb, xbk = xbr.next()
        P.act(lambda e, xb=xb, x32=x32: e.copy(out=xb[:], in_=x32[:]), rd=[x32k], wr=[xbk])
        xe, xek = xer.next()
        P.dve(lambda e, xe=xe, x32=x32, sm=sm: e.tensor_tensor(
            out=xs3(xe), in0=xs3(x32), in1=sm[:, 32:40].unsqueeze(2).to_broadcast([128, 8, 64]),
            op=ALU.mult), rd=[x32k, kdte], wr=[xek])
        xd, xdk = xdr.next()
        P.pool(lambda e, xd=xd, xs=xs, g=g: e.tensor_tensor(
            out=xs3(xd), in0=xs3(xs), in1=Dbc[:, g * 8:(g + 1) * 8, :], op=ALU.mult),
            rd=[xsk, cst], wr=[xdk])
        return dict(B_=B_, j=j, g=g, cs=cs, sm=sm, smk=smk, keacol=keacol, keal=keal, Bt=Bt, Btk=Btk, G=G, Gk=Gk, xb=xb, xbk=xbk, xe=xe, xek=xek,
                    xd=xd, xdk=xdk, xs3=xs3)

    def back(F_):
        keacol, keal = F_['keacol'], F_['keal']
        B_, j, g, cs, sm, smk, Bt, Btk, G, Gk, xb, xbk, xe, xek, xd, xdk, xs3 = (F_[k_] for k_ in (
            'B_', 'j', 'g', 'cs', 'sm', 'smk', 'Bt', 'Btk', 'G', 'Gk', 'xb', 'xbk', 'xe', 'xek', 'xd', 'xdk', 'xs3'))
        z_, zk, bcT, bcTk, oT, oTk = (B_[k_] for k_ in ('z_', 'zk', 'bcT', 'bcTk', 'oT', 'oTk'))
        yd, ydk = pyd.next()
        for h in range(8):
            P.pe(lambda e, yd=yd, G=G, xb=xb, h=h: e.matmul(
                yd[:, h * 64:(h + 1) * 64], lhsT=G[:, h, :], rhs=xb[:, h * 64:(h + 1) * 64],
                start=True, stop=True), rd=[Gk, xbk], wr=[ydk])
        yo, yok = pyo.next()
        P.pe(lambda e, yo=yo, bcT=bcT, g=g, cs=cs: e.matmul(
            yo[:], lhsT=bcT[:, 2 + g, cs], rhs=Sbf[g][:], start=True, stop=True),
            rd=[bcTk, Sbk[g]], wr=[yok])
        y1, y1k = y1r.next()
        P.dve(lambda e, y1=y1, yo=yo, sm=sm: e.tensor_tensor(
            out=xs3(y1), in0=yo[:].rearrange("p (h d) -> p h d", d=64),
            in1=sm[:, 16:24].unsqueeze(2).to_broadcast([128, 8, 64]), op=ALU.mult),
            rd=[yok, keacol], wr=[y1k])
        P.dve(lambda e, y1=y1, yd=yd: e.tensor_tensor(out=y1[:], in0=y1[:], in1=yd[:], op=ALU.add),
              rd=[y1k, ydk], wr=[y1k])
        P.dve(lambda e, y1=y1, xd=xd: e.tensor_tensor(out=y1[:], in0=y1[:], in1=xd[:], op=ALU.add),
              rd=[y1k, xdk], wr=[y1k])
        st_, stk = pst.next()
        P.pe(lambda e, st_=st_, Bt=Bt, xe=xe: e.matmul(st_[:], lhsT=Bt[:], rhs=xe[:], start=True, stop=True),
             rd=[Btk, xek], wr=[stk])
        P.dve(lambda e, g=g, sm=sm: e.tensor_tensor(
            out=xs3(S32[g]), in0=xs3(S32[g]), in1=sm[:, 24:32].unsqueeze(2).to_broadcast([128, 8, 64]),
            op=ALU.mult), rd=[Sk[g], keal], wr=[Sk[g]])
        P.dve(lambda e, g=g, st_=st_: e.tensor_tensor(out=S32[g][:], in0=S32[g][:], in1=st_[:], op=ALU.add),
              rd=[Sk[g], stk], wr=[Sk[g]])
        P.act(lambda e, g=g: e.copy(out=Sbf[g][:], in_=S32[g][:]), rd=[Sk[g]], wr=[Sbk[g]])
        py, pyk = pm.next()
        for f in range(4):
            P.pe(lambda e, py=py, y1=y1, f=f: e.transpose(
                out=py[:, f * 128:(f + 1) * 128], in_=y1[:, f * 128:(f + 1) * 128], identity=idf[:]),
                rd=[y1k], wr=[pyk])
        gv, gvk = gvr.next()
        P.dve(lambda e, gv=gv, py=py, z_=z_, g=g, cs=cs: e.tensor_tensor(
            out=gv[:], in0=py[:].rearrange("p (f t) -> p f t", t=128), in1=z_[:, g * 4:(g + 1) * 4, cs],
            op=ALU.mult), rd=[pyk, zk], wr=[gvk])
        sq, sqk = sqr.next()
        P.act(lambda e, sq=sq, gv=gv: e.activation(out=sq[:], in_=gv[:], func=AF.Square), rd=[gvk], wr=[sqk])
        pq, pqk = pm.next()
        for f in range(4):
            P.pe(lambda e, pq=pq, sq=sq, f=f: e.matmul(pq[:, 0:128], lhsT=ones_f[:], rhs=sq[:, f, :],
                                                       start=(f == 0), stop=(f == 3)),
                 rd=[sqk, ck], wr=[pqk])
        rs, rsk = rsr.next()
        P.act(lambda e, rs=rs, pq=pq: e.activation(out=rs[:], in_=pq[:, 0:128], func=AF.Ln, bias=EPS,
                                                   scale=1.0 / 512), rd=[pqk], wr=[rsk])
        P.act(lambda e, rs=rs: e.activation(out=rs[:], in_=rs[:], func=AF.Exp, scale=-0.5), rd=[rsk], wr=[rsk])
        for f in range(4):
            P.dve(lambda e, oT=oT, gv=gv, rs=rs, f=f, g=g, cs=cs: e.scalar_tensor_tensor(
                out=oT[:, g * 4 + f, cs], in0=gv[:, f, :], scalar=nw[:, g * 4 + f:g * 4 + f + 1], in1=rs[:],
                op0=ALU.mult, op1=ALU.mult), rd=[gvk, rsk, cst], wr=[oTk])

    def block_epilogue(B_):
        oT, oTk, t0 = B_['oT'], B_['oTk'], B_['t0']
        P.dma(lambda e, oT=oT, t0=t0: e.dma_start(
            out=yT_out.rearrange("(k p) t -> p k t", p=128)[:, :, t0:t0 + TB], in_=oT[:]), rd=[oTk])

    passes = [(b, j, g) for b in range(NB) for j in range(TB // 128) for g in range(2)]
    blocks = {}
    pend = []
    DEPTH_F = 1

    def retire():
        F0 = pend.pop(0)
        back(F0)
        pb, pj, pg = F0["key"]
        if (pj, pg) == (TB // 128 - 1, 1):
            block_epilogue(blocks.pop(pb))

    for (b, j, g) in passes:
        if b not in blocks:
            blocks[b] = block_prologue(b)
        F_ = front(blocks[b], j, g)
        F_["key"] = (b, j, g)
        pend.append(F_)
        if len(pend) > DEPTH_F:
            retire()
    while pend:
        retire()


NCORES = 8
BATCH = 2
TC = BATCH * SEQ // NCORES
DEPTH = 4


def lay_colchunk(W):
    K, N = W.shape
    return np.ascontiguousarray(
        W.reshape(K // 128, 128, N // 128, 128).transpose(2, 1, 0, 3).reshape(N // 128, 128, K))


def lay_slab(W):
    K, N = W.shape
    return np.ascontiguousarray(W.reshape(K // 128, 128, N // 512, 512).transpose(2, 1, 0, 3))


def ssd_core_inputs(inp, j, gl):
    w_in = inp["ssd_w_in"][j]
    cols = np.concatenate([
        np.arange(gl * 1024, (gl + 1) * 1024),
        4096 + np.arange(gl * 1024, (gl + 1) * 1024),
        8192 + np.arange(gl * 256, (gl + 1) * 256),
        9216 + np.arange(gl * 256, (gl + 1) * 256)])
    ch = cols[1024:] - 4096
    dtc = 10240 + np.arange(gl * 16, (gl + 1) * 16)
    return {
        "s_w": lay_colchunk(w_in[:, cols]),
        "s_wdt": np.ascontiguousarray(w_in[:, dtc].reshape(KC_D, 128, 16).transpose(1, 0, 2)),
        "s_dtb": np.ascontiguousarray(inp["ssd_dt_bias"][j, gl * 16:(gl + 1) * 16]),
        "s_cw": np.ascontiguousarray(inp["ssd_conv_w"][j][:, ch].reshape(4, 12, 128).transpose(2, 1, 0)),
        "s_cb": np.ascontiguousarray(inp["ssd_conv_b"][j][ch].reshape(12, 128).T),
        "s_alog": np.ascontiguousarray(inp["ssd_a_log"][j, gl * 16:(gl + 1) * 16]),
        "s_dsk": np.ascontiguousarray(inp["ssd_d"][j, gl * 16:(gl + 1) * 16]),
        "s_nw": np.ascontiguousarray(inp["ssd_norm"][j, gl * 1024:(gl + 1) * 1024].reshape(8, 128).T),
    }


def attn_core_inputs(inp, j, gl):
    w = inp["da_w_qkv"][j]
    cols = np.concatenate([which * D + np.arange(gl * 512, (gl + 1) * 512) for which in range(3)])
    return {
        "a_w": lay_colchunk(w[:, cols]),
        "a_lam": np.ascontiguousarray(np.stack([inp["da_lambda_q1"][j], inp["da_lambda_k1"][j],
                                                inp["da_lambda_q2"][j], inp["da_lambda_k2"][j]])),
        "a_sub": np.ascontiguousarray(inp["da_subln"][j]),
    }


def lambda_init(i):
    return 0.8 - 0.6 * math.exp(-0.3 * i)


def _new_nc():
    _UID[0] = 0
    return bass.Bass("TRN2", target_bir_lowering=False)


def build_first():
    import contextlib
    nc = _new_nc()
    x = nc.dram_tensor("x", [TC, D], F32, kind="ExternalInput").ap()
    wpre = nc.dram_tensor("wpre", [D], F32, kind="ExternalInput").ap()
    xo = nc.dram_tensor("xo", [TC, D], F32, kind="ExternalOutput").ap()
    hT = nc.dram_tensor("hT", [D, TC], BF16, kind="ExternalOutput").ap()
    with contextlib.ExitStack() as st0:
        P = Prog(nc, st0)
        with contextlib.ExitStack() as st:
            idf, idb, _ = make_ident(nc, P, st)
            phase_norm(nc, P, st, TC, x, None, None, wpre, xo, hT, idb)
            P.emit()
    return nc


def hT_blk_fn(hTg):
    v = hTg.rearrange("(r k p) t -> p r k t", r=4, p=128)
    return lambda b: v[:, b // 4, :, (b % 4) * 512:(b % 4 + 1) * 512]


def hT_blk_fn_bm(hTg):
    v = hTg.rearrange("(b p) (k t) -> p b k t", p=128, t=512)
    return lambda b: v[:, b, :, :]


def build_ssd():
    import contextlib
    nc = _new_nc()
    hTg = nc.dram_tensor("hTg", [4 * D, TC], BF16, kind="ExternalInput").ap()
    w = nc.dram_tensor("s_w", [NZX, 128, D], F32, kind="ExternalInput").ap()
    wdt = nc.dram_tensor("s_wdt", [128, KC_D, 16], F32, kind="ExternalInput").ap()
    dtb = nc.dram_tensor("s_dtb", [16], F32, kind="ExternalInput").ap()
    cw = nc.dram_tensor("s_cw", [128, 12, 4], F32, kind="ExternalInput").ap()
    cb = nc.dram_tensor("s_cb", [128, 12], F32, kind="ExternalInput").ap()
    alog = nc.dram_tensor("s_alog", [16], F32, kind="ExternalInput").ap()
    dsk = nc.dram_tensor("s_dsk", [16], F32, kind="ExternalInput").ap()
    nw = nc.dram_tensor("s_nw", [128, 8], F32, kind="ExternalInput").ap()
    zx = nc.dram_tensor("zx", [NZX * 128, SEQ], F32).ap()
    dt = nc.dram_tensor("dt", [SEQ, 16], F32).ap()
    yT = nc.dram_tensor("yT", [1024, SEQ], BF16, kind="ExternalOutput").ap()
    with contextlib.ExitStack() as st0:
        P = Prog(nc, st0)
        with contextlib.ExitStack() as st:
            phase_ssd_in(nc, P, st, hT_blk_fn(hTg), w, wdt, dtb, zx, dt)
            P.emit()
        with contextlib.ExitStack() as st:
            idf, idb, _ = make_ident(nc, P, st)
            C = make_consts(nc, P, st)
            phase_ssd_scan(nc, P, st, zx, dt, cw, cb, alog, dsk, nw, yT, idf, idb, C)
            P.emit()
    return nc


def build_attn(lam_init):
    import contextlib
    nc = _new_nc()
    hTg = nc.dram_tensor("hTg", [4 * D, TC], BF16, kind="ExternalInput").ap()
    w = nc.dram_tensor("a_w", [12, 128, D], F32, kind="ExternalInput").ap()
    lam = nc.dram_tensor("a_lam", [4, 64], F32, kind="ExternalInput").ap()
    sub = nc.dram_tensor("a_sub", [128], F32, kind="ExternalInput").ap()
    oT = nc.dram_tensor("yT", [512, SEQ], BF16, kind="ExternalOutput").ap()
    with contextlib.ExitStack() as st0:
        P = Prog(nc, st0)
        with contextlib.ExitStack() as st:
            idf, idb, _ = make_ident(nc, P, st)
            C = make_consts(nc, P, st)
            qTd = nc.dram_tensor("sc_qT", [512, SEQ], BF16).ap()
            kTd = nc.dram_tensor("sc_kT", [512, SEQ], BF16).ap()
            Vd = nc.dram_tensor("sc_V", [4, 128, SEQ // 128, 128], BF16).ap()
            negMd = nc.dram_tensor("sc_negM", [128, 8], F32).ap()
            phase_attn_proj(nc, P, st, hT_blk_fn(hTg), w, qTd, kTd, Vd, negMd, idf, idb, C)
            P.emit()
        with contextlib.ExitStack() as st:
            idf, idb, _ = make_ident(nc, P, st)
            C = make_consts(nc, P, st)
            phase_attn_core(nc, P, st, qTd, kTd, Vd, negMd, lam, sub, lam_init, oT, idf, idb, C)
            P.emit()
    return nc


def emit_token_phases(nc, P, K, AT, wo, x, npost, nfpre, nfpost, npre_next, wg, wu, wd, xo, hTn, scr):
    import contextlib
    with contextlib.ExitStack() as st:
        phase_mmT(nc, P, st, TC, K, AT, wo, scr["m"], pfx="mo")
        P.emit()
    with contextlib.ExitStack() as st:
        idf, idb, _ = make_ident(nc, P, st)
        phase_norm(nc, P, st, TC, x, scr["m"], npost, nfpre, scr["x1"], scr["h2T"], idb)
        P.emit()
    with contextlib.ExitStack() as st:
        phase_ffn_gu(nc, P, st, TC, scr["h2T"], wg, wu, scr["aT"])
        P.emit()
    with contextlib.ExitStack() as st:
        phase_mmT(nc, P, st, TC, D_FF, scr["aT"], wd, scr["y2"], pfx="md")
        P.emit()
    with contextlib.ExitStack() as st:
        idf, idb, _ = make_ident(nc, P, st)
        phase_norm(nc, P, st, TC, scr["x1"], scr["y2"], nfpost, npre_next, xo, hTn, idb)
        P.emit()


def build_tok(K, last):
    import contextlib
    nc = _new_nc()
    AT = nc.dram_tensor("AT", [K, TC], BF16, kind="ExternalInput").ap()
    wo = nc.dram_tensor("wo", [4, 128, K // 128, 512], F32, kind="ExternalInput").ap()
    x = nc.dram_tensor("x", [TC, D], F32, kind="ExternalInput").ap()
    npost = nc.dram_tensor("npost", [D], F32, kind="ExternalInput").ap()
    nfpre = nc.dram_tensor("nfpre", [D], F32, kind="ExternalInput").ap()
    nfpost = nc.dram_tensor("nfpost", [D], F32, kind="ExternalInput").ap()
    npre_next = None if last else nc.dram_tensor("npre_next", [D], F32, kind="ExternalInput").ap()
    wg = nc.dram_tensor("wg", [KC_F, 128, D], F32, kind="ExternalInput").ap()
    wu = nc.dram_tensor("wu", [KC_F, 128, D], F32, kind="ExternalInput").ap()
    wd = nc.dram_tensor("wd", [4, 128, KC_F, 512], F32, kind="ExternalInput").ap()
    xo = nc.dram_tensor("xo", [TC, D], F32, kind="ExternalOutput").ap()
    hTn = None if last else nc.dram_tensor("hT", [D, TC], BF16, kind="ExternalOutput").ap()
    scr = {"m": nc.dram_tensor("sc_m", [TC, D], F32).ap(), "x1": nc.dram_tensor("sc_x1", [TC, D], F32).ap(),
           "h2T": nc.dram_tensor("sc_h2T", [D, TC], BF16).ap(), "aT": nc.dram_tensor("sc_aT", [D_FF, TC], BF16).ap(),
           "y2": nc.dram_tensor("sc_y2", [TC, D], F32).ap()}
    with contextlib.ExitStack() as st0:
        P = Prog(nc, st0)
        emit_token_phases(nc, P, K, AT, wo, x, npost, nfpre, nfpost, npre_next, wg, wu, wd, xo, hTn, scr)
    return nc


def kernel_multilaunch(**inp):
    inp = {k: np.asarray(v) for k, v in inp.items()}
    cores = list(range(NCORES))
    xs = np.ascontiguousarray(inp["x"].reshape(NCORES, TC, D))
    res = run_bass_kernel_spmd(build_first(), [{"x": xs[c], "wpre": inp["norm_mix_pre"][0]} for c in cores],
                               core_ids=cores)
    xcur = [r["xo"] for r in res.results]
    hT = [np.asarray(r["hT"]) for r in res.results]
    for i in range(DEPTH):
        j = i // 2
        hTg = [np.concatenate(hT[4 * b:4 * b + 4], axis=0) for b in range(BATCH)]
        if i % 2 == 0:
            ncm = build_ssd()
            maps = [dict(ssd_core_inputs(inp, j, c % 4), hTg=hTg[c // 4]) for c in cores]
            K = 4096
            wo = lay_slab(inp["ssd_w_out"][j])
        else:
            ncm = build_attn(lambda_init(i))
            maps = [dict(attn_core_inputs(inp, j, c % 4), hTg=hTg[c // 4]) for c in cores]
            K = 2048
            wo = lay_slab(inp["da_w_out"][j])
        res = run_bass_kernel_spmd(ncm, maps, core_ids=cores)
        yT = [np.asarray(r["yT"]) for r in res.results]
        yall = [np.concatenate(yT[4 * b:4 * b + 4], axis=0) for b in range(BATCH)]
        last = i == DEPTH - 1
        wg, wu, wd = lay_colchunk(inp["ffn_w_gate"][i]), lay_colchunk(inp["ffn_w_up"][i]), lay_slab(inp["ffn_w_down"][i])
        maps = []
        for c in cores:
            m = {"AT": np.ascontiguousarray(yall[c // 4][:, (c % 4) * TC:(c % 4 + 1) * TC]), "wo": wo, "x": xcur[c],
                 "npost": inp["norm_mix_post"][i], "nfpre": inp["norm_ffn_pre"][i], "nfpost": inp["norm_ffn_post"][i],
                 "wg": wg, "wu": wu, "wd": wd}
            if not last:
                m["npre_next"] = inp["norm_mix_pre"][i + 1]
            maps.append(m)
        res = run_bass_kernel_spmd(build_tok(K, last), maps, core_ids=cores)
        xcur = [r["xo"] for r in res.results]
        if not last:
            hT = [np.asarray(r["hT"]) for r in res.results]
    out = np.stack([np.asarray(a) for a in xcur]).reshape(BATCH, SEQ, D).astype(np.float32)
    return out


RG4 = [[0, 1, 2, 3], [4, 5, 6, 7]]
RG8 = [list(range(NCORES))]


def phase_select(nc, P, stack, g8, bsel_in, hTg):
    sb = nc.sbuf_tensor
    w = stack.enter_context(sb(U("sel_w"), [128, 2], F32))
    wk = Tok("sel_w")
    P.dma(lambda e: e.dma_start(out=w[:], in_=bsel_in.partition_broadcast(128)), wr=[wk])
    ar = Ring(stack, sb, "sel_a", [128, KC_D, 512], BF16, 2)
    br = Ring(stack, sb, "sel_b", [128, KC_D, 512], BF16, 2)
    orr = Ring(stack, sb, "sel_o", [128, KC_D, 512], BF16, 2)
    v8 = g8.rearrange("(r k p) t -> p r k t", r=8, p=128)
    vo = hTg.rearrange("(r k p) t -> p r k t", r=4, p=128)
    for r in range(4):
        for tb in range(TC // 512):
            ts = slice(tb * 512, (tb + 1) * 512)
            a, ak = ar.next()
            b, bk = br.next()
            P.dma(lambda e, a=a, r=r, ts=ts: e.dma_start(out=a[:], in_=v8[:, r, :, ts]), wr=[ak])
            P.dma(lambda e, b=b, r=r, ts=ts: e.dma_start(out=b[:], in_=v8[:, 4 + r, :, ts]), wr=[bk], q="q_pool")
            o, ok = orr.next()
            P.dve(lambda e, o=o, a=a: e.tensor_scalar(out=o[:], in0=a[:], scalar1=w[:, 0:1], scalar2=None,
                                                      op0=ALU.mult), rd=[ak, wk], wr=[ok])
            P.dve(lambda e, o=o, b=b: e.scalar_tensor_tensor(out=o[:], in0=b[:], scalar=w[:, 1:2], in1=o[:],
                                                             op0=ALU.mult, op1=ALU.add), rd=[bk, wk, ok], wr=[ok])
            P.dma(lambda e, o=o, r=r, ts=ts: e.dma_start(out=vo[:, r, :, ts], in_=o[:]), rd=[ok])


def build_fused():
    import contextlib
    nc = _new_nc()
    dt_ = nc.dram_tensor
    ext = lambda n, s, d=F32: dt_(n, s, d, kind="ExternalInput").ap()
    x = ext("x", [TC, D])
    sel4 = ext("sel4", [4])
    nmpre, nmpost = ext("nmpre", [DEPTH, D]), ext("nmpost", [DEPTH, D])
    nfpre, nfpost = ext("nfpre", [DEPTH, D]), ext("nfpost", [DEPTH, D])
    ffn = [(ext(f"wg{i}", [KC_F, 128, D]), ext(f"wu{i}", [KC_F, 128, D]), ext(f"wd{i}", [4, 128, KC_F, 512]))
           for i in range(DEPTH)]
    ssd = [dict(w=ext(f"s_w{j}", [NZX, 128, D]), wdt=ext(f"s_wdt{j}", [128, KC_D, 16]), dtb=ext(f"s_dtb{j}", [16]),
                cw=ext(f"s_cw{j}", [128, 12, 4]), cb=ext(f"s_cb{j}", [128, 12]), alog=ext(f"s_alog{j}", [16]),
                dsk=ext(f"s_dsk{j}", [16]), nw=ext(f"s_nw{j}", [128, 8]), wo=ext(f"s_wo{j}", [4, 128, 8, 512]))
           for j in range(2)]
    att = [dict(w=ext(f"a_w{j}", [12, 128, D]), lam=ext(f"a_lam{j}", [4, 64]), sub=ext(f"a_sub{j}", [128]),
                wo=ext(f"a_wo{j}", [4, 128, 4, 512])) for j in range(2)]
    xo = dt_("xo", [TC, D], F32, kind="ExternalOutput").ap()
    scr = lambda n, s, d=F32: dt_(n, s, d).ap()
    hT4 = scr("sc_hT4", [16 * 128, KC_D * 512], BF16)
    hTg = scr("sc_hTg", [16 * 128, KC_D * 512], BF16)
    zx = scr("sc_zx", [NZX * 128, SEQ])
    dtt = scr("sc_dt", [SEQ, 16])
    yT = scr("sc_yT", [1024, SEQ], BF16)
    mpA, mpB = scr("sc_mpA", [SEQ, 1024]), scr("sc_mpB", [SEQ, 1024])
    mA, mB = scr("sc_mA", [TC, 1024]), scr("sc_mB", [TC, 1024])
    qTd = scr("sc_qT", [512, SEQ], BF16)
    kTd = scr("sc_kT", [512, SEQ], BF16)
    Vd = scr("sc_V", [4, 128, SEQ // 128, 128], BF16)
    negMd = scr("sc_negM", [128, 8])
    xa = scr("sc_xa", [TC, D])
    x1 = scr("sc_x1", [TC, D])
    h2T = scr("sc_h2T", [D, TC], BF16)
    aT = scr("sc_aT", [D_FF, TC], BF16)
    y2 = scr("sc_y2", [TC, D])
    with contextlib.ExitStack() as st0:
        P = Prog(nc, st0)
        with contextlib.ExitStack() as st:
            idf, idb, _ = make_ident(nc, P, st)
            phase_norm(nc, P, st, TC, x, None, None, nmpre[0], None, hT4, idb, sel4=sel4)
            P.emit(reorder=True)
        xcur = x
        for i in range(DEPTH):
            j = i // 2
            last = i == DEPTH - 1
            ptoks = [Tok(f"hTg{pc}") for pc in range(8)]
            for pc in range(8):
                rs_ = slice(pc * 256, (pc + 1) * 256)
                P.cc(lambda e, rs_=rs_: e.collective_compute("AllReduce", ALU.add, replica_groups=RG4,
                                                             ins=[hT4[rs_, :]], outs=[hTg[rs_, :]]),
                     wr=[ptoks[pc]])
            hT_tok = lambda b, ptoks=ptoks: ptoks[b // 2]
            if i % 2 == 0:
                s = ssd[j]
                with contextlib.ExitStack() as st:
                    phase_ssd_in(nc, P, st, hT_blk_fn_bm(hTg), s["w"], s["wdt"], s["dtb"], zx, dtt, hT_tok=hT_tok)
                    P.emit(reorder=True)
                with contextlib.ExitStack() as st:
                    idf, idb, _ = make_ident(nc, P, st)
                    C = make_consts(nc, P, st)
                    phase_ssd_scan(nc, P, st, zx, dtt, s["cw"], s["cb"], s["alog"], s["dsk"], s["nw"], yT,
                                   idf, idb, C)
                    P.emit(reorder=True)
                K, yT_use, wo = 1024, yT, s["wo"]
            else:
                a = att[j]
                with contextlib.ExitStack() as st:
                    idf, idb, _ = make_ident(nc, P, st)
                    C = make_consts(nc, P, st)
                    phase_attn_proj(nc, P, st, hT_blk_fn_bm(hTg), a["w"], qTd, kTd, Vd, negMd, idf, idb, C,
                                    hT_tok=hT_tok)
                    P.emit(reorder=True)
                with contextlib.ExitStack() as st:
                    idf, idb, _ = make_ident(nc, P, st)
                    C = make_consts(nc, P, st)
                    phase_attn_core(nc, P, st, qTd, kTd, Vd, negMd, a["lam"], a["sub"], lambda_init(i),
                                    yT[0:512, :], idf, idb, C)
                    P.emit(reorder=True)
                K, yT_use, wo = 512, yT[0:512, :], a["wo"]
            with contextlib.ExitStack() as st:
                htoks = ([], [])
                def rs_half(hf, htoks=htoks):
                    src, dst = (mpA, mA) if hf == 0 else (mpB, mB)
                    P.cc(lambda e: e.collective_compute("ReduceScatter", ALU.add, replica_groups=RG4,
                                                        ins=[src], outs=[dst]), rd=list(htoks[hf]))
                phase_mmT(nc, P, st, SEQ, K, yT_use, wo, None, Th=2048, pfx="mo", halves=(mpA, mpB), htoks=htoks,
                          half_done=rs_half)
                P.emit(reorder=True)
            with contextlib.ExitStack() as st:
                idf, idb, _ = make_ident(nc, P, st)
                phase_norm(nc, P, st, TC, xcur, (mA, mB), nmpost[i], nfpre[i], x1, h2T, idb)
                P.emit(reorder=True)
            wg, wu, wd = ffn[i]
            with contextlib.ExitStack() as st:
                phase_ffn_gu(nc, P, st, TC, h2T, wg, wu, aT)
                P.emit(reorder=True)
            with contextlib.ExitStack() as st:
                phase_mmT(nc, P, st, TC, D_FF, aT, wd, y2, pfx="md")
                P.emit(reorder=True)
            with contextlib.ExitStack() as st:
                idf, idb, _ = make_ident(nc, P, st)
                phase_norm(nc, P, st, TC, x1, y2, nfpost[i], None if last else nmpre[i + 1],
                           xo if last else xa, None if last else hT4, idb, sel4=None if last else sel4)
                P.emit(reorder=True)
            xcur = xa
    return nc


def fused_inputs(inp):
    inp = {k: np.asarray(v) for k, v in inp.items()}
    xs = np.ascontiguousarray(inp["x"].reshape(NCORES, TC, D))
    shared = {"nmpre": inp["norm_mix_pre"], "nmpost": inp["norm_mix_post"],
              "nfpre": inp["norm_ffn_pre"], "nfpost": inp["norm_ffn_post"]}
    for i in range(DEPTH):
        shared[f"wg{i}"] = lay_colchunk(inp["ffn_w_gate"][i])
        shared[f"wu{i}"] = lay_colchunk(inp["ffn_w_up"][i])
        shared[f"wd{i}"] = lay_slab(inp["ffn_w_down"][i])
    maps = []
    for c in range(NCORES):
        gl = c % 4
        m = dict(shared)
        m["x"] = xs[c]
        m["sel4"] = np.eye(4, dtype=np.float32)[c % 4]
        for j in range(2):
            for k, v in ssd_core_inputs(inp, j, gl).items():
                m[f"{k}{j}"] = v
            m[f"s_wo{j}"] = lay_slab(inp["ssd_w_out"][j][gl * 1024:(gl + 1) * 1024])
            for k, v in attn_core_inputs(inp, j, gl).items():
                m[f"{k}{j}"] = v
            m[f"a_wo{j}"] = lay_slab(inp["da_w_out"][j][gl * 512:(gl + 1) * 512])
        maps.append(m)
    return maps


def kernel_fused(**inp):
    maps = fused_inputs(inp)
    res = run_bass_kernel_spmd(build_fused(), maps, core_ids=list(range(NCORES)))
    return np.stack([np.asarray(r["xo"]) for r in res.results]).reshape(BATCH, SEQ, D).astype(np.float32)


def kernel(**inputs):
    return kernel_fused(**inputs)
```

```python
import math
import numpy as np
import ml_dtypes
import concourse.bass as bass
import concourse.mybir as mybir
from concourse.bass_utils import run_bass_kernel_spmd

F32 = mybir.dt.float32
BF16 = mybir.dt.bfloat16
AF = mybir.ActivationFunctionType
ALU = mybir.AluOpType
AX = mybir.AxisListType

NDMA_SLOTS = 6


class Tok:
    __slots__ = ("w", "r", "ra", "name")

    def __init__(self, name=""):
        self.w = {}
        self.r = {}
        self.ra = []
        self.name = name


class Prog:
    COMPUTE = ("pe", "act", "dve", "pool", "cc")
    QUEUES = ("q_sp", "q_pool", "q_act")
    ISSUE = {"pe": "pe", "act": "act", "dve": "dve", "pool": "pool", "cc": "pool",
             "q_sp": "sp", "q_pool": "pool", "q_act": "act"}

    def __init__(self, nc, stack):
        self.nc = nc
        self.ops = []
        self.base = 0
        self.cnt = {e: 0 for e in self.COMPUTE}
        self.dcnt = {q: 0 for q in self.QUEUES}
        self.waited = {e: {} for e in ("pe", "act", "dve", "pool", "sp")}
        self.sems = {}
        for e in self.COMPUTE:
            self.sems[e] = stack.enter_context(nc.semaphore(f"s_{e}"))
        for q in self.QUEUES:
            for s in range(NDMA_SLOTS):
                self.sems[(q, s)] = stack.enter_context(nc.semaphore(f"s_{q}{s}"))

    def op(self, eng, fn, rd=(), wr=()):
        self.ops.append((eng, fn, tuple(rd), tuple(wr)))

    def pe(self, fn, rd=(), wr=()):
        self.op("pe", fn, rd, wr)

    def act(self, fn, rd=(), wr=()):
        self.op("act", fn, rd, wr)

    def dve(self, fn, rd=(), wr=()):
        self.op("dve", fn, rd, wr)

    def pool(self, fn, rd=(), wr=()):
        self.op("pool", fn, rd, wr)

    def dma(self, fn, rd=(), wr=(), q="q_sp"):
        self.op(q, fn, rd, wr)

    def cc(self, fn, rd=(), wr=()):
        self.op("cc", fn, rd, wr)

    COST = {"pe": 0.27, "act": 0.5, "dve": 0.55, "pool": 1.2, "cc": 0.1, "q_sp": 0.06, "q_pool": 0.3, "q_act": 0.06}
    DONE_LAT = {"q_sp": 2.5, "q_pool": 3.0, "q_act": 2.5, "cc": 100.0}

    def _schedule(self, ops, order_deps, W=16):
        n = len(ops)
        issue = [self.ISSUE[o[0]] for o in ops]
        per = {e: [] for e in ("pe", "act", "dve", "pool", "sp")}
        for i in range(n):
            per[issue[i]].append(i)
        head = {e: 0 for e in per}
        done = [False] * n
        fin = [0.0] * n
        tfree = {e: 0.0 for e in per}
        order = []
        ready_t = [None] * n
        while len(order) < n:
            best = None
            for e, lst in per.items():
                h = head[e]
                while h < len(lst) and done[lst[h]]:
                    h += 1
                head[e] = h
                cnt = 0
                k = h
                while k < len(lst) and cnt < W:
                    i = lst[k]
                    k += 1
                    if done[i]:
                        continue
                    cnt += 1
                    rt = ready_t[i]
                    if rt is None:
                        ok = True
                        rt = 0.0
                        for d in order_deps[i]:
                            if not done[d]:
                                ok = False
                                break
                            f = fin[d] + (0.0 if issue[d] == e and not ops[d][0].startswith("q_") else 0.25)
                            if f > rt:
                                rt = f
                        if not ok:
                            continue
                        ready_t[i] = rt
                    st = rt if rt > tfree[e] else tfree[e]
                    key = (st, i)
                    if best is None or key < best[0]:
                        best = (key, i, e)
            (st, _), i, e = best
            c = self.COST[ops[i][0]]
            tfree[e] = st + c
            fin[i] = st + c + self.DONE_LAT.get(ops[i][0], 0.0)
            done[i] = True
            order.append(i)
        return order

    def emit(self, reorder=False):
        nc = self.nc
        ops = self.ops
        n = len(ops)
        base = self.base
        deps = [None] * n
        odeps = [None] * n
        signals = [False] * n
        for i, (eng, fn, rd, wr) in enumerate(ops):
            gi = base + i
            is_dma = eng.startswith("q_")
            key = ("dma", gi) if is_dma else eng
            d = set()
            od = set()
            for t in rd:
                for k, j in t.w.items():
                    if j >= base:
                        od.add(j - base)
                    if k == key and key == "pe":
                        continue
                    if j >= base:
                        d.add(j - base)
            for t in wr:
                for k, j in t.w.items():
                    if j >= base:
                        od.add(j - base)
                    if k == key:
                        continue
                    if j >= base:
                        d.add(j - base)
                for k, j in t.ra:
                    if j >= base:
                        od.add(j - base)
                    if k == key:
                        continue
                    if j >= base:
                        d.add(j - base)
            for t in rd:
                t.r[key] = gi
                t.ra.append((key, gi))
            for t in wr:
                t.w = {key: gi}
                t.r = {}
                t.ra = []
            od.discard(i)
            deps[i] = d
            odeps[i] = od
            for j in d:
                signals[j] = True
            if is_dma or eng == "cc":
                signals[i] = True
        if reorder and n > 2:
            order = self._schedule(ops, odeps)
            pos = [0] * n
            for p_, i in enumerate(order):
                pos[i] = p_
            ops = [ops[i] for i in order]
            deps = [{pos[j] for j in deps[i]} for i in order]
            signals = [signals[i] for i in order]
            self.ops = ops
        sig = [None] * n
        dma_idx = [None] * n
        for i, (eng, fn, rd, wr) in enumerate(ops):
            if eng.startswith("q_"):
                k = self.dcnt[eng]
                self.dcnt[eng] += 1
                dma_idx[i] = k
                sig[i] = ((eng, k % NDMA_SLOTS), 16 * (k // NDMA_SLOTS + 1))
            elif signals[i]:
                self.cnt[eng] += 1
                sig[i] = (eng, self.cnt[eng])
        waited = self.waited
        plan = {e: [] for e in ("pe", "act", "dve", "pool", "sp")}
        for i, (eng, fn, rd, wr) in enumerate(ops):
            ie = self.ISSUE[eng]
            ws = []
            need = {}
            for j in deps[i]:
                sk, v = sig[j]
                if need.get(sk, 0) < v:
                    need[sk] = v
            if eng.startswith("q_"):
                k = dma_idx[i]
                if k >= NDMA_SLOTS:
                    sk = (eng, k % NDMA_SLOTS)
                    v = 16 * (k // NDMA_SLOTS)
                    if need.get(sk, 0) < v:
                        need[sk] = v
            for sk, v in need.items():
                if waited[ie].get(sk, 0) < v:
                    waited[ie][sk] = v
                    ws.append((sk, v))
            plan[ie].append((i, ws))
        final = []
        for q, c in self.dcnt.items():
            for s in range(min(c, NDMA_SLOTS)):
                last = 16 * ((c - 1 - s) // NDMA_SLOTS + 1)
                final.append(((q, s), last))
        if self.cnt["cc"]:
            final.append(("cc", self.cnt["cc"]))
        sems = self.sems
        with nc.Block() as block:
            engs = {"pe": block.tensor, "act": block.scalar, "dve": block.vector,
                    "pool": block.gpsimd, "sp": block.sync}

            def mk(ie):
                def body(e):
                    for (i, ws) in plan[ie]:
                        for sk, v in ws:
                            e.wait_ge(sems[sk], v)
                        ins = ops[i][1](e)
                        if sig[i] is not None:
                            sk, v = sig[i]
                            ins.then_inc(sems[sk], 1 if isinstance(sk, str) else 16)
                    if ie == "sp":
                        for sk, v in final:
                            if waited["sp"].get(sk, 0) < v:
                                waited["sp"][sk] = v
                                e.wait_ge(sems[sk], v)
                return body

            for ie in ("sp", "pe", "act", "dve", "pool"):
                if plan[ie] or ie == "sp":
                    engs[ie](mk(ie))
        self.base += n
        self.ops = []


_UID = [0]


def U(name):
    _UID[0] += 1
    return f"{name}_{_UID[0]}"


class Ring:
    def __init__(self, stack, alloc, name, shape, dtype, n):
        self.bufs = []
        for i in range(n):
            t = stack.enter_context(alloc(U(f"{name}{i}"), list(shape), dtype))
            self.bufs.append((t, Tok(f"{name}{i}")))
        self.i = 0

    def next(self):
        b = self.bufs[self.i % len(self.bufs)]
        self.i += 1
        return b


D = 2048
KC_D = D // 128
EPS = 1e-6


def bcast_rows(ap_1d, nparts):
    return ap_1d.partition_broadcast(nparts)


def emit_rstd(P, ss, rstd, tok_ss, tok_rstd, n, width):
    P.act(lambda e: e.activation(out=rstd, in_=ss, func=AF.Ln, bias=EPS, scale=1.0 / width),
          rd=[tok_ss], wr=[tok_rstd])
    P.act(lambda e: e.activation(out=rstd, in_=rstd, func=AF.Exp, scale=-0.5),
          rd=[tok_rstd], wr=[tok_rstd])


def phase_norm(nc, P, stack, T, x_in, y_in, wpost, wpre, x_out, hT_out, ident_bf, sel4=None):
    nt = T // 128
    sb = nc.sbuf_tensor
    xr = Ring(stack, sb, "n_x", [128, D], F32, 2)
    yr = Ring(stack, sb, "n_y", [128, D], F32, 2) if y_in is not None else None
    tr = Ring(stack, sb, "n_t", [128, D], F32, 2)
    sqr = Ring(stack, sb, "n_sq", [128, D], BF16, 1)
    hr = Ring(stack, sb, "n_h", [128, D], BF16, 2)
    htr = Ring(stack, sb, "n_hT", [128, KC_D, 512], BF16, 2)
    str_ = Ring(stack, sb, "n_st", [128, 4], F32, 4)
    ptr = Ring(stack, nc.psum_tensor, "n_pt", [128, 1024], BF16, 4)
    if sel4 is not None:
        scr4 = Ring(stack, sb, "n_sc4", [128, KC_D, 512], BF16, 2)
        s4 = stack.enter_context(sb(U("n_sel4"), [128, 4], F32))
        s4k = Tok("n_sel4")
        P.dma(lambda e: e.dma_start(out=s4[:], in_=sel4.partition_broadcast(128)), wr=[s4k])
    consts = []
    for nm, w in (("n_wpost", wpost), ("n_wpre", wpre)):
        if w is None:
            consts.append((None, None))
            continue
        t = stack.enter_context(sb(U(nm), [128, D], F32))
        tk = Tok(nm)
        P.dma(lambda e, t=t, w=w: e.dma_start(out=t[:], in_=bcast_rows(w, 128)), wr=[tk])
        consts.append((t, tk))
    (wpost_t, wpost_k), (wpre_t, wpre_k) = consts
    for i in range(nt):
        rows = slice(i * 128, (i + 1) * 128)
        xt, xk = xr.next()
        P.dma(lambda e, xt=xt, rows=rows: e.dma_start(out=xt[:], in_=x_in[rows, :]), wr=[xk])
        cur, curk = xt, xk
        if y_in is not None:
            yt, yk = yr.next()
            if isinstance(y_in, tuple):
                P.dma(lambda e, yt=yt, rows=rows: e.dma_start(out=yt[:, 0:1024], in_=y_in[0][rows, :]), wr=[yk],
                      q="q_pool")
                P.dma(lambda e, yt=yt, rows=rows: e.dma_start(out=yt[:, 1024:2048], in_=y_in[1][rows, :]), wr=[yk],
                      q="q_pool")
            else:
                P.dma(lambda e, yt=yt, rows=rows: e.dma_start(out=yt[:], in_=y_in[rows, :]), wr=[yk], q="q_pool")
            sq, sqk = sqr.next()
            st, stk = str_.next()
            P.act(lambda e, sq=sq, yt=yt, st=st: e.activation(out=sq[:], in_=yt[:], func=AF.Square,
                                                               accum_out=st[:, 0:1]),
                  rd=[yk], wr=[sqk, stk])
            emit_rstd(P, st[:, 0:1], st[:, 1:2], stk, stk, 128, D)
            tt, tk = tr.next()
            P.dve(lambda e, tt=tt, yt=yt, st=st: e.scalar_tensor_tensor(
                out=tt[:], in0=yt[:], scalar=st[:, 1:2], in1=wpost_t[:], op0=ALU.mult, op1=ALU.mult),
                rd=[yk, stk, wpost_k], wr=[tk])
            P.dve(lambda e, tt=tt, xt=xt: e.tensor_tensor(out=tt[:], in0=tt[:], in1=xt[:], op=ALU.add),
                  rd=[tk, xk], wr=[tk])
            cur, curk = tt, tk
        if x_out is not None:
            P.dma(lambda e, cur=cur, rows=rows: e.dma_start(out=x_out[rows, :], in_=cur[:]), rd=[curk])
        if wpre is not None:
            sq, sqk = sqr.next()
            st, stk = str_.next()
            P.act(lambda e, sq=sq, cur=cur, st=st: e.activation(out=sq[:], in_=cur[:], func=AF.Square,
                                                                accum_out=st[:, 0:1]),
                  rd=[curk], wr=[sqk, stk])
            emit_rstd(P, st[:, 0:1], st[:, 1:2], stk, stk, 128, D)
            ht, hk = hr.next()
            P.dve(lambda e, ht=ht, cur=cur, st=st: e.scalar_tensor_tensor(
                out=ht[:], in0=cur[:], scalar=st[:, 1:2], in1=wpre_t[:], op0=ALU.mult, op1=ALU.mult),
                rd=[curk, stk, wpre_k], wr=[hk])
            if i % 4 == 0:
                hT, hTk = htr.next()
            tsl = slice((i % 4) * 128, (i % 4) * 128 + 128)
            for half in range(2):
                pt, ptk = ptr.next()
                for j in range(8):
                    kc = half * 8 + j
                    P.pe(lambda e, pt=pt, ht=ht, j=j, kc=kc: e.transpose(
                        out=pt[:, j * 128:(j + 1) * 128], in_=ht[:, kc * 128:(kc + 1) * 128],
                        identity=ident_bf[:]), rd=[hk], wr=[ptk])
                src = lambda pt: pt[:].rearrange("p (k t) -> p k t", t=128)
                if half == 0:
                    P.act(lambda e, hT=hT, pt=pt, tsl=tsl: e.copy(out=hT[:, 0:8, tsl], in_=src(pt)),
                          rd=[ptk], wr=[hTk])
                else:
                    P.dve(lambda e, hT=hT, pt=pt, tsl=tsl: e.tensor_copy(out=hT[:, 8:16, tsl], in_=src(pt)),
                          rd=[ptk], wr=[hTk])
            if i % 4 == 3 and sel4 is None:
                blk = i // 4
                P.dma(lambda e, hT=hT, blk=blk: e.dma_start(
                    out=hT_out.rearrange("(k p) t -> p k t", p=128)[:, :, blk * 512:(blk + 1) * 512],
                    in_=hT[:]), rd=[hTk], q="q_pool")
            elif i % 4 == 3:
                blk = i // 4
                for s_ in range(4):
                    sc, sck = scr4.next()
                    if s_ % 2 == 0:
                        P.dve(lambda e, sc=sc, hT=hT, s_=s_: e.tensor_scalar(
                            out=sc[:], in0=hT[:], scalar1=s4[:, s_:s_ + 1], scalar2=None, op0=ALU.mult),
                            rd=[hTk, s4k], wr=[sck])
                    else:
                        P.act(lambda e, sc=sc, hT=hT, s_=s_: e.activation(
                            out=sc[:], in_=hT[:], func=AF.Copy, scale=s4[:, s_:s_ + 1]),
                            rd=[hTk, s4k], wr=[sck])
                    P.dma(lambda e, sc=sc, blk=blk, s_=s_: e.dma_start(
                        out=hT_out.rearrange("(s b p) f -> p s b f", s=4, p=128)[:, s_, blk, :],
                        in_=sc[:].rearrange("p k t -> p (k t)")), rd=[sck], q="q_pool" if s_ % 2 else "q_sp")


def make_ident(nc, P, stack):
    idf = stack.enter_context(nc.sbuf_tensor(U("ident_f"), [128, 128], F32))
    idb = stack.enter_context(nc.sbuf_tensor(U("ident_b"), [128, 128], BF16))
    k = Tok("ident")
    P.pool(lambda e: e.memset(idf[:], 1.0), wr=[k])
    P.pool(lambda e: e.affine_select(out=idf[:], in_=idf[:], pattern=[[-1, 128]], compare_op=ALU.is_equal,
                                     fill=0.0, base=0, channel_multiplier=1), rd=[k], wr=[k])
    P.dve(lambda e: e.tensor_copy(out=idb[:], in_=idf[:]), rd=[k], wr=[k])
    return idf, idb, k


D_FF = 5632
KC_F = D_FF // 128


def cast_op(P, idx, out, in_, rd, wr):
    if idx % 2 == 0:
        P.dve(lambda e: e.tensor_copy(out=out, in_=in_), rd=rd, wr=wr)
    else:
        P.act(lambda e: e.copy(out=out, in_=in_), rd=rd, wr=wr)


def phase_ffn_gu(nc, P, stack, T, hT_in, wg_l, wu_l, actT_out):
    sb = nc.sbuf_tensor
    hT = stack.enter_context(sb(U("gu_hT"), [128, KC_D, T], BF16))
    hTk = [Tok(f"gu_hT{k}") for k in range(KC_D // 4)]
    hv = hT_in.rearrange("(k p) t -> p k t", p=128)
    for g in range(KC_D // 4):
        P.dma(lambda e, g=g: e.dma_start(out=hT[:, g * 4:(g + 1) * 4, :], in_=hv[:, g * 4:(g + 1) * 4, :]),
              wr=[hTk[g]], q="q_pool")
    wst = [Ring(stack, sb, f"gu_wst{m}", [128, D], F32, 2) for m in range(2)]
    wbf = [Ring(stack, sb, f"gu_wbf{m}", [128, KC_D, 128], BF16, 2) for m in range(2)]
    ps = [Ring(stack, nc.psum_tensor, f"gu_ps{m}", [128, 512], F32, 3) for m in range(2)]
    sgr = Ring(stack, sb, "gu_sg", [128, 512], F32, 2)
    ar = Ring(stack, sb, "gu_a", [128, 512], BF16, 4)
    nsl = T // 512
    ci = 0
    for fc in range(KC_F):
        wb = []
        for m, wl in enumerate((wg_l, wu_l)):
            st_, stk = wst[m].next()
            P.dma(lambda e, st_=st_, wl=wl, fc=fc: e.dma_start(out=st_[:], in_=wl[fc]), wr=[stk])
            b, bk = wbf[m].next()
            cast_op(P, ci, b[:].rearrange("p k j -> p (k j)"), st_[:], [stk], [bk])
            ci += 1
            wb.append((b, bk))
        for sl in range(nsl):
            tsl = slice(sl * 512, (sl + 1) * 512)
            pp = []
            for m in range(2):
                p_, pk = ps[m].next()
                b, bk = wb[m]
                for kc in range(KC_D):
                    P.pe(lambda e, p_=p_, b=b, kc=kc, tsl=tsl: e.matmul(
                        p_[:], lhsT=b[:, kc, :], rhs=hT[:, kc, tsl], start=(kc == 0), stop=(kc == KC_D - 1)),
                        rd=[bk, hTk[kc // 4]], wr=[pk])
                pp.append((p_, pk))
            sg, sgk = sgr.next()
            P.act(lambda e, sg=sg, p_=pp[0][0]: e.activation(out=sg[:], in_=p_[:], func=AF.Silu),
                  rd=[pp[0][1]], wr=[sgk])
            a, ak = ar.next()
            P.dve(lambda e, a=a, sg=sg, p_=pp[1][0]: e.tensor_tensor(out=a[:], in0=sg[:], in1=p_[:], op=ALU.mult),
                  rd=[sgk, pp[1][1]], wr=[ak])
            P.dma(lambda e, a=a, fc=fc, tsl=tsl: e.dma_start(out=actT_out[fc * 128:(fc + 1) * 128, tsl], in_=a[:]),
                  rd=[ak], q="q_pool")


def phase_mmT(nc, P, stack, T, K, AT_in, w_l, y_out, Th=1024, pfx="mt", halves=None, htoks=None, half_done=None):
    KC = K // 128
    G = 4
    NG = KC // G
    Th = min(Th, T, 1024)
    NT = Th // 128
    sb = nc.sbuf_tensor
    nAT = 1 if halves is None else 2
    ATs = [(stack.enter_context(sb(U(f"{pfx}_AT"), [128, KC, Th], BF16)), [Tok(f"{pfx}_AT{g}") for g in range(NG)])
           for _ in range(nAT)]
    ati = 0
    AT, ATk = ATs[0]
    wst = Ring(stack, sb, f"{pfx}_wst", [128, G, 512], F32, 3)
    wbf = Ring(stack, sb, f"{pfx}_wbf", [128, G, 512], BF16, 4)
    ps = Ring(stack, nc.psum_tensor, f"{pfx}_ps", [128, 512], F32, 8)
    ys = Ring(stack, sb, f"{pfx}_ys", [128, 512], F32, 4)
    av = AT_in.rearrange("(k p) t -> p k t", p=128)
    ci = 0
    ei = 0
    if halves is None:
        order = [(th, s) for th in range(T // Th) for s in range(4)]
    else:
        order = [(th, s) for s in range(4) for th in range(T // Th)]
    last_th = None
    for (th, s) in order:
        if th != last_th:
            last_th = th
            AT, ATk = ATs[ati % nAT]
            ati += 1
            for g in range(NG):
                P.dma(lambda e, g=g, th=th, AT=AT: e.dma_start(out=AT[:, g * G:(g + 1) * G, :],
                                                        in_=av[:, g * G:(g + 1) * G, th * Th:(th + 1) * Th]),
                      wr=[ATk[g]], q="q_pool" if halves is None else "q_sp")
        if True:
            banks = [ps.next() for _ in range(NT)]
            for g in range(NG):
                st_, stk = wst.next()
                P.dma(lambda e, st_=st_, s=s, g=g: e.dma_start(out=st_[:], in_=w_l[s, :, g * G:(g + 1) * G, :]),
                      wr=[stk])
                wb, wbk = wbf.next()
                cast_op(P, ci, wb[:], st_[:], [stk], [wbk])
                ci += 1
                for tl in range(NT):
                    p_, pk = banks[tl]
                    for kk in range(G):
                        kc = g * G + kk
                        P.pe(lambda e, p_=p_, kc=kc, kk=kk, tl=tl, wb=wb, AT=AT: e.matmul(
                            p_[:], lhsT=AT[:, kc, tl * 128:(tl + 1) * 128], rhs=wb[:, kk, :],
                            start=(kc == 0), stop=(kc == KC - 1)),
                            rd=[ATk[g], wbk], wr=[pk])
            for tl in range(NT):
                p_, pk = banks[tl]
                y_, yk = ys.next()
                if ei % 2 == 0:
                    P.act(lambda e, y_=y_, p_=p_: e.copy(out=y_[:], in_=p_[:]), rd=[pk], wr=[yk])
                else:
                    P.dve(lambda e, y_=y_, p_=p_: e.tensor_copy(out=y_[:], in_=p_[:]), rd=[pk], wr=[yk])
                ei += 1
                r0 = th * Th + tl * 128
                if halves is None:
                    P.dma(lambda e, y_=y_, r0=r0, s=s: e.dma_start(out=y_out[r0:r0 + 128, s * 512:(s + 1) * 512],
                                                                   in_=y_[:]), rd=[yk], q="q_pool")
                else:
                    dst = halves[s // 2]
                    tk_ = Tok("yhalf")
                    htoks[s // 2].append(tk_)
                    P.dma(lambda e, y_=y_, r0=r0, s=s, dst=dst: e.dma_start(
                        out=dst[r0:r0 + 128, (s % 2) * 512:(s % 2 + 1) * 512], in_=y_[:]), rd=[yk], wr=[tk_],
                        q="q_act")
            if halves is not None and half_done is not None and s % 2 == 1 and th == T // Th - 1:
                half_done(s // 2)


SEQ = 8192
NBLK = SEQ // 512


def make_consts(nc, P, stack):
    sb = nc.sbuf_tensor
    c = {}
    c["ones_bf"] = stack.enter_context(sb(U("ones_bf"), [128, 128], BF16))
    c["ones_f"] = stack.enter_context(sb(U("ones_f"), [128, 128], F32))
    c["sel2"] = stack.enter_context(sb(U("sel2"), [128, 2], BF16))
    k = Tok("consts")
    c["tok"] = k
    P.pool(lambda e: e.memset(c["ones_bf"][:], 1.0), wr=[k])
    P.pool(lambda e: e.memset(c["ones_f"][:], 1.0), wr=[k])
    P.pool(lambda e: e.memset(c["sel2"][:], 0.0), wr=[k])
    P.pool(lambda e: e.memset(c["sel2"][0:64, 0:1], 1.0), wr=[k])
    P.pool(lambda e: e.memset(c["sel2"][64:128, 1:2], 1.0), wr=[k])
    return c


def phase_attn_proj(nc, P, stack, hT_blk, w_l, qTd, kTd, Vd, negMd, idf, idb, C, S=SEQ, hT_tok=None):
    sb = nc.sbuf_tensor
    ps = nc.psum_tensor
    NB = S // 512
    ones_f, sel2, ck = C["ones_f"], C["sel2"], C["tok"]
    W = stack.enter_context(sb(U("ap_w"), [128, KC_D, 12 * 128], BF16))
    Wk = [Tok(f"ap_w{f}") for f in range(12)]
    wst = Ring(stack, sb, "ap_wst", [128, D], F32, 2)
    for fc in range(12):
        st_, stk = wst.next()
        P.dma(lambda e, st_=st_, fc=fc: e.dma_start(out=st_[:], in_=w_l[fc]), wr=[stk], q="q_pool")
        cast_op(P, fc, W[:, :, fc * 128:(fc + 1) * 128], st_[:].rearrange("p (k j) -> p k j", j=128), [stk], [Wk[fc]])
    hr = Ring(stack, sb, "ap_h", [128, KC_D, 512], BF16, 2)
    pw = Ring(stack, ps, "ap_pw", [128, 512], F32, 5)
    pm = Ring(stack, ps, "ap_pm", [128, 512], F32, 2)
    osr = Ring(stack, sb, "ap_os", [128, 512], BF16, 6)
    sqr = Ring(stack, sb, "ap_sq", [128, 512], BF16, 3)
    vor = Ring(stack, sb, "ap_vo", [128, 4, 128], BF16, 3)
    nmx = stack.enter_context(sb(U("ap_nmx"), [2, 8, NB], F32))
    nmxk = Tok("ap_nmx")
    ei = 0
    for b in range(NB):
        h_, hk = hr.next()
        P.dma(lambda e, h_=h_, b=b: e.dma_start(out=h_[:], in_=hT_blk(b)), rd=([hT_tok(b)] if hT_tok else []),
              wr=[hk])
        tsl = slice(b * 512, (b + 1) * 512)
        for ch in range(12):
            which, hl = ch // 4, ch % 4
            p_, pk = pw.next()
            for kc in range(KC_D):
                P.pe(lambda e, p_=p_, ch=ch, kc=kc, h_=h_: e.matmul(
                    p_[:], lhsT=W[:, kc, ch * 128:(ch + 1) * 128], rhs=h_[:, kc, :],
                    start=(kc == 0), stop=(kc == KC_D - 1)), rd=[Wk[ch], hk], wr=[pk])
            o_, ok = osr.next()
            sc = 0.125 if which == 0 else 1.0
            if ei % 2 == 0:
                P.act(lambda e, o_=o_, p_=p_, sc=sc: e.activation(out=o_[:], in_=p_[:], func=AF.Copy, scale=sc),
                      rd=[pk], wr=[ok])
            else:
                P.dve(lambda e, o_=o_, p_=p_, sc=sc: e.tensor_scalar(out=o_[:], in0=p_[:], scalar1=sc, scalar2=None,
                                                                    op0=ALU.mult), rd=[pk], wr=[ok])
            ei += 1
            if which < 2:
                dst = qTd if which == 0 else kTd
                P.dma(lambda e, o_=o_, dst=dst, hl=hl, tsl=tsl: e.dma_start(
                    out=dst[hl * 128:(hl + 1) * 128, tsl], in_=o_[:]), rd=[ok], q="q_pool")
                sq, sqk = sqr.next()
                P.dve(lambda e, sq=sq, o_=o_: e.tensor_tensor(out=sq[:], in0=o_[:], in1=o_[:], op=ALU.mult),
                      rd=[ok], wr=[sqk])
                p2, p2k = pm.next()
                P.pe(lambda e, p2=p2, sq=sq: e.matmul(p2[0:2, :], lhsT=sel2[:], rhs=sq[:], start=True, stop=True),
                     rd=[sqk, ck], wr=[p2k])
                P.dve(lambda e, p2=p2, ch=ch, b=b: e.tensor_reduce(
                    out=nmx[:, ch, b:b + 1], in_=p2[0:2, :], axis=AX.X, op=ALU.max), rd=[p2k], wr=[nmxk])
            else:
                p2, p2k = pm.next()
                p2b = p2[:].bitcast(BF16)
                for j in range(4):
                    P.pe(lambda e, p2b=p2b, o_=o_, j=j: e.transpose(
                        out=p2b[:, j * 128:(j + 1) * 128], in_=o_[:, j * 128:(j + 1) * 128], identity=idb[:]),
                        rd=[ok], wr=[p2k])
                vo, vok = vor.next()
                P.act(lambda e, vo=vo, p2b=p2b: e.copy(out=vo[:], in_=p2b[:, 0:512].rearrange("p (j d) -> p j d", d=128)),
                      rd=[p2k], wr=[vok])
                P.dma(lambda e, vo=vo, hl=hl, b=b: e.dma_start(out=Vd[hl, :, b * 4:(b + 1) * 4, :], in_=vo[:]),
                      rd=[vok], q="q_pool")
    msc = stack.enter_context(sb(U("ap_msc"), [2, 32], F32))
    P.dve(lambda e: e.tensor_reduce(out=msc[:, 0:8], in_=nmx[:], axis=AX.X, op=ALU.max), rd=[nmxk], wr=[nmxk])
    P.dve(lambda e: e.tensor_tensor(out=msc[:, 8:12], in0=msc[:, 0:4], in1=msc[:, 4:8], op=ALU.mult),
          rd=[nmxk], wr=[nmxk])
    P.act(lambda e: e.activation(out=msc[:, 12:16], in_=msc[:, 8:12], func=AF.Ln), rd=[nmxk], wr=[nmxk])
    P.act(lambda e: e.activation(out=msc[:, 12:16], in_=msc[:, 12:16], func=AF.Exp, scale=0.5), rd=[nmxk], wr=[nmxk])
    for hl in range(4):
        P.dve(lambda e, hl=hl: e.tensor_scalar(out=msc[:, 16 + hl * 2:18 + hl * 2], in0=idf[0:2, 0:2],
                                               scalar1=msc[:, 12 + hl:13 + hl], scalar2=-1.02,
                                               op0=ALU.mult, op1=ALU.mult), rd=[nmxk], wr=[nmxk])
    p2, p2k = pm.next()
    P.pe(lambda e: e.matmul(p2[:, 0:8], lhsT=ones_f[0:2, :], rhs=msc[:, 16:24], start=True, stop=True),
         rd=[nmxk, ck], wr=[p2k])
    nm = stack.enter_context(sb(U("ap_nm"), [128, 8], F32))
    nmk = Tok("ap_nm")
    P.dve(lambda e: e.tensor_copy(out=nm[:], in_=p2[:, 0:8]), rd=[p2k], wr=[nmk])
    P.dma(lambda e: e.dma_start(out=negMd, in_=nm[:]), rd=[nmk])


def phase_attn_core(nc, P, stack, qTd, kTd, Vd, negMd, lam_in, subln_in, lambda_init, oT_out, idf, idb, C, S=SEQ):
    sb = nc.sbuf_tensor
    ps = nc.psum_tensor
    NB = S // 512
    ones_bf, ones_f, sel2, ck = C["ones_bf"], C["ones_f"], C["sel2"], C["tok"]
    lam4 = stack.enter_context(sb(U("at_lam4"), [128, 4, 64], F32))
    lamk = Tok("lam")
    P.dma(lambda e: e.dma_start(out=lam4[:].rearrange("p a d -> p (a d)"),
                                in_=lam_in.rearrange("a d -> (a d)").partition_broadcast(128)), wr=[lamk])
    lsc = stack.enter_context(sb(U("at_lsc"), [128, 8], F32))
    lpr = stack.enter_context(sb(U("at_lpr"), [128, 2, 64], F32))
    P.dve(lambda e: e.tensor_tensor(out=lpr[:, 0, :], in0=lam4[:, 0, :], in1=lam4[:, 1, :], op=ALU.mult),
          rd=[lamk], wr=[lamk])
    P.dve(lambda e: e.tensor_tensor(out=lpr[:, 1, :], in0=lam4[:, 2, :], in1=lam4[:, 3, :], op=ALU.mult),
          rd=[lamk], wr=[lamk])
    P.dve(lambda e: e.tensor_reduce(out=lsc[:, 0:2], in_=lpr[:], axis=AX.X, op=ALU.add), rd=[lamk], wr=[lamk])
    P.act(lambda e: e.activation(out=lsc[:, 2:4], in_=lsc[:, 0:2], func=AF.Exp), rd=[lamk], wr=[lamk])
    P.dve(lambda e: e.scalar_tensor_tensor(out=lsc[:, 4:5], in0=lsc[:, 3:4], scalar=-float(lambda_init),
                                           in1=lsc[:, 2:3], op0=ALU.add, op1=ALU.subtract), rd=[lamk], wr=[lamk])
    P.dma(lambda e: e.dma_start(out=lsc[:, 5:6], in_=subln_in.rearrange("(p o) -> p o", o=1)), wr=[lamk])
    P.dve(lambda e: e.tensor_scalar(out=lsc[:, 6:7], in0=lsc[:, 5:6], scalar1=1.0 - float(lambda_init),
                                    scalar2=None, op0=ALU.mult), rd=[lamk], wr=[lamk])
    neglam = lsc[:, 4:5]
    sublnw = lsc[:, 6:7]

    qTc = [stack.enter_context(sb(U(f"at_qT{c}"), [128, S], BF16)) for c in range(2)]
    qzk = Tok("at_qz")
    P.dve(lambda e: e.memset(qTc[0][64:128, :], 0.0), wr=[qzk])
    P.dve(lambda e: e.memset(qTc[1][0:64, :], 0.0), wr=[qzk])
    kT = stack.enter_context(sb(U("at_kT"), [128, S], BF16))
    V = stack.enter_context(sb(U("at_V"), [128, S // 128, 128], BF16))
    qk1, kk1, vk1 = Tok("at_q"), Tok("at_k"), Tok("at_v")
    qk_ = [qk1] * NB
    kk_ = [kk1] * NB
    vk_ = [vk1] * NB
    negM = stack.enter_context(sb(U("at_negM"), [128, 8], F32))
    negMk = Tok("negM")
    P.dma(lambda e: e.dma_start(out=negM[:], in_=negMd), wr=[negMk])
    pw = Ring(stack, ps, "at_pw", [128, 512], F32, 3)
    po = [Ring(stack, ps, f"at_po{c}", [128, 512], F32, 1) for c in range(2)]
    pl = [Ring(stack, ps, f"at_pl{c}", [128, 512], F32, 1) for c in range(2)]
    lacc = Ring(stack, sb, "at_la", [128, 512], F32, 4)
    pm = Ring(stack, ps, "at_pm", [128, 512], F32, 1)
    ptr = Ring(stack, sb, "at_pt", [128, 512], BF16, 6)
    e32 = Ring(stack, sb, "at_e32", [128, 512], F32, 8)
    obf = Ring(stack, sb, "at_obf", [128, 512], BF16, 2)
    for hl in range(4):
        hs = slice(hl * 128, (hl + 1) * 128)
        P.dma(lambda e, hl=hl: e.dma_start(out=qTc[0][0:64, :], in_=qTd[hl * 128:hl * 128 + 64, :]),
              rd=[qzk], wr=[qk1])
        P.dma(lambda e, hl=hl: e.dma_start(out=qTc[1][64:128, :], in_=qTd[hl * 128 + 64:hl * 128 + 128, :]),
              rd=[qzk], wr=[qk1], q="q_pool")
        P.dma(lambda e, hs=hs: e.dma_start(out=kT[:], in_=kTd[hs, :]), wr=[kk1])
        P.dma(lambda e, hl=hl: e.dma_start(out=V[:], in_=Vd[hl]), wr=[vk1], q="q_pool")
        steps = [(qt, c, sbk) for qt in range(NB) for c in range(2) for sbk in range(qt * 4 + 4)]
        LA = 2
        inflight = {}
        accs = {}
        tparts = {}
        deferred = []

        def stepA(k):
            qt, c, sbk = steps[k]
            q0 = qt * 512
            rows = slice(c * 64, (c + 1) * 64)
            d = max(0, sbk - qt * 4)
            cs = slice(d * 128, 512)
            w_, wk_ = pw.next()
            P.pe(lambda e: e.matmul(w_[:, cs], lhsT=kT[:, sbk * 128:(sbk + 1) * 128],
                                    rhs=qTc[c][:, q0 + cs.start:q0 + 512], start=True, stop=True),
                 rd=[kk_[sbk // 4], qk_[qt], qzk], wr=[wk_])
            pt, ptk = ptr.next()
            bcol = hl * 2 + c
            P.act(lambda e: e.activation(out=pt[:, cs], in_=w_[:, cs], func=AF.Exp, bias=negM[:, bcol:bcol + 1],
                                         scale=1.0), rd=[wk_, negMk], wr=[ptk])
            if sbk >= qt * 4:
                base = q0 + cs.start - sbk * 128
                P.pool(lambda e: e.affine_select(out=pt[:, cs], in_=pt[:, cs], pattern=[[1, 512 - cs.start]],
                                                 compare_op=ALU.is_ge, fill=0.0, base=base,
                                                 channel_multiplier=-1), rd=[ptk], wr=[ptk])
            inflight[k] = (pt, ptk, cs)

        def epi_c(qt, c, k):
            o_, ok, l_, lk, la, lak = accs.pop((qt, c))

            def part2():
                P.pe(lambda e: e.matmul(l_[:], lhsT=ones_f[:], rhs=la[:], start=False, stop=True),
                     rd=[lak, ck], wr=[lk])
                r_, rk = e32.next()
                P.dve(lambda e: e.reciprocal(out=r_[:], in_=l_[:]), rd=[lk], wr=[rk])
                t_, tk = e32.next()
                P.dve(lambda e: e.tensor_tensor(out=t_[:], in0=o_[:], in1=r_[:], op=ALU.mult), rd=[ok, rk], wr=[tk])
                tparts[(qt, c)] = (t_, tk)
                if c == 1:
                    epi_1(qt, k + 2)
            deferred.append((k + 2, part2))

        def epi_1(qt, k):
            t0, t0k = tparts.pop((qt, 0))
            t1, t1k = tparts.pop((qt, 1))
            of, ofk = e32.next()
            P.dve(lambda e: e.scalar_tensor_tensor(out=of[:], in0=t1[:], scalar=neglam, in1=t0[:],
                                                   op0=ALU.mult, op1=ALU.add), rd=[t0k, t1k, lamk], wr=[ofk])
            sq, sqk2 = e32.next()
            P.act(lambda e: e.activation(out=sq[:], in_=of[:], func=AF.Square), rd=[ofk], wr=[sqk2])

            def epi_2():
                m_, mk = pm.next()
                P.pe(lambda e: e.matmul(m_[:], lhsT=ones_f[:], rhs=sq[:], start=True, stop=True),
                     rd=[sqk2, ck], wr=[mk])
                rs, rsk = e32.next()
                P.act(lambda e: e.activation(out=rs[:], in_=m_[:], func=AF.Ln, bias=EPS, scale=1.0 / 128),
                      rd=[mk], wr=[rsk])
                P.act(lambda e: e.activation(out=rs[:], in_=rs[:], func=AF.Exp, scale=-0.5), rd=[rsk], wr=[rsk])
                ob, obk = obf.next()
                P.dve(lambda e: e.scalar_tensor_tensor(out=ob[:], in0=of[:], scalar=sublnw, in1=rs[:],
                                                       op0=ALU.mult, op1=ALU.mult), rd=[ofk, rsk, lamk], wr=[obk])
                P.dma(lambda e, hl=hl: e.dma_start(out=oT_out[hl * 128:(hl + 1) * 128, qt * 512:(qt + 1) * 512],
                                                   in_=ob[:]), rd=[obk])
            deferred.append((k + 6, epi_2))

        def stepB(k):
            qt, c, sbk = steps[k]
            nsb = qt * 4 + 4
            pt, ptk, cs = inflight.pop(k)
            if sbk == 0:
                o_, ok = po[c].next()
                l_, lk = pl[c].next()
                la, lak = lacc.next()
                accs[(qt, c)] = (o_, ok, l_, lk, la, lak)
            o_, ok, l_, lk, la, lak = accs[(qt, c)]
            P.pe(lambda e: e.matmul(o_[:, cs], lhsT=V[:, sbk, :], rhs=pt[:, cs], start=(sbk == 0),
                                    stop=(sbk == nsb - 1)), rd=[vk_[sbk // 4], ptk], wr=[ok])
            if sbk % 2 == 0:
                P.pe(lambda e: e.matmul(l_[:, cs], lhsT=ones_bf[:], rhs=pt[:, cs], start=(sbk == 0), stop=False),
                     rd=[ck, ptk], wr=[lk])
            elif sbk == 1:
                if cs.start > 0:
                    P.dve(lambda e: e.memset(la[:, 0:cs.start], 0.0), wr=[lak])
                P.dve(lambda e: e.tensor_copy(out=la[:, cs], in_=pt[:, cs]), rd=[ptk], wr=[lak])
            else:
                P.dve(lambda e: e.tensor_tensor(out=la[:, cs], in0=la[:, cs], in1=pt[:, cs], op=ALU.add),
                      rd=[ptk, lak], wr=[lak])
            if sbk == nsb - 1:
                epi_c(qt, c, k)

        ns = len(steps)
        for k in range(ns + LA):
            if k < ns:
                stepA(k)
            if k - LA >= 0:
                stepB(k - LA)
            for item in [d_ for d_ in deferred if d_[0] <= k]:
                deferred.remove(item)
                item[1]()
        for item in deferred:
            item[1]()


NZX = 20


def phase_ssd_in(nc, P, stack, hT_blk, w_l, wdt_l, dtb_in, zxT_out, dt_out, S=SEQ, hT_tok=None):
    sb = nc.sbuf_tensor
    NB = S // 512
    W = stack.enter_context(sb(U("si_w"), [128, KC_D, NZX * 128], BF16))
    Wk = [Tok(f"si_w{f}") for f in range(NZX)]
    wst = Ring(stack, sb, "si_wst", [128, D], F32, 2)
    for fc in range(NZX):
        st_, stk = wst.next()
        P.dma(lambda e, st_=st_, fc=fc: e.dma_start(out=st_[:], in_=w_l[fc]), wr=[stk])
        cast_op(P, fc, W[:, :, fc * 128:(fc + 1) * 128], st_[:].rearrange("p (k j) -> p k j", j=128), [stk], [Wk[fc]])
    wdtf = stack.enter_context(sb(U("si_wdtf"), [128, KC_D, 16], F32))
    wdt = stack.enter_context(sb(U("si_wdt"), [128, KC_D, 16], BF16))
    wdk = Tok("si_wdt")
    P.dma(lambda e: e.dma_start(out=wdtf[:], in_=wdt_l), wr=[wdk])
    P.dve(lambda e: e.tensor_copy(out=wdt[:], in_=wdtf[:]), rd=[wdk], wr=[wdk])
    dtb = stack.enter_context(sb(U("si_dtb"), [128, 16], F32))
    dtbk = Tok("si_dtb")
    P.dma(lambda e: e.dma_start(out=dtb[:], in_=dtb_in.partition_broadcast(128)), wr=[dtbk])
    hr = Ring(stack, sb, "si_h", [128, KC_D, 512], BF16, 2)
    pw = Ring(stack, nc.psum_tensor, "si_pw", [128, 512], F32, 4)
    pd = Ring(stack, nc.psum_tensor, "si_pd", [128, 16], F32, 2)
    osr = Ring(stack, sb, "si_os", [128, 512], F32, 4)
    dr = Ring(stack, sb, "si_d", [128, 4, 16], F32, 6)
    dto = Ring(stack, sb, "si_dto", [128, 4, 16], F32, 2)
    ei = 0
    for b in range(NB):
        h_, hk = hr.next()
        P.dma(lambda e, h_=h_, b=b: e.dma_start(out=h_[:], in_=hT_blk(b)), rd=([hT_tok(b)] if hT_tok else []),
              wr=[hk])
        tsl = slice(b * 512, (b + 1) * 512)
        for fc in range(NZX):
            p_, pk = pw.next()
            for kc in range(KC_D):
                P.pe(lambda e, p_=p_, fc=fc, kc=kc, h_=h_: e.matmul(
                    p_[:], lhsT=W[:, kc, fc * 128:(fc + 1) * 128], rhs=h_[:, kc, :],
                    start=(kc == 0), stop=(kc == KC_D - 1)), rd=[Wk[fc], hk], wr=[pk])
            o_, ok = osr.next()
            if ei % 2 == 0:
                P.act(lambda e, o_=o_, p_=p_: e.copy(out=o_[:], in_=p_[:]), rd=[pk], wr=[ok])
            else:
                P.dve(lambda e, o_=o_, p_=p_: e.tensor_copy(out=o_[:], in_=p_[:]), rd=[pk], wr=[ok])
            ei += 1
            P.dma(lambda e, o_=o_, fc=fc, tsl=tsl: e.dma_start(out=zxT_out[fc * 128:(fc + 1) * 128, tsl], in_=o_[:]),
                  rd=[ok])
        x_, xk = dr.next()
        for j in range(4):
            p_, pk = pd.next()
            for kc in range(KC_D):
                P.pe(lambda e, p_=p_, kc=kc, h_=h_, j=j: e.matmul(
                    p_[:], lhsT=h_[:, kc, j * 128:(j + 1) * 128], rhs=wdt[:, kc, :],
                    start=(kc == 0), stop=(kc == KC_D - 1)), rd=[wdk, hk], wr=[pk])
            P.dve(lambda e, x_=x_, p_=p_, j=j: e.tensor_tensor(out=x_[:, j, :], in0=p_[:], in1=dtb[:], op=ALU.add),
                  rd=[pk, dtbk], wr=[xk])
        a_, ak = dr.next()
        P.dve(lambda e, a_=a_, x_=x_: e.scalar_tensor_tensor(out=a_[:], in0=x_[:], scalar=-1.0, in1=x_[:],
                                                             op0=ALU.mult, op1=ALU.max), rd=[xk], wr=[ak])
        P.act(lambda e, a_=a_: e.activation(out=a_[:], in_=a_[:], func=AF.Exp, scale=-1.0), rd=[ak], wr=[ak])
        P.act(lambda e, a_=a_: e.activation(out=a_[:], in_=a_[:], func=AF.Ln, bias=1.0, scale=1.0), rd=[ak], wr=[ak])
        d_, dk = dto.next()
        P.dve(lambda e, d_=d_, x_=x_, a_=a_: e.scalar_tensor_tensor(
            out=d_[:], in0=x_[:], scalar=0.0, in1=a_[:], op0=ALU.max, op1=ALU.add), rd=[xk, ak], wr=[dk])
        P.dma(lambda e, d_=d_, b=b: e.dma_start(
            out=dt_out[b * 512:(b + 1) * 512, :].rearrange("(j p) h -> p j h", p=128), in_=d_[:]), rd=[dk])


def phase_ssd_scan(nc, P, stack, zxT_in, dt_in, convw_in, convb_in, alog_in, dsk_in, normw_in, yT_out,
                   idf, idb, C, S=SEQ):
    sb = nc.sbuf_tensor
    ps = nc.psum_tensor
    ones_f, ck = C["ones_f"], C["tok"]
    TB = 256
    NB = S // TB
    zv = zxT_in.rearrange("(k p) t -> p k t", p=128)
    tri = stack.enter_context(sb(U("ss_tri"), [128, 128], F32))
    cst = Tok("ss_const")
    P.pool(lambda e: e.memset(tri[:], 1.0), wr=[cst])
    P.pool(lambda e: e.affine_select(out=tri[:], in_=tri[:], pattern=[[1, 128]], compare_op=ALU.is_ge,
                                     fill=0.0, base=0, channel_multiplier=-1), rd=[cst], wr=[cst])
    cw = stack.enter_context(sb(U("ss_cw"), [128, 12, 4], F32))
    cb = stack.enter_context(sb(U("ss_cb"), [128, 12], F32))
    nw = stack.enter_context(sb(U("ss_nw"), [128, 8], F32))
    abc = stack.enter_context(sb(U("ss_abc"), [128, 16], F32))
    d16 = stack.enter_context(sb(U("ss_d16"), [128, 16], F32))
    Dbc = stack.enter_context(sb(U("ss_Dbc"), [128, 16, 64], F32))
    P.dma(lambda e: e.dma_start(out=cw[:], in_=convw_in), wr=[cst])
    P.dma(lambda e: e.dma_start(out=cb[:], in_=convb_in), wr=[cst])
    P.dma(lambda e: e.dma_start(out=nw[:], in_=normw_in), wr=[cst])
    P.dma(lambda e: e.dma_start(out=abc[:], in_=alog_in.partition_broadcast(128)), wr=[cst])
    P.dma(lambda e: e.dma_start(out=d16[:], in_=dsk_in.partition_broadcast(128)), wr=[cst])
    P.act(lambda e: e.activation(out=abc[:], in_=abc[:], func=AF.Exp), rd=[cst], wr=[cst])
    P.dve(lambda e: e.tensor_scalar(out=abc[:], in0=abc[:], scalar1=-1.0, scalar2=None, op0=ALU.mult),
          rd=[cst], wr=[cst])
    P.dve(lambda e: e.tensor_copy(out=Dbc[:], in_=d16[:].unsqueeze(2).to_broadcast([128, 16, 64])),
          rd=[cst], wr=[cst])
    S32 = [stack.enter_context(sb(U(f"ss_S32{g}"), [128, 512], F32)) for g in range(2)]
    Sbf = [stack.enter_context(sb(U(f"ss_Sbf{g}"), [128, 512], BF16)) for g in range(2)]
    Sk = [Tok(f"ss_S{g}") for g in range(2)]
    Sbk = [Tok(f"ss_Sb{g}") for g in range(2)]
    for g in range(2):
        P.pool(lambda e, g=g: e.memset(S32[g][:], 0.0), wr=[Sk[g]])
        P.pool(lambda e, g=g: e.memset(Sbf[g][:], 0.0), wr=[Sbk[g]])
    rawr = Ring(stack, sb, "ss_raw", [128, 12, TB + 3], F32, 2)
    zr = Ring(stack, sb, "ss_z", [128, 8, TB], F32, 3)
    accr = Ring(stack, sb, "ss_acc", [128, 12, TB], F32, 1)
    xTr = Ring(stack, sb, "ss_xT", [128, 8, TB], F32, 2)
    bcTr = Ring(stack, sb, "ss_bcT", [128, 4, TB], BF16, 3)
    dtr = Ring(stack, sb, "ss_dt", [128, TB // 128, 16], F32, 3)
    oTr = Ring(stack, sb, "ss_oT", [128, 8, TB], BF16, 3)
    xsr = Ring(stack, sb, "ss_xs", [128, 512], F32, 2)
    Btr = Ring(stack, sb, "ss_Bt", [128, 128], BF16, 4)
    smr = Ring(stack, sb, "ss_sm", [128, 48], F32, 5)
    rbr = Ring(stack, sb, "ss_rb", [128, 8, 128], F32, 2)
    segr = Ring(stack, sb, "ss_seg", [128, 8, 128], F32, 2)
    cbmr = Ring(stack, sb, "ss_cbm", [128, 128], BF16, 3)
    ebr = Ring(stack, sb, "ss_eb", [128, 8, 128], BF16, 3)
    Gr = Ring(stack, sb, "ss_G", [128, 8, 128], BF16, 4)
    x32r = Ring(stack, sb, "ss_x32", [128, 512], F32, 2)
    xbr = Ring(stack, sb, "ss_xb", [128, 512], BF16, 4)
    xer = Ring(stack, sb, "ss_xe", [128, 512], BF16, 4)
    y1r = Ring(stack, sb, "ss_y1", [128, 512], F32, 2)
    xdr = Ring(stack, sb, "ss_xd", [128, 512], F32, 4)
    gvr = Ring(stack, sb, "ss_gv", [128, 4, 128], F32, 2)
    sqr = Ring(stack, sb, "ss_sq", [128, 4, 128], F32, 2)
    rsr = Ring(stack, sb, "ss_rs", [128, 128], F32, 2)
    pbc = Ring(stack, ps, "ss_pbc", [128, 1024], F32, 1)
    pm = Ring(stack, ps, "ss_pm", [128, 512], F32, 3)
    pyd = Ring(stack, ps, "ss_pyd", [128, 512], F32, 1)
    pyo = Ring(stack, ps, "ss_pyo", [128, 512], F32, 1)
    pst = Ring(stack, ps, "ss_pst", [128, 512], F32, 1)

    def block_prologue(b):
        t0 = b * TB
        raw, rk = rawr.next()
        if b == 0:
            P.pool(lambda e, raw=raw: e.memset(raw[:, :, 0:3], 0.0), wr=[rk])
            P.dma(lambda e, raw=raw: e.dma_start(out=raw[:, :, 3:], in_=zv[:, 8:20, 0:TB]), wr=[rk])
        else:
            P.dma(lambda e, raw=raw, t0=t0: e.dma_start(out=raw[:], in_=zv[:, 8:20, t0 - 3:t0 + TB]), wr=[rk])
        z_, zk = zr.next()
        P.dma(lambda e, z_=z_, t0=t0: e.dma_start(out=z_[:], in_=zv[:, 0:8, t0:t0 + TB]), wr=[zk], q="q_pool")
        dt_, dtk = dtr.next()
        P.dma(lambda e, dt_=dt_, t0=t0: e.dma_start(
            out=dt_[:], in_=dt_in[t0:t0 + TB, :].rearrange("(j p) h -> p j h", p=128)), wr=[dtk], q="q_pool")
        P.act(lambda e, z_=z_: e.activation(out=z_[:], in_=z_[:], func=AF.Silu), rd=[zk], wr=[zk])
        acc, acck = accr.next()
        acks = [Tok(f"acc{k}") for k in range(12)]
        for w in range(4):
            for k in range(12):
                if w == 0:
                    P.dve(lambda e, k=k, w=w, raw=raw, acc=acc: e.tensor_scalar(
                        out=acc[:, k, :], in0=raw[:, k, w:w + TB], scalar1=cw[:, k, w:w + 1], scalar2=None,
                        op0=ALU.mult), rd=[rk, cst], wr=[acks[k]])
                else:
                    P.dve(lambda e, k=k, w=w, raw=raw, acc=acc: e.scalar_tensor_tensor(
                        out=acc[:, k, :], in0=raw[:, k, w:w + TB], scalar=cw[:, k, w:w + 1], in1=acc[:, k, :],
                        op0=ALU.mult, op1=ALU.add), rd=[rk, cst, acks[k]], wr=[acks[k]])
        xT, xTk = xTr.next()
        bcT, bcTk = bcTr.next()
        for k in range(12):
            if k < 8:
                P.act(lambda e, k=k, xT=xT, acc=acc: e.activation(out=xT[:, k, :], in_=acc[:, k, :], func=AF.Silu,
                                                                  bias=cb[:, k:k + 1], scale=1.0),
                      rd=[acks[k], cst], wr=[xTk])
            else:
                P.act(lambda e, k=k, bcT=bcT, acc=acc: e.activation(out=bcT[:, k - 8, :], in_=acc[:, k, :],
                                                                    func=AF.Silu, bias=cb[:, k:k + 1], scale=1.0),
                      rd=[acks[k], cst], wr=[bcTk])
        oT, oTk = oTr.next()
        return dict(t0=t0, z_=z_, zk=zk, dt_=dt_, dtk=dtk, xT=xT, xTk=xTk, bcT=bcT, bcTk=bcTk, oT=oT, oTk=oTk)

    def front(B_, j, g):
        z_, zk, dt_, dtk, xT, xTk, bcT, bcTk = (B_[k_] for k_ in ('z_', 'zk', 'dt_', 'dtk', 'xT', 'xTk', 'bcT', 'bcTk'))
        cs = slice(j * 128, (j + 1) * 128)
        px, pxk = pm.next()
        for f in range(4):
            P.pe(lambda e, px=px, f=f, g=g, xT=xT, cs=cs: e.transpose(
                out=px[:, f * 128:(f + 1) * 128], in_=xT[:, g * 4 + f, cs], identity=idf[:]),
                rd=[xTk], wr=[pxk])
        xs, xsk = xsr.next()
        P.act(lambda e, xs=xs, px=px: e.copy(out=xs[:], in_=px[:]), rd=[pxk], wr=[xsk])
        pb, pbk = pm.next()
        pbb = pb[:].bitcast(BF16)
        P.pe(lambda e, pbb=pbb, bcT=bcT, g=g, cs=cs: e.transpose(out=pbb[:, 0:128], in_=bcT[:, g, cs],
                                                                 identity=idb[:]), rd=[bcTk], wr=[pbk])
        Bt, Btk = Btr.next()
        P.dve(lambda e, Bt=Bt, pbb=pbb: e.tensor_copy(out=Bt[:], in_=pbb[:, 0:128]), rd=[pbk], wr=[Btk])
        sm, smk = smr.next()
        kdA, kacol, keacol, keal, kdte = (Tok(n_) for n_ in ('dA', 'acol', 'eacol', 'eal', 'dte'))
        dtg = dt_[:, j, g * 8:(g + 1) * 8]
        P.dve(lambda e, sm=sm, dtg=dtg, g=g: e.tensor_tensor(out=sm[:, 0:8], in0=dtg,
                                                             in1=abc[:, g * 8:(g + 1) * 8], op=ALU.mult),
              rd=[dtk, cst], wr=[smk, kdA])
        pa, pak = pm.next()
        P.pe(lambda e, pa=pa, sm=sm: e.matmul(pa[:, 0:8], lhsT=tri[:], rhs=sm[:, 0:8], start=True, stop=True),
             rd=[kdA, cst], wr=[pak])
        P.act(lambda e, sm=sm, pa=pa: e.copy(out=sm[:, 8:16], in_=pa[:, 0:8]), rd=[pak, smk], wr=[kacol])
        P.act(lambda e, sm=sm, pa=pa: e.activation(out=sm[:, 16:24], in_=pa[:, 0:8], func=AF.Exp),
              rd=[pak, smk], wr=[keacol])
        rb, rbk = rbr.next()
        P.dve(lambda e, rb=rb, sm=sm: e.tensor_tensor(
            out=rb[:], in0=tri[:].unsqueeze(1).to_broadcast([128, 8, 128]),
            in1=sm[:, 0:8].unsqueeze(2).to_broadcast([128, 8, 128]), op=ALU.mult),
            rd=[kdA, cst], wr=[rbk])
        bc, bck = pbc.next()
        for hh in range(2):
            P.pe(lambda e, bc=bc, rb=rb, hh=hh: e.matmul(
                bc[:, hh * 512:(hh + 1) * 512], lhsT=ones_f[:],
                rhs=rb[:, hh * 4:(hh + 1) * 4, :].rearrange("p h l -> p (h l)"), start=True, stop=True),
                rd=[rbk, ck], wr=[bck])
        bc3 = bc[:].rearrange("p (h l) -> p h l", l=128)
        seg, segk = segr.next()
        for h in range(8):
            P.dve(lambda e, seg=seg, bc3=bc3, sm=sm, h=h: e.tensor_scalar(
                out=seg[:, h, :], in0=bc3[:, h, :], scalar1=sm[:, 8 + h:9 + h], scalar2=0.0,
                op0=ALU.subtract, op1=ALU.min), rd=[bck, kacol], wr=[segk])
        eb, ebk = ebr.next()
        P.act(lambda e, seg=seg, eb=eb: e.activation(out=eb[:], in_=seg[:], func=AF.Exp), rd=[segk], wr=[ebk])
        P.act(lambda e, sm=sm, bc3=bc3: e.activation(out=sm[:, 24:32], in_=bc3[:, :, 127], func=AF.Exp),
              rd=[bck, smk], wr=[keal])
        P.dve(lambda e, sm=sm, bc3=bc3: e.tensor_tensor(out=sm[:, 32:40], in0=bc3[:, :, 127],
                                                        in1=sm[:, 8:16], op=ALU.subtract),
              rd=[bck, kacol, smk], wr=[kdte])
        P.act(lambda e, sm=sm: e.activation(out=sm[:, 32:40], in_=sm[:, 32:40], func=AF.Exp),
              rd=[kdte], wr=[kdte])
        pc, pck = pm.next()
        P.pe(lambda e, pc=pc, bcT=bcT, g=g, cs=cs: e.matmul(
            pc[:, 0:128], lhsT=bcT[:, g, cs], rhs=bcT[:, 2 + g, cs], start=True, stop=True),
            rd=[bcTk], wr=[pck])
        cbm, cbmk = cbmr.next()
        P.dve(lambda e, cbm=cbm, pc=pc: e.tensor_tensor(out=cbm[:], in0=pc[:, 0:128], in1=tri[:], op=ALU.mult),
              rd=[pck, cst], wr=[cbmk])
        G, Gk = Gr.next()
        P.dve(lambda e, G=G, eb=eb, cbm=cbm: e.tensor_tensor(
            out=G[:], in0=eb[:], in1=cbm[:].unsqueeze(1).to_broadcast([128, 8, 128]), op=ALU.mult),
            rd=[ebk, cbmk], wr=[Gk])
        x32, x32k = x32r.next()
        xs3 = lambda t: t[:].rearrange("p (h d) -> p h d", d=64)
        P.pool(lambda e, x32=x32, xs=xs, dtg=dtg: e.tensor_tensor(
            out=xs3(x32), in0=xs3(xs), in1=dtg.unsqueeze(2).to_broadcast([128, 8, 64]), op=ALU.mult),
            rd=[xsk, dtk], wr=[x32k])
        xb, xbk = xbr.next()
        P.act(lambda e, xb=xb, x32=x32: e.copy(out=xb[:], in_=x32[:]), rd=[x32k], wr=[xbk])
        xe, xek = xer.next()
        P.dve(lambda e, xe=xe, x32=x32, sm=sm: e.tensor_tensor(
            out=xs3(xe), in0=xs3(x32), in1=sm[:, 32:40].unsqueeze(2).to_broadcast([128, 8, 64]),
            op=ALU.mult), rd=[x32k, kdte], wr=[xek])
        xd, xdk = xdr.next()
        P.pool(lambda e, xd=xd, xs=xs, g=g: e.tensor_tensor(
            out=xs3(xd), in0=xs3(xs), in1=Dbc[:, g * 8:(g + 1) * 8, :], op=ALU.mult),
            rd=[xsk, cst], wr=[xdk])
        return dict(B_=B_, j=j, g=g, cs=cs, sm=sm, smk=smk, keacol=keacol, keal=keal, Bt=Bt, Btk=Btk, G=G, Gk=Gk, xb=xb, xbk=xbk, xe=xe, xek=xek,
                    xd=xd, xdk=xdk, xs3=xs3)

    def back(F_):
        keacol, keal = F_['keacol'], F_['keal']
        B_, j, g, cs, sm, smk, Bt, Btk, G, Gk, xb, xbk, xe, xek, xd, xdk, xs3 = (F_[k_] for k_ in (
            'B_', 'j', 'g', 'cs', 'sm', 'smk', 'Bt', 'Btk', 'G', 'Gk', 'xb', 'xbk', 'xe', 'xek', 'xd', 'xdk', 'xs3'))
        z_, zk, bcT, bcTk, oT, oTk = (B_[k_] for k_ in ('z_', 'zk', 'bcT', 'bcTk', 'oT', 'oTk'))
        yd, ydk = pyd.next()
        for h in range(8):
            P.pe(lambda e, yd=yd, G=G, xb=xb, h=h: e.matmul(
                yd[:, h * 64:(h + 1) * 64], lhsT=G[:, h, :], rhs=xb[:, h * 64:(h + 1) * 64],
                start=True, stop=True), rd=[Gk, xbk], wr=[ydk])
        yo, yok = pyo.next()
        P.pe(lambda e, yo=yo, bcT=bcT, g=g, cs=cs: e.matmul(
            yo[:], lhsT=bcT[:, 2 + g, cs], rhs=Sbf[g][:], start=True, stop=True),
            rd=[bcTk, Sbk[g]], wr=[yok])
        y1, y1k = y1r.next()
        P.dve(lambda e, y1=y1, yo=yo, sm=sm: e.tensor_tensor(
            out=xs3(y1), in0=yo[:].rearrange("p (h d) -> p h d", d=64),
            in1=sm[:, 16:24].unsqueeze(2).to_broadcast([128, 8, 64]), op=ALU.mult),
            rd=[yok, keacol], wr=[y1k])
        P.dve(lambda e, y1=y1, yd=yd: e.tensor_tensor(out=y1[:], in0=y1[:], in1=yd[:], op=ALU.add),
              rd=[y1k, ydk], wr=[y1k])
        P.dve(lambda e, y1=y1, xd=xd: e.tensor_tensor(out=y1[:], in0=y1[:], in1=xd[:], op=ALU.add),
              rd=[y1k, xdk], wr=[y1k])
        st_, stk = pst.next()
        P.pe(lambda e, st_=st_, Bt=Bt, xe=xe: e.matmul(st_[:], lhsT=Bt[:], rhs=xe[:], start=True, stop=True),
             rd=[Btk, xek], wr=[stk])
        P.dve(lambda e, g=g, sm=sm: e.tensor_tensor(
            out=xs3(S32[g]), in0=xs3(S32[g]), in1=sm[:, 24:32].unsqueeze(2).to_broadcast([128, 8, 64]),
            op=ALU.mult), rd=[Sk[g], keal], wr=[Sk[g]])
        P.dve(lambda e, g=g, st_=st_: e.tensor_tensor(out=S32[g][:], in0=S32[g][:], in1=st_[:], op=ALU.add),
              rd=[Sk[g], stk], wr=[Sk[g]])
        P.act(lambda e, g=g: e.copy(out=Sbf[g][:], in_=S32[g][:]), rd=[Sk[g]], wr=[Sbk[g]])
        py, pyk = pm.next()
        for f in range(4):
            P.pe(lambda e, py=py, y1=y1, f=f: e.transpose(
                out=py[:, f * 128:(f + 1) * 128], in_=y1[:, f * 128:(f + 1) * 128], identity=idf[:]),
                rd=[y1k], wr=[pyk])
        gv, gvk = gvr.next()
        P.dve(lambda e, gv=gv, py=py, z_=z_, g=g, cs=cs: e.tensor_tensor(
            out=gv[:], in0=py[:].rearrange("p (f t) -> p f t", t=128), in1=z_[:, g * 4:(g + 1) * 4, cs],
            op=ALU.mult), rd=[pyk, zk], wr=[gvk])
        sq, sqk = sqr.next()
        P.act(lambda e, sq=sq, gv=gv: e.activation(out=sq[:], in_=gv[:], func=AF.Square), rd=[gvk], wr=[sqk])
        pq, pqk = pm.next()
        for f in range(4):
            P.pe(lambda e, pq=pq, sq=sq, f=f: e.matmul(pq[:, 0:128], lhsT=ones_f[:], rhs=sq[:, f, :],
                                                       start=(f == 0), stop=(f == 3)),
                 rd=[sqk, ck], wr=[pqk])
        rs, rsk = rsr.next()
        P.act(lambda e, rs=rs, pq=pq: e.activation(out=rs[:], in_=pq[:, 0:128], func=AF.Ln, bias=EPS,
                                                   scale=1.0 / 512), rd=[pqk], wr=[rsk])
        P.act(lambda e, rs=rs: e.activation(out=rs[:], in_=rs[:], func=AF.Exp, scale=-0.5), rd=[rsk], wr=[rsk])
        for f in range(4):
            P.dve(lambda e, oT=oT, gv=gv, rs=rs, f=f, g=g, cs=cs: e.scalar_tensor_tensor(
                out=oT[:, g * 4 + f, cs], in0=gv[:, f, :], scalar=nw[:, g * 4 + f:g * 4 + f + 1], in1=rs[:],
                op0=ALU.mult, op1=ALU.mult), rd=[gvk, rsk, cst], wr=[oTk])

    def block_epilogue(B_):
        oT, oTk, t0 = B_['oT'], B_['oTk'], B_['t0']
        P.dma(lambda e, oT=oT, t0=t0: e.dma_start(
            out=yT_out.rearrange("(k p) t -> p k t", p=128)[:, :, t0:t0 + TB], in_=oT[:]), rd=[oTk])

    passes = [(b, j, g) for b in range(NB) for j in range(TB // 128) for g in range(2)]
    blocks = {}
    pend = []
    DEPTH_F = 1

    def retire():
        F0 = pend.pop(0)
        back(F0)
        pb, pj, pg = F0["key"]
        if (pj, pg) == (TB // 128 - 1, 1):
            block_epilogue(blocks.pop(pb))

    for (b, j, g) in passes:
        if b not in blocks:
            blocks[b] = block_prologue(b)
        F_ = front(blocks[b], j, g)
        F_["key"] = (b, j, g)
        pend.append(F_)
        if len(pend) > DEPTH_F:
            retire()
    while pend:
        retire()


NCORES = 8
BATCH = 2
TC = BATCH * SEQ // NCORES
DEPTH = 4


def lay_colchunk(W):
    K, N = W.shape
    return np.ascontiguousarray(
        W.reshape(K // 128, 128, N // 128, 128).transpose(2, 1, 0, 3).reshape(N // 128, 128, K))


def lay_slab(W):
    K, N = W.shape
    return np.ascontiguousarray(W.reshape(K // 128, 128, N // 512, 512).transpose(2, 1, 0, 3))


def ssd_core_inputs(inp, j, gl):
    w_in = inp["ssd_w_in"][j]
    cols = np.concatenate([
        np.arange(gl * 1024, (gl + 1) * 1024),
        4096 + np.arange(gl * 1024, (gl + 1) * 1024),
        8192 + np.arange(gl * 256, (gl + 1) * 256),
        9216 + np.arange(gl * 256, (gl + 1) * 256)])
    ch = cols[1024:] - 4096
    dtc = 10240 + np.arange(gl * 16, (gl + 1) * 16)
    return {
        "s_w": lay_colchunk(w_in[:, cols]),
        "s_wdt": np.ascontiguousarray(w_in[:, dtc].reshape(KC_D, 128, 16).transpose(1, 0, 2)),
        "s_dtb": np.ascontiguousarray(inp["ssd_dt_bias"][j, gl * 16:(gl + 1) * 16]),
        "s_cw": np.ascontiguousarray(inp["ssd_conv_w"][j][:, ch].reshape(4, 12, 128).transpose(2, 1, 0)),
        "s_cb": np.ascontiguousarray(inp["ssd_conv_b"][j][ch].reshape(12, 128).T),
        "s_alog": np.ascontiguousarray(inp["ssd_a_log"][j, gl * 16:(gl + 1) * 16]),
        "s_dsk": np.ascontiguousarray(inp["ssd_d"][j, gl * 16:(gl + 1) * 16]),
        "s_nw": np.ascontiguousarray(inp["ssd_norm"][j, gl * 1024:(gl + 1) * 1024].reshape(8, 128).T),
    }


def attn_core_inputs(inp, j, gl):
    w = inp["da_w_qkv"][j]
    cols = np.concatenate([which * D + np.arange(gl * 512, (gl + 1) * 512) for which in range(3)])
    return {
        "a_w": lay_colchunk(w[:, cols]),
        "a_lam": np.ascontiguousarray(np.stack([inp["da_lambda_q1"][j], inp["da_lambda_k1"][j],
                                                inp["da_lambda_q2"][j], inp["da_lambda_k2"][j]])),
        "a_sub": np.ascontiguousarray(inp["da_subln"][j]),
    }


def lambda_init(i):
    return 0.8 - 0.6 * math.exp(-0.3 * i)


def _new_nc():
    _UID[0] = 0
    return bass.Bass("TRN2", target_bir_lowering=False)


def build_first():
    import contextlib
    nc = _new_nc()
    x = nc.dram_tensor("x", [TC, D], F32, kind="ExternalInput").ap()
    wpre = nc.dram_tensor("wpre", [D], F32, kind="ExternalInput").ap()
    xo = nc.dram_tensor("xo", [TC, D], F32, kind="ExternalOutput").ap()
    hT = nc.dram_tensor("hT", [D, TC], BF16, kind="ExternalOutput").ap()
    with contextlib.ExitStack() as st0:
        P = Prog(nc, st0)
        with contextlib.ExitStack() as st:
            idf, idb, _ = make_ident(nc, P, st)
            phase_norm(nc, P, st, TC, x, None, None, wpre, xo, hT, idb)
            P.emit()
    return nc


def hT_blk_fn(hTg):
    v = hTg.rearrange("(r k p) t -> p r k t", r=4, p=128)
    return lambda b: v[:, b // 4, :, (b % 4) * 512:(b % 4 + 1) * 512]


def hT_blk_fn_bm(hTg):
    v = hTg.rearrange("(b p) (k t) -> p b k t", p=128, t=512)
    return lambda b: v[:, b, :, :]


def build_ssd():
    import contextlib
    nc = _new_nc()
    hTg = nc.dram_tensor("hTg", [4 * D, TC], BF16, kind="ExternalInput").ap()
    w = nc.dram_tensor("s_w", [NZX, 128, D], F32, kind="ExternalInput").ap()
    wdt = nc.dram_tensor("s_wdt", [128, KC_D, 16], F32, kind="ExternalInput").ap()
    dtb = nc.dram_tensor("s_dtb", [16], F32, kind="ExternalInput").ap()
    cw = nc.dram_tensor("s_cw", [128, 12, 4], F32, kind="ExternalInput").ap()
    cb = nc.dram_tensor("s_cb", [128, 12], F32, kind="ExternalInput").ap()
    alog = nc.dram_tensor("s_alog", [16], F32, kind="ExternalInput").ap()
    dsk = nc.dram_tensor("s_dsk", [16], F32, kind="ExternalInput").ap()
    nw = nc.dram_tensor("s_nw", [128, 8], F32, kind="ExternalInput").ap()
    zx = nc.dram_tensor("zx", [NZX * 128, SEQ], F32).ap()
    dt = nc.dram_tensor("dt", [SEQ, 16], F32).ap()
    yT = nc.dram_tensor("yT", [1024, SEQ], BF16, kind="ExternalOutput").ap()
    with contextlib.ExitStack() as st0:
        P = Prog(nc, st0)
        with contextlib.ExitStack() as st:
            phase_ssd_in(nc, P, st, hT_blk_fn(hTg), w, wdt, dtb, zx, dt)
            P.emit()
        with contextlib.ExitStack() as st:
            idf, idb, _ = make_ident(nc, P, st)
            C = make_consts(nc, P, st)
            phase_ssd_scan(nc, P, st, zx, dt, cw, cb, alog, dsk, nw, yT, idf, idb, C)
            P.emit()
    return nc


def build_attn(lam_init):
    import contextlib
    nc = _new_nc()
    hTg = nc.dram_tensor("hTg", [4 * D, TC], BF16, kind="ExternalInput").ap()
    w = nc.dram_tensor("a_w", [12, 128, D], F32, kind="ExternalInput").ap()
    lam = nc.dram_tensor("a_lam", [4, 64], F32, kind="ExternalInput").ap()
    sub = nc.dram_tensor("a_sub", [128], F32, kind="ExternalInput").ap()
    oT = nc.dram_tensor("yT", [512, SEQ], BF16, kind="ExternalOutput").ap()
    with contextlib.ExitStack() as st0:
        P = Prog(nc, st0)
        with contextlib.ExitStack() as st:
            idf, idb, _ = make_ident(nc, P, st)
            C = make_consts(nc, P, st)
            qTd = nc.dram_tensor("sc_qT", [512, SEQ], BF16).ap()
            kTd = nc.dram_tensor("sc_kT", [512, SEQ], BF16).ap()
            Vd = nc.dram_tensor("sc_V", [4, 128, SEQ // 128, 128], BF16).ap()
            negMd = nc.dram_tensor("sc_negM", [128, 8], F32).ap()
            phase_attn_proj(nc, P, st, hT_blk_fn(hTg), w, qTd, kTd, Vd, negMd, idf, idb, C)
            P.emit()
        with contextlib.ExitStack() as st:
            idf, idb, _ = make_ident(nc, P, st)
            C = make_consts(nc, P, st)
            phase_attn_core(nc, P, st, qTd, kTd, Vd, negMd, lam, sub, lam_init, oT, idf, idb, C)
            P.emit()
    return nc


def emit_token_phases(nc, P, K, AT, wo, x, npost, nfpre, nfpost, npre_next, wg, wu, wd, xo, hTn, scr):
    import contextlib
    with contextlib.ExitStack() as st:
        phase_mmT(nc, P, st, TC, K, AT, wo, scr["m"], pfx="mo")
        P.emit()
    with contextlib.ExitStack() as st:
        idf, idb, _ = make_ident(nc, P, st)
        phase_norm(nc, P, st, TC, x, scr["m"], npost, nfpre, scr["x1"], scr["h2T"], idb)
        P.emit()
    with contextlib.ExitStack() as st:
        phase_ffn_gu(nc, P, st, TC, scr["h2T"], wg, wu, scr["aT"])
        P.emit()
    with contextlib.ExitStack() as st:
        phase_mmT(nc, P, st, TC, D_FF, scr["aT"], wd, scr["y2"], pfx="md")
        P.emit()
    with contextlib.ExitStack() as st:
        idf, idb, _ = make_ident(nc, P, st)
        phase_norm(nc, P, st, TC, scr["x1"], scr["y2"], nfpost, npre_next, xo, hTn, idb)
        P.emit()


def build_tok(K, last):
    import contextlib
    nc = _new_nc()
    AT = nc.dram_tensor("AT", [K, TC], BF16, kind="ExternalInput").ap()
    wo = nc.dram_tensor("wo", [4, 128, K // 128, 512], F32, kind="ExternalInput").ap()
    x = nc.dram_tensor("x", [TC, D], F32, kind="ExternalInput").ap()
    npost = nc.dram_tensor("npost", [D], F32, kind="ExternalInput").ap()
    nfpre = nc.dram_tensor("nfpre", [D], F32, kind="ExternalInput").ap()
    nfpost = nc.dram_tensor("nfpost", [D], F32, kind="ExternalInput").ap()
    npre_next = None if last else nc.dram_tensor("npre_next", [D], F32, kind="ExternalInput").ap()
    wg = nc.dram_tensor("wg", [KC_F, 128, D], F32, kind="ExternalInput").ap()
    wu = nc.dram_tensor("wu", [KC_F, 128, D], F32, kind="ExternalInput").ap()
    wd = nc.dram_tensor("wd", [4, 128, KC_F, 512], F32, kind="ExternalInput").ap()
    xo = nc.dram_tensor("xo", [TC, D], F32, kind="ExternalOutput").ap()
    hTn = None if last else nc.dram_tensor("hT", [D, TC], BF16, kind="ExternalOutput").ap()
    scr = {"m": nc.dram_tensor("sc_m", [TC, D], F32).ap(), "x1": nc.dram_tensor("sc_x1", [TC, D], F32).ap(),
           "h2T": nc.dram_tensor("sc_h2T", [D, TC], BF16).ap(), "aT": nc.dram_tensor("sc_aT", [D_FF, TC], BF16).ap(),
           "y2": nc.dram_tensor("sc_y2", [TC, D], F32).ap()}
    with contextlib.ExitStack() as st0:
        P = Prog(nc, st0)
        emit_token_phases(nc, P, K, AT, wo, x, npost, nfpre, nfpost, npre_next, wg, wu, wd, xo, hTn, scr)
    return nc


def kernel_multilaunch(**inp):
    inp = {k: np.asarray(v) for k, v in inp.items()}
    cores = list(range(NCORES))
    xs = np.ascontiguousarray(inp["x"].reshape(NCORES, TC, D))
    res = run_bass_kernel_spmd(build_first(), [{"x": xs[c], "wpre": inp["norm_mix_pre"][0]} for c in cores],
                               core_ids=cores)
    xcur = [r["xo"] for r in res.results]
    hT = [np.asarray(r["hT"]) for r in res.results]
    for i in range(DEPTH):
        j = i // 2
        hTg = [np.concatenate(hT[4 * b:4 * b + 4], axis=0) for b in range(BATCH)]
        if i % 2 == 0:
            ncm = build_ssd()
            maps = [dict(ssd_core_inputs(inp, j, c % 4), hTg=hTg[c // 4]) for c in cores]
            K = 4096
            wo = lay_slab(inp["ssd_w_out"][j])
        else:
            ncm = build_attn(lambda_init(i))
            maps = [dict(attn_core_inputs(inp, j, c % 4), hTg=hTg[c // 4]) for c in cores]
            K = 2048
            wo = lay_slab(inp["da_w_out"][j])
        res = run_bass_kernel_spmd(ncm, maps, core_ids=cores)
        yT = [np.asarray(r["yT"]) for r in res.results]
        yall = [np.concatenate(yT[4 * b:4 * b + 4], axis=0) for b in range(BATCH)]
        last = i == DEPTH - 1
        wg, wu, wd = lay_colchunk(inp["ffn_w_gate"][i]), lay_colchunk(inp["ffn_w_up"][i]), lay_slab(inp["ffn_w_down"][i])
        maps = []
        for c in cores:
            m = {"AT": np.ascontiguousarray(yall[c // 4][:, (c % 4) * TC:(c % 4 + 1) * TC]), "wo": wo, "x": xcur[c],
                 "npost": inp["norm_mix_post"][i], "nfpre": inp["norm_ffn_pre"][i], "nfpost": inp["norm_ffn_post"][i],
                 "wg": wg, "wu": wu, "wd": wd}
            if not last:
                m["npre_next"] = inp["norm_mix_pre"][i + 1]
            maps.append(m)
        res = run_bass_kernel_spmd(build_tok(K, last), maps, core_ids=cores)
        xcur = [r["xo"] for r in res.results]
        if not last:
            hT = [np.asarray(r["hT"]) for r in res.results]
    out = np.stack([np.asarray(a) for a in xcur]).reshape(BATCH, SEQ, D).astype(np.float32)
    return out


RG4 = [[0, 1, 2, 3], [4, 5, 6, 7]]
RG8 = [list(range(NCORES))]


def phase_select(nc, P, stack, g8, bsel_in, hTg):
    sb = nc.sbuf_tensor
    w = stack.enter_context(sb(U("sel_w"), [128, 2], F32))
    wk = Tok("sel_w")
    P.dma(lambda e: e.dma_start(out=w[:], in_=bsel_in.partition_broadcast(128)), wr=[wk])
    ar = Ring(stack, sb, "sel_a", [128, KC_D, 512], BF16, 2)
    br = Ring(stack, sb, "sel_b", [128, KC_D, 512], BF16, 2)
    orr = Ring(stack, sb, "sel_o", [128, KC_D, 512], BF16, 2)
    v8 = g8.rearrange("(r k p) t -> p r k t", r=8, p=128)
    vo = hTg.rearrange("(r k p) t -> p r k t", r=4, p=128)
    for r in range(4):
        for tb in range(TC // 512):
            ts = slice(tb * 512, (tb + 1) * 512)
            a, ak = ar.next()
            b, bk = br.next()
            P.dma(lambda e, a=a, r=r, ts=ts: e.dma_start(out=a[:], in_=v8[:, r, :, ts]), wr=[ak])
            P.dma(lambda e, b=b, r=r, ts=ts: e.dma_start(out=b[:], in_=v8[:, 4 + r, :, ts]), wr=[bk], q="q_pool")
            o, ok = orr.next()
            P.dve(lambda e, o=o, a=a: e.tensor_scalar(out=o[:], in0=a[:], scalar1=w[:, 0:1], scalar2=None,
                                                      op0=ALU.mult), rd=[ak, wk], wr=[ok])
            P.dve(lambda e, o=o, b=b: e.scalar_tensor_tensor(out=o[:], in0=b[:], scalar=w[:, 1:2], in1=o[:],
                                                             op0=ALU.mult, op1=ALU.add), rd=[bk, wk, ok], wr=[ok])
            P.dma(lambda e, o=o, r=r, ts=ts: e.dma_start(out=vo[:, r, :, ts], in_=o[:]), rd=[ok])


def build_fused():
    import contextlib
    nc = _new_nc()
    dt_ = nc.dram_tensor
    ext = lambda n, s, d=F32: dt_(n, s, d, kind="ExternalInput").ap()
    x = ext("x", [TC, D])
    sel4 = ext("sel4", [4])
    nmpre, nmpost = ext("nmpre", [DEPTH, D]), ext("nmpost", [DEPTH, D])
    nfpre, nfpost = ext("nfpre", [DEPTH, D]), ext("nfpost", [DEPTH, D])
    ffn = [(ext(f"wg{i}", [KC_F, 128, D]), ext(f"wu{i}", [KC_F, 128, D]), ext(f"wd{i}", [4, 128, KC_F, 512]))
           for i in range(DEPTH)]
    ssd = [dict(w=ext(f"s_w{j}", [NZX, 128, D]), wdt=ext(f"s_wdt{j}", [128, KC_D, 16]), dtb=ext(f"s_dtb{j}", [16]),
                cw=ext(f"s_cw{j}", [128, 12, 4]), cb=ext(f"s_cb{j}", [128, 12]), alog=ext(f"s_alog{j}", [16]),
                dsk=ext(f"s_dsk{j}", [16]), nw=ext(f"s_nw{j}", [128, 8]), wo=ext(f"s_wo{j}", [4, 128, 8, 512]))
           for j in range(2)]
    att = [dict(w=ext(f"a_w{j}", [12, 128, D]), lam=ext(f"a_lam{j}", [4, 64]), sub=ext(f"a_sub{j}", [128]),
                wo=ext(f"a_wo{j}", [4, 128, 4, 512])) for j in range(2)]
    xo = dt_("xo", [TC, D], F32, kind="ExternalOutput").ap()
    scr = lambda n, s, d=F32: dt_(n, s, d).ap()
    hT4 = scr("sc_hT4", [16 * 128, KC_D * 512], BF16)
    hTg = scr("sc_hTg", [16 * 128, KC_D * 512], BF16)
    zx = scr("sc_zx", [NZX * 128, SEQ])
    dtt = scr("sc_dt", [SEQ, 16])
    yT = scr("sc_yT", [1024, SEQ], BF16)
    mpA, mpB = scr("sc_mpA", [SEQ, 1024]), scr("sc_mpB", [SEQ, 1024])
    mA, mB = scr("sc_mA", [TC, 1024]), scr("sc_mB", [TC, 1024])
    qTd = scr("sc_qT", [512, SEQ], BF16)
    kTd = scr("sc_kT", [512, SEQ], BF16)
    Vd = scr("sc_V", [4, 128, SEQ // 128, 128], BF16)
    negMd = scr("sc_negM", [128, 8])
    xa = scr("sc_xa", [TC, D])
    x1 = scr("sc_x1", [TC, D])
    h2T = scr("sc_h2T", [D, TC], BF16)
    aT = scr("sc_aT", [D_FF, TC], BF16)
    y2 = scr("sc_y2", [TC, D])
    with contextlib.ExitStack() as st0:
        P = Prog(nc, st0)
        with contextlib.ExitStack() as st:
            idf, idb, _ = make_ident(nc, P, st)
            phase_norm(nc, P, st, TC, x, None, None, nmpre[0], None, hT4, idb, sel4=sel4)
            P.emit(reorder=True)
        xcur = x
        for i in range(DEPTH):
            j = i // 2
            last = i == DEPTH - 1
            ptoks = [Tok(f"hTg{pc}") for pc in range(8)]
            for pc in range(8):
                rs_ = slice(pc * 256, (pc + 1) * 256)
                P.cc(lambda e, rs_=rs_: e.collective_compute("AllReduce", ALU.add, replica_groups=RG4,
                                                             ins=[hT4[rs_, :]], outs=[hTg[rs_, :]]),
                     wr=[ptoks[pc]])
            hT_tok = lambda b, ptoks=ptoks: ptoks[b // 2]
            if i % 2 == 0:
                s = ssd[j]
                with contextlib.ExitStack() as st:
                    phase_ssd_in(nc, P, st, hT_blk_fn_bm(hTg), s["w"], s["wdt"], s["dtb"], zx, dtt, hT_tok=hT_tok)
                    P.emit(reorder=True)
                with contextlib.ExitStack() as st:
                    idf, idb, _ = make_ident(nc, P, st)
                    C = make_consts(nc, P, st)
                    phase_ssd_scan(nc, P, st, zx, dtt, s["cw"], s["cb"], s["alog"], s["dsk"], s["nw"], yT,
                                   idf, idb, C)
                    P.emit(reorder=True)
                K, yT_use, wo = 1024, yT, s["wo"]
            else:
                a = att[j]
                with contextlib.ExitStack() as st:
                    idf, idb, _ = make_ident(nc, P, st)
                    C = make_consts(nc, P, st)
                    phase_attn_proj(nc, P, st, hT_blk_fn_bm(hTg), a["w"], qTd, kTd, Vd, negMd, idf, idb, C,
                                    hT_tok=hT_tok)
                    P.emit(reorder=True)
                with contextlib.ExitStack() as st:
                    idf, idb, _ = make_ident(nc, P, st)
                    C = make_consts(nc, P, st)
                    phase_attn_core(nc, P, st, qTd, kTd, Vd, negMd, a["lam"], a["sub"], lambda_init(i),
                                    yT[0:512, :], idf, idb, C)
                    P.emit(reorder=True)
                K, yT_use, wo = 512, yT[0:512, :], a["wo"]
            with contextlib.ExitStack() as st:
                htoks = ([], [])
                def rs_half(hf, htoks=htoks):
                    src, dst = (mpA, mA) if hf == 0 else (mpB, mB)
                    P.cc(lambda e: e.collective_compute("ReduceScatter", ALU.add, replica_groups=RG4,
                                                        ins=[src], outs=[dst]), rd=list(htoks[hf]))
                phase_mmT(nc, P, st, SEQ, K, yT_use, wo, None, Th=2048, pfx="mo", halves=(mpA, mpB), htoks=htoks,
                          half_done=rs_half)
                P.emit(reorder=True)
            with contextlib.ExitStack() as st:
                idf, idb, _ = make_ident(nc, P, st)
                phase_norm(nc, P, st, TC, xcur, (mA, mB), nmpost[i], nfpre[i], x1, h2T, idb)
                P.emit(reorder=True)
            wg, wu, wd = ffn[i]
            with contextlib.ExitStack() as st:
                phase_ffn_gu(nc, P, st, TC, h2T, wg, wu, aT)
                P.emit(reorder=True)
            with contextlib.ExitStack() as st:
                phase_mmT(nc, P, st, TC, D_FF, aT, wd, y2, pfx="md")
                P.emit(reorder=True)
            with contextlib.ExitStack() as st:
                idf, idb, _ = make_ident(nc, P, st)
                phase_norm(nc, P, st, TC, x1, y2, nfpost[i], None if last else nmpre[i + 1],
                           xo if last else xa, None if last else hT4, idb, sel4=None if last else sel4)
                P.emit(reorder=True)
            xcur = xa
    return nc


def fused_inputs(inp):
    inp = {k: np.asarray(v) for k, v in inp.items()}
    xs = np.ascontiguousarray(inp["x"].reshape(NCORES, TC, D))
    shared = {"nmpre": inp["norm_mix_pre"], "nmpost": inp["norm_mix_post"],
              "nfpre": inp["norm_ffn_pre"], "nfpost": inp["norm_ffn_post"]}
    for i in range(DEPTH):
        shared[f"wg{i}"] = lay_colchunk(inp["ffn_w_gate"][i])
        shared[f"wu{i}"] = lay_colchunk(inp["ffn_w_up"][i])
        shared[f"wd{i}"] = lay_slab(inp["ffn_w_down"][i])
    maps = []
    for c in range(NCORES):
        gl = c % 4
        m = dict(shared)
        m["x"] = xs[c]
        m["sel4"] = np.eye(4, dtype=np.float32)[c % 4]
        for j in range(2):
            for k, v in ssd_core_inputs(inp, j, gl).items():
                m[f"{k}{j}"] = v
            m[f"s_wo{j}"] = lay_slab(inp["ssd_w_out"][j][gl * 1024:(gl + 1) * 1024])
            for k, v in attn_core_inputs(inp, j, gl).items():
                m[f"{k}{j}"] = v
            m[f"a_wo{j}"] = lay_slab(inp["da_w_out"][j][gl * 512:(gl + 1) * 512])
        maps.append(m)
    return maps


def kernel_fused(**inp):
    maps = fused_inputs(inp)
    res = run_bass_kernel_spmd(build_fused(), maps, core_ids=list(range(NCORES)))
    return np.stack([np.asarray(r["xo"]) for r in res.results]).reshape(BATCH, SEQ, D).astype(np.float32)


def kernel(**inputs):
    return kernel_fused(**inputs)
```

```python
import math
import numpy as np
import ml_dtypes
import concourse.bass as bass
import concourse.mybir as mybir
from concourse.bass_utils import run_bass_kernel_spmd

F32 = mybir.dt.float32
BF16 = mybir.dt.bfloat16
AF = mybir.ActivationFunctionType
ALU = mybir.AluOpType
AX = mybir.AxisListType

NDMA_SLOTS = 6


class Tok:
    __slots__ = ("w", "r", "ra", "name")

    def __init__(self, name=""):
        self.w = {}
        self.r = {}
        self.ra = []
        self.name = name


class Prog:
    COMPUTE = ("pe", "act", "dve", "pool", "cc")
    QUEUES = ("q_sp", "q_pool", "q_act")
    ISSUE = {"pe": "pe", "act": "act", "dve": "dve", "pool": "pool", "cc": "pool",
             "q_sp": "sp", "q_pool": "pool", "q_act": "act"}

    def __init__(self, nc, stack):
        self.nc = nc
        self.ops = []
        self.base = 0
        self.cnt = {e: 0 for e in self.COMPUTE}
        self.dcnt = {q: 0 for q in self.QUEUES}
        self.waited = {e: {} for e in ("pe", "act", "dve", "pool", "sp")}
        self.sems = {}
        for e in self.COMPUTE:
            self.sems[e] = stack.enter_context(nc.semaphore(f"s_{e}"))
        for q in self.QUEUES:
            for s in range(NDMA_SLOTS):
                self.sems[(q, s)] = stack.enter_context(nc.semaphore(f"s_{q}{s}"))

    def op(self, eng, fn, rd=(), wr=()):
        self.ops.append((eng, fn, tuple(rd), tuple(wr)))

    def pe(self, fn, rd=(), wr=()):
        self.op("pe", fn, rd, wr)

    def act(self, fn, rd=(), wr=()):
        self.op("act", fn, rd, wr)

    def dve(self, fn, rd=(), wr=()):
        self.op("dve", fn, rd, wr)

    def pool(self, fn, rd=(), wr=()):
        self.op("pool", fn, rd, wr)

    def dma(self, fn, rd=(), wr=(), q="q_sp"):
        self.op(q, fn, rd, wr)

    def cc(self, fn, rd=(), wr=()):
        self.op("cc", fn, rd, wr)

    COST = {"pe": 0.27, "act": 0.5, "dve": 0.55, "pool": 1.2, "cc": 0.1, "q_sp": 0.06, "q_pool": 0.3, "q_act": 0.06}
    DONE_LAT = {"q_sp": 2.5, "q_pool": 3.0, "q_act": 2.5, "cc": 100.0}

    def _schedule(self, ops, order_deps, W=16):
        n = len(ops)
        issue = [self.ISSUE[o[0]] for o in ops]
        per = {e: [] for e in ("pe", "act", "dve", "pool", "sp")}
        for i in range(n):
            per[issue[i]].append(i)
        head = {e: 0 for e in per}
        done = [False] * n
        fin = [0.0] * n
        tfree = {e: 0.0 for e in per}
        order = []
        ready_t = [None] * n
        while len(order) < n:
            best = None
            for e, lst in per.items():
                h = head[e]
                while h < len(lst) and done[lst[h]]:
                    h += 1
                head[e] = h
                cnt = 0
                k = h
                while k < len(lst) and cnt < W:
                    i = lst[k]
                    k += 1
                    if done[i]:
                        continue
                    cnt += 1
                    rt = ready_t[i]
                    if rt is None:
                        ok = True
                        rt = 0.0
                        for d in order_deps[i]:
                            if not done[d]:
                                ok = False
                                break
                            f = fin[d] + (0.0 if issue[d] == e and not ops[d][0].startswith("q_") else 0.25)
                            if f > rt:
                                rt = f
                        if not ok:
                            continue
                        ready_t[i] = rt
                    st = rt if rt > tfree[e] else tfree[e]
                    key = (st, i)
                    if best is None or key < best[0]:
                        best = (key, i, e)
            (st, _), i, e = best
            c = self.COST[ops[i][0]]
            tfree[e] = st + c
            fin[i] = st + c + self.DONE_LAT.get(ops[i][0], 0.0)
            done[i] = True
            order.append(i)
        return order

    def emit(self, reorder=False):
        nc = self.nc
        ops = self.ops
        n = len(ops)
        base = self.base
        deps = [None] * n
        odeps = [None] * n
        signals = [False] * n
        for i, (eng, fn, rd, wr) in enumerate(ops):
            gi = base + i
            is_dma = eng.startswith("q_")
            key = ("dma", gi) if is_dma else eng
            d = set()
            od = set()
            for t in rd:
                for k, j in t.w.items():
                    if j >= base:
                        od.add(j - base)
                    if k == key and key == "pe":
                        continue
                    if j >= base:
                        d.add(j - base)
            for t in wr:
                for k, j in t.w.items():
                    if j >= base:
                        od.add(j - base)
                    if k == key:
                        continue
                    if j >= base:
                        d.add(j - base)
                for k, j in t.ra:
                    if j >= base:
                        od.add(j - base)
                    if k == key:
                        continue
                    if j >= base:
                        d.add(j - base)
            for t in rd:
                t.r[key] = gi
                t.ra.append((key, gi))
            for t in wr:
                t.w = {key: gi}
                t.r = {}
                t.ra = []
            od.discard(i)
            deps[i] = d
            odeps[i] = od
            for j in d:
                signals[j] = True
            if is_dma or eng == "cc":
                signals[i] = True
        if reorder and n > 2:
            order = self._schedule(ops, odeps)
            pos = [0] * n
            for p_, i in enumerate(order):
                pos[i] = p_
            ops = [ops[i] for i in order]
            deps = [{pos[j] for j in deps[i]} for i in order]
            signals = [signals[i] for i in order]
            self.ops = ops
        sig = [None] * n
        dma_idx = [None] * n
        for i, (eng, fn, rd, wr) in enumerate(ops):
            if eng.startswith("q_"):
                k = self.dcnt[eng]
                self.dcnt[eng] += 1
                dma_idx[i] = k
                sig[i] = ((eng, k % NDMA_SLOTS), 16 * (k // NDMA_SLOTS + 1))
            elif signals[i]:
                self.cnt[eng] += 1
                sig[i] = (eng, self.cnt[eng])
        waited = self.waited
        plan = {e: [] for e in ("pe", "act", "dve", "pool", "sp")}
        for i, (eng, fn, rd, wr) in enumerate(ops):
            ie = self.ISSUE[eng]
            ws = []
            need = {}
            for j in deps[i]:
                sk, v = sig[j]
                if need.get(sk, 0) < v:
                    need[sk] = v
            if eng.startswith("q_"):
                k = dma_idx[i]
                if k >= NDMA_SLOTS:
                    sk = (eng, k % NDMA_SLOTS)
                    v = 16 * (k // NDMA_SLOTS)
                    if need.get(sk, 0) < v:
                        need[sk] = v
            for sk, v in need.items():
                if waited[ie].get(sk, 0) < v:
                    waited[ie][sk] = v
                    ws.append((sk, v))
            plan[ie].append((i, ws))
        final = []
        for q, c in self.dcnt.items():
            for s in range(min(c, NDMA_SLOTS)):
                last = 16 * ((c - 1 - s) // NDMA_SLOTS + 1)
                final.append(((q, s), last))
        if self.cnt["cc"]:
            final.append(("cc", self.cnt["cc"]))
        sems = self.sems
        with nc.Block() as block:
            engs = {"pe": block.tensor, "act": block.scalar, "dve": block.vector,
                    "pool": block.gpsimd, "sp": block.sync}

            def mk(ie):
                def body(e):
                    for (i, ws) in plan[ie]:
                        for sk, v in ws:
                            e.wait_ge(sems[sk], v)
                        ins = ops[i][1](e)
                        if sig[i] is not None:
                            sk, v = sig[i]
                            ins.then_inc(sems[sk], 1 if isinstance(sk, str) else 16)
                    if ie == "sp":
                        for sk, v in final:
                            if waited["sp"].get(sk, 0) < v:
                                waited["sp"][sk] = v
                                e.wait_ge(sems[sk], v)
                return body

            for ie in ("sp", "pe", "act", "dve", "pool"):
                if plan[ie] or ie == "sp":
                    engs[ie](mk(ie))
        self.base += n
        self.ops = []


_UID = [0]


def U(name):
    _UID[0] += 1
    return f"{name}_{_UID[0]}"


class Ring:
    def __init__(self, stack, alloc, name, shape, dtype, n):
        self.bufs = []
        for i in range(n):
            t = stack.enter_context(alloc(U(f"{name}{i}"), list(shape), dtype))
            self.bufs.append((t, Tok(f"{name}{i}")))
        self.i = 0

    def next(self):
        b = self.bufs[self.i % len(self.bufs)]
        self.i += 1
        return b


D = 2048
KC_D = D // 128
EPS = 1e-6


def bcast_rows(ap_1d, nparts):
    return ap_1d.partition_broadcast(nparts)


def emit_rstd(P, ss, rstd, tok_ss, tok_rstd, n, width):
    P.act(lambda e: e.activation(out=rstd, in_=ss, func=AF.Ln, bias=EPS, scale=1.0 / width),
          rd=[tok_ss], wr=[tok_rstd])
    P.act(lambda e: e.activation(out=rstd, in_=rstd, func=AF.Exp, scale=-0.5),
          rd=[tok_rstd], wr=[tok_rstd])


def phase_norm(nc, P, stack, T, x_in, y_in, wpost, wpre, x_out, hT_out, ident_bf, sel4=None):
    nt = T // 128
    sb = nc.sbuf_tensor
    xr = Ring(stack, sb, "n_x", [128, D], F32, 2)
    yr = Ring(stack, sb, "n_y", [128, D], F32, 2) if y_in is not None else None
    tr = Ring(stack, sb, "n_t", [128, D], F32, 2)
    sqr = Ring(stack, sb, "n_sq", [128, D], BF16, 1)
    hr = Ring(stack, sb, "n_h", [128, D], BF16, 2)
    htr = Ring(stack, sb, "n_hT", [128, KC_D, 512], BF16, 2)
    str_ = Ring(stack, sb, "n_st", [128, 4], F32, 4)
    ptr = Ring(stack, nc.psum_tensor, "n_pt", [128, 1024], BF16, 4)
    if sel4 is not None:
        scr4 = Ring(stack, sb, "n_sc4", [128, KC_D, 512], BF16, 2)
        s4 = stack.enter_context(sb(U("n_sel4"), [128, 4], F32))
        s4k = Tok("n_sel4")
        P.dma(lambda e: e.dma_start(out=s4[:], in_=sel4.partition_broadcast(128)), wr=[s4k])
    consts = []
    for nm, w in (("n_wpost", wpost), ("n_wpre", wpre)):
        if w is None:
            consts.append((None, None))
            continue
        t = stack.enter_context(sb(U(nm), [128, D], F32))
        tk = Tok(nm)
        P.dma(lambda e, t=t, w=w: e.dma_start(out=t[:], in_=bcast_rows(w, 128)), wr=[tk])
        consts.append((t, tk))
    (wpost_t, wpost_k), (wpre_t, wpre_k) = consts
    for i in range(nt):
        rows = slice(i * 128, (i + 1) * 128)
        xt, xk = xr.next()
        P.dma(lambda e, xt=xt, rows=rows: e.dma_start(out=xt[:], in_=x_in[rows, :]), wr=[xk])
        cur, curk = xt, xk
        if y_in is not None:
            yt, yk = yr.next()
            if isinstance(y_in, tuple):
                P.dma(lambda e, yt=yt, rows=rows: e.dma_start(out=yt[:, 0:1024], in_=y_in[0][rows, :]), wr=[yk],
                      q="q_pool")
                P.dma(lambda e, yt=yt, rows=rows: e.dma_start(out=yt[:, 1024:2048], in_=y_in[1][rows, :]), wr=[yk],
                      q="q_pool")
            else:
                P.dma(lambda e, yt=yt, rows=rows: e.dma_start(out=yt[:], in_=y_in[rows, :]), wr=[yk], q="q_pool")
            sq, sqk = sqr.next()
            st, stk = str_.next()
            P.act(lambda e, sq=sq, yt=yt, st=st: e.activation(out=sq[:], in_=yt[:], func=AF.Square,
                                                               accum_out=st[:, 0:1]),
                  rd=[yk], wr=[sqk, stk])
            emit_rstd(P, st[:, 0:1], st[:, 1:2], stk, stk, 128, D)
            tt, tk = tr.next()
            P.dve(lambda e, tt=tt, yt=yt, st=st: e.scalar_tensor_tensor(
                out=tt[:], in0=yt[:], scalar=st[:, 1:2], in1=wpost_t[:], op0=ALU.mult, op1=ALU.mult),
                rd=[yk, stk, wpost_k], wr=[tk])
            P.dve(lambda e, tt=tt, xt=xt: e.tensor_tensor(out=tt[:], in0=tt[:], in1=xt[:], op=ALU.add),
                  rd=[tk, xk], wr=[tk])
            cur, curk = tt, tk
        if x_out is not None:
            P.dma(lambda e, cur=cur, rows=rows: e.dma_start(out=x_out[rows, :], in_=cur[:]), rd=[curk])
        if wpre is not None:
            sq, sqk = sqr.next()
            st, stk = str_.next()
            P.act(lambda e, sq=sq, cur=cur, st=st: e.activation(out=sq[:], in_=cur[:], func=AF.Square,
                                                                accum_out=st[:, 0:1]),
                  rd=[curk], wr=[sqk, stk])
            emit_rstd(P, st[:, 0:1], st[:, 1:2], stk, stk, 128, D)
            ht, hk = hr.next()
            P.dve(lambda e, ht=ht, cur=cur, st=st: e.scalar_tensor_tensor(
                out=ht[:], in0=cur[:], scalar=st[:, 1:2], in1=wpre_t[:], op0=ALU.mult, op1=ALU.mult),
                rd=[curk, stk, wpre_k], wr=[hk])
            if i % 4 == 0:
                hT, hTk = htr.next()
            tsl = slice((i % 4) * 128, (i % 4) * 128 + 128)
            for half in range(2):
                pt, ptk = ptr.next()
                for j in range(8):
                    kc = half * 8 + j
                    P.pe(lambda e, pt=pt, ht=ht, j=j, kc=kc: e.transpose(
                        out=pt[:, j * 128:(j + 1) * 128], in_=ht[:, kc * 128:(kc + 1) * 128],
                        identity=ident_bf[:]), rd=[hk], wr=[ptk])
                src = lambda pt: pt[:].rearrange("p (k t) -> p k t", t=128)
                if half == 0:
                    P.act(lambda e, hT=hT, pt=pt, tsl=tsl: e.copy(out=hT[:, 0:8, tsl], in_=src(pt)),
                          rd=[ptk], wr=[hTk])
                else:
                    P.dve(lambda e, hT=hT, pt=pt, tsl=tsl: e.tensor_copy(out=hT[:, 8:16, tsl], in_=src(pt)),
                          rd=[ptk], wr=[hTk])
            if i % 4 == 3 and sel4 is None:
                blk = i // 4
                P.dma(lambda e, hT=hT, blk=blk: e.dma_start(
                    out=hT_out.rearrange("(k p) t -> p k t", p=128)[:, :, blk * 512:(blk + 1) * 512],
                    in_=hT[:]), rd=[hTk], q="q_pool")
            elif i % 4 == 3:
                blk = i // 4
                for s_ in range(4):
                    sc, sck = scr4.next()
                    if s_ % 2 == 0:
                        P.dve(lambda e, sc=sc, hT=hT, s_=s_: e.tensor_scalar(
                            out=sc[:], in0=hT[:], scalar1=s4[:, s_:s_ + 1], scalar2=None, op0=ALU.mult),
                            rd=[hTk, s4k], wr=[sck])
                    else:
                        P.act(lambda e, sc=sc, hT=hT, s_=s_: e.activation(
                            out=sc[:], in_=hT[:], func=AF.Copy, scale=s4[:, s_:s_ + 1]),
                            rd=[hTk, s4k], wr=[sck])
                    P.dma(lambda e, sc=sc, blk=blk, s_=s_: e.dma_start(
                        out=hT_out.rearrange("(s b p) f -> p s b f", s=4, p=128)[:, s_, blk, :],
                        in_=sc[:].rearrange("p k t -> p (k t)")), rd=[sck], q="q_pool" if s_ % 2 else "q_sp")


def make_ident(nc, P, stack):
    idf = stack.enter_context(nc.sbuf_tensor(U("ident_f"), [128, 128], F32))
    idb = stack.enter_context(nc.sbuf_tensor(U("ident_b"), [128, 128], BF16))
    k = Tok("ident")
    P.pool(lambda e: e.memset(idf[:], 1.0), wr=[k])
    P.pool(lambda e: e.affine_select(out=idf[:], in_=idf[:], pattern=[[-1, 128]], compare_op=ALU.is_equal,
                                     fill=0.0, base=0, channel_multiplier=1), rd=[k], wr=[k])
    P.dve(lambda e: e.tensor_copy(out=idb[:], in_=idf[:]), rd=[k], wr=[k])
    return idf, idb, k


D_FF = 5632
KC_F = D_FF // 128


def cast_op(P, idx, out, in_, rd, wr):
    if idx % 2 == 0:
        P.dve(lambda e: e.tensor_copy(out=out, in_=in_), rd=rd, wr=wr)
    else:
        P.act(lambda e: e.copy(out=out, in_=in_), rd=rd, wr=wr)


def phase_ffn_gu(nc, P, stack, T, hT_in, wg_l, wu_l, actT_out):
    sb = nc.sbuf_tensor
    hT = stack.enter_context(sb(U("gu_hT"), [128, KC_D, T], BF16))
    hTk = [Tok(f"gu_hT{k}") for k in range(KC_D // 4)]
    hv = hT_in.rearrange("(k p) t -> p k t", p=128)
    for g in range(KC_D // 4):
        P.dma(lambda e, g=g: e.dma_start(out=hT[:, g * 4:(g + 1) * 4, :], in_=hv[:, g * 4:(g + 1) * 4, :]),
              wr=[hTk[g]], q="q_pool")
    wst = [Ring(stack, sb, f"gu_wst{m}", [128, D], F32, 2) for m in range(2)]
    wbf = [Ring(stack, sb, f"gu_wbf{m}", [128, KC_D, 128], BF16, 2) for m in range(2)]
    ps = [Ring(stack, nc.psum_tensor, f"gu_ps{m}", [128, 512], F32, 3) for m in range(2)]
    sgr = Ring(stack, sb, "gu_sg", [128, 512], F32, 2)
    ar = Ring(stack, sb, "gu_a", [128, 512], BF16, 4)
    nsl = T // 512
    ci = 0
    for fc in range(KC_F):
        wb = []
        for m, wl in enumerate((wg_l, wu_l)):
            st_, stk = wst[m].next()
            P.dma(lambda e, st_=st_, wl=wl, fc=fc: e.dma_start(out=st_[:], in_=wl[fc]), wr=[stk])
            b, bk = wbf[m].next()
            cast_op(P, ci, b[:].rearrange("p k j -> p (k j)"), st_[:], [stk], [bk])
            ci += 1
            wb.append((b, bk))
        for sl in range(nsl):
            tsl = slice(sl * 512, (sl + 1) * 512)
            pp = []
            for m in range(2):
                p_, pk = ps[m].next()
                b, bk = wb[m]
                for kc in range(KC_D):
                    P.pe(lambda e, p_=p_, b=b, kc=kc, tsl=tsl: e.matmul(
                        p_[:], lhsT=b[:, kc, :], rhs=hT[:, kc, tsl], start=(kc == 0), stop=(kc == KC_D - 1)),
                        rd=[bk, hTk[kc // 4]], wr=[pk])
                pp.append((p_, pk))
            sg, sgk = sgr.next()
            P.act(lambda e, sg=sg, p_=pp[0][0]: e.activation(out=sg[:], in_=p_[:], func=AF.Silu),
                  rd=[pp[0][1]], wr=[sgk])
            a, ak = ar.next()
            P.dve(lambda e, a=a, sg=sg, p_=pp[1][0]: e.tensor_tensor(out=a[:], in0=sg[:], in1=p_[:], op=ALU.mult),
                  rd=[sgk, pp[1][1]], wr=[ak])
            P.dma(lambda e, a=a, fc=fc, tsl=tsl: e.dma_start(out=actT_out[fc * 128:(fc + 1) * 128, tsl], in_=a[:]),
                  rd=[ak], q="q_pool")


def phase_mmT(nc, P, stack, T, K, AT_in, w_l, y_out, Th=1024, pfx="mt", halves=None, htoks=None, half_done=None):
    KC = K // 128
    G = 4
    NG = KC // G
    Th = min(Th, T, 1024)
    NT = Th // 128
    sb = nc.sbuf_tensor
    nAT = 1 if halves is None else 2
    ATs = [(stack.enter_context(sb(U(f"{pfx}_AT"), [128, KC, Th], BF16)), [Tok(f"{pfx}_AT{g}") for g in range(NG)])
           for _ in range(nAT)]
    ati = 0
    AT, ATk = ATs[0]
    wst = Ring(stack, sb, f"{pfx}_wst", [128, G, 512], F32, 3)
    wbf = Ring(stack, sb, f"{pfx}_wbf", [128, G, 512], BF16, 4)
    ps = Ring(stack, nc.psum_tensor, f"{pfx}_ps", [128, 512], F32, 8)
    ys = Ring(stack, sb, f"{pfx}_ys", [128, 512], F32, 4)
    av = AT_in.rearrange("(k p) t -> p k t", p=128)
    ci = 0
    ei = 0
    if halves is None:
        order = [(th, s) for th in range(T // Th) for s in range(4)]
    else:
        order = [(th, s) for s in range(4) for th in range(T // Th)]
    last_th = None
    for (th, s) in order:
        if th != last_th:
            last_th = th
            AT, ATk = ATs[ati % nAT]
            ati += 1
            for g in range(NG):
                P.dma(lambda e, g=g, th=th, AT=AT: e.dma_start(out=AT[:, g * G:(g + 1) * G, :],
                                                        in_=av[:, g * G:(g + 1) * G, th * Th:(th + 1) * Th]),
                      wr=[ATk[g]], q="q_pool" if halves is None else "q_sp")
        if True:
            banks = [ps.next() for _ in range(NT)]
            for g in range(NG):
                st_, stk = wst.next()
                P.dma(lambda e, st_=st_, s=s, g=g: e.dma_start(out=st_[:], in_=w_l[s, :, g * G:(g + 1) * G, :]),
                      wr=[stk])
                wb, wbk = wbf.next()
                cast_op(P, ci, wb[:], st_[:], [stk], [wbk])
                ci += 1
                for tl in range(NT):
                    p_, pk = banks[tl]
                    for kk in range(G):
                        kc = g * G + kk
                        P.pe(lambda e, p_=p_, kc=kc, kk=kk, tl=tl, wb=wb, AT=AT: e.matmul(
                            p_[:], lhsT=AT[:, kc, tl * 128:(tl + 1) * 128], rhs=wb[:, kk, :],
                            start=(kc == 0), stop=(kc == KC - 1)),
                            rd=[ATk[g], wbk], wr=[pk])
            for tl in range(NT):
                p_, pk = banks[tl]
                y_, yk = ys.next()
                if ei % 2 == 0:
                    P.act(lambda e, y_=y_, p_=p_: e.copy(out=y_[:], in_=p_[:]), rd=[pk], wr=[yk])
                else:
                    P.dve(lambda e, y_=y_, p_=p_: e.tensor_copy(out=y_[:], in_=p_[:]), rd=[pk], wr=[yk])
                ei += 1
                r0 = th * Th + tl * 128
                if halves is None:
                    P.dma(lambda e, y_=y_, r0=r0, s=s: e.dma_start(out=y_out[r0:r0 + 128, s * 512:(s + 1) * 512],
                                                                   in_=y_[:]), rd=[yk], q="q_pool")
                else:
                    dst = halves[s // 2]
                    tk_ = Tok("yhalf")
                    htoks[s // 2].append(tk_)
                    P.dma(lambda e, y_=y_, r0=r0, s=s, dst=dst: e.dma_start(
                        out=dst[r0:r0 + 128, (s % 2) * 512:(s % 2 + 1) * 512], in_=y_[:]), rd=[yk], wr=[tk_],
                        q="q_act")
            if halves is not None and half_done is not None and s % 2 == 1 and th == T // Th - 1:
                half_done(s // 2)


SEQ = 8192
NBLK = SEQ // 512


def make_consts(nc, P, stack):
    sb = nc.sbuf_tensor
    c = {}
    c["ones_bf"] = stack.enter_context(sb(U("ones_bf"), [128, 128], BF16))
    c["ones_f"] = stack.enter_context(sb(U("ones_f"), [128, 128], F32))
    c["sel2"] = stack.enter_context(sb(U("sel2"), [128, 2], BF16))
    k = Tok("consts")
    c["tok"] = k
    P.pool(lambda e: e.memset(c["ones_bf"][:], 1.0), wr=[k])
    P.pool(lambda e: e.memset(c["ones_f"][:], 1.0), wr=[k])
    P.pool(lambda e: e.memset(c["sel2"][:], 0.0), wr=[k])
    P.pool(lambda e: e.memset(c["sel2"][0:64, 0:1], 1.0), wr=[k])
    P.pool(lambda e: e.memset(c["sel2"][64:128, 1:2], 1.0), wr=[k])
    return c


def phase_attn_proj(nc, P, stack, hT_blk, w_l, qTd, kTd, Vd, negMd, idf, idb, C, S=SEQ, hT_tok=None):
    sb = nc.sbuf_tensor
    ps = nc.psum_tensor
    NB = S // 512
    ones_f, sel2, ck = C["ones_f"], C["sel2"], C["tok"]
    W = stack.enter_context(sb(U("ap_w"), [128, KC_D, 12 * 128], BF16))
    Wk = [Tok(f"ap_w{f}") for f in range(12)]
    wst = Ring(stack, sb, "ap_wst", [128, D], F32, 2)
    for fc in range(12):
        st_, stk = wst.next()
        P.dma(lambda e, st_=st_, fc=fc: e.dma_start(out=st_[:], in_=w_l[fc]), wr=[stk], q="q_pool")
        cast_op(P, fc, W[:, :, fc * 128:(fc + 1) * 128], st_[:].rearrange("p (k j) -> p k j", j=128), [stk], [Wk[fc]])
    hr = Ring(stack, sb, "ap_h", [128, KC_D, 512], BF16, 2)
    pw = Ring(stack, ps, "ap_pw", [128, 512], F32, 5)
    pm = Ring(stack, ps, "ap_pm", [128, 512], F32, 2)
    osr = Ring(stack, sb, "ap_os", [128, 512], BF16, 6)
    sqr = Ring(stack, sb, "ap_sq", [128, 512], BF16, 3)
    vor = Ring(stack, sb, "ap_vo", [128, 4, 128], BF16, 3)
    nmx = stack.enter_context(sb(U("ap_nmx"), [2, 8, NB], F32))
    nmxk = Tok("ap_nmx")
    ei = 0
    for b in range(NB):
        h_, hk = hr.next()
        P.dma(lambda e, h_=h_, b=b: e.dma_start(out=h_[:], in_=hT_blk(b)), rd=([hT_tok(b)] if hT_tok else []),
              wr=[hk])
        tsl = slice(b * 512, (b + 1) * 512)
        for ch in range(12):
            which, hl = ch // 4, ch % 4
            p_, pk = pw.next()
            for kc in range(KC_D):
                P.pe(lambda e, p_=p_, ch=ch, kc=kc, h_=h_: e.matmul(
                    p_[:], lhsT=W[:, kc, ch * 128:(ch + 1) * 128], rhs=h_[:, kc, :],
                    start=(kc == 0), stop=(kc == KC_D - 1)), rd=[Wk[ch], hk], wr=[pk])
            o_, ok = osr.next()
            sc = 0.125 if which == 0 else 1.0
            if ei % 2 == 0:
                P.act(lambda e, o_=o_, p_=p_, sc=sc: e.activation(out=o_[:], in_=p_[:], func=AF.Copy, scale=sc),
                      rd=[pk], wr=[ok])
            else:
                P.dve(lambda e, o_=o_, p_=p_, sc=sc: e.tensor_scalar(out=o_[:], in0=p_[:], scalar1=sc, scalar2=None,
                                                                    op0=ALU.mult), rd=[pk], wr=[ok])
            ei += 1
            if which < 2:
                dst = qTd if which == 0 else kTd
                P.dma(lambda e, o_=o_, dst=dst, hl=hl, tsl=tsl: e.dma_start(
                    out=dst[hl * 128:(hl + 1) * 128, tsl], in_=o_[:]), rd=[ok], q="q_pool")
                sq, sqk = sqr.next()
                P.dve(lambda e, sq=sq, o_=o_: e.tensor_tensor(out=sq[:], in0=o_[:], in1=o_[:], op=ALU.mult),
                      rd=[ok], wr=[sqk])
                p2, p2k = pm.next()
                P.pe(lambda e, p2=p2, sq=sq: e.matmul(p2[0:2, :], lhsT=sel2[:], rhs=sq[:], start=True, stop=True),
                     rd=[sqk, ck], wr=[p2k])
                P.dve(lambda e, p2=p2, ch=ch, b=b: e.tensor_reduce(
                    out=nmx[:, ch, b:b + 1], in_=p2[0:2, :], axis=AX.X, op=ALU.max), rd=[p2k], wr=[nmxk])
            else:
                p2, p2k = pm.next()
                p2b = p2[:].bitcast(BF16)
                for j in range(4):
                    P.pe(lambda e, p2b=p2b, o_=o_, j=j: e.transpose(
                        out=p2b[:, j * 128:(j + 1) * 128], in_=o_[:, j * 128:(j + 1) * 128], identity=idb[:]),
                        rd=[ok], wr=[p2k])
                vo, vok = vor.next()
                P.act(lambda e, vo=vo, p2b=p2b: e.copy(out=vo[:], in_=p2b[:, 0:512].rearrange("p (j d) -> p j d", d=128)),
                      rd=[p2k], wr=[vok])
                P.dma(lambda e, vo=vo, hl=hl, b=b: e.dma_start(out=Vd[hl, :, b * 4:(b + 1) * 4, :], in_=vo[:]),
                      rd=[vok], q="q_pool")
    msc = stack.enter_context(sb(U("ap_msc"), [2, 32], F32))
    P.dve(lambda e: e.tensor_reduce(out=msc[:, 0:8], in_=nmx[:], axis=AX.X, op=ALU.max), rd=[nmxk], wr=[nmxk])
    P.dve(lambda e: e.tensor_tensor(out=msc[:, 8:12], in0=msc[:, 0:4], in1=msc[:, 4:8], op=ALU.mult),
          rd=[nmxk], wr=[nmxk])
    P.act(lambda e: e.activation(out=msc[:, 12:16], in_=msc[:, 8:12], func=AF.Ln), rd=[nmxk], wr=[nmxk])
    P.act(lambda e: e.activation(out=msc[:, 12:16], in_=msc[:, 12:16], func=AF.Exp, scale=0.5), rd=[nmxk], wr=[nmxk])
    for hl in range(4):
        P.dve(lambda e, hl=hl: e.tensor_scalar(out=msc[:, 16 + hl * 2:18 + hl * 2], in0=idf[0:2, 0:2],
                                               scalar1=msc[:, 12 + hl:13 + hl], scalar2=-1.02,
                                               op0=ALU.mult, op1=ALU.mult), rd=[nmxk], wr=[nmxk])
    p2, p2k = pm.next()
    P.pe(lambda e: e.matmul(p2[:, 0:8], lhsT=ones_f[0:2, :], rhs=msc[:, 16:24], start=True, stop=True),
         rd=[nmxk, ck], wr=[p2k])
    nm = stack.enter_context(sb(U("ap_nm"), [128, 8], F32))
    nmk = Tok("ap_nm")
    P.dve(lambda e: e.tensor_copy(out=nm[:], in_=p2[:, 0:8]), rd=[p2k], wr=[nmk])
    P.dma(lambda e: e.dma_start(out=negMd, in_=nm[:]), rd=[nmk])


def phase_attn_core(nc, P, stack, qTd, kTd, Vd, negMd, lam_in, subln_in, lambda_init, oT_out, idf, idb, C, S=SEQ):
    sb = nc.sbuf_tensor
    ps = nc.psum_tensor
    NB = S // 512
    ones_bf, ones_f, sel2, ck = C["ones_bf"], C["ones_f"], C["sel2"], C["tok"]
    lam4 = stack.enter_context(sb(U("at_lam4"), [128, 4, 64], F32))
    lamk = Tok("lam")
    P.dma(lambda e: e.dma_start(out=lam4[:].rearrange("p a d -> p (a d)"),
                                in_=lam_in.rearrange("a d -> (a d)").partition_broadcast(128)), wr=[lamk])
    lsc = stack.enter_context(sb(U("at_lsc"), [128, 8], F32))
    lpr = stack.enter_context(sb(U("at_lpr"), [128, 2, 64], F32))
    P.dve(lambda e: e.tensor_tensor(out=lpr[:, 0, :], in0=lam4[:, 0, :], in1=lam4[:, 1, :], op=ALU.mult),
          rd=[lamk], wr=[lamk])
    P.dve(lambda e: e.tensor_tensor(out=lpr[:, 1, :], in0=lam4[:, 2, :], in1=lam4[:, 3, :], op=ALU.mult),
          rd=[lamk], wr=[lamk])
    P.dve(lambda e: e.tensor_reduce(out=lsc[:, 0:2], in_=lpr[:], axis=AX.X, op=ALU.add), rd=[lamk], wr=[lamk])
    P.act(lambda e: e.activation(out=lsc[:, 2:4], in_=lsc[:, 0:2], func=AF.Exp), rd=[lamk], wr=[lamk])
    P.dve(lambda e: e.scalar_tensor_tensor(out=lsc[:, 4:5], in0=lsc[:, 3:4], scalar=-float(lambda_init),
                                           in1=lsc[:, 2:3], op0=ALU.add, op1=ALU.subtract), rd=[lamk], wr=[lamk])
    P.dma(lambda e: e.dma_start(out=lsc[:, 5:6], in_=subln_in.rearrange("(p o) -> p o", o=1)), wr=[lamk])
    P.dve(lambda e: e.tensor_scalar(out=lsc[:, 6:7], in0=lsc[:, 5:6], scalar1=1.0 - float(lambda_init),
                                    scalar2=None, op0=ALU.mult), rd=[lamk], wr=[lamk])
    neglam = lsc[:, 4:5]
    sublnw = lsc[:, 6:7]

    qTc = [stack.enter_context(sb(U(f"at_qT{c}"), [128, S], BF16)) for c in range(2)]
    qzk = Tok("at_qz")
    P.dve(lambda e: e.memset(qTc[0][64:128, :], 0.0), wr=[qzk])
    P.dve(lambda e: e.memset(qTc[1][0:64, :], 0.0), wr=[qzk])
    kT = stack.enter_context(sb(U("at_kT"), [128, S], BF16))
    V = stack.enter_context(sb(U("at_V"), [128, S // 128, 128], BF16))
    qk1, kk1, vk1 = Tok("at_q"), Tok("at_k"), Tok("at_v")
    qk_ = [qk1] * NB
    kk_ = [kk1] * NB
    vk_ = [vk1] * NB
    negM = stack.enter_context(sb(U("at_negM"), [128, 8], F32))
    negMk = Tok("negM")
    P.dma(lambda e: e.dma_start(out=negM[:], in_=negMd), wr=[negMk])
    pw = Ring(stack, ps, "at_pw", [128, 512], F32, 3)
    po = [Ring(stack, ps, f"at_po{c}", [128, 512], F32, 1) for c in range(2)]
    pl = [Ring(stack, ps, f"at_pl{c}", [128, 512], F32, 1) for c in range(2)]
    lacc = Ring(stack, sb, "at_la", [128, 512], F32, 4)
    pm = Ring(stack, ps, "at_pm", [128, 512], F32, 1)
    ptr = Ring(stack, sb, "at_pt", [128, 512], BF16, 6)
    e32 = Ring(stack, sb, "at_e32", [128, 512], F32, 8)
    obf = Ring(stack, sb, "at_obf", [128, 512], BF16, 2)
    for hl in range(4):
        hs = slice(hl * 128, (hl + 1) * 128)
        P.dma(lambda e, hl=hl: e.dma_start(out=qTc[0][0:64, :], in_=qTd[hl * 128:hl * 128 + 64, :]),
              rd=[qzk], wr=[qk1])
        P.dma(lambda e, hl=hl: e.dma_start(out=qTc[1][64:128, :], in_=qTd[hl * 128 + 64:hl * 128 + 128, :]),
              rd=[qzk], wr=[qk1], q="q_pool")
        P.dma(lambda e, hs=hs: e.dma_start(out=kT[:], in_=kTd[hs, :]), wr=[kk1])
        P.dma(lambda e, hl=hl: e.dma_start(out=V[:], in_=Vd[hl]), wr=[vk1], q="q_pool")
        steps = [(qt, c, sbk) for qt in range(NB) for c in range(2) for sbk in range(qt * 4 + 4)]
        LA = 2
        inflight = {}
        accs = {}
        tparts = {}
        deferred = []

        def stepA(k):
            qt, c, sbk = steps[k]
            q0 = qt * 512
            rows = slice(c * 64, (c + 1) * 64)
            d = max(0, sbk - qt * 4)
            cs = slice(d * 128, 512)
            w_, wk_ = pw.next()
            P.pe(lambda e: e.matmul(w_[:, cs], lhsT=kT[:, sbk * 128:(sbk + 1) * 128],
                                    rhs=qTc[c][:, q0 + cs.start:q0 + 512], start=True, stop=True),
                 rd=[kk_[sbk // 4], qk_[qt], qzk], wr=[wk_])
            pt, ptk = ptr.next()
            bcol = hl * 2 + c
            P.act(lambda e: e.activation(out=pt[:, cs], in_=w_[:, cs], func=AF.Exp, bias=negM[:, bcol:bcol + 1],
                                         scale=1.0), rd=[wk_, negMk], wr=[ptk])
            if sbk >= qt * 4:
                base = q0 + cs.start - sbk * 128
                P.pool(lambda e: e.affine_select(out=pt[:, cs], in_=pt[:, cs], pattern=[[1, 512 - cs.start]],
                                                 compare_op=ALU.is_ge, fill=0.0, base=base,
                                                 channel_multiplier=-1), rd=[ptk], wr=[ptk])
            inflight[k] = (pt, ptk, cs)

        def epi_c(qt, c, k):
            o_, ok, l_, lk, la, lak = accs.pop((qt, c))

            def part2():
                P.pe(lambda e: e.matmul(l_[:], lhsT=ones_f[:], rhs=la[:], start=False, stop=True),
                     rd=[lak, ck], wr=[lk])
                r_, rk = e32.next()
                P.dve(lambda e: e.reciprocal(out=r_[:], in_=l_[:]), rd=[lk], wr=[rk])
                t_, tk = e32.next()
                P.dve(lambda e: e.tensor_tensor(out=t_[:], in0=o_[:], in1=r_[:], op=ALU.mult), rd=[ok, rk], wr=[tk])
                tparts[(qt, c)] = (t_, tk)
                if c == 1:
                    epi_1(qt, k + 2)
            deferred.append((k + 2, part2))

        def epi_1(qt, k):
            t0, t0k = tparts.pop((qt, 0))
            t1, t1k = tparts.pop((qt, 1))
            of, ofk = e32.next()
            P.dve(lambda e: e.scalar_tensor_tensor(out=of[:], in0=t1[:], scalar=neglam, in1=t0[:],
                                                   op0=ALU.mult, op1=ALU.add), rd=[t0k, t1k, lamk], wr=[ofk])
            sq, sqk2 = e32.next()
            P.act(lambda e: e.activation(out=sq[:], in_=of[:], func=AF.Square), rd=[ofk], wr=[sqk2])

            def epi_2():
                m_, mk = pm.next()
                P.pe(lambda e: e.matmul(m_[:], lhsT=ones_f[:], rhs=sq[:], start=True, stop=True),
                     rd=[sqk2, ck], wr=[mk])
                rs, rsk = e32.next()
                P.act(lambda e: e.activation(out=rs[:], in_=m_[:], func=AF.Ln, bias=EPS, scale=1.0 / 128),
                      rd=[mk], wr=[rsk])
                P.act(lambda e: e.activation(out=rs[:], in_=rs[:], func=AF.Exp, scale=-0.5), rd=[rsk], wr=[rsk])
                ob, obk = obf.next()
                P.dve(lambda e: e.scalar_tensor_tensor(out=ob[:], in0=of[:], scalar=sublnw, in1=rs[:],
                                                       op0=ALU.mult, op1=ALU.mult), rd=[ofk, rsk, lamk], wr=[obk])
                P.dma(lambda e, hl=hl: e.dma_start(out=oT_out[hl * 128:(hl + 1) * 128, qt * 512:(qt + 1) * 512],
                                                   in_=ob[:]), rd=[obk])
            deferred.append((k + 6, epi_2))

        def stepB(k):
            qt, c, sbk = steps[k]
            nsb = qt * 4 + 4
            pt, ptk, cs = inflight.pop(k)
            if sbk == 0:
                o_, ok = po[c].next()
                l_, lk = pl[c].next()
                la, lak = lacc.next()
                accs[(qt, c)] = (o_, ok, l_, lk, la, lak)
            o_, ok, l_, lk, la, lak = accs[(qt, c)]
            P.pe(lambda e: e.matmul(o_[:, cs], lhsT=V[:, sbk, :], rhs=pt[:, cs], start=(sbk == 0),
                                    stop=(sbk == nsb - 1)), rd=[vk_[sbk // 4], ptk], wr=[ok])
            if sbk % 2 == 0:
                P.pe(lambda e: e.matmul(l_[:, cs], lhsT=ones_bf[:], rhs=pt[:, cs], start=(sbk == 0), stop=False),
                     rd=[ck, ptk], wr=[lk])
            elif sbk == 1:
                if cs.start > 0:
                    P.dve(lambda e: e.memset(la[:, 0:cs.start], 0.0), wr=[lak])
                P.dve(lambda e: e.tensor_copy(out=la[:, cs], in_=pt[:, cs]), rd=[ptk], wr=[lak])
            else:
                P.dve(lambda e: e.tensor_tensor(out=la[:, cs], in0=la[:, cs], in1=pt[:, cs], op=ALU.add),
                      rd=[ptk, lak], wr=[lak])
            if sbk == nsb - 1:
                epi_c(qt, c, k)

        ns = len(steps)
        for k in range(ns + LA):
            if k < ns:
                stepA(k)
            if k - LA >= 0:
                stepB(k - LA)
            for item in [d_ for d_ in deferred if d_[0] <= k]:
                deferred.remove(item)
                item[1]()
        for item in deferred:
            item[1]()


NZX = 20


def phase_ssd_in(nc, P, stack, hT_blk, w_l, wdt_l, dtb_in, zxT_out, dt_out, S=SEQ, hT_tok=None):
    sb = nc.sbuf_tensor
    NB = S // 512
    W = stack.enter_context(sb(U("si_w"), [128, KC_D, NZX * 128], BF16))
    Wk = [Tok(f"si_w{f}") for f in range(NZX)]
    wst = Ring(stack, sb, "si_wst", [128, D], F32, 2)
    for fc in range(NZX):
        st_, stk = wst.next()
        P.dma(lambda e, st_=st_, fc=fc: e.dma_start(out=st_[:], in_=w_l[fc]), wr=[stk])
        cast_op(P, fc, W[:, :, fc * 128:(fc + 1) * 128], st_[:].rearrange("p (k j) -> p k j", j=128), [stk], [Wk[fc]])
    wdtf = stack.enter_context(sb(U("si_wdtf"), [128, KC_D, 16], F32))
    wdt = stack.enter_context(sb(U("si_wdt"), [128, KC_D, 16], BF16))
    wdk = Tok("si_wdt")
    P.dma(lambda e: e.dma_start(out=wdtf[:], in_=wdt_l), wr=[wdk])
    P.dve(lambda e: e.tensor_copy(out=wdt[:], in_=wdtf[:]), rd=[wdk], wr=[wdk])
    dtb = stack.enter_context(sb(U("si_dtb"), [128, 16], F32))
    dtbk = Tok("si_dtb")
    P.dma(lambda e: e.dma_start(out=dtb[:], in_=dtb_in.partition_broadcast(128)), wr=[dtbk])
    hr = Ring(stack, sb, "si_h", [128, KC_D, 512], BF16, 2)
    pw = Ring(stack, nc.psum_tensor, "si_pw", [128, 512], F32, 4)
    pd = Ring(stack, nc.psum_tensor, "si_pd", [128, 16], F32, 2)
    osr = Ring(stack, sb, "si_os", [128, 512], F32, 4)
    dr = Ring(stack, sb, "si_d", [128, 4, 16], F32, 6)
    dto = Ring(stack, sb, "si_dto", [128, 4, 16], F32, 2)
    ei = 0
    for b in range(NB):
        h_, hk = hr.next()
        P.dma(lambda e, h_=h_, b=b: e.dma_start(out=h_[:], in_=hT_blk(b)), rd=([hT_tok(b)] if hT_tok else []),
              wr=[hk])
        tsl = slice(b * 512, (b + 1) * 512)
        for fc in range(NZX):
            p_, pk = pw.next()
            for kc in range(KC_D):
                P.pe(lambda e, p_=p_, fc=fc, kc=kc, h_=h_: e.matmul(
                    p_[:], lhsT=W[:, kc, fc * 128:(fc + 1) * 128], rhs=h_[:, kc, :],
                    start=(kc == 0), stop=(kc == KC_D - 1)), rd=[Wk[fc], hk], wr=[pk])
            o_, ok = osr.next()
            if ei % 2 == 0:
                P.act(lambda e, o_=o_, p_=p_: e.copy(out=o_[:], in_=p_[:]), rd=[pk], wr=[ok])
            else:
                P.dve(lambda e, o_=o_, p_=p_: e.tensor_copy(out=o_[:], in_=p_[:]), rd=[pk], wr=[ok])
            ei += 1
            P.dma(lambda e, o_=o_, fc=fc, tsl=tsl: e.dma_start(out=zxT_out[fc * 128:(fc + 1) * 128, tsl], in_=o_[:]),
                  rd=[ok])
        x_, xk = dr.next()
        for j in range(4):
            p_, pk = pd.next()
            for kc in range(KC_D):
                P.pe(lambda e, p_=p_, kc=kc, h_=h_, j=j: e.matmul(
                    p_[:], lhsT=h_[:, kc, j * 128:(j + 1) * 128], rhs=wdt[:, kc, :],
                    start=(kc == 0), stop=(kc == KC_D - 1)), rd=[wdk, hk], wr=[pk])
            P.dve(lambda e, x_=x_, p_=p_, j=j: e.tensor_tensor(out=x_[:, j, :], in0=p_[:], in1=dtb[:], op=ALU.add),
                  rd=[pk, dtbk], wr=[xk])
        a_, ak = dr.next()
        P.dve(lambda e, a_=a_, x_=x_: e.scalar_tensor_tensor(out=a_[:], in0=x_[:], scalar=-1.0, in1=x_[:],
                                                             op0=ALU.mult, op1=ALU.max), rd=[xk], wr=[ak])
        P.act(lambda e, a_=a_: e.activation(out=a_[:], in_=a_[:], func=AF.Exp, scale=-1.0), rd=[ak], wr=[ak])
        P.act(lambda e, a_=a_: e.activation(out=a_[:], in_=a_[:], func=AF.Ln, bias=1.0, scale=1.0), rd=[ak], wr=[ak])
        d_, dk = dto.next()
        P.dve(lambda e, d_=d_, x_=x_, a_=a_: e.scalar_tensor_tensor(
            out=d_[:], in0=x_[:], scalar=0.0, in1=a_[:], op0=ALU.max, op1=ALU.add), rd=[xk, ak], wr=[dk])
        P.dma(lambda e, d_=d_, b=b: e.dma_start(
            out=dt_out[b * 512:(b + 1) * 512, :].rearrange("(j p) h -> p j h", p=128), in_=d_[:]), rd=[dk])


def phase_ssd_scan(nc, P, stack, zxT_in, dt_in, convw_in, convb_in, alog_in, dsk_in, normw_in, yT_out,
                   idf, idb, C, S=SEQ):
    sb = nc.sbuf_tensor
    ps = nc.psum_tensor
    ones_f, ck = C["ones_f"], C["tok"]
    TB = 256
    NB = S // TB
    zv = zxT_in.rearrange("(k p) t -> p k t", p=128)
    tri = stack.enter_context(sb(U("ss_tri"), [128, 128], F32))
    cst = Tok("ss_const")
    P.pool(lambda e: e.memset(tri[:], 1.0), wr=[cst])
    P.pool(lambda e: e.affine_select(out=tri[:], in_=tri[:], pattern=[[1, 128]], compare_op=ALU.is_ge,
                                     fill=0.0, base=0, channel_multiplier=-1), rd=[cst], wr=[cst])
    cw = stack.enter_context(sb(U("ss_cw"), [128, 12, 4], F32))
    cb = stack.enter_context(sb(U("ss_cb"), [128, 12], F32))
    nw = stack.enter_context(sb(U("ss_nw"), [128, 8], F32))
    abc = stack.enter_context(sb(U("ss_abc"), [128, 16], F32))
    d16 = stack.enter_context(sb(U("ss_d16"), [128, 16], F32))
    Dbc = stack.enter_context(sb(U("ss_Dbc"), [128, 16, 64], F32))
    P.dma(lambda e: e.dma_start(out=cw[:], in_=convw_in), wr=[cst])
    P.dma(lambda e: e.dma_start(out=cb[:], in_=convb_in), wr=[cst])
    P.dma(lambda e: e.dma_start(out=nw[:], in_=normw_in), wr=[cst])
    P.dma(lambda e: e.dma_start(out=abc[:], in_=alog_in.partition_broadcast(128)), wr=[cst])
    P.dma(lambda e: e.dma_start(out=d16[:], in_=dsk_in.partition_broadcast(128)), wr=[cst])
    P.act(lambda e: e.activation(out=abc[:], in_=abc[:], func=AF.Exp), rd=[cst], wr=[cst])
    P.dve(lambda e: e.tensor_scalar(out=abc[:], in0=abc[:], scalar1=-1.0, scalar2=None, op0=ALU.mult),
          rd=[cst], wr=[cst])
    P.dve(lambda e: e.tensor_copy(out=Dbc[:], in_=d16[:].unsqueeze(2).to_broadcast([128, 16, 64])),
          rd=[cst], wr=[cst])
    S32 = [stack.enter_context(sb(U(f"ss_S32{g}"), [128, 512], F32)) for g in range(2)]
    Sbf = [stack.enter_context(sb(U(f"ss_Sbf{g}"), [128, 512], BF16)) for g in range(2)]
    Sk = [Tok(f"ss_S{g}") for g in range(2)]
    Sbk = [Tok(f"ss_Sb{g}") for g in range(2)]
    for g in range(2):
        P.pool(lambda e, g=g: e.memset(S32[g][:], 0.0), wr=[Sk[g]])
        P.pool(lambda e, g=g: e.memset(Sbf[g][:], 0.0), wr=[Sbk[g]])
    rawr = Ring(stack, sb, "ss_raw", [128, 12, TB + 3], F32, 2)
    zr = Ring(stack, sb, "ss_z", [128, 8, TB], F32, 3)
    accr = Ring(stack, sb, "ss_acc", [128, 12, TB], F32, 1)
    xTr = Ring(stack, sb, "ss_xT", [128, 8, TB], F32, 2)
    bcTr = Ring(stack, sb, "ss_bcT", [128, 4, TB], BF16, 3)
    dtr = Ring(stack, sb, "ss_dt", [128, TB // 128, 16], F32, 3)
    oTr = Ring(stack, sb, "ss_oT", [128, 8, TB], BF16, 3)
    xsr = Ring(stack, sb, "ss_xs", [128, 512], F32, 2)
    Btr = Ring(stack, sb, "ss_Bt", [128, 128], BF16, 4)
    smr = Ring(stack, sb, "ss_sm", [128, 48], F32, 5)
    rbr = Ring(stack, sb, "ss_rb", [128, 8, 128], F32, 2)
    segr = Ring(stack, sb, "ss_seg", [128, 8, 128], F32, 2)
    cbmr = Ring(stack, sb, "ss_cbm", [128, 128], BF16, 3)
    ebr = Ring(stack, sb, "ss_eb", [128, 8, 128], BF16, 3)
    Gr = Ring(stack, sb, "ss_G", [128, 8, 128], BF16, 4)
    x32r = Ring(stack, sb, "ss_x32", [128, 512], F32, 2)
    xbr = Ring(stack, sb, "ss_xb", [128, 512], BF16, 4)
    xer = Ring(stack, sb, "ss_xe", [128, 512], BF16, 4)
    y1r = Ring(stack, sb, "ss_y1", [128, 512], F32, 2)
    xdr = Ring(stack, sb, "ss_xd", [128, 512], F32, 4)
    gvr = Ring(stack, sb, "ss_gv", [128, 4, 128], F32, 2)
    sqr = Ring(stack, sb, "ss_sq", [128, 4, 128], F32, 2)
    rsr = Ring(stack, sb, "ss_rs", [128, 128], F32, 2)
    pbc = Ring(stack, ps, "ss_pbc", [128, 1024], F32, 1)
    pm = Ring(stack, ps, "ss_pm", [128, 512], F32, 3)
    pyd = Ring(stack, ps, "ss_pyd", [128, 512], F32, 1)
    pyo = Ring(stack, ps, "ss_pyo", [128, 512], F32, 1)
    pst = Ring(stack, ps, "ss_pst", [128, 512], F32, 1)

    def block_prologue(b):
        t0 = b * TB
        raw, rk = rawr.next()
        if b == 0:
            P.pool(lambda e, raw=raw: e.memset(raw[:, :, 0:3], 0.0), wr=[rk])
            P.dma(lambda e, raw=raw: e.dma_start(out=raw[:, :, 3:], in_=zv[:, 8:20, 0:TB]), wr=[rk])
        else:
            P.dma(lambda e, raw=raw, t0=t0: e.dma_start(out=raw[:], in_=zv[:, 8:20, t0 - 3:t0 + TB]), wr=[rk])
        z_, zk = zr.next()
        P.dma(lambda e, z_=z_, t0=t0: e.dma_start(out=z_[:], in_=zv[:, 0:8, t0:t0 + TB]), wr=[zk], q="q_pool")
        dt_, dtk = dtr.next()
        P.dma(lambda e, dt_=dt_, t0=t0: e.dma_start(
            out=dt_[:], in_=dt_in[t0:t0 + TB, :].rearrange("(j p) h -> p j h", p=128)), wr=[dtk], q="q_pool")
        P.act(lambda e, z_=z_: e.activation(out=z_[:], in_=z_[:], func=AF.Silu), rd=[zk], wr=[zk])
        acc, acck = accr.next()
        acks = [Tok(f"acc{k}") for k in range(12)]
        for w in range(4):
            for k in range(12):
                if w == 0:
                    P.dve(lambda e, k=k, w=w, raw=raw, acc=acc: e.tensor_scalar(
                        out=acc[:, k, :], in0=raw[:, k, w:w + TB], scalar1=cw[:, k, w:w + 1], scalar2=None,
                        op0=ALU.mult), rd=[rk, cst], wr=[acks[k]])
                else:
                    P.dve(lambda e, k=k, w=w, raw=raw, acc=acc: e.scalar_tensor_tensor(
                        out=acc[:, k, :], in0=raw[:, k, w:w + TB], scalar=cw[:, k, w:w + 1], in1=acc[:, k, :],
                        op0=ALU.mult, op1=ALU.add), rd=[rk, cst, acks[k]], wr=[acks[k]])
        xT, xTk = xTr.next()
        bcT, bcTk = bcTr.next()
        for k in range(12):
            if k < 8:
                P.act(lambda e, k=k, xT=xT, acc=acc: e.activation(out=xT[:, k, :], in_=acc[:, k, :], func=AF.Silu,
                                                                  bias=cb[:, k:k + 1], scale=1.0),
                      rd=[acks[k], cst], wr=[xTk])
            else:
                P.act(lambda e, k=k, bcT=bcT, acc=acc: e.activation(out=bcT[:, k - 8, :], in_=acc[:, k, :],
                                                                    func=AF.Silu, bias=cb[:, k:k + 1], scale=1.0),
                      rd=[acks[k], cst], wr=[bcTk])
        oT, oTk = oTr.next()
        return dict(t0=t0, z_=z_, zk=zk, dt_=dt_, dtk=dtk, xT=xT, xTk=xTk, bcT=bcT, bcTk=bcTk, oT=oT, oTk=oTk)

    def front(B_, j, g):
        z_, zk, dt_, dtk, xT, xTk, bcT, bcTk = (B_[k_] for k_ in ('z_', 'zk', 'dt_', 'dtk', 'xT', 'xTk', 'bcT', 'bcTk'))
        cs = slice(j * 128, (j + 1) * 128)
        px, pxk = pm.next()
        for f in range(4):
            P.pe(lambda e, px=px, f=f, g=g, xT=xT, cs=cs: e.transpose(
                out=px[:, f * 128:(f + 1) * 128], in_=xT[:, g * 4 + f, cs], identity=idf[:]),
                rd=[xTk], wr=[pxk])
        xs, xsk = xsr.next()
        P.act(lambda e, xs=xs, px=px: e.copy(out=xs[:], in_=px[:]), rd=[pxk], wr=[xsk])
        pb, pbk = pm.next()
        pbb = pb[:].bitcast(BF16)
        P.pe(lambda e, pbb=pbb, bcT=bcT, g=g, cs=cs: e.transpose(out=pbb[:, 0:128], in_=bcT[:, g, cs],
                                                                 identity=idb[:]), rd=[bcTk], wr=[pbk])
        Bt, Btk = Btr.next()
        P.dve(lambda e, Bt=Bt, pbb=pbb: e.tensor_copy(out=Bt[:], in_=pbb[:, 0:128]), rd=[pbk], wr=[Btk])
        sm, smk = smr.next()
        kdA, kacol, keacol, keal, kdte = (Tok(n_) for n_ in ('dA', 'acol', 'eacol', 'eal', 'dte'))
        dtg = dt_[:, j, g * 8:(g + 1) * 8]
        P.dve(lambda e, sm=sm, dtg=dtg, g=g: e.tensor_tensor(out=sm[:, 0:8], in0=dtg,
                                                             in1=abc[:, g * 8:(g + 1) * 8], op=ALU.mult),
              rd=[dtk, cst], wr=[smk, kdA])
        pa, pak = pm.next()
        P.pe(lambda e, pa=pa, sm=sm: e.matmul(pa[:, 0:8], lhsT=tri[:], rhs=sm[:, 0:8], start=True, stop=True),
             rd=[kdA, cst], wr=[pak])
        P.act(lambda e, sm=sm, pa=pa: e.copy(out=sm[:, 8:16], in_=pa[:, 0:8]), rd=[pak, smk], wr=[kacol])
        P.act(lambda e, sm=sm, pa=pa: e.activation(out=sm[:, 16:24], in_=pa[:, 0:8], func=AF.Exp),
              rd=[pak, smk], wr=[keacol])
        rb, rbk = rbr.next()
        P.dve(lambda e, rb=rb, sm=sm: e.tensor_tensor(
            out=rb[:], in0=tri[:].unsqueeze(1).to_broadcast([128, 8, 128]),
            in1=sm[:, 0:8].unsqueeze(2).to_broadcast([128, 8, 128]), op=ALU.mult),
            rd=[kdA, cst], wr=[rbk])
        bc, bck = pbc.next()
        for hh in range(2):
            P.pe(lambda e, bc=bc, rb=rb, hh=hh: e.matmul(
                bc[:, hh * 512:(hh + 1) * 512], lhsT=ones_f[:],
                rhs=rb[:, hh * 4:(hh + 1) * 4, :].rearrange("p h l -> p (h l)"), start=True, stop=True),
                rd=[rbk, ck], wr=[bck])
        bc3 = bc[:].rearrange("p (h l) -> p h l", l=128)
        seg, segk = segr.next()
        for h in range(8):
            P.dve(lambda e, seg=seg, bc3=bc3, sm=sm, h=h: e.tensor_scalar(
                out=seg[:, h, :], in0=bc3[:, h, :], scalar1=sm[:, 8 + h:9 + h], scalar2=0.0,
                op0=ALU.subtract, op1=ALU.min), rd=[bck, kacol], wr=[segk])
        eb, ebk = ebr.next()
        P.act(lambda e, seg=seg, eb=eb: e.activation(out=eb[:], in_=seg[:], func=AF.Exp), rd=[segk], wr=[ebk])
        P.act(lambda e, sm=sm, bc3=bc3: e.activation(out=sm[:, 24:32], in_=bc3[:, :, 127], func=AF.Exp),
              rd=[bck, smk], wr=[keal])
        P.dve(lambda e, sm=sm, bc3=bc3: e.tensor_tensor(out=sm[:, 32:40], in0=bc3[:, :, 127],
                                                        in1=sm[:, 8:16], op=ALU.subtract),
              rd=[bck, kacol, smk], wr=[kdte])
        P.act(lambda e, sm=sm: e.activation(out=sm[:, 32:40], in_=sm[:, 32:40], func=AF.Exp),
              rd=[kdte], wr=[kdte])
        pc, pck = pm.next()
        P.pe(lambda e, pc=pc, bcT=bcT, g=g, cs=cs: e.matmul(
            pc[:, 0:128], lhsT=bcT[:, g, cs], rhs=bcT[:, 2 + g, cs], start=True, stop=True),
            rd=[bcTk], wr=[pck])
        cbm, cbmk = cbmr.next()
        P.dve(lambda e, cbm=cbm, pc=pc: e.tensor_tensor(out=cbm[:], in0=pc[:, 0:128], in1=tri[:], op=ALU.mult),
              rd=[pck, cst], wr=[cbmk])
        G, Gk = Gr.next()
        P.dve(lambda e, G=G, eb=eb, cbm=cbm: e.tensor_tensor(
            out=G[:], in0=eb[:], in1=cbm[:].unsqueeze(1).to_broadcast([128, 8, 128]), op=ALU.mult),
            rd=[ebk, cbmk], wr=[Gk])
        x32, x32k = x32r.next()
        xs3 = lambda t: t[:].rearrange("p (h d) -> p h d", d=64)
        P.pool(lambda e, x32=x32, xs=xs, dtg=dtg: e.tensor_tensor(
            out=xs3(x32), in0=xs3(xs), in1=dtg.unsqueeze(2).to_broadcast([128, 8, 64]), op=ALU.mult),
            rd=[xsk, dtk], wr=[x32k])
        xb, xbk = xbr.next()
        P.act(lambda e, xb=xb, x32=x32: e.copy(out=xb[:], in_=x32[:]), rd=[x32k], wr=[xbk])
        xe, xek = xer.next()
        P.dve(lambda e, xe=xe, x32=x32, sm=sm: e.tensor_tensor(
            out=xs3(xe), in0=xs3(x32), in1=sm[:, 32:40].unsqueeze(2).to_broadcast([128, 8, 64]),
            op=ALU.mult), rd=[x32k, kdte], wr=[xek])
        xd, xdk = xdr.next()
        P.pool(lambda e, xd=xd, xs=xs, g=g: e.tensor_tensor(
            out=xs3(xd), in0=xs3(xs), in1=Dbc[:, g * 8:(g + 1) * 8, :], op=ALU.mult),
            rd=[xsk, cst], wr=[xdk])
        return dict(B_=B_, j=j, g=g, cs=cs, sm=sm, smk=smk, keacol=keacol, keal=keal, Bt=Bt, Btk=Btk, G=G, Gk=Gk, xb=xb, xbk=xbk, xe=xe, xek=xek,
                    xd=xd, xdk=xdk, xs3=xs3)

    def back(F_):
        keacol, keal = F_['keacol'], F_['keal']
        B_, j, g, cs, sm, smk, Bt, Btk, G, Gk, xb, xbk, xe, xek, xd, xdk, xs3 = (F_[k_] for k_ in (
            'B_', 'j', 'g', 'cs', 'sm', 'smk', 'Bt', 'Btk', 'G', 'Gk', 'xb', 'xbk', 'xe', 'xek', 'xd', 'xdk', 'xs3'))
        z_, zk, bcT, bcTk, oT, oTk = (B_[k_] for k_ in ('z_', 'zk', 'bcT', 'bcTk', 'oT', 'oTk'))
        yd, ydk = pyd.next()
        for h in range(8):
            P.pe(lambda e, yd=yd, G=G, xb=xb, h=h: e.matmul(
                yd[:, h * 64:(h + 1) * 64], lhsT=G[:, h, :], rhs=xb[:, h * 64:(h + 1) * 64],
                start=True, stop=True), rd=[Gk, xbk], wr=[ydk])
        yo, yok = pyo.next()
        P.pe(lambda e, yo=yo, bcT=bcT, g=g, cs=cs: e.matmul(
            yo[:], lhsT=bcT[:, 2 + g, cs], rhs=Sbf[g][:], start=True, stop=True),
            rd=[bcTk, Sbk[g]], wr=[yok])
        y1, y1k = y1r.next()
        P.dve(lambda e, y1=y1, yo=yo, sm=sm: e.tensor_tensor(
            out=xs3(y1), in0=yo[:].rearrange("p (h d) -> p h d", d=64),
            in1=sm[:, 16:24].unsqueeze(2).to_broadcast([128, 8, 64]), op=ALU.mult),
            rd=[yok, keacol], wr=[y1k])
        P.dve(lambda e, y1=y1, yd=yd: e.tensor_tensor(out=y1[:], in0=y1[:], in1=yd[:], op=ALU.add),
              rd=[y1k, ydk], wr=[y1k])
        P.dve(lambda e, y1=y1, xd=xd: e.tensor_tensor(out=y1[:], in0=y1[:], in1=xd[:], op=ALU.add),
              rd=[y1k, xdk], wr=[y1k])
        st_, stk = pst.next()
        P.pe(lambda e, st_=st_, Bt=Bt, xe=xe: e.matmul(st_[:], lhsT=Bt[:], rhs=xe[:], start=True, stop=True),
             rd=[Btk, xek], wr=[stk])
        P.dve(lambda e, g=g, sm=sm: e.tensor_tensor(
            out=xs3(S32[g]), in0=xs3(S32[g]), in1=sm[:, 24:32].unsqueeze(2).to_broadcast([128, 8, 64]),
            op=ALU.mult), rd=[Sk[g], keal], wr=[Sk[g]])
        P.dve(lambda e, g=g, st_=st_: e.tensor_tensor(out=S32[g][:], in0=S32[g][:], in1=st_[:], op=ALU.add),
              rd=[Sk[g], stk], wr=[Sk[g]])
        P.act(lambda e, g=g: e.copy(out=Sbf[g][:], in_=S32[g][:]), rd=[Sk[g]], wr=[Sbk[g]])
        py, pyk = pm.next()
        for f in range(4):
            P.pe(lambda e, py=py, y1=y1, f=f: e.transpose(
                out=py[:, f * 128:(f + 1) * 128], in_=y1[:, f * 128:(f + 1) * 128], identity=idf[:]),
                rd=[y1k], wr=[pyk])
        gv, gvk = gvr.next()
        P.dve(lambda e, gv=gv, py=py, z_=z_, g=g, cs=cs: e.tensor_tensor(
            out=gv[:], in0=py[:].rearrange("p (f t) -> p f t", t=128), in1=z_[:, g * 4:(g + 1) * 4, cs],
            op=ALU.mult), rd=[pyk, zk], wr=[gvk])
        sq, sqk = sqr.next()
        P.act(lambda e, sq=sq, gv=gv: e.activation(out=sq[:], in_=gv[:], func=AF.Square), rd=[gvk], wr=[sqk])
        pq, pqk = pm.next()
        for f in range(4):
            P.pe(lambda e, pq=pq, sq=sq, f=f: e.matmul(pq[:, 0:128], lhsT=ones_f[:], rhs=sq[:, f, :],
                                                       start=(f == 0), stop=(f == 3)),
                 rd=[sqk, ck], wr=[pqk])
        rs, rsk = rsr.next()
        P.act(lambda e, rs=rs, pq=pq: e.activation(out=rs[:], in_=pq[:, 0:128], func=AF.Ln, bias=EPS,
                                                   scale=1.0 / 512), rd=[pqk], wr=[rsk])
        P.act(lambda e, rs=rs: e.activation(out=rs[:], in_=rs[:], func=AF.Exp, scale=-0.5), rd=[rsk], wr=[rsk])
        for f in range(4):
            P.dve(lambda e, oT=oT, gv=gv, rs=rs, f=f, g=g, cs=cs: e.scalar_tensor_tensor(
                out=oT[:, g * 4 + f, cs], in0=gv[:, f, :], scalar=nw[:, g * 4 + f:g * 4 + f + 1], in1=rs[:],
                op0=ALU.mult, op1=ALU.mult), rd=[gvk, rsk, cst], wr=[oTk])

    def block_epilogue(B_):
        oT, oTk, t0 = B_['oT'], B_['oTk'], B_['t0']
        P.dma(lambda e, oT=oT, t0=t0: e.dma_start(
            out=yT_out.rearrange("(k p) t -> p k t", p=128)[:, :, t0:t0 + TB], in_=oT[:]), rd=[oTk])

    passes = [(b, j, g) for b in range(NB) for j in range(TB // 128) for g in range(2)]
    blocks = {}
    pend = []
    DEPTH_F = 1

    def retire():
        F0 = pend.pop(0)
        back(F0)
        pb, pj, pg = F0["key"]
        if (pj, pg) == (TB // 128 - 1, 1):
            block_epilogue(blocks.pop(pb))

    for (b, j, g) in passes:
        if b not in blocks:
            blocks[b] = block_prologue(b)
        F_ = front(blocks[b], j, g)
        F_["key"] = (b, j, g)
        pend.append(F_)
        if len(pend) > DEPTH_F:
            retire()
    while pend:
        retire()


NCORES = 8
BATCH = 2
TC = BATCH * SEQ // NCORES
DEPTH = 4


def lay_colchunk(W):
    K, N = W.shape
    return np.ascontiguousarray(
        W.reshape(K // 128, 128, N // 128, 128).transpose(2, 1, 0, 3).reshape(N // 128, 128, K))


def lay_slab(W):
    K, N = W.shape
    return np.ascontiguousarray(W.reshape(K // 128, 128, N // 512, 512).transpose(2, 1, 0, 3))


def ssd_core_inputs(inp, j, gl):
    w_in = inp["ssd_w_in"][j]
    cols = np.concatenate([
        np.arange(gl * 1024, (gl + 1) * 1024),
        4096 + np.arange(gl * 1024, (gl + 1) * 1024),
        8192 + np.arange(gl * 256, (gl + 1) * 256),
        9216 + np.arange(gl * 256, (gl + 1) * 256)])
    ch = cols[1024:] - 4096
    dtc = 10240 + np.arange(gl * 16, (gl + 1) * 16)
    return {
        "s_w": lay_colchunk(w_in[:, cols]),
        "s_wdt": np.ascontiguousarray(w_in[:, dtc].reshape(KC_D, 128, 16).transpose(1, 0, 2)),
        "s_dtb": np.ascontiguousarray(inp["ssd_dt_bias"][j, gl * 16:(gl + 1) * 16]),
        "s_cw": np.ascontiguousarray(inp["ssd_conv_w"][j][:, ch].reshape(4, 12, 128).transpose(2, 1, 0)),
        "s_cb": np.ascontiguousarray(inp["ssd_conv_b"][j][ch].reshape(12, 128).T),
        "s_alog": np.ascontiguousarray(inp["ssd_a_log"][j, gl * 16:(gl + 1) * 16]),
        "s_dsk": np.ascontiguousarray(inp["ssd_d"][j, gl * 16:(gl + 1) * 16]),
        "s_nw": np.ascontiguousarray(inp["ssd_norm"][j, gl * 1024:(gl + 1) * 1024].reshape(8, 128).T),
    }


def attn_core_inputs(inp, j, gl):
    w = inp["da_w_qkv"][j]
    cols = np.concatenate([which * D + np.arange(gl * 512, (gl + 1) * 512) for which in range(3)])
    return {
        "a_w": lay_colchunk(w[:, cols]),
        "a_lam": np.ascontiguousarray(np.stack([inp["da_lambda_q1"][j], inp["da_lambda_k1"][j],
                                                inp["da_lambda_q2"][j], inp["da_lambda_k2"][j]])),
        "a_sub": np.ascontiguousarray(inp["da_subln"][j]),
    }


def lambda_init(i):
    return 0.8 - 0.6 * math.exp(-0.3 * i)


def _new_nc():
    _UID[0] = 0
    return bass.Bass("TRN2", target_bir_lowering=False)


def build_first():
    import contextlib
    nc = _new_nc()
    x = nc.dram_tensor("x", [TC, D], F32, kind="ExternalInput").ap()
    wpre = nc.dram_tensor("wpre", [D], F32, kind="ExternalInput").ap()
    xo = nc.dram_tensor("xo", [TC, D], F32, kind="ExternalOutput").ap()
    hT = nc.dram_tensor("hT", [D, TC], BF16, kind="ExternalOutput").ap()
    with contextlib.ExitStack() as st0:
        P = Prog(nc, st0)
        with contextlib.ExitStack() as st:
            idf, idb, _ = make_ident(nc, P, st)
            phase_norm(nc, P, st, TC, x, None, None, wpre, xo, hT, idb)
            P.emit()
    return nc


def hT_blk_fn(hTg):
    v = hTg.rearrange("(r k p) t -> p r k t", r=4, p=128)
    return lambda b: v[:, b // 4, :, (b % 4) * 512:(b % 4 + 1) * 512]


def hT_blk_fn_bm(hTg):
    v = hTg.rearrange("(b p) (k t) -> p b k t", p=128, t=512)
    return lambda b: v[:, b, :, :]


def build_ssd():
    import contextlib
    nc = _new_nc()
    hTg = nc.dram_tensor("hTg", [4 * D, TC], BF16, kind="ExternalInput").ap()
    w = nc.dram_tensor("s_w", [NZX, 128, D], F32, kind="ExternalInput").ap()
    wdt = nc.dram_tensor("s_wdt", [128, KC_D, 16], F32, kind="ExternalInput").ap()
    dtb = nc.dram_tensor("s_dtb", [16], F32, kind="ExternalInput").ap()
    cw = nc.dram_tensor("s_cw", [128, 12, 4], F32, kind="ExternalInput").ap()
    cb = nc.dram_tensor("s_cb", [128, 12], F32, kind="ExternalInput").ap()
    alog = nc.dram_tensor("s_alog", [16], F32, kind="ExternalInput").ap()
    dsk = nc.dram_tensor("s_dsk", [16], F32, kind="ExternalInput").ap()
    nw = nc.dram_tensor("s_nw", [128, 8], F32, kind="ExternalInput").ap()
    zx = nc.dram_tensor("zx", [NZX * 128, SEQ], F32).ap()
    dt = nc.dram_tensor("dt", [SEQ, 16], F32).ap()
    yT = nc.dram_tensor("yT", [1024, SEQ], BF16, kind="ExternalOutput").ap()
    with contextlib.ExitStack() as st0:
        P = Prog(nc, st0)
        with contextlib.ExitStack() as st:
            phase_ssd_in(nc, P, st, hT_blk_fn(hTg), w, wdt, dtb, zx, dt)
            P.emit()
        with contextlib.ExitStack() as st:
            idf, idb, _ = make_ident(nc, P, st)
            C = make_consts(nc, P, st)
            phase_ssd_scan(nc, P, st, zx, dt, cw, cb, alog, dsk, nw, yT, idf, idb, C)
            P.emit()
    return nc


def build_attn(lam_init):
    import contextlib
    nc = _new_nc()
    hTg = nc.dram_tensor("hTg", [4 * D, TC], BF16, kind="ExternalInput").ap()
    w = nc.dram_tensor("a_w", [12, 128, D], F32, kind="ExternalInput").ap()
    lam = nc.dram_tensor("a_lam", [4, 64], F32, kind="ExternalInput").ap()
    sub = nc.dram_tensor("a_sub", [128], F32, kind="ExternalInput").ap()
    oT = nc.dram_tensor("yT", [512, SEQ], BF16, kind="ExternalOutput").ap()
    with contextlib.ExitStack() as st0:
        P = Prog(nc, st0)
        with contextlib.ExitStack() as st:
            idf, idb, _ = make_ident(nc, P, st)
            C = make_consts(nc, P, st)
            qTd = nc.dram_tensor("sc_qT", [512, SEQ], BF16).ap()
            kTd = nc.dram_tensor("sc_kT", [512, SEQ], BF16).ap()
            Vd = nc.dram_tensor("sc_V", [4, 128, SEQ // 128, 128], BF16).ap()
            negMd = nc.dram_tensor("sc_negM", [128, 8], F32).ap()
            phase_attn_proj(nc, P, st, hT_blk_fn(hTg), w, qTd, kTd, Vd, negMd, idf, idb, C)
            P.emit()
        with contextlib.ExitStack() as st:
            idf, idb, _ = make_ident(nc, P, st)
            C = make_consts(nc, P, st)
            phase_attn_core(nc, P, st, qTd, kTd, Vd, negMd, lam, sub, lam_init, oT, idf, idb, C)
            P.emit()
    return nc


def emit_token_phases(nc, P, K, AT, wo, x, npost, nfpre, nfpost, npre_next, wg, wu, wd, xo, hTn, scr):
    import contextlib
    with contextlib.ExitStack() as st:
        phase_mmT(nc, P, st, TC, K, AT, wo, scr["m"], pfx="mo")
        P.emit()
    with contextlib.ExitStack() as st:
        idf, idb, _ = make_ident(nc, P, st)
        phase_norm(nc, P, st, TC, x, scr["m"], npost, nfpre, scr["x1"], scr["h2T"], idb)
        P.emit()
    with contextlib.ExitStack() as st:
        phase_ffn_gu(nc, P, st, TC, scr["h2T"], wg, wu, scr["aT"])
        P.emit()
    with contextlib.ExitStack() as st:
        phase_mmT(nc, P, st, TC, D_FF, scr["aT"], wd, scr["y2"], pfx="md")
        P.emit()
    with contextlib.ExitStack() as st:
        idf, idb, _ = make_ident(nc, P, st)
        phase_norm(nc, P, st, TC, scr["x1"], scr["y2"], nfpost, npre_next, xo, hTn, idb)
        P.emit()


def build_tok(K, last):
    import contextlib
    nc = _new_nc()
    AT = nc.dram_tensor("AT", [K, TC], BF16, kind="ExternalInput").ap()
    wo = nc.dram_tensor("wo", [4, 128, K // 128, 512], F32, kind="ExternalInput").ap()
    x = nc.dram_tensor("x", [TC, D], F32, kind="ExternalInput").ap()
    npost = nc.dram_tensor("npost", [D], F32, kind="ExternalInput").ap()
    nfpre = nc.dram_tensor("nfpre", [D], F32, kind="ExternalInput").ap()
    nfpost = nc.dram_tensor("nfpost", [D], F32, kind="ExternalInput").ap()
    npre_next = None if last else nc.dram_tensor("npre_next", [D], F32, kind="ExternalInput").ap()
    wg = nc.dram_tensor("wg", [KC_F, 128, D], F32, kind="ExternalInput").ap()
    wu = nc.dram_tensor("wu", [KC_F, 128, D], F32, kind="ExternalInput").ap()
    wd = nc.dram_tensor("wd", [4, 128, KC_F, 512], F32, kind="ExternalInput").ap()
    xo = nc.dram_tensor("xo", [TC, D], F32, kind="ExternalOutput").ap()
    hTn = None if last else nc.dram_tensor("hT", [D, TC], BF16, kind="ExternalOutput").ap()
    scr = {"m": nc.dram_tensor("sc_m", [TC, D], F32).ap(), "x1": nc.dram_tensor("sc_x1", [TC, D], F32).ap(),
           "h2T": nc.dram_tensor("sc_h2T", [D, TC], BF16).ap(), "aT": nc.dram_tensor("sc_aT", [D_FF, TC], BF16).ap(),
           "y2": nc.dram_tensor("sc_y2", [TC, D], F32).ap()}
    with contextlib.ExitStack() as st0:
        P = Prog(nc, st0)
        emit_token_phases(nc, P, K, AT, wo, x, npost, nfpre, nfpost, npre_next, wg, wu, wd, xo, hTn, scr)
    return nc


def kernel_multilaunch(**inp):
    inp = {k: np.asarray(v) for k, v in inp.items()}
    cores = list(range(NCORES))
    xs = np.ascontiguousarray(inp["x"].reshape(NCORES, TC, D))
    res = run_bass_kernel_spmd(build_first(), [{"x": xs[c], "wpre": inp["norm_mix_pre"][0]} for c in cores],
                               core_ids=cores)
    xcur = [r["xo"] for r in res.results]
    hT = [np.asarray(r["hT"]) for r in res.results]
    for i in range(DEPTH):
        j = i // 2
        hTg = [np.concatenate(hT[4 * b:4 * b + 4], axis=0) for b in range(BATCH)]
        if i % 2 == 0:
            ncm = build_ssd()
            maps = [dict(ssd_core_inputs(inp, j, c % 4), hTg=hTg[c // 4]) for c in cores]
            K = 4096
            wo = lay_slab(inp["ssd_w_out"][j])
        else:
            ncm = build_attn(lambda_init(i))
            maps = [dict(attn_core_inputs(inp, j, c % 4), hTg=hTg[c // 4]) for c in cores]
            K = 2048
            wo = lay_slab(inp["da_w_out"][j])
        res = run_bass_kernel_spmd(ncm, maps, core_ids=cores)
        yT = [np.asarray(r["yT"]) for r in res.results]
        yall = [np.concatenate(yT[4 * b:4 * b + 4], axis=0) for b in range(BATCH)]
        last = i == DEPTH - 1
        wg, wu, wd = lay_colchunk(inp["ffn_w_gate"][i]), lay_colchunk(inp["ffn_w_up"][i]), lay_slab(inp["ffn_w_down"][i])
        maps = []
        for c in cores:
            m = {"AT": np.ascontiguousarray(yall[c // 4][:, (c % 4) * TC:(c % 4 + 1) * TC]), "wo": wo, "x": xcur[c],
                 "npost": inp["norm_mix_post"][i], "nfpre": inp["norm_ffn_pre"][i], "nfpost": inp["norm_ffn_post"][i],
                 "wg": wg, "wu": wu, "wd": wd}
            if not last:
                m["npre_next"] = inp["norm_mix_pre"][i + 1]
            maps.append(m)
        res = run_bass_kernel_spmd(build_tok(K, last), maps, core_ids=cores)
        xcur = [r["xo"] for r in res.results]
        if not last:
            hT = [np.asarray(r["hT"]) for r in res.results]
    out = np.stack([np.asarray(a) for a in xcur]).reshape(BATCH, SEQ, D).astype(np.float32)
    return out


RG4 = [[0, 1, 2, 3], [4, 5, 6, 7]]
RG8 = [list(range(NCORES))]


def phase_select(nc, P, stack, g8, bsel_in, hTg):
    sb = nc.sbuf_tensor
    w = stack.enter_context(sb(U("sel_w"), [128, 2], F32))
    wk = Tok("sel_w")
    P.dma(lambda e: e.dma_start(out=w[:], in_=bsel_in.partition_broadcast(128)), wr=[wk])
    ar = Ring(stack, sb, "sel_a", [128, KC_D, 512], BF16, 2)
    br = Ring(stack, sb, "sel_b", [128, KC_D, 512], BF16, 2)
    orr = Ring(stack, sb, "sel_o", [128, KC_D, 512], BF16, 2)
    v8 = g8.rearrange("(r k p) t -> p r k t", r=8, p=128)
    vo = hTg.rearrange("(r k p) t -> p r k t", r=4, p=128)
    for r in range(4):
        for tb in range(TC // 512):
            ts = slice(tb * 512, (tb + 1) * 512)
            a, ak = ar.next()
            b, bk = br.next()
            P.dma(lambda e, a=a, r=r, ts=ts: e.dma_start(out=a[:], in_=v8[:, r, :, ts]), wr=[ak])
            P.dma(lambda e, b=b, r=r, ts=ts: e.dma_start(out=b[:], in_=v8[:, 4 + r, :, ts]), wr=[bk], q="q_pool")
            o, ok = orr.next()
            P.dve(lambda e, o=o, a=a: e.tensor_scalar(out=o[:], in0=a[:], scalar1=w[:, 0:1], scalar2=None,
                                                      op0=ALU.mult), rd=[ak, wk], wr=[ok])
            P.dve(lambda e, o=o, b=b: e.scalar_tensor_tensor(out=o[:], in0=b[:], scalar=w[:, 1:2], in1=o[:],
                                                             op0=ALU.mult, op1=ALU.add), rd=[bk, wk, ok], wr=[ok])
            P.dma(lambda e, o=o, r=r, ts=ts: e.dma_start(out=vo[:, r, :, ts], in_=o[:]), rd=[ok])


def build_fused():
    import contextlib
    nc = _new_nc()
    dt_ = nc.dram_tensor
    ext = lambda n, s, d=F32: dt_(n, s, d, kind="ExternalInput").ap()
    x = ext("x", [TC, D])
    sel4 = ext("sel4", [4])
    nmpre, nmpost = ext("nmpre", [DEPTH, D]), ext("nmpost", [DEPTH, D])
    nfpre, nfpost = ext("nfpre", [DEPTH, D]), ext("nfpost", [DEPTH, D])
    ffn = [(ext(f"wg{i}", [KC_F, 128, D]), ext(f"wu{i}", [KC_F, 128, D]), ext(f"wd{i}", [4, 128, KC_F, 512]))
           for i in range(DEPTH)]
    ssd = [dict(w=ext(f"s_w{j}", [NZX, 128, D]), wdt=ext(f"s_wdt{j}", [128, KC_D, 16]), dtb=ext(f"s_dtb{j}", [16]),
                cw=ext(f"s_cw{j}", [128, 12, 4]), cb=ext(f"s_cb{j}", [128, 12]), alog=ext(f"s_alog{j}", [16]),
                dsk=ext(f"s_dsk{j}", [16]), nw=ext(f"s_nw{j}", [128, 8]), wo=ext(f"s_wo{j}", [4, 128, 8, 512]))
           for j in range(2)]
    att = [dict(w=ext(f"a_w{j}", [12, 128, D]), lam=ext(f"a_lam{j}", [4, 64]), sub=ext(f"a_sub{j}", [128]),
                wo=ext(f"a_wo{j}", [4, 128, 4, 512])) for j in range(2)]
    xo = dt_("xo", [TC, D], F32, kind="ExternalOutput").ap()
    scr = lambda n, s, d=F32: dt_(n, s, d).ap()
    hT4 = scr("sc_hT4", [16 * 128, KC_D * 512], BF16)
    hTg = scr("sc_hTg", [16 * 128, KC_D * 512], BF16)
    zx = scr("sc_zx", [NZX * 128, SEQ])
    dtt = scr("sc_dt", [SEQ, 16])
    yT = scr("sc_yT", [1024, SEQ], BF16)
    mpA, mpB = scr("sc_mpA", [SEQ, 1024]), scr("sc_mpB", [SEQ, 1024])
    mA, mB = scr("sc_mA", [TC, 1024]), scr("sc_mB", [TC, 1024])
    qTd = scr("sc_qT", [512, SEQ], BF16)
    kTd = scr("sc_kT", [512, SEQ], BF16)
    Vd = scr("sc_V", [4, 128, SEQ // 128, 128], BF16)
    negMd = scr("sc_negM", [128, 8])
    xa = scr("sc_xa", [TC, D])
    x1 = scr("sc_x1", [TC, D])
    h2T = scr("sc_h2T", [D, TC], BF16)
    aT = scr("sc_aT", [D_FF, TC], BF16)
    y2 = scr("sc_y2", [TC, D])
    with contextlib.ExitStack() as st0:
        P = Prog(nc, st0)
        with contextlib.ExitStack() as st:
            idf, idb, _ = make_ident(nc, P, st)
            phase_norm(nc, P, st, TC, x, None, None, nmpre[0], None, hT4, idb, sel4=sel4)
            P.emit(reorder=True)
        xcur = x
        for i in range(DEPTH):
            j = i // 2
            last = i == DEPTH - 1
            ptoks = [Tok(f"hTg{pc}") for pc in range(8)]
            for pc in range(8):
                rs_ = slice(pc * 256, (pc + 1) * 256)
                P.cc(lambda e, rs_=rs_: e.collective_compute("AllReduce", ALU.add, replica_groups=RG4,
                                                             ins=[hT4[rs_, :]], outs=[hTg[rs_, :]]),
                     wr=[ptoks[pc]])
            hT_tok = lambda b, ptoks=ptoks: ptoks[b // 2]
            if i % 2 == 0:
                s = ssd[j]
                with contextlib.ExitStack() as st:
                    phase_ssd_in(nc, P, st, hT_blk_fn_bm(hTg), s["w"], s["wdt"], s["dtb"], zx, dtt, hT_tok=hT_tok)
                    P.emit(reorder=True)
                with contextlib.ExitStack() as st:
                    idf, idb, _ = make_ident(nc, P, st)
                    C = make_consts(nc, P, st)
                    phase_ssd_scan(nc, P, st, zx, dtt, s["cw"], s["cb"], s["alog"], s["dsk"], s["nw"], yT,
                                   idf, idb, C)
                    P.emit(reorder=True)
                K, yT_use, wo = 1024, yT, s["wo"]
            else:
                a = att[j]
                with contextlib.ExitStack() as st:
                    idf, idb, _ = make_ident(nc, P, st)
                    C = make_consts(nc, P, st)
                    phase_attn_proj(nc, P, st, hT_blk_fn_bm(hTg), a["w"], qTd, kTd, Vd, negMd, idf, idb, C,
                                    hT_tok=hT_tok)
                    P.emit(reorder=True)
                with contextlib.ExitStack() as st:
                    idf, idb, _ = make_ident(nc, P, st)
                    C = make_consts(nc, P, st)
                    phase_attn_core(nc, P, st, qTd, kTd, Vd, negMd, a["lam"], a["sub"], lambda_init(i),
                                    yT[0:512, :], idf, idb, C)
                    P.emit(reorder=True)
                K, yT_use, wo = 512, yT[0:512, :], a["wo"]
            with contextlib.ExitStack() as st:
                htoks = ([], [])
                def rs_half(hf, htoks=htoks):
                    src, dst = (mpA, mA) if hf == 0 else (mpB, mB)
                    P.cc(lambda e: e.collective_compute("ReduceScatter", ALU.add, replica_groups=RG4,
                                                        ins=[src], outs=[dst], dma_qos="P2"), rd=list(htoks[hf]))
                phase_mmT(nc, P, st, SEQ, K, yT_use, wo, None, Th=2048, pfx="mo", halves=(mpA, mpB), htoks=htoks,
                          half_done=rs_half)
                P.emit(reorder=True)
            with contextlib.ExitStack() as st:
                idf, idb, _ = make_ident(nc, P, st)
                phase_norm(nc, P, st, TC, xcur, (mA, mB), nmpost[i], nfpre[i], x1, h2T, idb)
                P.emit(reorder=True)
            wg, wu, wd = ffn[i]
            with contextlib.ExitStack() as st:
                phase_ffn_gu(nc, P, st, TC, h2T, wg, wu, aT)
                P.emit(reorder=True)
            with contextlib.ExitStack() as st:
                phase_mmT(nc, P, st, TC, D_FF, aT, wd, y2, pfx="md")
                P.emit(reorder=True)
            with contextlib.ExitStack() as st:
                idf, idb, _ = make_ident(nc, P, st)
                phase_norm(nc, P, st, TC, x1, y2, nfpost[i], None if last else nmpre[i + 1],
                           xo if last else xa, None if last else hT4, idb, sel4=None if last else sel4)
                P.emit(reorder=True)
            xcur = xa
    return nc


def fused_inputs(inp):
    inp = {k: np.asarray(v) for k, v in inp.items()}
    xs = np.ascontiguousarray(inp["x"].reshape(NCORES, TC, D))
    shared = {"nmpre": inp["norm_mix_pre"], "nmpost": inp["norm_mix_post"],
              "nfpre": inp["norm_ffn_pre"], "nfpost": inp["norm_ffn_post"]}
    for i in range(DEPTH):
        shared[f"wg{i}"] = lay_colchunk(inp["ffn_w_gate"][i])
        shared[f"wu{i}"] = lay_colchunk(inp["ffn_w_up"][i])
        shared[f"wd{i}"] = lay_slab(inp["ffn_w_down"][i])
    maps = []
    for c in range(NCORES):
        gl = c % 4
        m = dict(shared)
        m["x"] = xs[c]
        m["sel4"] = np.eye(4, dtype=np.float32)[c % 4]
        for j in range(2):
            for k, v in ssd_core_inputs(inp, j, gl).items():
                m[f"{k}{j}"] = v
            m[f"s_wo{j}"] = lay_slab(inp["ssd_w_out"][j][gl * 1024:(gl + 1) * 1024])
            for k, v in attn_core_inputs(inp, j, gl).items():
                m[f"{k}{j}"] = v
            m[f"a_wo{j}"] = lay_slab(inp["da_w_out"][j][gl * 512:(gl + 1) * 512])
        maps.append(m)
    return maps


def kernel_fused(**inp):
    maps = fused_inputs(inp)
    res = run_bass_kernel_spmd(build_fused(), maps, core_ids=list(range(NCORES)))
    return np.stack([np.asarray(r["xo"]) for r in res.results]).reshape(BATCH, SEQ, D).astype(np.float32)


def kernel(**inputs):
    return kernel_fused(**inputs)
```

```python
import math
import numpy as np
import ml_dtypes
import concourse.bass as bass
import concourse.mybir as mybir
from concourse.bass_utils import run_bass_kernel_spmd

F32 = mybir.dt.float32
BF16 = mybir.dt.bfloat16
AF = mybir.ActivationFunctionType
ALU = mybir.AluOpType
AX = mybir.AxisListType

NDMA_SLOTS = 6


class Tok:
    __slots__ = ("w", "r", "ra", "name")

    def __init__(self, name=""):
        self.w = {}
        self.r = {}
        self.ra = []
        self.name = name


class Prog:
    COMPUTE = ("pe", "act", "dve", "pool", "cc")
    QUEUES = ("q_sp", "q_pool", "q_act")
    ISSUE = {"pe": "pe", "act": "act", "dve": "dve", "pool": "pool", "cc": "pool",
             "q_sp": "sp", "q_pool": "pool", "q_act": "act"}

    def __init__(self, nc, stack):
        self.nc = nc
        self.ops = []
        self.base = 0
        self.cnt = {e: 0 for e in self.COMPUTE}
        self.dcnt = {q: 0 for q in self.QUEUES}
        self.waited = {e: {} for e in ("pe", "act", "dve", "pool", "sp")}
        self.sems = {}
        for e in self.COMPUTE:
            self.sems[e] = stack.enter_context(nc.semaphore(f"s_{e}"))
        for q in self.QUEUES:
            for s in range(NDMA_SLOTS):
                self.sems[(q, s)] = stack.enter_context(nc.semaphore(f"s_{q}{s}"))

    def op(self, eng, fn, rd=(), wr=()):
        self.ops.append((eng, fn, tuple(rd), tuple(wr)))

    def pe(self, fn, rd=(), wr=()):
        self.op("pe", fn, rd, wr)

    def act(self, fn, rd=(), wr=()):
        self.op("act", fn, rd, wr)

    def dve(self, fn, rd=(), wr=()):
        self.op("dve", fn, rd, wr)

    def pool(self, fn, rd=(), wr=()):
        self.op("pool", fn, rd, wr)

    def dma(self, fn, rd=(), wr=(), q="q_sp"):
        self.op(q, fn, rd, wr)

    def cc(self, fn, rd=(), wr=()):
        self.op("cc", fn, rd, wr)

    COST = {"pe": 0.27, "act": 0.5, "dve": 0.55, "pool": 1.2, "cc": 0.1, "q_sp": 0.06, "q_pool": 0.3, "q_act": 0.06}
    DONE_LAT = {"q_sp": 2.5, "q_pool": 3.0, "q_act": 2.5, "cc": 100.0}

    def _schedule(self, ops, order_deps, W=16):
        n = len(ops)
        issue = [self.ISSUE[o[0]] for o in ops]
        per = {e: [] for e in ("pe", "act", "dve", "pool", "sp")}
        for i in range(n):
            per[issue[i]].append(i)
        head = {e: 0 for e in per}
        done = [False] * n
        fin = [0.0] * n
        tfree = {e: 0.0 for e in per}
        order = []
        ready_t = [None] * n
        while len(order) < n:
            best = None
            for e, lst in per.items():
                h = head[e]
                while h < len(lst) and done[lst[h]]:
                    h += 1
                head[e] = h
                cnt = 0
                k = h
                while k < len(lst) and cnt < W:
                    i = lst[k]
                    k += 1
                    if done[i]:
                        continue
                    cnt += 1
                    rt = ready_t[i]
                    if rt is None:
                        ok = True
                        rt = 0.0
                        for d in order_deps[i]:
                            if not done[d]:
                                ok = False
                                break
                            f = fin[d] + (0.0 if issue[d] == e and not ops[d][0].startswith("q_") else 0.25)
                            if f > rt:
                                rt = f
                        if not ok:
                            continue
                        ready_t[i] = rt
                    st = rt if rt > tfree[e] else tfree[e]
                    key = (st, i)
                    if best is None or key < best[0]:
                        best = (key, i, e)
            (st, _), i, e = best
            c = self.COST[ops[i][0]]
            tfree[e] = st + c
            fin[i] = st + c + self.DONE_LAT.get(ops[i][0], 0.0)
            done[i] = True
            order.append(i)
        return order

    def emit(self, reorder=False):
        nc = self.nc
        ops = self.ops
        n = len(ops)
        base = self.base
        deps = [None] * n
        odeps = [None] * n
        signals = [False] * n
        for i, (eng, fn, rd, wr) in enumerate(ops):
            gi = base + i
            is_dma = eng.startswith("q_")
            key = ("dma", gi) if is_dma else eng
            d = set()
            od = set()
            for t in rd:
                for k, j in t.w.items():
                    if j >= base:
                        od.add(j - base)
                    if k == key and key == "pe":
                        continue
                    if j >= base:
                        d.add(j - base)
            for t in wr:
                for k, j in t.w.items():
                    if j >= base:
                        od.add(j - base)
                    if k == key:
                        continue
                    if j >= base:
                        d.add(j - base)
                for k, j in t.ra:
                    if j >= base:
                        od.add(j - base)
                    if k == key:
                        continue
                    if j >= base:
                        d.add(j - base)
            for t in rd:
                t.r[key] = gi
                t.ra.append((key, gi))
            for t in wr:
                t.w = {key: gi}
                t.r = {}
                t.ra = []
            od.discard(i)
            deps[i] = d
            odeps[i] = od
            for j in d:
                signals[j] = True
            if is_dma or eng == "cc":
                signals[i] = True
        if reorder and n > 2:
            order = self._schedule(ops, odeps)
            pos = [0] * n
            for p_, i in enumerate(order):
                pos[i] = p_
            ops = [ops[i] for i in order]
            deps = [{pos[j] for j in deps[i]} for i in order]
            signals = [signals[i] for i in order]
            self.ops = ops
        sig = [None] * n
        dma_idx = [None] * n
        for i, (eng, fn, rd, wr) in enumerate(ops):
            if eng.startswith("q_"):
                k = self.dcnt[eng]
                self.dcnt[eng] += 1
                dma_idx[i] = k
                sig[i] = ((eng, k % NDMA_SLOTS), 16 * (k // NDMA_SLOTS + 1))
            elif signals[i]:
                self.cnt[eng] += 1
                sig[i] = (eng, self.cnt[eng])
        waited = self.waited
        plan = {e: [] for e in ("pe", "act", "dve", "pool", "sp")}
        for i, (eng, fn, rd, wr) in enumerate(ops):
            ie = self.ISSUE[eng]
            ws = []
            need = {}
            for j in deps[i]:
                sk, v = sig[j]
                if need.get(sk, 0) < v:
                    need[sk] = v
            if eng.startswith("q_"):
                k = dma_idx[i]
                if k >= NDMA_SLOTS:
                    sk = (eng, k % NDMA_SLOTS)
                    v = 16 * (k // NDMA_SLOTS)
                    if need.get(sk, 0) < v:
                        need[sk] = v
            for sk, v in need.items():
                if waited[ie].get(sk, 0) < v:
                    waited[ie][sk] = v
                    ws.append((sk, v))
            plan[ie].append((i, ws))
        final = []
        for q, c in self.dcnt.items():
            for s in range(min(c, NDMA_SLOTS)):
                last = 16 * ((c - 1 - s) // NDMA_SLOTS + 1)
                final.append(((q, s), last))
        if self.cnt["cc"]:
            final.append(("cc", self.cnt["cc"]))
        sems = self.sems
        with nc.Block() as block:
            engs = {"pe": block.tensor, "act": block.scalar, "dve": block.vector,
                    "pool": block.gpsimd, "sp": block.sync}

            def mk(ie):
                def body(e):
                    for (i, ws) in plan[ie]:
                        for sk, v in ws:
                            e.wait_ge(sems[sk], v)
                        ins = ops[i][1](e)
                        if sig[i] is not None:
                            sk, v = sig[i]
                            ins.then_inc(sems[sk], 1 if isinstance(sk, str) else 16)
                    if ie == "sp":
                        for sk, v in final:
                            if waited["sp"].get(sk, 0) < v:
                                waited["sp"][sk] = v
                                e.wait_ge(sems[sk], v)
                return body

            for ie in ("sp", "pe", "act", "dve", "pool"):
                if plan[ie] or ie == "sp":
                    engs[ie](mk(ie))
        self.base += n
        self.ops = []


_UID = [0]


def U(name):
    _UID[0] += 1
    return f"{name}_{_UID[0]}"


class Ring:
    def __init__(self, stack, alloc, name, shape, dtype, n):
        self.bufs = []
        for i in range(n):
            t = stack.enter_context(alloc(U(f"{name}{i}"), list(shape), dtype))
            self.bufs.append((t, Tok(f"{name}{i}")))
        self.i = 0

    def next(self):
        b = self.bufs[self.i % len(self.bufs)]
        self.i += 1
        return b


D = 2048
KC_D = D // 128
EPS = 1e-6


def bcast_rows(ap_1d, nparts):
    return ap_1d.partition_broadcast(nparts)


def emit_rstd(P, ss, rstd, tok_ss, tok_rstd, n, width):
    P.act(lambda e: e.activation(out=rstd, in_=ss, func=AF.Ln, bias=EPS, scale=1.0 / width),
          rd=[tok_ss], wr=[tok_rstd])
    P.act(lambda e: e.activation(out=rstd, in_=rstd, func=AF.Exp, scale=-0.5),
          rd=[tok_rstd], wr=[tok_rstd])


def phase_norm(nc, P, stack, T, x_in, y_in, wpost, wpre, x_out, hT_out, ident_bf, sel4=None):
    nt = T // 128
    sb = nc.sbuf_tensor
    xr = Ring(stack, sb, "n_x", [128, D], F32, 2)
    yr = Ring(stack, sb, "n_y", [128, D], F32, 2) if y_in is not None else None
    tr = Ring(stack, sb, "n_t", [128, D], F32, 2)
    sqr = Ring(stack, sb, "n_sq", [128, D], BF16, 1)
    hr = Ring(stack, sb, "n_h", [128, D], BF16, 2)
    htr = Ring(stack, sb, "n_hT", [128, KC_D, 512], BF16, 2)
    str_ = Ring(stack, sb, "n_st", [128, 4], F32, 4)
    ptr = Ring(stack, nc.psum_tensor, "n_pt", [128, 1024], BF16, 4)
    if sel4 is not None:
        scr4 = Ring(stack, sb, "n_sc4", [128, KC_D, 512], BF16, 2)
        s4 = stack.enter_context(sb(U("n_sel4"), [128, 4], F32))
        s4k = Tok("n_sel4")
        P.dma(lambda e: e.dma_start(out=s4[:], in_=sel4.partition_broadcast(128)), wr=[s4k])
    consts = []
    for nm, w in (("n_wpost", wpost), ("n_wpre", wpre)):
        if w is None:
            consts.append((None, None))
            continue
        t = stack.enter_context(sb(U(nm), [128, D], F32))
        tk = Tok(nm)
        P.dma(lambda e, t=t, w=w: e.dma_start(out=t[:], in_=bcast_rows(w, 128)), wr=[tk])
        consts.append((t, tk))
    (wpost_t, wpost_k), (wpre_t, wpre_k) = consts
    for i in range(nt):
        rows = slice(i * 128, (i + 1) * 128)
        xt, xk = xr.next()
        P.dma(lambda e, xt=xt, rows=rows: e.dma_start(out=xt[:], in_=x_in[rows, :]), wr=[xk])
        cur, curk = xt, xk
        if y_in is not None:
            yt, yk = yr.next()
            if isinstance(y_in, tuple):
                P.dma(lambda e, yt=yt, rows=rows: e.dma_start(out=yt[:, 0:1024], in_=y_in[0][rows, :]), wr=[yk],
                      q="q_pool")
                P.dma(lambda e, yt=yt, rows=rows: e.dma_start(out=yt[:, 1024:2048], in_=y_in[1][rows, :]), wr=[yk],
                      q="q_pool")
            else:
                P.dma(lambda e, yt=yt, rows=rows: e.dma_start(out=yt[:], in_=y_in[rows, :]), wr=[yk], q="q_pool")
            sq, sqk = sqr.next()
            st, stk = str_.next()
            P.act(lambda e, sq=sq, yt=yt, st=st: e.activation(out=sq[:], in_=yt[:], func=AF.Square,
                                                               accum_out=st[:, 0:1]),
                  rd=[yk], wr=[sqk, stk])
            emit_rstd(P, st[:, 0:1], st[:, 1:2], stk, stk, 128, D)
            tt, tk = tr.next()
            P.dve(lambda e, tt=tt, yt=yt, st=st: e.scalar_tensor_tensor(
                out=tt[:], in0=yt[:], scalar=st[:, 1:2], in1=wpost_t[:], op0=ALU.mult, op1=ALU.mult),
                rd=[yk, stk, wpost_k], wr=[tk])
            P.dve(lambda e, tt=tt, xt=xt: e.tensor_tensor(out=tt[:], in0=tt[:], in1=xt[:], op=ALU.add),
                  rd=[tk, xk], wr=[tk])
            cur, curk = tt, tk
        if x_out is not None:
            P.dma(lambda e, cur=cur, rows=rows: e.dma_start(out=x_out[rows, :], in_=cur[:]), rd=[curk])
        if wpre is not None:
            sq, sqk = sqr.next()
            st, stk = str_.next()
            P.act(lambda e, sq=sq, cur=cur, st=st: e.activation(out=sq[:], in_=cur[:], func=AF.Square,
                                                                accum_out=st[:, 0:1]),
                  rd=[curk], wr=[sqk, stk])
            emit_rstd(P, st[:, 0:1], st[:, 1:2], stk, stk, 128, D)
            ht, hk = hr.next()
            P.dve(lambda e, ht=ht, cur=cur, st=st: e.scalar_tensor_tensor(
                out=ht[:], in0=cur[:], scalar=st[:, 1:2], in1=wpre_t[:], op0=ALU.mult, op1=ALU.mult),
                rd=[curk, stk, wpre_k], wr=[hk])
            if i % 4 == 0:
                hT, hTk = htr.next()
            tsl = slice((i % 4) * 128, (i % 4) * 128 + 128)
            for half in range(2):
                pt, ptk = ptr.next()
                for j in range(8):
                    kc = half * 8 + j
                    P.pe(lambda e, pt=pt, ht=ht, j=j, kc=kc: e.transpose(
                        out=pt[:, j * 128:(j + 1) * 128], in_=ht[:, kc * 128:(kc + 1) * 128],
                        identity=ident_bf[:]), rd=[hk], wr=[ptk])
                src = lambda pt: pt[:].rearrange("p (k t) -> p k t", t=128)
                if half == 0:
                    P.act(lambda e, hT=hT, pt=pt, tsl=tsl: e.copy(out=hT[:, 0:8, tsl], in_=src(pt)),
                          rd=[ptk], wr=[hTk])
                else:
                    P.dve(lambda e, hT=hT, pt=pt, tsl=tsl: e.tensor_copy(out=hT[:, 8:16, tsl], in_=src(pt)),
                          rd=[ptk], wr=[hTk])
            if i % 4 == 3 and sel4 is None:
                blk = i // 4
                P.dma(lambda e, hT=hT, blk=blk: e.dma_start(
                    out=hT_out.rearrange("(k p) t -> p k t", p=128)[:, :, blk * 512:(blk + 1) * 512],
                    in_=hT[:]), rd=[hTk], q="q_pool")
            elif i % 4 == 3:
                blk = i // 4
                for s_ in range(4):
                    sc, sck = scr4.next()
                    if s_ % 2 == 0:
                        P.dve(lambda e, sc=sc, hT=hT, s_=s_: e.tensor_scalar(
                            out=sc[:], in0=hT[:], scalar1=s4[:, s_:s_ + 1], scalar2=None, op0=ALU.mult),
                            rd=[hTk, s4k], wr=[sck])
                    else:
                        P.act(lambda e, sc=sc, hT=hT, s_=s_: e.activation(
                            out=sc[:], in_=hT[:], func=AF.Copy, scale=s4[:, s_:s_ + 1]),
                            rd=[hTk, s4k], wr=[sck])
                    P.dma(lambda e, sc=sc, blk=blk, s_=s_: e.dma_start(
                        out=hT_out.rearrange("(s b p) f -> p s b f", s=4, p=128)[:, s_, blk, :],
                        in_=sc[:].rearrange("p k t -> p (k t)")), rd=[sck], q="q_pool" if s_ % 2 else "q_sp")


def make_ident(nc, P, stack):
    idf = stack.enter_context(nc.sbuf_tensor(U("ident_f"), [128, 128], F32))
    idb = stack.enter_context(nc.sbuf_tensor(U("ident_b"), [128, 128], BF16))
    k = Tok("ident")
    P.pool(lambda e: e.memset(idf[:], 1.0), wr=[k])
    P.pool(lambda e: e.affine_select(out=idf[:], in_=idf[:], pattern=[[-1, 128]], compare_op=ALU.is_equal,
                                     fill=0.0, base=0, channel_multiplier=1), rd=[k], wr=[k])
    P.dve(lambda e: e.tensor_copy(out=idb[:], in_=idf[:]), rd=[k], wr=[k])
    return idf, idb, k


D_FF = 5632
KC_F = D_FF // 128


def cast_op(P, idx, out, in_, rd, wr):
    if idx % 2 == 0:
        P.dve(lambda e: e.tensor_copy(out=out, in_=in_), rd=rd, wr=wr)
    else:
        P.act(lambda e: e.copy(out=out, in_=in_), rd=rd, wr=wr)


def phase_ffn_gu(nc, P, stack, T, hT_in, wg_l, wu_l, actT_out):
    sb = nc.sbuf_tensor
    hT = stack.enter_context(sb(U("gu_hT"), [128, KC_D, T], BF16))
    hTk = [Tok(f"gu_hT{k}") for k in range(KC_D // 4)]
    hv = hT_in.rearrange("(k p) t -> p k t", p=128)
    for g in range(KC_D // 4):
        P.dma(lambda e, g=g: e.dma_start(out=hT[:, g * 4:(g + 1) * 4, :], in_=hv[:, g * 4:(g + 1) * 4, :]),
              wr=[hTk[g]], q="q_pool")
    wst = [Ring(stack, sb, f"gu_wst{m}", [128, D], F32, 2) for m in range(2)]
    wbf = [Ring(stack, sb, f"gu_wbf{m}", [128, KC_D, 128], BF16, 2) for m in range(2)]
    ps = [Ring(stack, nc.psum_tensor, f"gu_ps{m}", [128, 512], F32, 3) for m in range(2)]
    sgr = Ring(stack, sb, "gu_sg", [128, 512], F32, 2)
    ar = Ring(stack, sb, "gu_a", [128, 512], BF16, 4)
    nsl = T // 512
    ci = 0
    for fc in range(KC_F):
        wb = []
        for m, wl in enumerate((wg_l, wu_l)):
            st_, stk = wst[m].next()
            P.dma(lambda e, st_=st_, wl=wl, fc=fc: e.dma_start(out=st_[:], in_=wl[fc]), wr=[stk])
            b, bk = wbf[m].next()
            cast_op(P, ci, b[:].rearrange("p k j -> p (k j)"), st_[:], [stk], [bk])
            ci += 1
            wb.append((b, bk))
        for sl in range(nsl):
            tsl = slice(sl * 512, (sl + 1) * 512)
            pp = []
            for m in range(2):
                p_, pk = ps[m].next()
                b, bk = wb[m]
                for kc in range(KC_D):
                    P.pe(lambda e, p_=p_, b=b, kc=kc, tsl=tsl: e.matmul(
                        p_[:], lhsT=b[:, kc, :], rhs=hT[:, kc, tsl], start=(kc == 0), stop=(kc == KC_D - 1)),
                        rd=[bk, hTk[kc // 4]], wr=[pk])
                pp.append((p_, pk))
            sg, sgk = sgr.next()
            P.act(lambda e, sg=sg, p_=pp[0][0]: e.activation(out=sg[:], in_=p_[:], func=AF.Silu),
                  rd=[pp[0][1]], wr=[sgk])
            a, ak = ar.next()
            P.dve(lambda e, a=a, sg=sg, p_=pp[1][0]: e.tensor_tensor(out=a[:], in0=sg[:], in1=p_[:], op=ALU.mult),
                  rd=[sgk, pp[1][1]], wr=[ak])
            P.dma(lambda e, a=a, fc=fc, tsl=tsl: e.dma_start(out=actT_out[fc * 128:(fc + 1) * 128, tsl], in_=a[:]),
                  rd=[ak], q="q_pool")


def phase_mmT(nc, P, stack, T, K, AT_in, w_l, y_out, Th=1024, pfx="mt", halves=None, htoks=None, half_done=None):
    KC = K // 128
    G = 4
    NG = KC // G
    Th = min(Th, T, 1024)
    NT = Th // 128
    sb = nc.sbuf_tensor
    nAT = 1 if halves is None else 2
    ATs = [(stack.enter_context(sb(U(f"{pfx}_AT"), [128, KC, Th], BF16)), [Tok(f"{pfx}_AT{g}") for g in range(NG)])
           for _ in range(nAT)]
    ati = 0
    AT, ATk = ATs[0]
    wst = Ring(stack, sb, f"{pfx}_wst", [128, G, 512], F32, 3)
    wbf = Ring(stack, sb, f"{pfx}_wbf", [128, G, 512], BF16, 4)
    ps = Ring(stack, nc.psum_tensor, f"{pfx}_ps", [128, 512], F32, 8)
    ys = Ring(stack, sb, f"{pfx}_ys", [128, 512], F32, 4)
    av = AT_in.rearrange("(k p) t -> p k t", p=128)
    ci = 0
    ei = 0
    if halves is None:
        order = [(th, s) for th in range(T // Th) for s in range(4)]
    else:
        order = [(th, s) for hf in range(2) for th in range(T // Th) for s in (2 * hf, 2 * hf + 1)]
    last_th = None
    for (th, s) in order:
        if th != last_th:
            last_th = th
            AT, ATk = ATs[ati % nAT]
            ati += 1
            for g in range(NG):
                P.dma(lambda e, g=g, th=th, AT=AT: e.dma_start(out=AT[:, g * G:(g + 1) * G, :],
                                                        in_=av[:, g * G:(g + 1) * G, th * Th:(th + 1) * Th]),
                      wr=[ATk[g]], q="q_pool" if halves is None else "q_sp")
        if True:
            banks = [ps.next() for _ in range(NT)]
            for g in range(NG):
                st_, stk = wst.next()
                P.dma(lambda e, st_=st_, s=s, g=g: e.dma_start(out=st_[:], in_=w_l[s, :, g * G:(g + 1) * G, :]),
                      wr=[stk])
                wb, wbk = wbf.next()
                cast_op(P, ci, wb[:], st_[:], [stk], [wbk])
                ci += 1
                for tl in range(NT):
                    p_, pk = banks[tl]
                    for kk in range(G):
                        kc = g * G + kk
                        P.pe(lambda e, p_=p_, kc=kc, kk=kk, tl=tl, wb=wb, AT=AT: e.matmul(
                            p_[:], lhsT=AT[:, kc, tl * 128:(tl + 1) * 128], rhs=wb[:, kk, :],
                            start=(kc == 0), stop=(kc == KC - 1)),
                            rd=[ATk[g], wbk], wr=[pk])
            for tl in range(NT):
                p_, pk = banks[tl]
                y_, yk = ys.next()
                if ei % 2 == 0:
                    P.act(lambda e, y_=y_, p_=p_: e.copy(out=y_[:], in_=p_[:]), rd=[pk], wr=[yk])
                else:
                    P.dve(lambda e, y_=y_, p_=p_: e.tensor_copy(out=y_[:], in_=p_[:]), rd=[pk], wr=[yk])
                ei += 1
                r0 = th * Th + tl * 128
                if halves is None:
                    P.dma(lambda e, y_=y_, r0=r0, s=s: e.dma_start(out=y_out[r0:r0 + 128, s * 512:(s + 1) * 512],
                                                                   in_=y_[:]), rd=[yk], q="q_pool")
                else:
                    dst = halves[s // 2]
                    tk_ = Tok("yhalf")
                    htoks[s // 2].append(tk_)
                    P.dma(lambda e, y_=y_, r0=r0, s=s, dst=dst: e.dma_start(
                        out=dst[r0:r0 + 128, (s % 2) * 512:(s % 2 + 1) * 512], in_=y_[:]), rd=[yk], wr=[tk_],
                        q="q_act")
            if halves is not None and half_done is not None and s % 2 == 1 and th == T // Th - 1:
                half_done(s // 2)


SEQ = 8192
NBLK = SEQ // 512


def make_consts(nc, P, stack):
    sb = nc.sbuf_tensor
    c = {}
    c["ones_bf"] = stack.enter_context(sb(U("ones_bf"), [128, 128], BF16))
    c["ones_f"] = stack.enter_context(sb(U("ones_f"), [128, 128], F32))
    c["sel2"] = stack.enter_context(sb(U("sel2"), [128, 2], BF16))
    k = Tok("consts")
    c["tok"] = k
    P.pool(lambda e: e.memset(c["ones_bf"][:], 1.0), wr=[k])
    P.pool(lambda e: e.memset(c["ones_f"][:], 1.0), wr=[k])
    P.pool(lambda e: e.memset(c["sel2"][:], 0.0), wr=[k])
    P.pool(lambda e: e.memset(c["sel2"][0:64, 0:1], 1.0), wr=[k])
    P.pool(lambda e: e.memset(c["sel2"][64:128, 1:2], 1.0), wr=[k])
    return c


def phase_attn_proj(nc, P, stack, hT_blk, w_l, qTd, kTd, Vd, negMd, idf, idb, C, S=SEQ, hT_tok=None):
    sb = nc.sbuf_tensor
    ps = nc.psum_tensor
    NB = S // 512
    ones_f, sel2, ck = C["ones_f"], C["sel2"], C["tok"]
    W = stack.enter_context(sb(U("ap_w"), [128, KC_D, 12 * 128], BF16))
    Wk = [Tok(f"ap_w{f}") for f in range(12)]
    wst = Ring(stack, sb, "ap_wst", [128, D], F32, 2)
    for fc in range(12):
        st_, stk = wst.next()
        P.dma(lambda e, st_=st_, fc=fc: e.dma_start(out=st_[:], in_=w_l[fc]), wr=[stk], q="q_pool")
        cast_op(P, fc, W[:, :, fc * 128:(fc + 1) * 128], st_[:].rearrange("p (k j) -> p k j", j=128), [stk], [Wk[fc]])
    hr = Ring(stack, sb, "ap_h", [128, KC_D, 512], BF16, 2)
    pw = Ring(stack, ps, "ap_pw", [128, 512], F32, 5)
    pm = Ring(stack, ps, "ap_pm", [128, 512], F32, 2)
    osr = Ring(stack, sb, "ap_os", [128, 512], BF16, 6)
    sqr = Ring(stack, sb, "ap_sq", [128, 512], BF16, 3)
    vor = Ring(stack, sb, "ap_vo", [128, 4, 128], BF16, 3)
    nmx = stack.enter_context(sb(U("ap_nmx"), [2, 8, NB], F32))
    nmxk = Tok("ap_nmx")
    ei = 0
    for b in range(NB):
        h_, hk = hr.next()
        P.dma(lambda e, h_=h_, b=b: e.dma_start(out=h_[:], in_=hT_blk(b)), rd=([hT_tok(b)] if hT_tok else []),
              wr=[hk])
        tsl = slice(b * 512, (b + 1) * 512)
        for ch in range(12):
            which, hl = ch // 4, ch % 4
            p_, pk = pw.next()
            for kc in range(KC_D):
                P.pe(lambda e, p_=p_, ch=ch, kc=kc, h_=h_: e.matmul(
                    p_[:], lhsT=W[:, kc, ch * 128:(ch + 1) * 128], rhs=h_[:, kc, :],
                    start=(kc == 0), stop=(kc == KC_D - 1)), rd=[Wk[ch], hk], wr=[pk])
            o_, ok = osr.next()
            sc = 0.125 if which == 0 else 1.0
            if ei % 2 == 0:
                P.act(lambda e, o_=o_, p_=p_, sc=sc: e.activation(out=o_[:], in_=p_[:], func=AF.Copy, scale=sc),
                      rd=[pk], wr=[ok])
            else:
                P.dve(lambda e, o_=o_, p_=p_, sc=sc: e.tensor_scalar(out=o_[:], in0=p_[:], scalar1=sc, scalar2=None,
                                                                    op0=ALU.mult), rd=[pk], wr=[ok])
            ei += 1
            if which < 2:
                dst = qTd if which == 0 else kTd
                P.dma(lambda e, o_=o_, dst=dst, hl=hl, tsl=tsl: e.dma_start(
                    out=dst[hl * 128:(hl + 1) * 128, tsl], in_=o_[:]), rd=[ok], q="q_pool")
                sq, sqk = sqr.next()
                P.dve(lambda e, sq=sq, o_=o_: e.tensor_tensor(out=sq[:], in0=o_[:], in1=o_[:], op=ALU.mult),
                      rd=[ok], wr=[sqk])
                p2, p2k = pm.next()
                P.pe(lambda e, p2=p2, sq=sq: e.matmul(p2[0:2, :], lhsT=sel2[:], rhs=sq[:], start=True, stop=True),
                     rd=[sqk, ck], wr=[p2k])
                P.dve(lambda e, p2=p2, ch=ch, b=b: e.tensor_reduce(
                    out=nmx[:, ch, b:b + 1], in_=p2[0:2, :], axis=AX.X, op=ALU.max), rd=[p2k], wr=[nmxk])
            else:
                p2, p2k = pm.next()
                p2b = p2[:].bitcast(BF16)
                for j in range(4):
                    P.pe(lambda e, p2b=p2b, o_=o_, j=j: e.transpose(
                        out=p2b[:, j * 128:(j + 1) * 128], in_=o_[:, j * 128:(j + 1) * 128], identity=idb[:]),
                        rd=[ok], wr=[p2k])
                vo, vok = vor.next()
                P.act(lambda e, vo=vo, p2b=p2b: e.copy(out=vo[:], in_=p2b[:, 0:512].rearrange("p (j d) -> p j d", d=128)),
                      rd=[p2k], wr=[vok])
                P.dma(lambda e, vo=vo, hl=hl, b=b: e.dma_start(out=Vd[hl, :, b * 4:(b + 1) * 4, :], in_=vo[:]),
                      rd=[vok], q="q_pool")
    msc = stack.enter_context(sb(U("ap_msc"), [2, 32], F32))
    P.dve(lambda e: e.tensor_reduce(out=msc[:, 0:8], in_=nmx[:], axis=AX.X, op=ALU.max), rd=[nmxk], wr=[nmxk])
    P.dve(lambda e: e.tensor_tensor(out=msc[:, 8:12], in0=msc[:, 0:4], in1=msc[:, 4:8], op=ALU.mult),
          rd=[nmxk], wr=[nmxk])
    P.act(lambda e: e.activation(out=msc[:, 12:16], in_=msc[:, 8:12], func=AF.Ln), rd=[nmxk], wr=[nmxk])
    P.act(lambda e: e.activation(out=msc[:, 12:16], in_=msc[:, 12:16], func=AF.Exp, scale=0.5), rd=[nmxk], wr=[nmxk])
    for hl in range(4):
        P.dve(lambda e, hl=hl: e.tensor_scalar(out=msc[:, 16 + hl * 2:18 + hl * 2], in0=idf[0:2, 0:2],
                                               scalar1=msc[:, 12 + hl:13 + hl], scalar2=-1.02,
                                               op0=ALU.mult, op1=ALU.mult), rd=[nmxk], wr=[nmxk])
    p2, p2k = pm.next()
    P.pe(lambda e: e.matmul(p2[:, 0:8], lhsT=ones_f[0:2, :], rhs=msc[:, 16:24], start=True, stop=True),
         rd=[nmxk, ck], wr=[p2k])
    nm = stack.enter_context(sb(U("ap_nm"), [128, 8], F32))
    nmk = Tok("ap_nm")
    P.dve(lambda e: e.tensor_copy(out=nm[:], in_=p2[:, 0:8]), rd=[p2k], wr=[nmk])
    P.dma(lambda e: e.dma_start(out=negMd, in_=nm[:]), rd=[nmk])


def phase_attn_core(nc, P, stack, qTd, kTd, Vd, negMd, lam_in, subln_in, lambda_init, oT_out, idf, idb, C, S=SEQ):
    sb = nc.sbuf_tensor
    ps = nc.psum_tensor
    NB = S // 512
    ones_bf, ones_f, sel2, ck = C["ones_bf"], C["ones_f"], C["sel2"], C["tok"]
    lam4 = stack.enter_context(sb(U("at_lam4"), [128, 4, 64], F32))
    lamk = Tok("lam")
    P.dma(lambda e: e.dma_start(out=lam4[:].rearrange("p a d -> p (a d)"),
                                in_=lam_in.rearrange("a d -> (a d)").partition_broadcast(128)), wr=[lamk])
    lsc = stack.enter_context(sb(U("at_lsc"), [128, 8], F32))
    lpr = stack.enter_context(sb(U("at_lpr"), [128, 2, 64], F32))
    P.dve(lambda e: e.tensor_tensor(out=lpr[:, 0, :], in0=lam4[:, 0, :], in1=lam4[:, 1, :], op=ALU.mult),
          rd=[lamk], wr=[lamk])
    P.dve(lambda e: e.tensor_tensor(out=lpr[:, 1, :], in0=lam4[:, 2, :], in1=lam4[:, 3, :], op=ALU.mult),
          rd=[lamk], wr=[lamk])
    P.dve(lambda e: e.tensor_reduce(out=lsc[:, 0:2], in_=lpr[:], axis=AX.X, op=ALU.add), rd=[lamk], wr=[lamk])
    P.act(lambda e: e.activation(out=lsc[:, 2:4], in_=lsc[:, 0:2], func=AF.Exp), rd=[lamk], wr=[lamk])
    P.dve(lambda e: e.scalar_tensor_tensor(out=lsc[:, 4:5], in0=lsc[:, 3:4], scalar=-float(lambda_init),
                                           in1=lsc[:, 2:3], op0=ALU.add, op1=ALU.subtract), rd=[lamk], wr=[lamk])
    P.dma(lambda e: e.dma_start(out=lsc[:, 5:6], in_=subln_in.rearrange("(p o) -> p o", o=1)), wr=[lamk])
    P.dve(lambda e: e.tensor_scalar(out=lsc[:, 6:7], in0=lsc[:, 5:6], scalar1=1.0 - float(lambda_init),
                                    scalar2=None, op0=ALU.mult), rd=[lamk], wr=[lamk])
    neglam = lsc[:, 4:5]
    sublnw = lsc[:, 6:7]

    qTc = [stack.enter_context(sb(U(f"at_qT{c}"), [128, S], BF16)) for c in range(2)]
    qzk = Tok("at_qz")
    P.dve(lambda e: e.memset(qTc[0][64:128, :], 0.0), wr=[qzk])
    P.dve(lambda e: e.memset(qTc[1][0:64, :], 0.0), wr=[qzk])
    kT = stack.enter_context(sb(U("at_kT"), [128, S], BF16))
    V = stack.enter_context(sb(U("at_V"), [128, S // 128, 128], BF16))
    qk1, kk1, vk1 = Tok("at_q"), Tok("at_k"), Tok("at_v")
    qk_ = [qk1] * NB
    kk_ = [kk1] * NB
    vk_ = [vk1] * NB
    negM = stack.enter_context(sb(U("at_negM"), [128, 8], F32))
    negMk = Tok("negM")
    P.dma(lambda e: e.dma_start(out=negM[:], in_=negMd), wr=[negMk])
    pw = Ring(stack, ps, "at_pw", [128, 512], F32, 3)
    po = [Ring(stack, ps, f"at_po{c}", [128, 512], F32, 1) for c in range(2)]
    pl = [Ring(stack, ps, f"at_pl{c}", [128, 512], F32, 1) for c in range(2)]
    lacc = Ring(stack, sb, "at_la", [128, 512], F32, 4)
    pm = Ring(stack, ps, "at_pm", [128, 512], F32, 1)
    ptr = Ring(stack, sb, "at_pt", [128, 512], BF16, 6)
    e32 = Ring(stack, sb, "at_e32", [128, 512], F32, 8)
    obf = Ring(stack, sb, "at_obf", [128, 512], BF16, 2)
    for hl in range(4):
        hs = slice(hl * 128, (hl + 1) * 128)
        P.dma(lambda e, hl=hl: e.dma_start(out=qTc[0][0:64, :], in_=qTd[hl * 128:hl * 128 + 64, :]),
              rd=[qzk], wr=[qk1])
        P.dma(lambda e, hl=hl: e.dma_start(out=qTc[1][64:128, :], in_=qTd[hl * 128 + 64:hl * 128 + 128, :]),
              rd=[qzk], wr=[qk1], q="q_pool")
        P.dma(lambda e, hs=hs: e.dma_start(out=kT[:], in_=kTd[hs, :]), wr=[kk1])
        P.dma(lambda e, hl=hl: e.dma_start(out=V[:], in_=Vd[hl]), wr=[vk1], q="q_pool")
        steps = [(qt, c, sbk) for qt in range(NB) for c in range(2) for sbk in range(qt * 4 + 4)]
        LA = 2
        inflight = {}
        accs = {}
        tparts = {}
        deferred = []

        def stepA(k):
            qt, c, sbk = steps[k]
            q0 = qt * 512
            rows = slice(c * 64, (c + 1) * 64)
            d = max(0, sbk - qt * 4)
            cs = slice(d * 128, 512)
            w_, wk_ = pw.next()
            P.pe(lambda e: e.matmul(w_[:, cs], lhsT=kT[:, sbk * 128:(sbk + 1) * 128],
                                    rhs=qTc[c][:, q0 + cs.start:q0 + 512], start=True, stop=True),
                 rd=[kk_[sbk // 4], qk_[qt], qzk], wr=[wk_])
            pt, ptk = ptr.next()
            bcol = hl * 2 + c
            P.act(lambda e: e.activation(out=pt[:, cs], in_=w_[:, cs], func=AF.Exp, bias=negM[:, bcol:bcol + 1],
                                         scale=1.0), rd=[wk_, negMk], wr=[ptk])
            if sbk >= qt * 4:
                base = q0 + cs.start - sbk * 128
                P.pool(lambda e: e.affine_select(out=pt[:, cs], in_=pt[:, cs], pattern=[[1, 512 - cs.start]],
                                                 compare_op=ALU.is_ge, fill=0.0, base=base,
                                                 channel_multiplier=-1), rd=[ptk], wr=[ptk])
            inflight[k] = (pt, ptk, cs)

        def epi_c(qt, c, k):
            o_, ok, l_, lk, la, lak = accs.pop((qt, c))

            def part2():
                P.pe(lambda e: e.matmul(l_[:], lhsT=ones_f[:], rhs=la[:], start=False, stop=True),
                     rd=[lak, ck], wr=[lk])
                r_, rk = e32.next()
                P.dve(lambda e: e.reciprocal(out=r_[:], in_=l_[:]), rd=[lk], wr=[rk])
                t_, tk = e32.next()
                P.dve(lambda e: e.tensor_tensor(out=t_[:], in0=o_[:], in1=r_[:], op=ALU.mult), rd=[ok, rk], wr=[tk])
                tparts[(qt, c)] = (t_, tk)
                if c == 1:
                    epi_1(qt, k + 2)
            deferred.append((k + 2, part2))

        def epi_1(qt, k):
            t0, t0k = tparts.pop((qt, 0))
            t1, t1k = tparts.pop((qt, 1))
            of, ofk = e32.next()
            P.dve(lambda e: e.scalar_tensor_tensor(out=of[:], in0=t1[:], scalar=neglam, in1=t0[:],
                                                   op0=ALU.mult, op1=ALU.add), rd=[t0k, t1k, lamk], wr=[ofk])
            sq, sqk2 = e32.next()
            P.act(lambda e: e.activation(out=sq[:], in_=of[:], func=AF.Square), rd=[ofk], wr=[sqk2])

            def epi_2():
                m_, mk = pm.next()
                P.pe(lambda e: e.matmul(m_[:], lhsT=ones_f[:], rhs=sq[:], start=True, stop=True),
                     rd=[sqk2, ck], wr=[mk])
                rs, rsk = e32.next()
                P.act(lambda e: e.activation(out=rs[:], in_=m_[:], func=AF.Ln, bias=EPS, scale=1.0 / 128),
                      rd=[mk], wr=[rsk])
                P.act(lambda e: e.activation(out=rs[:], in_=rs[:], func=AF.Exp, scale=-0.5), rd=[rsk], wr=[rsk])
                ob, obk = obf.next()
                P.dve(lambda e: e.scalar_tensor_tensor(out=ob[:], in0=of[:], scalar=sublnw, in1=rs[:],
                                                       op0=ALU.mult, op1=ALU.mult), rd=[ofk, rsk, lamk], wr=[obk])
                P.dma(lambda e, hl=hl: e.dma_start(out=oT_out[hl * 128:(hl + 1) * 128, qt * 512:(qt + 1) * 512],
                                                   in_=ob[:]), rd=[obk])
            deferred.append((k + 6, epi_2))

        def stepB(k):
            qt, c, sbk = steps[k]
            nsb = qt * 4 + 4
            pt, ptk, cs = inflight.pop(k)
            if sbk == 0:
                o_, ok = po[c].next()
                l_, lk = pl[c].next()
                la, lak = lacc.next()
                accs[(qt, c)] = (o_, ok, l_, lk, la, lak)
            o_, ok, l_, lk, la, lak = accs[(qt, c)]
            P.pe(lambda e: e.matmul(o_[:, cs], lhsT=V[:, sbk, :], rhs=pt[:, cs], start=(sbk == 0),
                                    stop=(sbk == nsb - 1)), rd=[vk_[sbk // 4], ptk], wr=[ok])
            if sbk % 2 == 0:
                P.pe(lambda e: e.matmul(l_[:, cs], lhsT=ones_bf[:], rhs=pt[:, cs], start=(sbk == 0), stop=False),
                     rd=[ck, ptk], wr=[lk])
            elif sbk == 1:
                if cs.start > 0:
                    P.dve(lambda e: e.memset(la[:, 0:cs.start], 0.0), wr=[lak])
                P.dve(lambda e: e.tensor_copy(out=la[:, cs], in_=pt[:, cs]), rd=[ptk], wr=[lak])
            else:
                P.dve(lambda e: e.tensor_tensor(out=la[:, cs], in0=la[:, cs], in1=pt[:, cs], op=ALU.add),
                      rd=[ptk, lak], wr=[lak])
            if sbk == nsb - 1:
                epi_c(qt, c, k)

        ns = len(steps)
        for k in range(ns + LA):
            if k < ns:
                stepA(k)
            if k - LA >= 0:
                stepB(k - LA)
            for item in [d_ for d_ in deferred if d_[0] <= k]:
                deferred.remove(item)
                item[1]()
        for item in deferred:
            item[1]()


NZX = 20


def phase_ssd_in(nc, P, stack, hT_blk, w_l, wdt_l, dtb_in, zxT_out, dt_out, S=SEQ, hT_tok=None):
    sb = nc.sbuf_tensor
    NB = S // 512
    W = stack.enter_context(sb(U("si_w"), [128, KC_D, NZX * 128], BF16))
    Wk = [Tok(f"si_w{f}") for f in range(NZX)]
    wst = Ring(stack, sb, "si_wst", [128, D], F32, 2)
    for fc in range(NZX):
        st_, stk = wst.next()
        P.dma(lambda e, st_=st_, fc=fc: e.dma_start(out=st_[:], in_=w_l[fc]), wr=[stk])
        cast_op(P, fc, W[:, :, fc * 128:(fc + 1) * 128], st_[:].rearrange("p (k j) -> p k j", j=128), [stk], [Wk[fc]])
    wdtf = stack.enter_context(sb(U("si_wdtf"), [128, KC_D, 16], F32))
    wdt = stack.enter_context(sb(U("si_wdt"), [128, KC_D, 16], BF16))
    wdk = Tok("si_wdt")
    P.dma(lambda e: e.dma_start(out=wdtf[:], in_=wdt_l), wr=[wdk])
    P.dve(lambda e: e.tensor_copy(out=wdt[:], in_=wdtf[:]), rd=[wdk], wr=[wdk])
    dtb = stack.enter_context(sb(U("si_dtb"), [128, 16], F32))
    dtbk = Tok("si_dtb")
    P.dma(lambda e: e.dma_start(out=dtb[:], in_=dtb_in.partition_broadcast(128)), wr=[dtbk])
    hr = Ring(stack, sb, "si_h", [128, KC_D, 512], BF16, 2)
    pw = Ring(stack, nc.psum_tensor, "si_pw", [128, 512], F32, 4)
    pd = Ring(stack, nc.psum_tensor, "si_pd", [128, 16], F32, 2)
    osr = Ring(stack, sb, "si_os", [128, 512], F32, 4)
    dr = Ring(stack, sb, "si_d", [128, 4, 16], F32, 6)
    dto = Ring(stack, sb, "si_dto", [128, 4, 16], F32, 2)
    ei = 0
    for b in range(NB):
        h_, hk = hr.next()
        P.dma(lambda e, h_=h_, b=b: e.dma_start(out=h_[:], in_=hT_blk(b)), rd=([hT_tok(b)] if hT_tok else []),
              wr=[hk])
        tsl = slice(b * 512, (b + 1) * 512)
        for fc in range(NZX):
            p_, pk = pw.next()
            for kc in range(KC_D):
                P.pe(lambda e, p_=p_, fc=fc, kc=kc, h_=h_: e.matmul(
                    p_[:], lhsT=W[:, kc, fc * 128:(fc + 1) * 128], rhs=h_[:, kc, :],
                    start=(kc == 0), stop=(kc == KC_D - 1)), rd=[Wk[fc], hk], wr=[pk])
            o_, ok = osr.next()
            if ei % 2 == 0:
                P.act(lambda e, o_=o_, p_=p_: e.copy(out=o_[:], in_=p_[:]), rd=[pk], wr=[ok])
            else:
                P.dve(lambda e, o_=o_, p_=p_: e.tensor_copy(out=o_[:], in_=p_[:]), rd=[pk], wr=[ok])
            ei += 1
            P.dma(lambda e, o_=o_, fc=fc, tsl=tsl: e.dma_start(out=zxT_out[fc * 128:(fc + 1) * 128, tsl], in_=o_[:]),
                  rd=[ok])
        x_, xk = dr.next()
        for j in range(4):
            p_, pk = pd.next()
            for kc in range(KC_D):
                P.pe(lambda e, p_=p_, kc=kc, h_=h_, j=j: e.matmul(
                    p_[:], lhsT=h_[:, kc, j * 128:(j + 1) * 128], rhs=wdt[:, kc, :],
                    start=(kc == 0), stop=(kc == KC_D - 1)), rd=[wdk, hk], wr=[pk])
            P.dve(lambda e, x_=x_, p_=p_, j=j: e.tensor_tensor(out=x_[:, j, :], in0=p_[:], in1=dtb[:], op=ALU.add),
                  rd=[pk, dtbk], wr=[xk])
        a_, ak = dr.next()
        P.dve(lambda e, a_=a_, x_=x_: e.scalar_tensor_tensor(out=a_[:], in0=x_[:], scalar=-1.0, in1=x_[:],
                                                             op0=ALU.mult, op1=ALU.max), rd=[xk], wr=[ak])
        P.act(lambda e, a_=a_: e.activation(out=a_[:], in_=a_[:], func=AF.Exp, scale=-1.0), rd=[ak], wr=[ak])
        P.act(lambda e, a_=a_: e.activation(out=a_[:], in_=a_[:], func=AF.Ln, bias=1.0, scale=1.0), rd=[ak], wr=[ak])
        d_, dk = dto.next()
        P.dve(lambda e, d_=d_, x_=x_, a_=a_: e.scalar_tensor_tensor(
            out=d_[:], in0=x_[:], scalar=0.0, in1=a_[:], op0=ALU.max, op1=ALU.add), rd=[xk, ak], wr=[dk])
        P.dma(lambda e, d_=d_, b=b: e.dma_start(
            out=dt_out[b * 512:(b + 1) * 512, :].rearrange("(j p) h -> p j h", p=128), in_=d_[:]), rd=[dk])


def phase_ssd_scan(nc, P, stack, zxT_in, dt_in, convw_in, convb_in, alog_in, dsk_in, normw_in, yT_out,
                   idf, idb, C, S=SEQ):
    sb = nc.sbuf_tensor
    ps = nc.psum_tensor
    ones_f, ck = C["ones_f"], C["tok"]
    TB = 256
    NB = S // TB
    zv = zxT_in.rearrange("(k p) t -> p k t", p=128)
    tri = stack.enter_context(sb(U("ss_tri"), [128, 128], F32))
    cst = Tok("ss_const")
    P.pool(lambda e: e.memset(tri[:], 1.0), wr=[cst])
    P.pool(lambda e: e.affine_select(out=tri[:], in_=tri[:], pattern=[[1, 128]], compare_op=ALU.is_ge,
                                     fill=0.0, base=0, channel_multiplier=-1), rd=[cst], wr=[cst])
    cw = stack.enter_context(sb(U("ss_cw"), [128, 12, 4], F32))
    cb = stack.enter_context(sb(U("ss_cb"), [128, 12], F32))
    nw = stack.enter_context(sb(U("ss_nw"), [128, 8], F32))
    abc = stack.enter_context(sb(U("ss_abc"), [128, 16], F32))
    d16 = stack.enter_context(sb(U("ss_d16"), [128, 16], F32))
    Dbc = stack.enter_context(sb(U("ss_Dbc"), [128, 16, 64], F32))
    P.dma(lambda e: e.dma_start(out=cw[:], in_=convw_in), wr=[cst])
    P.dma(lambda e: e.dma_start(out=cb[:], in_=convb_in), wr=[cst])
    P.dma(lambda e: e.dma_start(out=nw[:], in_=normw_in), wr=[cst])
    P.dma(lambda e: e.dma_start(out=abc[:], in_=alog_in.partition_broadcast(128)), wr=[cst])
    P.dma(lambda e: e.dma_start(out=d16[:], in_=dsk_in.partition_broadcast(128)), wr=[cst])
    P.act(lambda e: e.activation(out=abc[:], in_=abc[:], func=AF.Exp), rd=[cst], wr=[cst])
    P.dve(lambda e: e.tensor_scalar(out=abc[:], in0=abc[:], scalar1=-1.0, scalar2=None, op0=ALU.mult),
          rd=[cst], wr=[cst])
    P.dve(lambda e: e.tensor_copy(out=Dbc[:], in_=d16[:].unsqueeze(2).to_broadcast([128, 16, 64])),
          rd=[cst], wr=[cst])
    S32 = [stack.enter_context(sb(U(f"ss_S32{g}"), [128, 512], F32)) for g in range(2)]
    Sbf = [stack.enter_context(sb(U(f"ss_Sbf{g}"), [128, 512], BF16)) for g in range(2)]
    Sk = [Tok(f"ss_S{g}") for g in range(2)]
    Sbk = [Tok(f"ss_Sb{g}") for g in range(2)]
    for g in range(2):
        P.pool(lambda e, g=g: e.memset(S32[g][:], 0.0), wr=[Sk[g]])
        P.pool(lambda e, g=g: e.memset(Sbf[g][:], 0.0), wr=[Sbk[g]])
    rawr = Ring(stack, sb, "ss_raw", [128, 12, TB + 3], F32, 2)
    zr = Ring(stack, sb, "ss_z", [128, 8, TB], F32, 3)
    accr = Ring(stack, sb, "ss_acc", [128, 12, TB], F32, 1)
    xTr = Ring(stack, sb, "ss_xT", [128, 8, TB], F32, 2)
    bcTr = Ring(stack, sb, "ss_bcT", [128, 4, TB], BF16, 3)
    dtr = Ring(stack, sb, "ss_dt", [128, TB // 128, 16], F32, 3)
    oTr = Ring(stack, sb, "ss_oT", [128, 8, TB], BF16, 3)
    xsr = Ring(stack, sb, "ss_xs", [128, 512], F32, 2)
    Btr = Ring(stack, sb, "ss_Bt", [128, 128], BF16, 4)
    smr = Ring(stack, sb, "ss_sm", [128, 48], F32, 5)
    rbr = Ring(stack, sb, "ss_rb", [128, 8, 128], F32, 2)
    segr = Ring(stack, sb, "ss_seg", [128, 8, 128], F32, 2)
    cbmr = Ring(stack, sb, "ss_cbm", [128, 128], BF16, 3)
    ebr = Ring(stack, sb, "ss_eb", [128, 8, 128], BF16, 3)
    Gr = Ring(stack, sb, "ss_G", [128, 8, 128], BF16, 4)
    x32r = Ring(stack, sb, "ss_x32", [128, 512], F32, 2)
    xbr = Ring(stack, sb, "ss_xb", [128, 512], BF16, 4)
    xer = Ring(stack, sb, "ss_xe", [128, 512], BF16, 4)
    y1r = Ring(stack, sb, "ss_y1", [128, 512], F32, 2)
    xdr = Ring(stack, sb, "ss_xd", [128, 512], F32, 4)
    gvr = Ring(stack, sb, "ss_gv", [128, 4, 128], F32, 2)
    sqr = Ring(stack, sb, "ss_sq", [128, 4, 128], F32, 2)
    rsr = Ring(stack, sb, "ss_rs", [128, 128], F32, 2)
    pbc = Ring(stack, ps, "ss_pbc", [128, 1024], F32, 1)
    pm = Ring(stack, ps, "ss_pm", [128, 512], F32, 3)
    pyd = Ring(stack, ps, "ss_pyd", [128, 512], F32, 1)
    pyo = Ring(stack, ps, "ss_pyo", [128, 512], F32, 1)
    pst = Ring(stack, ps, "ss_pst", [128, 512], F32, 1)

    def block_prologue(b):
        t0 = b * TB
        raw, rk = rawr.next()
        if b == 0:
            P.pool(lambda e, raw=raw: e.memset(raw[:, :, 0:3], 0.0), wr=[rk])
            P.dma(lambda e, raw=raw: e.dma_start(out=raw[:, :, 3:], in_=zv[:, 8:20, 0:TB]), wr=[rk])
        else:
            P.dma(lambda e, raw=raw, t0=t0: e.dma_start(out=raw[:], in_=zv[:, 8:20, t0 - 3:t0 + TB]), wr=[rk])
        z_, zk = zr.next()
        P.dma(lambda e, z_=z_, t0=t0: e.dma_start(out=z_[:], in_=zv[:, 0:8, t0:t0 + TB]), wr=[zk], q="q_pool")
        dt_, dtk = dtr.next()
        P.dma(lambda e, dt_=dt_, t0=t0: e.dma_start(
            out=dt_[:], in_=dt_in[t0:t0 + TB, :].rearrange("(j p) h -> p j h", p=128)), wr=[dtk], q="q_pool")
        P.act(lambda e, z_=z_: e.activation(out=z_[:], in_=z_[:], func=AF.Silu), rd=[zk], wr=[zk])
        acc, acck = accr.next()
        acks = [Tok(f"acc{k}") for k in range(12)]
        for w in range(4):
            for k in range(12):
                if w == 0:
                    P.dve(lambda e, k=k, w=w, raw=raw, acc=acc: e.tensor_scalar(
                        out=acc[:, k, :], in0=raw[:, k, w:w + TB], scalar1=cw[:, k, w:w + 1], scalar2=None,
                        op0=ALU.mult), rd=[rk, cst], wr=[acks[k]])
                else:
                    P.dve(lambda e, k=k, w=w, raw=raw, acc=acc: e.scalar_tensor_tensor(
                        out=acc[:, k, :], in0=raw[:, k, w:w + TB], scalar=cw[:, k, w:w + 1], in1=acc[:, k, :],
                        op0=ALU.mult, op1=ALU.add), rd=[rk, cst, acks[k]], wr=[acks[k]])
        xT, xTk = xTr.next()
        bcT, bcTk = bcTr.next()
        for k in range(12):
            if k < 8:
                P.act(lambda e, k=k, xT=xT, acc=acc: e.activation(out=xT[:, k, :], in_=acc[:, k, :], func=AF.Silu,
                                                                  bias=cb[:, k:k + 1], scale=1.0),
                      rd=[acks[k], cst], wr=[xTk])
            else:
                P.act(lambda e, k=k, bcT=bcT, acc=acc: e.activation(out=bcT[:, k - 8, :], in_=acc[:, k, :],
                                                                    func=AF.Silu, bias=cb[:, k:k + 1], scale=1.0),
                      rd=[acks[k], cst], wr=[bcTk])
        oT, oTk = oTr.next()
        return dict(t0=t0, z_=z_, zk=zk, dt_=dt_, dtk=dtk, xT=xT, xTk=xTk, bcT=bcT, bcTk=bcTk, oT=oT, oTk=oTk)

    def front(B_, j, g):
        z_, zk, dt_, dtk, xT, xTk, bcT, bcTk = (B_[k_] for k_ in ('z_', 'zk', 'dt_', 'dtk', 'xT', 'xTk', 'bcT', 'bcTk'))
        cs = slice(j * 128, (j + 1) * 128)
        px, pxk = pm.next()
        for f in range(4):
            P.pe(lambda e, px=px, f=f, g=g, xT=xT, cs=cs: e.transpose(
                out=px[:, f * 128:(f + 1) * 128], in_=xT[:, g * 4 + f, cs], identity=idf[:]),
                rd=[xTk], wr=[pxk])
        xs, xsk = xsr.next()
        P.act(lambda e, xs=xs, px=px: e.copy(out=xs[:], in_=px[:]), rd=[pxk], wr=[xsk])
        pb, pbk = pm.next()
        pbb = pb[:].bitcast(BF16)
        P.pe(lambda e, pbb=pbb, bcT=bcT, g=g, cs=cs: e.transpose(out=pbb[:, 0:128], in_=bcT[:, g, cs],
                                                                 identity=idb[:]), rd=[bcTk], wr=[pbk])
        Bt, Btk = Btr.next()
        P.dve(lambda e, Bt=Bt, pbb=pbb: e.tensor_copy(out=Bt[:], in_=pbb[:, 0:128]), rd=[pbk], wr=[Btk])
        sm, smk = smr.next()
        kdA, kacol, keacol, keal, kdte = (Tok(n_) for n_ in ('dA', 'acol', 'eacol', 'eal', 'dte'))
        dtg = dt_[:, j, g * 8:(g + 1) * 8]
        P.dve(lambda e, sm=sm, dtg=dtg, g=g: e.tensor_tensor(out=sm[:, 0:8], in0=dtg,
                                                             in1=abc[:, g * 8:(g + 1) * 8], op=ALU.mult),
              rd=[dtk, cst], wr=[smk, kdA])
        pa, pak = pm.next()
        P.pe(lambda e, pa=pa, sm=sm: e.matmul(pa[:, 0:8], lhsT=tri[:], rhs=sm[:, 0:8], start=True, stop=True),
             rd=[kdA, cst], wr=[pak])
        P.act(lambda e, sm=sm, pa=pa: e.copy(out=sm[:, 8:16], in_=pa[:, 0:8]), rd=[pak, smk], wr=[kacol])
        P.act(lambda e, sm=sm, pa=pa: e.activation(out=sm[:, 16:24], in_=pa[:, 0:8], func=AF.Exp),
              rd=[pak, smk], wr=[keacol])
        rb, rbk = rbr.next()
        P.dve(lambda e, rb=rb, sm=sm: e.tensor_tensor(
            out=rb[:], in0=tri[:].unsqueeze(1).to_broadcast([128, 8, 128]),
            in1=sm[:, 0:8].unsqueeze(2).to_broadcast([128, 8, 128]), op=ALU.mult),
            rd=[kdA, cst], wr=[rbk])
        bc, bck = pbc.next()
        for hh in range(2):
            P.pe(lambda e, bc=bc, rb=rb, hh=hh: e.matmul(
                bc[:, hh * 512:(hh + 1) * 512], lhsT=ones_f[:],
                rhs=rb[:, hh * 4:(hh + 1) * 4, :].rearrange("p h l -> p (h l)"), start=True, stop=True),
                rd=[rbk, ck], wr=[bck])
        bc3 = bc[:].rearrange("p (h l) -> p h l", l=128)
        seg, segk = segr.next()
        for h in range(8):
            P.dve(lambda e, seg=seg, bc3=bc3, sm=sm, h=h: e.tensor_scalar(
                out=seg[:, h, :], in0=bc3[:, h, :], scalar1=sm[:, 8 + h:9 + h], scalar2=0.0,
                op0=ALU.subtract, op1=ALU.min), rd=[bck, kacol], wr=[segk])
        eb, ebk = ebr.next()
        P.act(lambda e, seg=seg, eb=eb: e.activation(out=eb[:], in_=seg[:], func=AF.Exp), rd=[segk], wr=[ebk])
        P.act(lambda e, sm=sm, bc3=bc3: e.activation(out=sm[:, 24:32], in_=bc3[:, :, 127], func=AF.Exp),
              rd=[bck, smk], wr=[keal])
        P.dve(lambda e, sm=sm, bc3=bc3: e.tensor_tensor(out=sm[:, 32:40], in0=bc3[:, :, 127],
                                                        in1=sm[:, 8:16], op=ALU.subtract),
              rd=[bck, kacol, smk], wr=[kdte])
        P.act(lambda e, sm=sm: e.activation(out=sm[:, 32:40], in_=sm[:, 32:40], func=AF.Exp),
              rd=[kdte], wr=[kdte])
        pc, pck = pm.next()
        P.pe(lambda e, pc=pc, bcT=bcT, g=g, cs=cs: e.matmul(
            pc[:, 0:128], lhsT=bcT[:, g, cs], rhs=bcT[:, 2 + g, cs], start=True, stop=True),
            rd=[bcTk], wr=[pck])
        cbm, cbmk = cbmr.next()
        P.dve(lambda e, cbm=cbm, pc=pc: e.tensor_tensor(out=cbm[:], in0=pc[:, 0:128], in1=tri[:], op=ALU.mult),
              rd=[pck, cst], wr=[cbmk])
        G, Gk = Gr.next()
        P.dve(lambda e, G=G, eb=eb, cbm=cbm: e.tensor_tensor(
            out=G[:], in0=eb[:], in1=cbm[:].unsqueeze(1).to_broadcast([128, 8, 128]), op=ALU.mult),
            rd=[ebk, cbmk], wr=[Gk])
        x32, x32k = x32r.next()
        xs3 = lambda t: t[:].rearrange("p (h d) -> p h d", d=64)
        P.pool(lambda e, x32=x32, xs=xs, dtg=dtg: e.tensor_tensor(
            out=xs3(x32), in0=xs3(xs), in1=dtg.unsqueeze(2).to_broadcast([128, 8, 64]), op=ALU.mult),
            rd=[xsk, dtk], wr=[x32k])
        xb, xbk = xbr.next()
        P.act(lambda e, xb=xb, x32=x32: e.copy(out=xb[:], in_=x32[:]), rd=[x32k], wr=[xbk])
        xe, xek = xer.next()
        P.dve(lambda e, xe=xe, x32=x32, sm=sm: e.tensor_tensor(
            out=xs3(xe), in0=xs3(x32), in1=sm[:, 32:40].unsqueeze(2).to_broadcast([128, 8, 64]),
            op=ALU.mult), rd=[x32k, kdte], wr=[xek])
        xd, xdk = xdr.next()
        P.pool(lambda e, xd=xd, xs=xs, g=g: e.tensor_tensor(
            out=xs3(xd), in0=xs3(xs), in1=Dbc[:, g * 8:(g + 1) * 8, :], op=ALU.mult),
            rd=[xsk, cst], wr=[xdk])
        return dict(B_=B_, j=j, g=g, cs=cs, sm=sm, smk=smk, keacol=keacol, keal=keal, Bt=Bt, Btk=Btk, G=G, Gk=Gk, xb=xb, xbk=xbk, xe=xe, xek=xek,
                    xd=xd, xdk=xdk, xs3=xs3)

    def back(F_):
        keacol, keal = F_['keacol'], F_['keal']
        B_, j, g, cs, sm, smk, Bt, Btk, G, Gk, xb, xbk, xe, xek, xd, xdk, xs3 = (F_[k_] for k_ in (
            'B_', 'j', 'g', 'cs', 'sm', 'smk', 'Bt', 'Btk', 'G', 'Gk', 'xb', 'xbk', 'xe', 'xek', 'xd', 'xdk', 'xs3'))
        z_, zk, bcT, bcTk, oT, oTk = (B_[k_] for k_ in ('z_', 'zk', 'bcT', 'bcTk', 'oT', 'oTk'))
        yd, ydk = pyd.next()
        for h in range(8):
            P.pe(lambda e, yd=yd, G=G, xb=xb, h=h: e.matmul(
                yd[:, h * 64:(h + 1) * 64], lhsT=G[:, h, :], rhs=xb[:, h * 64:(h + 1) * 64],
                start=True, stop=True), rd=[Gk, xbk], wr=[ydk])
        yo, yok = pyo.next()
        P.pe(lambda e, yo=yo, bcT=bcT, g=g, cs=cs: e.matmul(
            yo[:], lhsT=bcT[:, 2 + g, cs], rhs=Sbf[g][:], start=True, stop=True),
            rd=[bcTk, Sbk[g]], wr=[yok])
        y1, y1k = y1r.next()
        P.dve(lambda e, y1=y1, yo=yo, sm=sm: e.tensor_tensor(
            out=xs3(y1), in0=yo[:].rearrange("p (h d) -> p h d", d=64),
            in1=sm[:, 16:24].unsqueeze(2).to_broadcast([128, 8, 64]), op=ALU.mult),
            rd=[yok, keacol], wr=[y1k])
        P.dve(lambda e, y1=y1, yd=yd: e.tensor_tensor(out=y1[:], in0=y1[:], in1=yd[:], op=ALU.add),
              rd=[y1k, ydk], wr=[y1k])
        P.dve(lambda e, y1=y1, xd=xd: e.tensor_tensor(out=y1[:], in0=y1[:], in1=xd[:], op=ALU.add),
              rd=[y1k, xdk], wr=[y1k])
        st_, stk = pst.next()
        P.pe(lambda e, st_=st_, Bt=Bt, xe=xe: e.matmul(st_[:], lhsT=Bt[:], rhs=xe[:], start=True, stop=True),
             rd=[Btk, xek], wr=[stk])
        P.dve(lambda e, g=g, sm=sm: e.tensor_tensor(
            out=xs3(S32[g]), in0=xs3(S32[g]), in1=sm[:, 24:32].unsqueeze(2).to_broadcast([128, 8, 64]),
            op=ALU.mult), rd=[Sk[g], keal], wr=[Sk[g]])
        P.dve(lambda e, g=g, st_=st_: e.tensor_tensor(out=S32[g][:], in0=S32[g][:], in1=st_[:], op=ALU.add),
              rd=[Sk[g], stk], wr=[Sk[g]])
        P.act(lambda e, g=g: e.copy(out=Sbf[g][:], in_=S32[g][:]), rd=[Sk[g]], wr=[Sbk[g]])
        py, pyk = pm.next()
        for f in range(4):
            P.pe(lambda e, py=py, y1=y1, f=f: e.transpose(
                out=py[:, f * 128:(f + 1) * 128], in_=y1[:, f * 128:(f + 1) * 128], identity=idf[:]),
                rd=[y1k], wr=[pyk])
        gv, gvk = gvr.next()
        P.dve(lambda e, gv=gv, py=py, z_=z_, g=g, cs=cs: e.tensor_tensor(
            out=gv[:], in0=py[:].rearrange("p (f t) -> p f t", t=128), in1=z_[:, g * 4:(g + 1) * 4, cs],
            op=ALU.mult), rd=[pyk, zk], wr=[gvk])
        sq, sqk = sqr.next()
        P.act(lambda e, sq=sq, gv=gv: e.activation(out=sq[:], in_=gv[:], func=AF.Square), rd=[gvk], wr=[sqk])
        pq, pqk = pm.next()
        for f in range(4):
            P.pe(lambda e, pq=pq, sq=sq, f=f: e.matmul(pq[:, 0:128], lhsT=ones_f[:], rhs=sq[:, f, :],
                                                       start=(f == 0), stop=(f == 3)),
                 rd=[sqk, ck], wr=[pqk])
        rs, rsk = rsr.next()
        P.act(lambda e, rs=rs, pq=pq: e.activation(out=rs[:], in_=pq[:, 0:128], func=AF.Ln, bias=EPS,
                                                   scale=1.0 / 512), rd=[pqk], wr=[rsk])
        P.act(lambda e, rs=rs: e.activation(out=rs[:], in_=rs[:], func=AF.Exp, scale=-0.5), rd=[rsk], wr=[rsk])
        for f in range(4):
            P.dve(lambda e, oT=oT, gv=gv, rs=rs, f=f, g=g, cs=cs: e.scalar_tensor_tensor(
                out=oT[:, g * 4 + f, cs], in0=gv[:, f, :], scalar=nw[:, g * 4 + f:g * 4 + f + 1], in1=rs[:],
                op0=ALU.mult, op1=ALU.mult), rd=[gvk, rsk, cst], wr=[oTk])

    def block_epilogue(B_):
        oT, oTk, t0 = B_['oT'], B_['oTk'], B_['t0']
        P.dma(lambda e, oT=oT, t0=t0: e.dma_start(
            out=yT_out.rearrange("(k p) t -> p k t", p=128)[:, :, t0:t0 + TB], in_=oT[:]), rd=[oTk])

    passes = [(b, j, g) for b in range(NB) for j in range(TB // 128) for g in range(2)]
    blocks = {}
    pend = []
    DEPTH_F = 1

    def retire():
        F0 = pend.pop(0)
        back(F0)
        pb, pj, pg = F0["key"]
        if (pj, pg) == (TB // 128 - 1, 1):
            block_epilogue(blocks.pop(pb))

    for (b, j, g) in passes:
        if b not in blocks:
            blocks[b] = block_prologue(b)
        F_ = front(blocks[b], j, g)
        F_["key"] = (b, j, g)
        pend.append(F_)
        if len(pend) > DEPTH_F:
            retire()
    while pend:
        retire()


NCORES = 8
BATCH = 2
TC = BATCH * SEQ // NCORES
DEPTH = 4


def lay_colchunk(W):
    K, N = W.shape
    return np.ascontiguousarray(
        W.reshape(K // 128, 128, N // 128, 128).transpose(2, 1, 0, 3).reshape(N // 128, 128, K))


def lay_slab(W):
    K, N = W.shape
    return np.ascontiguousarray(W.reshape(K // 128, 128, N // 512, 512).transpose(2, 1, 0, 3))


def ssd_core_inputs(inp, j, gl):
    w_in = inp["ssd_w_in"][j]
    cols = np.concatenate([
        np.arange(gl * 1024, (gl + 1) * 1024),
        4096 + np.arange(gl * 1024, (gl + 1) * 1024),
        8192 + np.arange(gl * 256, (gl + 1) * 256),
        9216 + np.arange(gl * 256, (gl + 1) * 256)])
    ch = cols[1024:] - 4096
    dtc = 10240 + np.arange(gl * 16, (gl + 1) * 16)
    return {
        "s_w": lay_colchunk(w_in[:, cols]),
        "s_wdt": np.ascontiguousarray(w_in[:, dtc].reshape(KC_D, 128, 16).transpose(1, 0, 2)),
        "s_dtb": np.ascontiguousarray(inp["ssd_dt_bias"][j, gl * 16:(gl + 1) * 16]),
        "s_cw": np.ascontiguousarray(inp["ssd_conv_w"][j][:, ch].reshape(4, 12, 128).transpose(2, 1, 0)),
        "s_cb": np.ascontiguousarray(inp["ssd_conv_b"][j][ch].reshape(12, 128).T),
        "s_alog": np.ascontiguousarray(inp["ssd_a_log"][j, gl * 16:(gl + 1) * 16]),
        "s_dsk": np.ascontiguousarray(inp["ssd_d"][j, gl * 16:(gl + 1) * 16]),
        "s_nw": np.ascontiguousarray(inp["ssd_norm"][j, gl * 1024:(gl + 1) * 1024].reshape(8, 128).T),
    }


def attn_core_inputs(inp, j, gl):
    w = inp["da_w_qkv"][j]
    cols = np.concatenate([which * D + np.arange(gl * 512, (gl + 1) * 512) for which in range(3)])
    return {
        "a_w": lay_colchunk(w[:, cols]),
        "a_lam": np.ascontiguousarray(np.stack([inp["da_lambda_q1"][j], inp["da_lambda_k1"][j],
                                                inp["da_lambda_q2"][j], inp["da_lambda_k2"][j]])),
        "a_sub": np.ascontiguousarray(inp["da_subln"][j]),
    }


def lambda_init(i):
    return 0.8 - 0.6 * math.exp(-0.3 * i)


def _new_nc():
    _UID[0] = 0
    return bass.Bass("TRN2", target_bir_lowering=False)


def build_first():
    import contextlib
    nc = _new_nc()
    x = nc.dram_tensor("x", [TC, D], F32, kind="ExternalInput").ap()
    wpre = nc.dram_tensor("wpre", [D], F32, kind="ExternalInput").ap()
    xo = nc.dram_tensor("xo", [TC, D], F32, kind="ExternalOutput").ap()
    hT = nc.dram_tensor("hT", [D, TC], BF16, kind="ExternalOutput").ap()
    with contextlib.ExitStack() as st0:
        P = Prog(nc, st0)
        with contextlib.ExitStack() as st:
            idf, idb, _ = make_ident(nc, P, st)
            phase_norm(nc, P, st, TC, x, None, None, wpre, xo, hT, idb)
            P.emit()
    return nc


def hT_blk_fn(hTg):
    v = hTg.rearrange("(r k p) t -> p r k t", r=4, p=128)
    return lambda b: v[:, b // 4, :, (b % 4) * 512:(b % 4 + 1) * 512]


def hT_blk_fn_bm(hTg):
    v = hTg.rearrange("(b p) (k t) -> p b k t", p=128, t=512)
    return lambda b: v[:, b, :, :]


def build_ssd():
    import contextlib
    nc = _new_nc()
    hTg = nc.dram_tensor("hTg", [4 * D, TC], BF16, kind="ExternalInput").ap()
    w = nc.dram_tensor("s_w", [NZX, 128, D], F32, kind="ExternalInput").ap()
    wdt = nc.dram_tensor("s_wdt", [128, KC_D, 16], F32, kind="ExternalInput").ap()
    dtb = nc.dram_tensor("s_dtb", [16], F32, kind="ExternalInput").ap()
    cw = nc.dram_tensor("s_cw", [128, 12, 4], F32, kind="ExternalInput").ap()
    cb = nc.dram_tensor("s_cb", [128, 12], F32, kind="ExternalInput").ap()
    alog = nc.dram_tensor("s_alog", [16], F32, kind="ExternalInput").ap()
    dsk = nc.dram_tensor("s_dsk", [16], F32, kind="ExternalInput").ap()
    nw = nc.dram_tensor("s_nw", [128, 8], F32, kind="ExternalInput").ap()
    zx = nc.dram_tensor("zx", [NZX * 128, SEQ], F32).ap()
    dt = nc.dram_tensor("dt", [SEQ, 16], F32).ap()
    yT = nc.dram_tensor("yT", [1024, SEQ], BF16, kind="ExternalOutput").ap()
    with contextlib.ExitStack() as st0:
        P = Prog(nc, st0)
        with contextlib.ExitStack() as st:
            phase_ssd_in(nc, P, st, hT_blk_fn(hTg), w, wdt, dtb, zx, dt)
            P.emit()
        with contextlib.ExitStack() as st:
            idf, idb, _ = make_ident(nc, P, st)
            C = make_consts(nc, P, st)
            phase_ssd_scan(nc, P, st, zx, dt, cw, cb, alog, dsk, nw, yT, idf, idb, C)
            P.emit()
    return nc


def build_attn(lam_init):
    import contextlib
    nc = _new_nc()
    hTg = nc.dram_tensor("hTg", [4 * D, TC], BF16, kind="ExternalInput").ap()
    w = nc.dram_tensor("a_w", [12, 128, D], F32, kind="ExternalInput").ap()
    lam = nc.dram_tensor("a_lam", [4, 64], F32, kind="ExternalInput").ap()
    sub = nc.dram_tensor("a_sub", [128], F32, kind="ExternalInput").ap()
    oT = nc.dram_tensor("yT", [512, SEQ], BF16, kind="ExternalOutput").ap()
    with contextlib.ExitStack() as st0:
        P = Prog(nc, st0)
        with contextlib.ExitStack() as st:
            idf, idb, _ = make_ident(nc, P, st)
            C = make_consts(nc, P, st)
            qTd = nc.dram_tensor("sc_qT", [512, SEQ], BF16).ap()
            kTd = nc.dram_tensor("sc_kT", [512, SEQ], BF16).ap()
            Vd = nc.dram_tensor("sc_V", [4, 128, SEQ // 128, 128], BF16).ap()
            negMd = nc.dram_tensor("sc_negM", [128, 8], F32).ap()
            phase_attn_proj(nc, P, st, hT_blk_fn(hTg), w, qTd, kTd, Vd, negMd, idf, idb, C)
            P.emit()
        with contextlib.ExitStack() as st:
            idf, idb, _ = make_ident(nc, P, st)
            C = make_consts(nc, P, st)
            phase_attn_core(nc, P, st, qTd, kTd, Vd, negMd, lam, sub, lam_init, oT, idf, idb, C)
            P.emit()
    return nc


def emit_token_phases(nc, P, K, AT, wo, x, npost, nfpre, nfpost, npre_next, wg, wu, wd, xo, hTn, scr):
    import contextlib
    with contextlib.ExitStack() as st:
        phase_mmT(nc, P, st, TC, K, AT, wo, scr["m"], pfx="mo")
        P.emit()
    with contextlib.ExitStack() as st:
        idf, idb, _ = make_ident(nc, P, st)
        phase_norm(nc, P, st, TC, x, scr["m"], npost, nfpre, scr["x1"], scr["h2T"], idb)
        P.emit()
    with contextlib.ExitStack() as st:
        phase_ffn_gu(nc, P, st, TC, scr["h2T"], wg, wu, scr["aT"])
        P.emit()
    with contextlib.ExitStack() as st:
        phase_mmT(nc, P, st, TC, D_FF, scr["aT"], wd, scr["y2"], pfx="md")
        P.emit()
    with contextlib.ExitStack() as st:
        idf, idb, _ = make_ident(nc, P, st)
        phase_norm(nc, P, st, TC, scr["x1"], scr["y2"], nfpost, npre_next, xo, hTn, idb)
        P.emit()


def build_tok(K, last):
    import contextlib
    nc = _new_nc()
    AT = nc.dram_tensor("AT", [K, TC], BF16, kind="ExternalInput").ap()
    wo = nc.dram_tensor("wo", [4, 128, K // 128, 512], F32, kind="ExternalInput").ap()
    x = nc.dram_tensor("x", [TC, D], F32, kind="ExternalInput").ap()
    npost = nc.dram_tensor("npost", [D], F32, kind="ExternalInput").ap()
    nfpre = nc.dram_tensor("nfpre", [D], F32, kind="ExternalInput").ap()
    nfpost = nc.dram_tensor("nfpost", [D], F32, kind="ExternalInput").ap()
    npre_next = None if last else nc.dram_tensor("npre_next", [D], F32, kind="ExternalInput").ap()
    wg = nc.dram_tensor("wg", [KC_F, 128, D], F32, kind="ExternalInput").ap()
    wu = nc.dram_tensor("wu", [KC_F, 128, D], F32, kind="ExternalInput").ap()
    wd = nc.dram_tensor("wd", [4, 128, KC_F, 512], F32, kind="ExternalInput").ap()
    xo = nc.dram_tensor("xo", [TC, D], F32, kind="ExternalOutput").ap()
    hTn = None if last else nc.dram_tensor("hT", [D, TC], BF16, kind="ExternalOutput").ap()
    scr = {"m": nc.dram_tensor("sc_m", [TC, D], F32).ap(), "x1": nc.dram_tensor("sc_x1", [TC, D], F32).ap(),
           "h2T": nc.dram_tensor("sc_h2T", [D, TC], BF16).ap(), "aT": nc.dram_tensor("sc_aT", [D_FF, TC], BF16).ap(),
           "y2": nc.dram_tensor("sc_y2", [TC, D], F32).ap()}
    with contextlib.ExitStack() as st0:
        P = Prog(nc, st0)
        emit_token_phases(nc, P, K, AT, wo, x, npost, nfpre, nfpost, npre_next, wg, wu, wd, xo, hTn, scr)
    return nc


def kernel_multilaunch(**inp):
    inp = {k: np.asarray(v) for k, v in inp.items()}
    cores = list(range(NCORES))
    xs = np.ascontiguousarray(inp["x"].reshape(NCORES, TC, D))
    res = run_bass_kernel_spmd(build_first(), [{"x": xs[c], "wpre": inp["norm_mix_pre"][0]} for c in cores],
                               core_ids=cores)
    xcur = [r["xo"] for r in res.results]
    hT = [np.asarray(r["hT"]) for r in res.results]
    for i in range(DEPTH):
        j = i // 2
        hTg = [np.concatenate(hT[4 * b:4 * b + 4], axis=0) for b in range(BATCH)]
        if i % 2 == 0:
            ncm = build_ssd()
            maps = [dict(ssd_core_inputs(inp, j, c % 4), hTg=hTg[c // 4]) for c in cores]
            K = 4096
            wo = lay_slab(inp["ssd_w_out"][j])
        else:
            ncm = build_attn(lambda_init(i))
            maps = [dict(attn_core_inputs(inp, j, c % 4), hTg=hTg[c // 4]) for c in cores]
            K = 2048
            wo = lay_slab(inp["da_w_out"][j])
        res = run_bass_kernel_spmd(ncm, maps, core_ids=cores)
        yT = [np.asarray(r["yT"]) for r in res.results]
        yall = [np.concatenate(yT[4 * b:4 * b + 4], axis=0) for b in range(BATCH)]
        last = i == DEPTH - 1
        wg, wu, wd = lay_colchunk(inp["ffn_w_gate"][i]), lay_colchunk(inp["ffn_w_up"][i]), lay_slab(inp["ffn_w_down"][i])
        maps = []
        for c in cores:
            m = {"AT": np.ascontiguousarray(yall[c // 4][:, (c % 4) * TC:(c % 4 + 1) * TC]), "wo": wo, "x": xcur[c],
                 "npost": inp["norm_mix_post"][i], "nfpre": inp["norm_ffn_pre"][i], "nfpost": inp["norm_ffn_post"][i],
                 "wg": wg, "wu": wu, "wd": wd}
            if not last:
                m["npre_next"] = inp["norm_mix_pre"][i + 1]
            maps.append(m)
        res = run_bass_kernel_spmd(build_tok(K, last), maps, core_ids=cores)
        xcur = [r["xo"] for r in res.results]
        if not last:
            hT = [np.asarray(r["hT"]) for r in res.results]
    out = np.stack([np.asarray(a) for a in xcur]).reshape(BATCH, SEQ, D).astype(np.float32)
    return out


RG4 = [[0, 1, 2, 3], [4, 5, 6, 7]]
RG8 = [list(range(NCORES))]


def phase_select(nc, P, stack, g8, bsel_in, hTg):
    sb = nc.sbuf_tensor
    w = stack.enter_context(sb(U("sel_w"), [128, 2], F32))
    wk = Tok("sel_w")
    P.dma(lambda e: e.dma_start(out=w[:], in_=bsel_in.partition_broadcast(128)), wr=[wk])
    ar = Ring(stack, sb, "sel_a", [128, KC_D, 512], BF16, 2)
    br = Ring(stack, sb, "sel_b", [128, KC_D, 512], BF16, 2)
    orr = Ring(stack, sb, "sel_o", [128, KC_D, 512], BF16, 2)
    v8 = g8.rearrange("(r k p) t -> p r k t", r=8, p=128)
    vo = hTg.rearrange("(r k p) t -> p r k t", r=4, p=128)
    for r in range(4):
        for tb in range(TC // 512):
            ts = slice(tb * 512, (tb + 1) * 512)
            a, ak = ar.next()
            b, bk = br.next()
            P.dma(lambda e, a=a, r=r, ts=ts: e.dma_start(out=a[:], in_=v8[:, r, :, ts]), wr=[ak])
            P.dma(lambda e, b=b, r=r, ts=ts: e.dma_start(out=b[:], in_=v8[:, 4 + r, :, ts]), wr=[bk], q="q_pool")
            o, ok = orr.next()
            P.dve(lambda e, o=o, a=a: e.tensor_scalar(out=o[:], in0=a[:], scalar1=w[:, 0:1], scalar2=None,
                                                      op0=ALU.mult), rd=[ak, wk], wr=[ok])
            P.dve(lambda e, o=o, b=b: e.scalar_tensor_tensor(out=o[:], in0=b[:], scalar=w[:, 1:2], in1=o[:],
                                                             op0=ALU.mult, op1=ALU.add), rd=[bk, wk, ok], wr=[ok])
            P.dma(lambda e, o=o, r=r, ts=ts: e.dma_start(out=vo[:, r, :, ts], in_=o[:]), rd=[ok])


def build_fused():
    import contextlib
    nc = _new_nc()
    dt_ = nc.dram_tensor
    ext = lambda n, s, d=F32: dt_(n, s, d, kind="ExternalInput").ap()
    x = ext("x", [TC, D])
    sel4 = ext("sel4", [4])
    nmpre, nmpost = ext("nmpre", [DEPTH, D]), ext("nmpost", [DEPTH, D])
    nfpre, nfpost = ext("nfpre", [DEPTH, D]), ext("nfpost", [DEPTH, D])
    ffn = [(ext(f"wg{i}", [KC_F, 128, D]), ext(f"wu{i}", [KC_F, 128, D]), ext(f"wd{i}", [4, 128, KC_F, 512]))
           for i in range(DEPTH)]
    ssd = [dict(w=ext(f"s_w{j}", [NZX, 128, D]), wdt=ext(f"s_wdt{j}", [128, KC_D, 16]), dtb=ext(f"s_dtb{j}", [16]),
                cw=ext(f"s_cw{j}", [128, 12, 4]), cb=ext(f"s_cb{j}", [128, 12]), alog=ext(f"s_alog{j}", [16]),
                dsk=ext(f"s_dsk{j}", [16]), nw=ext(f"s_nw{j}", [128, 8]), wo=ext(f"s_wo{j}", [4, 128, 8, 512]))
           for j in range(2)]
    att = [dict(w=ext(f"a_w{j}", [12, 128, D]), lam=ext(f"a_lam{j}", [4, 64]), sub=ext(f"a_sub{j}", [128]),
                wo=ext(f"a_wo{j}", [4, 128, 4, 512])) for j in range(2)]
    xo = dt_("xo", [TC, D], F32, kind="ExternalOutput").ap()
    scr = lambda n, s, d=F32: dt_(n, s, d).ap()
    hT4 = scr("sc_hT4", [16 * 128, KC_D * 512], BF16)
    hTg = scr("sc_hTg", [16 * 128, KC_D * 512], BF16)
    zx = scr("sc_zx", [NZX * 128, SEQ])
    dtt = scr("sc_dt", [SEQ, 16])
    yT = scr("sc_yT", [1024, SEQ], BF16)
    mpA, mpB = scr("sc_mpA", [SEQ, 1024]), scr("sc_mpB", [SEQ, 1024])
    mA, mB = scr("sc_mA", [TC, 1024]), scr("sc_mB", [TC, 1024])
    qTd = scr("sc_qT", [512, SEQ], BF16)
    kTd = scr("sc_kT", [512, SEQ], BF16)
    Vd = scr("sc_V", [4, 128, SEQ // 128, 128], BF16)
    negMd = scr("sc_negM", [128, 8])
    xa = scr("sc_xa", [TC, D])
    x1 = scr("sc_x1", [TC, D])
    h2T = scr("sc_h2T", [D, TC], BF16)
    aT = scr("sc_aT", [D_FF, TC], BF16)
    y2 = scr("sc_y2", [TC, D])
    with contextlib.ExitStack() as st0:
        P = Prog(nc, st0)
        with contextlib.ExitStack() as st:
            idf, idb, _ = make_ident(nc, P, st)
            phase_norm(nc, P, st, TC, x, None, None, nmpre[0], None, hT4, idb, sel4=sel4)
            P.emit(reorder=True)
        xcur = x
        for i in range(DEPTH):
            j = i // 2
            last = i == DEPTH - 1
            ptoks = [Tok(f"hTg{pc}") for pc in range(8)]
            for pc in range(8):
                rs_ = slice(pc * 256, (pc + 1) * 256)
                P.cc(lambda e, rs_=rs_: e.collective_compute("AllReduce", ALU.add, replica_groups=RG4,
                                                             ins=[hT4[rs_, :]], outs=[hTg[rs_, :]]),
                     wr=[ptoks[pc]])
            hT_tok = lambda b, ptoks=ptoks: ptoks[b // 2]
            if i % 2 == 0:
                s = ssd[j]
                with contextlib.ExitStack() as st:
                    phase_ssd_in(nc, P, st, hT_blk_fn_bm(hTg), s["w"], s["wdt"], s["dtb"], zx, dtt, hT_tok=hT_tok)
                    P.emit(reorder=True)
                with contextlib.ExitStack() as st:
                    idf, idb, _ = make_ident(nc, P, st)
                    C = make_consts(nc, P, st)
                    phase_ssd_scan(nc, P, st, zx, dtt, s["cw"], s["cb"], s["alog"], s["dsk"], s["nw"], yT,
                                   idf, idb, C)
                    P.emit(reorder=True)
                K, yT_use, wo = 1024, yT, s["wo"]
            else:
                a = att[j]
                with contextlib.ExitStack() as st:
                    idf, idb, _ = make_ident(nc, P, st)
                    C = make_consts(nc, P, st)
                    phase_attn_proj(nc, P, st, hT_blk_fn_bm(hTg), a["w"], qTd, kTd, Vd, negMd, idf, idb, C,
                                    hT_tok=hT_tok)
                    P.emit(reorder=True)
                with contextlib.ExitStack() as st:
                    idf, idb, _ = make_ident(nc, P, st)
                    C = make_consts(nc, P, st)
                    phase_attn_core(nc, P, st, qTd, kTd, Vd, negMd, a["lam"], a["sub"], lambda_init(i),
                                    yT[0:512, :], idf, idb, C)
                    P.emit(reorder=True)
                K, yT_use, wo = 512, yT[0:512, :], a["wo"]
            with contextlib.ExitStack() as st:
                htoks = ([], [])
                def rs_half(hf, htoks=htoks):
                    src, dst = (mpA, mA) if hf == 0 else (mpB, mB)
                    P.cc(lambda e: e.collective_compute("ReduceScatter", ALU.add, replica_groups=RG4,
                                                        ins=[src], outs=[dst], dma_qos="P2"), rd=list(htoks[hf]))
                phase_mmT(nc, P, st, SEQ, K, yT_use, wo, None, Th=2048, pfx="mo", halves=(mpA, mpB), htoks=htoks,
                          half_done=rs_half)
                P.emit(reorder=True)
            with contextlib.ExitStack() as st:
                idf, idb, _ = make_ident(nc, P, st)
                phase_norm(nc, P, st, TC, xcur, (mA, mB), nmpost[i], nfpre[i], x1, h2T, idb)
                P.emit(reorder=True)
            wg, wu, wd = ffn[i]
            with contextlib.ExitStack() as st:
                phase_ffn_gu(nc, P, st, TC, h2T, wg, wu, aT)
                P.emit(reorder=True)
            with contextlib.ExitStack() as st:
                phase_mmT(nc, P, st, TC, D_FF, aT, wd, y2, pfx="md")
                P.emit(reorder=True)
            with contextlib.ExitStack() as st:
                idf, idb, _ = make_ident(nc, P, st)
                phase_norm(nc, P, st, TC, x1, y2, nfpost[i], None if last else nmpre[i + 1],
                           xo if last else xa, None if last else hT4, idb, sel4=None if last else sel4)
                P.emit(reorder=True)
            xcur = xa
    return nc


def fused_inputs(inp):
    inp = {k: np.asarray(v) for k, v in inp.items()}
    xs = np.ascontiguousarray(inp["x"].reshape(NCORES, TC, D))
    shared = {"nmpre": inp["norm_mix_pre"], "nmpost": inp["norm_mix_post"],
              "nfpre": inp["norm_ffn_pre"], "nfpost": inp["norm_ffn_post"]}
    for i in range(DEPTH):
        shared[f"wg{i}"] = lay_colchunk(inp["ffn_w_gate"][i])
        shared[f"wu{i}"] = lay_colchunk(inp["ffn_w_up"][i])
        shared[f"wd{i}"] = lay_slab(inp["ffn_w_down"][i])
    maps = []
    for c in range(NCORES):
        gl = c % 4
        m = dict(shared)
        m["x"] = xs[c]
        m["sel4"] = np.eye(4, dtype=np.float32)[c % 4]
        for j in range(2):
            for k, v in ssd_core_inputs(inp, j, gl).items():
                m[f"{k}{j}"] = v
            m[f"s_wo{j}"] = lay_slab(inp["ssd_w_out"][j][gl * 1024:(gl + 1) * 1024])
            for k, v in attn_core_inputs(inp, j, gl).items():
                m[f"{k}{j}"] = v
            m[f"a_wo{j}"] = lay_slab(inp["da_w_out"][j][gl * 512:(gl + 1) * 512])
        maps.append(m)
    return maps


def kernel_fused(**inp):
    maps = fused_inputs(inp)
    res = run_bass_kernel_spmd(build_fused(), maps, core_ids=list(range(NCORES)))
    return np.stack([np.asarray(r["xo"]) for r in res.results]).reshape(BATCH, SEQ, D).astype(np.float32)


def kernel(**inputs):
    return kernel_fused(**inputs)
```

```python
import math
import numpy as np
import ml_dtypes
import concourse.bass as bass
import concourse.mybir as mybir
from concourse.bass_utils import run_bass_kernel_spmd

F32 = mybir.dt.float32
BF16 = mybir.dt.bfloat16
AF = mybir.ActivationFunctionType
ALU = mybir.AluOpType
AX = mybir.AxisListType

NDMA_SLOTS = 6


class Tok:
    __slots__ = ("w", "r", "ra", "name")

    def __init__(self, name=""):
        self.w = {}
        self.r = {}
        self.ra = []
        self.name = name


class Prog:
    COMPUTE = ("pe", "act", "dve", "pool", "cc")
    QUEUES = ("q_sp", "q_pool", "q_act")
    ISSUE = {"pe": "pe", "act": "act", "dve": "dve", "pool": "pool", "cc": "pool",
             "q_sp": "sp", "q_pool": "pool", "q_act": "act"}

    def __init__(self, nc, stack):
        self.nc = nc
        self.ops = []
        self.base = 0
        self.cnt = {e: 0 for e in self.COMPUTE}
        self.dcnt = {q: 0 for q in self.QUEUES}
        self.waited = {e: {} for e in ("pe", "act", "dve", "pool", "sp")}
        self.sems = {}
        for e in self.COMPUTE:
            self.sems[e] = stack.enter_context(nc.semaphore(f"s_{e}"))
        for q in self.QUEUES:
            for s in range(NDMA_SLOTS):
                self.sems[(q, s)] = stack.enter_context(nc.semaphore(f"s_{q}{s}"))

    def op(self, eng, fn, rd=(), wr=()):
        self.ops.append((eng, fn, tuple(rd), tuple(wr)))

    def pe(self, fn, rd=(), wr=()):
        self.op("pe", fn, rd, wr)

    def act(self, fn, rd=(), wr=()):
        self.op("act", fn, rd, wr)

    def dve(self, fn, rd=(), wr=()):
        self.op("dve", fn, rd, wr)

    def pool(self, fn, rd=(), wr=()):
        self.op("pool", fn, rd, wr)

    def dma(self, fn, rd=(), wr=(), q="q_sp"):
        self.op(q, fn, rd, wr)

    def cc(self, fn, rd=(), wr=()):
        self.op("cc", fn, rd, wr)

    COST = {"pe": 0.27, "act": 0.5, "dve": 0.55, "pool": 1.2, "cc": 0.1, "q_sp": 0.06, "q_pool": 0.3, "q_act": 0.06}
    DONE_LAT = {"q_sp": 2.5, "q_pool": 3.0, "q_act": 2.5, "cc": 100.0}

    def _schedule(self, ops, order_deps, W=16):
        n = len(ops)
        issue = [self.ISSUE[o[0]] for o in ops]
        per = {e: [] for e in ("pe", "act", "dve", "pool", "sp")}
        for i in range(n):
            per[issue[i]].append(i)
        head = {e: 0 for e in per}
        done = [False] * n
        fin = [0.0] * n
        tfree = {e: 0.0 for e in per}
        order = []
        ready_t = [None] * n
        while len(order) < n:
            best = None
            for e, lst in per.items():
                h = head[e]
                while h < len(lst) and done[lst[h]]:
                    h += 1
                head[e] = h
                cnt = 0
                k = h
                while k < len(lst) and cnt < W:
                    i = lst[k]
                    k += 1
                    if done[i]:
                        continue
                    cnt += 1
                    rt = ready_t[i]
                    if rt is None:
                        ok = True
                        rt = 0.0
                        for d in order_deps[i]:
                            if not done[d]:
                                ok = False
                                break
                            f = fin[d] + (0.0 if issue[d] == e and not ops[d][0].startswith("q_") else 0.25)
                            if f > rt:
                                rt = f
                        if not ok:
                            continue
                        ready_t[i] = rt
                    st = rt if rt > tfree[e] else tfree[e]
                    key = (st, i)
                    if best is None or key < best[0]:
                        best = (key, i, e)
            (st, _), i, e = best
            c = self.COST[ops[i][0]]
            tfree[e] = st + c
            fin[i] = st + c + self.DONE_LAT.get(ops[i][0], 0.0)
            done[i] = True
            order.append(i)
        return order

    def emit(self, reorder=False):
        nc = self.nc
        ops = self.ops
        n = len(ops)
        base = self.base
        deps = [None] * n
        odeps = [None] * n
        signals = [False] * n
        for i, (eng, fn, rd, wr) in enumerate(ops):
            gi = base + i
            is_dma = eng.startswith("q_")
            key = ("dma", gi) if is_dma else eng
            d = set()
            od = set()
            for t in rd:
                for k, j in t.w.items():
                    if j >= base:
                        od.add(j - base)
                    if k == key and key == "pe":
                        continue
                    if j >= base:
                        d.add(j - base)
            for t in wr:
                for k, j in t.w.items():
                    if j >= base:
                        od.add(j - base)
                    if k == key:
                        continue
                    if j >= base:
                        d.add(j - base)
                for k, j in t.ra:
                    if j >= base:
                        od.add(j - base)
                    if k == key:
                        continue
                    if j >= base:
                        d.add(j - base)
            for t in rd:
                t.r[key] = gi
                t.ra.append((key, gi))
            for t in wr:
                t.w = {key: gi}
                t.r = {}
                t.ra = []
            od.discard(i)
            deps[i] = d
            odeps[i] = od
            for j in d:
                signals[j] = True
            if is_dma or eng == "cc":
                signals[i] = True
        if reorder and n > 2:
            order = self._schedule(ops, odeps)
            pos = [0] * n
            for p_, i in enumerate(order):
                pos[i] = p_
            ops = [ops[i] for i in order]
            deps = [{pos[j] for j in deps[i]} for i in order]
            signals = [signals[i] for i in order]
            self.ops = ops
        sig = [None] * n
        dma_idx = [None] * n
        for i, (eng, fn, rd, wr) in enumerate(ops):
            if eng.startswith("q_"):
                k = self.dcnt[eng]
                self.dcnt[eng] += 1
                dma_idx[i] = k
                sig[i] = ((eng, k % NDMA_SLOTS), 16 * (k // NDMA_SLOTS + 1))
            elif signals[i]:
                self.cnt[eng] += 1
                sig[i] = (eng, self.cnt[eng])
        waited = self.waited
        plan = {e: [] for e in ("pe", "act", "dve", "pool", "sp")}
        for i, (eng, fn, rd, wr) in enumerate(ops):
            ie = self.ISSUE[eng]
            ws = []
            need = {}
            for j in deps[i]:
                sk, v = sig[j]
                if need.get(sk, 0) < v:
                    need[sk] = v
            if eng.startswith("q_"):
                k = dma_idx[i]
                if k >= NDMA_SLOTS:
                    sk = (eng, k % NDMA_SLOTS)
                    v = 16 * (k // NDMA_SLOTS)
                    if need.get(sk, 0) < v:
                        need[sk] = v
            for sk, v in need.items():
                if waited[ie].get(sk, 0) < v:
                    waited[ie][sk] = v
                    ws.append((sk, v))
            plan[ie].append((i, ws))
        final = []
        for q, c in self.dcnt.items():
            for s in range(min(c, NDMA_SLOTS)):
                last = 16 * ((c - 1 - s) // NDMA_SLOTS + 1)
                final.append(((q, s), last))
        if self.cnt["cc"]:
            final.append(("cc", self.cnt["cc"]))
        sems = self.sems
        with nc.Block() as block:
            engs = {"pe": block.tensor, "act": block.scalar, "dve": block.vector,
                    "pool": block.gpsimd, "sp": block.sync}

            def mk(ie):
                def body(e):
                    for (i, ws) in plan[ie]:
                        for sk, v in ws:
                            e.wait_ge(sems[sk], v)
                        ins = ops[i][1](e)
                        if sig[i] is not None:
                            sk, v = sig[i]
                            ins.then_inc(sems[sk], 1 if isinstance(sk, str) else 16)
                    if ie == "sp":
                        for sk, v in final:
                            if waited["sp"].get(sk, 0) < v:
                                waited["sp"][sk] = v
                                e.wait_ge(sems[sk], v)
                return body

            for ie in ("sp", "pe", "act", "dve", "pool"):
                if plan[ie] or ie == "sp":
                    engs[ie](mk(ie))
        self.base += n
        self.ops = []


_UID = [0]


def U(name):
    _UID[0] += 1
    return f"{name}_{_UID[0]}"


class Ring:
    def __init__(self, stack, alloc, name, shape, dtype, n):
        self.bufs = []
        for i in range(n):
            t = stack.enter_context(alloc(U(f"{name}{i}"), list(shape), dtype))
            self.bufs.append((t, Tok(f"{name}{i}")))
        self.i = 0

    def next(self):
        b = self.bufs[self.i % len(self.bufs)]
        self.i += 1
        return b


D = 2048
KC_D = D // 128
EPS = 1e-6


def bcast_rows(ap_1d, nparts):
    return ap_1d.partition_broadcast(nparts)


def emit_rstd(P, ss, rstd, tok_ss, tok_rstd, n, width):
    P.act(lambda e: e.activation(out=rstd, in_=ss, func=AF.Ln, bias=EPS, scale=1.0 / width),
          rd=[tok_ss], wr=[tok_rstd])
    P.act(lambda e: e.activation(out=rstd, in_=rstd, func=AF.Exp, scale=-0.5),
          rd=[tok_rstd], wr=[tok_rstd])


def phase_norm(nc, P, stack, T, x_in, y_in, wpost, wpre, x_out, hT_out, ident_bf, sel4=None):
    nt = T // 128
    sb = nc.sbuf_tensor
    xr = Ring(stack, sb, "n_x", [128, D], F32, 2)
    yr = Ring(stack, sb, "n_y", [128, D], F32, 2) if y_in is not None else None
    tr = Ring(stack, sb, "n_t", [128, D], F32, 2)
    sqr = Ring(stack, sb, "n_sq", [128, D], BF16, 1)
    hr = Ring(stack, sb, "n_h", [128, D], BF16, 2)
    htr = Ring(stack, sb, "n_hT", [128, KC_D, 512], BF16, 2)
    str_ = Ring(stack, sb, "n_st", [128, 4], F32, 4)
    ptr = Ring(stack, nc.psum_tensor, "n_pt", [128, 1024], BF16, 4)
    if sel4 is not None:
        scr4 = Ring(stack, sb, "n_sc4", [128, KC_D, 512], BF16, 2)
        s4 = stack.enter_context(sb(U("n_sel4"), [128, 4], F32))
        s4k = Tok("n_sel4")
        P.dma(lambda e: e.dma_start(out=s4[:], in_=sel4.partition_broadcast(128)), wr=[s4k])
    consts = []
    for nm, w in (("n_wpost", wpost), ("n_wpre", wpre)):
        if w is None:
            consts.append((None, None))
            continue
        t = stack.enter_context(sb(U(nm), [128, D], F32))
        tk = Tok(nm)
        P.dma(lambda e, t=t, w=w: e.dma_start(out=t[:], in_=bcast_rows(w, 128)), wr=[tk])
        consts.append((t, tk))
    (wpost_t, wpost_k), (wpre_t, wpre_k) = consts
    for i in range(nt):
        rows = slice(i * 128, (i + 1) * 128)
        xt, xk = xr.next()
        P.dma(lambda e, xt=xt, rows=rows: e.dma_start(out=xt[:], in_=x_in[rows, :]), wr=[xk])
        cur, curk = xt, xk
        if y_in is not None:
            yt, yk = yr.next()
            if isinstance(y_in, tuple):
                P.dma(lambda e, yt=yt, rows=rows: e.dma_start(out=yt[:, 0:1024], in_=y_in[0][rows, :]), wr=[yk],
                      q="q_pool")
                P.dma(lambda e, yt=yt, rows=rows: e.dma_start(out=yt[:, 1024:2048], in_=y_in[1][rows, :]), wr=[yk],
                      q="q_pool")
            else:
                P.dma(lambda e, yt=yt, rows=rows: e.dma_start(out=yt[:], in_=y_in[rows, :]), wr=[yk], q="q_pool")
            sq, sqk = sqr.next()
            st, stk = str_.next()
            P.act(lambda e, sq=sq, yt=yt, st=st: e.activation(out=sq[:], in_=yt[:], func=AF.Square,
                                                               accum_out=st[:, 0:1]),
                  rd=[yk], wr=[sqk, stk])
            emit_rstd(P, st[:, 0:1], st[:, 1:2], stk, stk, 128, D)
            tt, tk = tr.next()
            P.dve(lambda e, tt=tt, yt=yt, st=st: e.scalar_tensor_tensor(
                out=tt[:], in0=yt[:], scalar=st[:, 1:2], in1=wpost_t[:], op0=ALU.mult, op1=ALU.mult),
                rd=[yk, stk, wpost_k], wr=[tk])
            P.dve(lambda e, tt=tt, xt=xt: e.tensor_tensor(out=tt[:], in0=tt[:], in1=xt[:], op=ALU.add),
                  rd=[tk, xk], wr=[tk])
            cur, curk = tt, tk
        if x_out is not None:
            P.dma(lambda e, cur=cur, rows=rows: e.dma_start(out=x_out[rows, :], in_=cur[:]), rd=[curk])
        if wpre is not None:
            sq, sqk = sqr.next()
            st, stk = str_.next()
            P.act(lambda e, sq=sq, cur=cur, st=st: e.activation(out=sq[:], in_=cur[:], func=AF.Square,
                                                                accum_out=st[:, 0:1]),
                  rd=[curk], wr=[sqk, stk])
            emit_rstd(P, st[:, 0:1], st[:, 1:2], stk, stk, 128, D)
            ht, hk = hr.next()
            P.dve(lambda e, ht=ht, cur=cur, st=st: e.scalar_tensor_tensor(
                out=ht[:], in0=cur[:], scalar=st[:, 1:2], in1=wpre_t[:], op0=ALU.mult, op1=ALU.mult),
                rd=[curk, stk, wpre_k], wr=[hk])
            if i % 4 == 0:
                hT, hTk = htr.next()
            tsl = slice((i % 4) * 128, (i % 4) * 128 + 128)
            for half in range(2):
                pt, ptk = ptr.next()
                for j in range(8):
                    kc = half * 8 + j
                    P.pe(lambda e, pt=pt, ht=ht, j=j, kc=kc: e.transpose(
                        out=pt[:, j * 128:(j + 1) * 128], in_=ht[:, kc * 128:(kc + 1) * 128],
                        identity=ident_bf[:]), rd=[hk], wr=[ptk])
                src = lambda pt: pt[:].rearrange("p (k t) -> p k t", t=128)
                if half == 0:
                    P.act(lambda e, hT=hT, pt=pt, tsl=tsl: e.copy(out=hT[:, 0:8, tsl], in_=src(pt)),
                          rd=[ptk], wr=[hTk])
                else:
                    P.dve(lambda e, hT=hT, pt=pt, tsl=tsl: e.tensor_copy(out=hT[:, 8:16, tsl], in_=src(pt)),
                          rd=[ptk], wr=[hTk])
            if i % 4 == 3 and sel4 is None:
                blk = i // 4
                P.dma(lambda e, hT=hT, blk=blk: e.dma_start(
                    out=hT_out.rearrange("(k p) t -> p k t", p=128)[:, :, blk * 512:(blk + 1) * 512],
                    in_=hT[:]), rd=[hTk], q="q_pool")
            elif i % 4 == 3:
                blk = i // 4
                for s_ in range(4):
                    sc, sck = scr4.next()
                    if s_ % 2 == 0:
                        P.dve(lambda e, sc=sc, hT=hT, s_=s_: e.tensor_scalar(
                            out=sc[:], in0=hT[:], scalar1=s4[:, s_:s_ + 1], scalar2=None, op0=ALU.mult),
                            rd=[hTk, s4k], wr=[sck])
                    else:
                        P.act(lambda e, sc=sc, hT=hT, s_=s_: e.activation(
                            out=sc[:], in_=hT[:], func=AF.Copy, scale=s4[:, s_:s_ + 1]),
                            rd=[hTk, s4k], wr=[sck])
                    P.dma(lambda e, sc=sc, blk=blk, s_=s_: e.dma_start(
                        out=hT_out.rearrange("(s b p) f -> p s b f", s=4, p=128)[:, s_, blk, :],
                        in_=sc[:].rearrange("p k t -> p (k t)")), rd=[sck], q="q_pool" if s_ % 2 else "q_sp")


def make_ident(nc, P, stack):
    idf = stack.enter_context(nc.sbuf_tensor(U("ident_f"), [128, 128], F32))
    idb = stack.enter_context(nc.sbuf_tensor(U("ident_b"), [128, 128], BF16))
    k = Tok("ident")
    P.pool(lambda e: e.memset(idf[:], 1.0), wr=[k])
    P.pool(lambda e: e.affine_select(out=idf[:], in_=idf[:], pattern=[[-1, 128]], compare_op=ALU.is_equal,
                                     fill=0.0, base=0, channel_multiplier=1), rd=[k], wr=[k])
    P.dve(lambda e: e.tensor_copy(out=idb[:], in_=idf[:]), rd=[k], wr=[k])
    return idf, idb, k


D_FF = 5632
KC_F = D_FF // 128


def cast_op(P, idx, out, in_, rd, wr):
    if idx % 2 == 0:
        P.dve(lambda e: e.tensor_copy(out=out, in_=in_), rd=rd, wr=wr)
    else:
        P.act(lambda e: e.copy(out=out, in_=in_), rd=rd, wr=wr)


def phase_ffn_gu(nc, P, stack, T, hT_in, wg_l, wu_l, actT_out):
    sb = nc.sbuf_tensor
    hT = stack.enter_context(sb(U("gu_hT"), [128, KC_D, T], BF16))
    hTk = [Tok(f"gu_hT{k}") for k in range(KC_D // 4)]
    hv = hT_in.rearrange("(k p) t -> p k t", p=128)
    for g in range(KC_D // 4):
        P.dma(lambda e, g=g: e.dma_start(out=hT[:, g * 4:(g + 1) * 4, :], in_=hv[:, g * 4:(g + 1) * 4, :]),
              wr=[hTk[g]], q="q_pool")
    wst = [Ring(stack, sb, f"gu_wst{m}", [128, D], F32, 2) for m in range(2)]
    wbf = [Ring(stack, sb, f"gu_wbf{m}", [128, KC_D, 128], BF16, 2) for m in range(2)]
    ps = [Ring(stack, nc.psum_tensor, f"gu_ps{m}", [128, 512], F32, 3) for m in range(2)]
    sgr = Ring(stack, sb, "gu_sg", [128, 512], F32, 2)
    ar = Ring(stack, sb, "gu_a", [128, 512], BF16, 4)
    nsl = T // 512
    ci = 0
    for fc in range(KC_F):
        wb = []
        for m, wl in enumerate((wg_l, wu_l)):
            st_, stk = wst[m].next()
            P.dma(lambda e, st_=st_, wl=wl, fc=fc: e.dma_start(out=st_[:], in_=wl[fc]), wr=[stk])
            b, bk = wbf[m].next()
            cast_op(P, ci, b[:].rearrange("p k j -> p (k j)"), st_[:], [stk], [bk])
            ci += 1
            wb.append((b, bk))
        for sl in range(nsl):
            tsl = slice(sl * 512, (sl + 1) * 512)
            pp = []
            for m in range(2):
                p_, pk = ps[m].next()
                b, bk = wb[m]
                for kc in range(KC_D):
                    P.pe(lambda e, p_=p_, b=b, kc=kc, tsl=tsl: e.matmul(
                        p_[:], lhsT=b[:, kc, :], rhs=hT[:, kc, tsl], start=(kc == 0), stop=(kc == KC_D - 1)),
                        rd=[bk, hTk[kc // 4]], wr=[pk])
                pp.append((p_, pk))
            sg, sgk = sgr.next()
            P.act(lambda e, sg=sg, p_=pp[0][0]: e.activation(out=sg[:], in_=p_[:], func=AF.Silu),
                  rd=[pp[0][1]], wr=[sgk])
            a, ak = ar.next()
            P.dve(lambda e, a=a, sg=sg, p_=pp[1][0]: e.tensor_tensor(out=a[:], in0=sg[:], in1=p_[:], op=ALU.mult),
                  rd=[sgk, pp[1][1]], wr=[ak])
            P.dma(lambda e, a=a, fc=fc, tsl=tsl: e.dma_start(out=actT_out[fc * 128:(fc + 1) * 128, tsl], in_=a[:]),
                  rd=[ak], q="q_pool")


def phase_mmT(nc, P, stack, T, K, AT_in, w_l, y_out, Th=1024, pfx="mt", halves=None, htoks=None, half_done=None):
    KC = K // 128
    G = 4
    NG = KC // G
    Th = min(Th, T, 1024)
    NT = Th // 128
    sb = nc.sbuf_tensor
    nAT = 1 if halves is None else 2
    ATs = [(stack.enter_context(sb(U(f"{pfx}_AT"), [128, KC, Th], BF16)), [Tok(f"{pfx}_AT{g}") for g in range(NG)])
           for _ in range(nAT)]
    ati = 0
    AT, ATk = ATs[0]
    wst = Ring(stack, sb, f"{pfx}_wst", [128, G, 512], F32, 3)
    wbf = Ring(stack, sb, f"{pfx}_wbf", [128, G, 512], BF16, 4)
    ps = Ring(stack, nc.psum_tensor, f"{pfx}_ps", [128, 512], F32, 8)
    ys = Ring(stack, sb, f"{pfx}_ys", [128, 512], F32, 4)
    av = AT_in.rearrange("(k p) t -> p k t", p=128)
    ci = 0
    ei = 0
    if halves is None:
        order = [(th, s) for th in range(T // Th) for s in range(4)]
    else:
        order = [(th, s) for hf in range(2) for th in range(T // Th) for s in (2 * hf, 2 * hf + 1)]
    last_th = None
    for (th, s) in order:
        if th != last_th:
            last_th = th
            AT, ATk = ATs[ati % nAT]
            ati += 1
            for g in range(NG):
                P.dma(lambda e, g=g, th=th, AT=AT: e.dma_start(out=AT[:, g * G:(g + 1) * G, :],
                                                        in_=av[:, g * G:(g + 1) * G, th * Th:(th + 1) * Th]),
                      wr=[ATk[g]], q="q_pool" if halves is None else "q_sp")
        if True:
            banks = [ps.next() for _ in range(NT)]
            for g in range(NG):
                st_, stk = wst.next()
                P.dma(lambda e, st_=st_, s=s, g=g: e.dma_start(out=st_[:], in_=w_l[s, :, g * G:(g + 1) * G, :]),
                      wr=[stk])
                wb, wbk = wbf.next()
                cast_op(P, ci, wb[:], st_[:], [stk], [wbk])
                ci += 1
                for tl in range(NT):
                    p_, pk = banks[tl]
                    for kk in range(G):
                        kc = g * G + kk
                        P.pe(lambda e, p_=p_, kc=kc, kk=kk, tl=tl, wb=wb, AT=AT: e.matmul(
                            p_[:], lhsT=AT[:, kc, tl * 128:(tl + 1) * 128], rhs=wb[:, kk, :],
                            start=(kc == 0), stop=(kc == KC - 1)),
                            rd=[ATk[g], wbk], wr=[pk])
            for tl in range(NT):
                p_, pk = banks[tl]
                y_, yk = ys.next()
                if ei % 2 == 0:
                    P.act(lambda e, y_=y_, p_=p_: e.copy(out=y_[:], in_=p_[:]), rd=[pk], wr=[yk])
                else:
                    P.dve(lambda e, y_=y_, p_=p_: e.tensor_copy(out=y_[:], in_=p_[:]), rd=[pk], wr=[yk])
                ei += 1
                r0 = th * Th + tl * 128
                if halves is None:
                    P.dma(lambda e, y_=y_, r0=r0, s=s: e.dma_start(out=y_out[r0:r0 + 128, s * 512:(s + 1) * 512],
                                                                   in_=y_[:]), rd=[yk], q="q_pool")
                else:
                    dst = halves[s // 2]
                    tk_ = Tok("yhalf")
                    htoks[s // 2].append(tk_)
                    P.dma(lambda e, y_=y_, r0=r0, s=s, dst=dst: e.dma_start(
                        out=dst[r0:r0 + 128, (s % 2) * 512:(s % 2 + 1) * 512], in_=y_[:]), rd=[yk], wr=[tk_],
                        q="q_act")
            if halves is not None and half_done is not None and s % 2 == 1 and th == T // Th - 1:
                half_done(s // 2)


SEQ = 8192
NBLK = SEQ // 512


def make_consts(nc, P, stack):
    sb = nc.sbuf_tensor
    c = {}
    c["ones_bf"] = stack.enter_context(sb(U("ones_bf"), [128, 128], BF16))
    c["ones_f"] = stack.enter_context(sb(U("ones_f"), [128, 128], F32))
    c["sel2"] = stack.enter_context(sb(U("sel2"), [128, 2], BF16))
    k = Tok("consts")
    c["tok"] = k
    P.pool(lambda e: e.memset(c["ones_bf"][:], 1.0), wr=[k])
    P.pool(lambda e: e.memset(c["ones_f"][:], 1.0), wr=[k])
    P.pool(lambda e: e.memset(c["sel2"][:], 0.0), wr=[k])
    P.pool(lambda e: e.memset(c["sel2"][0:64, 0:1], 1.0), wr=[k])
    P.pool(lambda e: e.memset(c["sel2"][64:128, 1:2], 1.0), wr=[k])
    return c


def phase_attn_proj(nc, P, stack, hT_blk, w_l, qTd, kTd, Vd, negMd, idf, idb, C, S=SEQ, hT_tok=None):
    sb = nc.sbuf_tensor
    ps = nc.psum_tensor
    NB = S // 512
    ones_f, sel2, ck = C["ones_f"], C["sel2"], C["tok"]
    W = stack.enter_context(sb(U("ap_w"), [128, KC_D, 12 * 128], BF16))
    Wk = [Tok(f"ap_w{f}") for f in range(12)]
    wst = Ring(stack, sb, "ap_wst", [128, D], F32, 2)
    for fc in range(12):
        st_, stk = wst.next()
        P.dma(lambda e, st_=st_, fc=fc: e.dma_start(out=st_[:], in_=w_l[fc]), wr=[stk], q="q_pool")
        cast_op(P, fc, W[:, :, fc * 128:(fc + 1) * 128], st_[:].rearrange("p (k j) -> p k j", j=128), [stk], [Wk[fc]])
    hr = Ring(stack, sb, "ap_h", [128, KC_D, 512], BF16, 2)
    pw = Ring(stack, ps, "ap_pw", [128, 512], F32, 5)
    pm = Ring(stack, ps, "ap_pm", [128, 512], F32, 2)
    osr = Ring(stack, sb, "ap_os", [128, 512], BF16, 6)
    sqr = Ring(stack, sb, "ap_sq", [128, 512], BF16, 3)
    vor = Ring(stack, sb, "ap_vo", [128, 4, 128], BF16, 3)
    nmx = stack.enter_context(sb(U("ap_nmx"), [2, 8, NB], F32))
    nmxk = Tok("ap_nmx")
    ei = 0
    for b in range(NB):
        h_, hk = hr.next()
        P.dma(lambda e, h_=h_, b=b: e.dma_start(out=h_[:], in_=hT_blk(b)), rd=([hT_tok(b)] if hT_tok else []),
              wr=[hk])
        tsl = slice(b * 512, (b + 1) * 512)
        for ch in range(12):
            which, hl = ch // 4, ch % 4
            p_, pk = pw.next()
            for kc in range(KC_D):
                P.pe(lambda e, p_=p_, ch=ch, kc=kc, h_=h_: e.matmul(
                    p_[:], lhsT=W[:, kc, ch * 128:(ch + 1) * 128], rhs=h_[:, kc, :],
                    start=(kc == 0), stop=(kc == KC_D - 1)), rd=[Wk[ch], hk], wr=[pk])
            o_, ok = osr.next()
            sc = 0.125 if which == 0 else 1.0
            if ei % 2 == 0:
                P.act(lambda e, o_=o_, p_=p_, sc=sc: e.activation(out=o_[:], in_=p_[:], func=AF.Copy, scale=sc),
                      rd=[pk], wr=[ok])
            else:
                P.dve(lambda e, o_=o_, p_=p_, sc=sc: e.tensor_scalar(out=o_[:], in0=p_[:], scalar1=sc, scalar2=None,
                                                                    op0=ALU.mult), rd=[pk], wr=[ok])
            ei += 1
            if which < 2:
                dst = qTd if which == 0 else kTd
                P.dma(lambda e, o_=o_, dst=dst, hl=hl, tsl=tsl: e.dma_start(
                    out=dst[hl * 128:(hl + 1) * 128, tsl], in_=o_[:]), rd=[ok], q="q_pool")
                sq, sqk = sqr.next()
                P.dve(lambda e, sq=sq, o_=o_: e.tensor_tensor(out=sq[:], in0=o_[:], in1=o_[:], op=ALU.mult),
                      rd=[ok], wr=[sqk])
                p2, p2k = pm.next()
                P.pe(lambda e, p2=p2, sq=sq: e.matmul(p2[0:2, :], lhsT=sel2[:], rhs=sq[:], start=True, stop=True),
                     rd=[sqk, ck], wr=[p2k])
                P.dve(lambda e, p2=p2, ch=ch, b=b: e.tensor_reduce(
                    out=nmx[:, ch, b:b + 1], in_=p2[0:2, :], axis=AX.X, op=ALU.max), rd=[p2k], wr=[nmxk])
            else:
                p2, p2k = pm.next()
                p2b = p2[:].bitcast(BF16)
                for j in range(4):
                    P.pe(lambda e, p2b=p2b, o_=o_, j=j: e.transpose(
                        out=p2b[:, j * 128:(j + 1) * 128], in_=o_[:, j * 128:(j + 1) * 128], identity=idb[:]),
                        rd=[ok], wr=[p2k])
                vo, vok = vor.next()
                P.act(lambda e, vo=vo, p2b=p2b: e.copy(out=vo[:], in_=p2b[:, 0:512].rearrange("p (j d) -> p j d", d=128)),
                      rd=[p2k], wr=[vok])
                P.dma(lambda e, vo=vo, hl=hl, b=b: e.dma_start(out=Vd[hl, :, b * 4:(b + 1) * 4, :], in_=vo[:]),
                      rd=[vok], q="q_pool")
    msc = stack.enter_context(sb(U("ap_msc"), [2, 32], F32))
    P.dve(lambda e: e.tensor_reduce(out=msc[:, 0:8], in_=nmx[:], axis=AX.X, op=ALU.max), rd=[nmxk], wr=[nmxk])
    P.dve(lambda e: e.tensor_tensor(out=msc[:, 8:12], in0=msc[:, 0:4], in1=msc[:, 4:8], op=ALU.mult),
          rd=[nmxk], wr=[nmxk])
    P.act(lambda e: e.activation(out=msc[:, 12:16], in_=msc[:, 8:12], func=AF.Ln), rd=[nmxk], wr=[nmxk])
    P.act(lambda e: e.activation(out=msc[:, 12:16], in_=msc[:, 12:16], func=AF.Exp, scale=0.5), rd=[nmxk], wr=[nmxk])
    for hl in range(4):
        P.dve(lambda e, hl=hl: e.tensor_scalar(out=msc[:, 16 + hl * 2:18 + hl * 2], in0=idf[0:2, 0:2],
                                               scalar1=msc[:, 12 + hl:13 + hl], scalar2=-1.02,
                                               op0=ALU.mult, op1=ALU.mult), rd=[nmxk], wr=[nmxk])
    p2, p2k = pm.next()
    P.pe(lambda e: e.matmul(p2[:, 0:8], lhsT=ones_f[0:2, :], rhs=msc[:, 16:24], start=True, stop=True),
         rd=[nmxk, ck], wr=[p2k])
    nm = stack.enter_context(sb(U("ap_nm"), [128, 8], F32))
    nmk = Tok("ap_nm")
    P.dve(lambda e: e.tensor_copy(out=nm[:], in_=p2[:, 0:8]), rd=[p2k], wr=[nmk])
    P.dma(lambda e: e.dma_start(out=negMd, in_=nm[:]), rd=[nmk])


def phase_attn_core(nc, P, stack, qTd, kTd, Vd, negMd, lam_in, subln_in, lambda_init, oT_out, idf, idb, C, S=SEQ):
    sb = nc.sbuf_tensor
    ps = nc.psum_tensor
    NB = S // 512
    ones_bf, ones_f, sel2, ck = C["ones_bf"], C["ones_f"], C["sel2"], C["tok"]
    lam4 = stack.enter_context(sb(U("at_lam4"), [128, 4, 64], F32))
    lamk = Tok("lam")
    P.dma(lambda e: e.dma_start(out=lam4[:].rearrange("p a d -> p (a d)"),
                                in_=lam_in.rearrange("a d -> (a d)").partition_broadcast(128)), wr=[lamk])
    lsc = stack.enter_context(sb(U("at_lsc"), [128, 8], F32))
    lpr = stack.enter_context(sb(U("at_lpr"), [128, 2, 64], F32))
    P.dve(lambda e: e.tensor_tensor(out=lpr[:, 0, :], in0=lam4[:, 0, :], in1=lam4[:, 1, :], op=ALU.mult),
          rd=[lamk], wr=[lamk])
    P.dve(lambda e: e.tensor_tensor(out=lpr[:, 1, :], in0=lam4[:, 2, :], in1=lam4[:, 3, :], op=ALU.mult),
          rd=[lamk], wr=[lamk])
    P.dve(lambda e: e.tensor_reduce(out=lsc[:, 0:2], in_=lpr[:], axis=AX.X, op=ALU.add), rd=[lamk], wr=[lamk])
    P.act(lambda e: e.activation(out=lsc[:, 2:4], in_=lsc[:, 0:2], func=AF.Exp), rd=[lamk], wr=[lamk])
    P.dve(lambda e: e.scalar_tensor_tensor(out=lsc[:, 4:5], in0=lsc[:, 3:4], scalar=-float(lambda_init),
                                           in1=lsc[:, 2:3], op0=ALU.add, op1=ALU.subtract), rd=[lamk], wr=[lamk])
    P.dma(lambda e: e.dma_start(out=lsc[:, 5:6], in_=subln_in.rearrange("(p o) -> p o", o=1)), wr=[lamk])
    P.dve(lambda e: e.tensor_scalar(out=lsc[:, 6:7], in0=lsc[:, 5:6], scalar1=1.0 - float(lambda_init),
                                    scalar2=None, op0=ALU.mult), rd=[lamk], wr=[lamk])
    neglam = lsc[:, 4:5]
    sublnw = lsc[:, 6:7]

    qTc = [stack.enter_context(sb(U(f"at_qT{c}"), [128, S], BF16)) for c in range(2)]
    qzk = Tok("at_qz")
    P.dve(lambda e: e.memset(qTc[0][64:128, :], 0.0), wr=[qzk])
    P.dve(lambda e: e.memset(qTc[1][0:64, :], 0.0), wr=[qzk])
    kT = stack.enter_context(sb(U("at_kT"), [128, S], BF16))
    V = stack.enter_context(sb(U("at_V"), [128, S // 128, 128], BF16))
    qk1, kk1, vk1 = Tok("at_q"), Tok("at_k"), Tok("at_v")
    qk_ = [qk1] * NB
    kk_ = [kk1] * NB
    vk_ = [vk1] * NB
    negM = stack.enter_context(sb(U("at_negM"), [128, 8], F32))
    negMk = Tok("negM")
    P.dma(lambda e: e.dma_start(out=negM[:], in_=negMd), wr=[negMk])
    pw = Ring(stack, ps, "at_pw", [128, 512], F32, 3)
    po = [Ring(stack, ps, f"at_po{c}", [128, 512], F32, 1) for c in range(2)]
    pl = [Ring(stack, ps, f"at_pl{c}", [128, 512], F32, 1) for c in range(2)]
    lacc = Ring(stack, sb, "at_la", [128, 512], F32, 4)
    pm = Ring(stack, ps, "at_pm", [128, 512], F32, 1)
    ptr = Ring(stack, sb, "at_pt", [128, 512], BF16, 6)
    e32 = Ring(stack, sb, "at_e32", [128, 512], F32, 8)
    obf = Ring(stack, sb, "at_obf", [128, 512], BF16, 2)
    for hl in range(4):
        hs = slice(hl * 128, (hl + 1) * 128)
        P.dma(lambda e, hl=hl: e.dma_start(out=qTc[0][0:64, :], in_=qTd[hl * 128:hl * 128 + 64, :]),
              rd=[qzk], wr=[qk1])
        P.dma(lambda e, hl=hl: e.dma_start(out=qTc[1][64:128, :], in_=qTd[hl * 128 + 64:hl * 128 + 128, :]),
              rd=[qzk], wr=[qk1], q="q_pool")
        P.dma(lambda e, hs=hs: e.dma_start(out=kT[:], in_=kTd[hs, :]), wr=[kk1])
        P.dma(lambda e, hl=hl: e.dma_start(out=V[:], in_=Vd[hl]), wr=[vk1], q="q_pool")
        steps = [(qt, c, sbk) for qt in range(NB) for c in range(2) for sbk in range(qt * 4 + 4)]
        LA = 2
        inflight = {}
        accs = {}
        tparts = {}
        deferred = []

        def stepA(k):
            qt, c, sbk = steps[k]
            q0 = qt * 512
            rows = slice(c * 64, (c + 1) * 64)
            d = max(0, sbk - qt * 4)
            cs = slice(d * 128, 512)
            w_, wk_ = pw.next()
            P.pe(lambda e: e.matmul(w_[:, cs], lhsT=kT[:, sbk * 128:(sbk + 1) * 128],
                                    rhs=qTc[c][:, q0 + cs.start:q0 + 512], start=True, stop=True),
                 rd=[kk_[sbk // 4], qk_[qt], qzk], wr=[wk_])
            pt, ptk = ptr.next()
            bcol = hl * 2 + c
            P.act(lambda e: e.activation(out=pt[:, cs], in_=w_[:, cs], func=AF.Exp, bias=negM[:, bcol:bcol + 1],
                                         scale=1.0), rd=[wk_, negMk], wr=[ptk])
            if sbk >= qt * 4:
                base = q0 + cs.start - sbk * 128
                P.pool(lambda e: e.affine_select(out=pt[:, cs], in_=pt[:, cs], pattern=[[1, 512 - cs.start]],
                                                 compare_op=ALU.is_ge, fill=0.0, base=base,
                                                 channel_multiplier=-1), rd=[ptk], wr=[ptk])
            inflight[k] = (pt, ptk, cs)

        def epi_c(qt, c, k):
            o_, ok, l_, lk, la, lak = accs.pop((qt, c))

            def part2():
                P.pe(lambda e: e.matmul(l_[:], lhsT=ones_f[:], rhs=la[:], start=False, stop=True),
                     rd=[lak, ck], wr=[lk])
                r_, rk = e32.next()
                P.dve(lambda e: e.reciprocal(out=r_[:], in_=l_[:]), rd=[lk], wr=[rk])
                t_, tk = e32.next()
                P.dve(lambda e: e.tensor_tensor(out=t_[:], in0=o_[:], in1=r_[:], op=ALU.mult), rd=[ok, rk], wr=[tk])
                tparts[(qt, c)] = (t_, tk)
                if c == 1:
                    epi_1(qt, k + 2)
            deferred.append((k + 2, part2))

        def epi_1(qt, k):
            t0, t0k = tparts.pop((qt, 0))
            t1, t1k = tparts.pop((qt, 1))
            of, ofk = e32.next()
            P.dve(lambda e: e.scalar_tensor_tensor(out=of[:], in0=t1[:], scalar=neglam, in1=t0[:],
                                                   op0=ALU.mult, op1=ALU.add), rd=[t0k, t1k, lamk], wr=[ofk])
            sq, sqk2 = e32.next()
            P.act(lambda e: e.activation(out=sq[:], in_=of[:], func=AF.Square), rd=[ofk], wr=[sqk2])

            def epi_2():
                m_, mk = pm.next()
                P.pe(lambda e: e.matmul(m_[:], lhsT=ones_f[:], rhs=sq[:], start=True, stop=True),
                     rd=[sqk2, ck], wr=[mk])
                rs, rsk = e32.next()
                P.act(lambda e: e.activation(out=rs[:], in_=m_[:], func=AF.Ln, bias=EPS, scale=1.0 / 128),
                      rd=[mk], wr=[rsk])
                P.act(lambda e: e.activation(out=rs[:], in_=rs[:], func=AF.Exp, scale=-0.5), rd=[rsk], wr=[rsk])
                ob, obk = obf.next()
                P.dve(lambda e: e.scalar_tensor_tensor(out=ob[:], in0=of[:], scalar=sublnw, in1=rs[:],
                                                       op0=ALU.mult, op1=ALU.mult), rd=[ofk, rsk, lamk], wr=[obk])
                P.dma(lambda e, hl=hl: e.dma_start(out=oT_out[hl * 128:(hl + 1) * 128, qt * 512:(qt + 1) * 512],
                                                   in_=ob[:]), rd=[obk])
            deferred.append((k + 6, epi_2))

        def stepB(k):
            qt, c, sbk = steps[k]
            nsb = qt * 4 + 4
            pt, ptk, cs = inflight.pop(k)
            if sbk == 0:
                o_, ok = po[c].next()
                l_, lk = pl[c].next()
                la, lak = lacc.next()
                accs[(qt, c)] = (o_, ok, l_, lk, la, lak)
            o_, ok, l_, lk, la, lak = accs[(qt, c)]
            P.pe(lambda e: e.matmul(o_[:, cs], lhsT=V[:, sbk, :], rhs=pt[:, cs], start=(sbk == 0),
                                    stop=(sbk == nsb - 1)), rd=[vk_[sbk // 4], ptk], wr=[ok])
            if sbk % 2 == 0:
                P.pe(lambda e: e.matmul(l_[:, cs], lhsT=ones_bf[:], rhs=pt[:, cs], start=(sbk == 0), stop=False),
                     rd=[ck, ptk], wr=[lk])
            elif sbk == 1:
                if cs.start > 0:
                    P.dve(lambda e: e.memset(la[:, 0:cs.start], 0.0), wr=[lak])
                P.dve(lambda e: e.tensor_copy(out=la[:, cs], in_=pt[:, cs]), rd=[ptk], wr=[lak])
            else:
                P.dve(lambda e: e.tensor_tensor(out=la[:, cs], in0=la[:, cs], in1=pt[:, cs], op=ALU.add),
                      rd=[ptk, lak], wr=[lak])
            if sbk == nsb - 1:
                epi_c(qt, c, k)

        ns = len(steps)
        for k in range(ns + LA):
            if k < ns:
                stepA(k)
            if k - LA >= 0:
                stepB(k - LA)
            for item in [d_ for d_ in deferred if d_[0] <= k]:
                deferred.remove(item)
                item[1]()
        for item in deferred:
            item[1]()


NZX = 20


def phase_ssd_in(nc, P, stack, hT_blk, w_l, wdt_l, dtb_in, zxT_out, dt_out, S=SEQ, hT_tok=None):
    sb = nc.sbuf_tensor
    NB = S // 512
    W = stack.enter_context(sb(U("si_w"), [128, KC_D, NZX * 128], BF16))
    Wk = [Tok(f"si_w{f}") for f in range(NZX)]
    wst = Ring(stack, sb, "si_wst", [128, D], F32, 2)
    for fc in range(NZX):
        st_, stk = wst.next()
        P.dma(lambda e, st_=st_, fc=fc: e.dma_start(out=st_[:], in_=w_l[fc]), wr=[stk])
        cast_op(P, fc, W[:, :, fc * 128:(fc + 1) * 128], st_[:].rearrange("p (k j) -> p k j", j=128), [stk], [Wk[fc]])
    wdtf = stack.enter_context(sb(U("si_wdtf"), [128, KC_D, 16], F32))
    wdt = stack.enter_context(sb(U("si_wdt"), [128, KC_D, 16], BF16))
    wdk = Tok("si_wdt")
    P.dma(lambda e: e.dma_start(out=wdtf[:], in_=wdt_l), wr=[wdk])
    P.dve(lambda e: e.tensor_copy(out=wdt[:], in_=wdtf[:]), rd=[wdk], wr=[wdk])
    dtb = stack.enter_context(sb(U("si_dtb"), [128, 16], F32))
    dtbk = Tok("si_dtb")
    P.dma(lambda e: e.dma_start(out=dtb[:], in_=dtb_in.partition_broadcast(128)), wr=[dtbk])
    hr = Ring(stack, sb, "si_h", [128, KC_D, 512], BF16, 2)
    pw = Ring(stack, nc.psum_tensor, "si_pw", [128, 512], F32, 4)
    pd = Ring(stack, nc.psum_tensor, "si_pd", [128, 512], F32, 2)
    osr = Ring(stack, sb, "si_os", [128, 512], F32, 4)
    dr = Ring(stack, sb, "si_d", [128, 4, 16], F32, 6)
    dto = Ring(stack, sb, "si_dto", [128, 4, 16], F32, 2)
    ei = 0
    for b in range(NB):
        h_, hk = hr.next()
        P.dma(lambda e, h_=h_, b=b: e.dma_start(out=h_[:], in_=hT_blk(b)), rd=([hT_tok(b)] if hT_tok else []),
              wr=[hk])
        tsl = slice(b * 512, (b + 1) * 512)
        for fc in range(NZX):
            p_, pk = pw.next()
            for kc in range(KC_D):
                P.pe(lambda e, p_=p_, fc=fc, kc=kc, h_=h_: e.matmul(
                    p_[:], lhsT=W[:, kc, fc * 128:(fc + 1) * 128], rhs=h_[:, kc, :],
                    start=(kc == 0), stop=(kc == KC_D - 1)), rd=[Wk[fc], hk], wr=[pk])
            o_, ok = osr.next()
            if ei % 2 == 0:
                P.act(lambda e, o_=o_, p_=p_: e.copy(out=o_[:], in_=p_[:]), rd=[pk], wr=[ok])
            else:
                P.dve(lambda e, o_=o_, p_=p_: e.tensor_copy(out=o_[:], in_=p_[:]), rd=[pk], wr=[ok])
            ei += 1
            P.dma(lambda e, o_=o_, fc=fc, tsl=tsl: e.dma_start(out=zxT_out[fc * 128:(fc + 1) * 128, tsl], in_=o_[:]),
                  rd=[ok])
        x_, xk = dr.next()
        for j in range(4):
            p_, pk = pd.next()
            for kc in range(KC_D):
                P.pe(lambda e, p_=p_, kc=kc, h_=h_, j=j: e.matmul(
                    p_[:, 0:16], lhsT=h_[:, kc, j * 128:(j + 1) * 128], rhs=wdt[:, kc, :],
                    start=(kc == 0), stop=(kc == KC_D - 1)), rd=[wdk, hk], wr=[pk])
            P.dve(lambda e, x_=x_, p_=p_, j=j: e.tensor_tensor(out=x_[:, j, :], in0=p_[:, 0:16], in1=dtb[:],
                                                              op=ALU.add), rd=[pk, dtbk], wr=[xk])
        a_, ak = dr.next()
        P.dve(lambda e, a_=a_, x_=x_: e.scalar_tensor_tensor(out=a_[:], in0=x_[:], scalar=-1.0, in1=x_[:],
                                                             op0=ALU.mult, op1=ALU.max), rd=[xk], wr=[ak])
        P.act(lambda e, a_=a_: e.activation(out=a_[:], in_=a_[:], func=AF.Exp, scale=-1.0), rd=[ak], wr=[ak])
        P.act(lambda e, a_=a_: e.activation(out=a_[:], in_=a_[:], func=AF.Ln, bias=1.0, scale=1.0), rd=[ak], wr=[ak])
        d_, dk = dto.next()
        P.dve(lambda e, d_=d_, x_=x_, a_=a_: e.scalar_tensor_tensor(
            out=d_[:], in0=x_[:], scalar=0.0, in1=a_[:], op0=ALU.max, op1=ALU.add), rd=[xk, ak], wr=[dk])
        P.dma(lambda e, d_=d_, b=b: e.dma_start(
            out=dt_out[b * 512:(b + 1) * 512, :].rearrange("(j p) h -> p j h", p=128), in_=d_[:]), rd=[dk])


def phase_ssd_scan(nc, P, stack, zxT_in, dt_in, convw_in, convb_in, alog_in, dsk_in, normw_in, yT_out,
                   idf, idb, C, S=SEQ):
    sb = nc.sbuf_tensor
    ps = nc.psum_tensor
    ones_f, ck = C["ones_f"], C["tok"]
    TB = 256
    NB = S // TB
    zv = zxT_in.rearrange("(k p) t -> p k t", p=128)
    tri = stack.enter_context(sb(U("ss_tri"), [128, 128], F32))
    cst = Tok("ss_const")
    P.pool(lambda e: e.memset(tri[:], 1.0), wr=[cst])
    P.pool(lambda e: e.affine_select(out=tri[:], in_=tri[:], pattern=[[1, 128]], compare_op=ALU.is_ge,
                                     fill=0.0, base=0, channel_multiplier=-1), rd=[cst], wr=[cst])
    cw = stack.enter_context(sb(U("ss_cw"), [128, 12, 4], F32))
    cb = stack.enter_context(sb(U("ss_cb"), [128, 12], F32))
    nw = stack.enter_context(sb(U("ss_nw"), [128, 8], F32))
    abc = stack.enter_context(sb(U("ss_abc"), [128, 16], F32))
    d16 = stack.enter_context(sb(U("ss_d16"), [128, 16], F32))
    Dbc = stack.enter_context(sb(U("ss_Dbc"), [128, 16, 64], F32))
    P.dma(lambda e: e.dma_start(out=cw[:], in_=convw_in), wr=[cst])
    P.dma(lambda e: e.dma_start(out=cb[:], in_=convb_in), wr=[cst])
    P.dma(lambda e: e.dma_start(out=nw[:], in_=normw_in), wr=[cst])
    P.dma(lambda e: e.dma_start(out=abc[:], in_=alog_in.partition_broadcast(128)), wr=[cst])
    P.dma(lambda e: e.dma_start(out=d16[:], in_=dsk_in.partition_broadcast(128)), wr=[cst])
    P.act(lambda e: e.activation(out=abc[:], in_=abc[:], func=AF.Exp), rd=[cst], wr=[cst])
    P.dve(lambda e: e.tensor_scalar(out=abc[:], in0=abc[:], scalar1=-1.0, scalar2=None, op0=ALU.mult),
          rd=[cst], wr=[cst])
    P.dve(lambda e: e.tensor_copy(out=Dbc[:], in_=d16[:].unsqueeze(2).to_broadcast([128, 16, 64])),
          rd=[cst], wr=[cst])
    S32 = [stack.enter_context(sb(U(f"ss_S32{g}"), [128, 512], F32)) for g in range(2)]
    Sbf = [stack.enter_context(sb(U(f"ss_Sbf{g}"), [128, 512], BF16)) for g in range(2)]
    Sk = [Tok(f"ss_S{g}") for g in range(2)]
    Sbk = [Tok(f"ss_Sb{g}") for g in range(2)]
    for g in range(2):
        P.pool(lambda e, g=g: e.memset(S32[g][:], 0.0), wr=[Sk[g]])
        P.pool(lambda e, g=g: e.memset(Sbf[g][:], 0.0), wr=[Sbk[g]])
    rawr = Ring(stack, sb, "ss_raw", [128, 12, TB + 3], F32, 2)
    zr = Ring(stack, sb, "ss_z", [128, 8, TB], F32, 3)
    accr = Ring(stack, sb, "ss_acc", [128, 12, TB], F32, 1)
    xTr = Ring(stack, sb, "ss_xT", [128, 8, TB], F32, 2)
    bcTr = Ring(stack, sb, "ss_bcT", [128, 4, TB], BF16, 3)
    dtr = Ring(stack, sb, "ss_dt", [128, TB // 128, 16], F32, 3)
    oTr = Ring(stack, sb, "ss_oT", [128, 8, TB], BF16, 3)
    xsr = Ring(stack, sb, "ss_xs", [128, 512], F32, 2)
    Btr = Ring(stack, sb, "ss_Bt", [128, 128], BF16, 4)
    smr = Ring(stack, sb, "ss_sm", [128, 48], F32, 5)
    rbr = Ring(stack, sb, "ss_rb", [128, 8, 128], F32, 2)
    segr = Ring(stack, sb, "ss_seg", [128, 8, 128], F32, 2)
    cbmr = Ring(stack, sb, "ss_cbm", [128, 128], BF16, 3)
    ebr = Ring(stack, sb, "ss_eb", [128, 8, 128], BF16, 3)
    Gr = Ring(stack, sb, "ss_G", [128, 8, 128], BF16, 4)
    x32r = Ring(stack, sb, "ss_x32", [128, 512], F32, 2)
    xbr = Ring(stack, sb, "ss_xb", [128, 512], BF16, 4)
    xer = Ring(stack, sb, "ss_xe", [128, 512], BF16, 4)
    y1r = Ring(stack, sb, "ss_y1", [128, 512], F32, 2)
    xdr = Ring(stack, sb, "ss_xd", [128, 512], F32, 4)
    gvr = Ring(stack, sb, "ss_gv", [128, 4, 128], F32, 2)
    sqr = Ring(stack, sb, "ss_sq", [128, 4, 128], F32, 2)
    rsr = Ring(stack, sb, "ss_rs", [128, 128], F32, 2)
    pbc = Ring(stack, ps, "ss_pbc", [128, 1024], F32, 1)
    pm = Ring(stack, ps, "ss_pm", [128, 512], F32, 3)
    pyd = Ring(stack, ps, "ss_pyd", [128, 512], F32, 1)
    pyo = Ring(stack, ps, "ss_pyo", [128, 512], F32, 1)
    pst = Ring(stack, ps, "ss_pst", [128, 512], F32, 1)

    def block_prologue(b):
        t0 = b * TB
        raw, rk = rawr.next()
        if b == 0:
            P.pool(lambda e, raw=raw: e.memset(raw[:, :, 0:3], 0.0), wr=[rk])
            P.dma(lambda e, raw=raw: e.dma_start(out=raw[:, :, 3:], in_=zv[:, 8:20, 0:TB]), wr=[rk])
        else:
            P.dma(lambda e, raw=raw, t0=t0: e.dma_start(out=raw[:], in_=zv[:, 8:20, t0 - 3:t0 + TB]), wr=[rk])
        z_, zk = zr.next()
        P.dma(lambda e, z_=z_, t0=t0: e.dma_start(out=z_[:], in_=zv[:, 0:8, t0:t0 + TB]), wr=[zk], q="q_pool")
        dt_, dtk = dtr.next()
        P.dma(lambda e, dt_=dt_, t0=t0: e.dma_start(
            out=dt_[:], in_=dt_in[t0:t0 + TB, :].rearrange("(j p) h -> p j h", p=128)), wr=[dtk], q="q_pool")
        P.act(lambda e, z_=z_: e.activation(out=z_[:], in_=z_[:], func=AF.Silu), rd=[zk], wr=[zk])
        acc, acck = accr.next()
        acks = [Tok(f"acc{k}") for k in range(12)]
        for w in range(4):
            for k in range(12):
                if w == 0:
                    P.act(lambda e, k=k, w=w, raw=raw, acc=acc: e.activation(
                        out=acc[:, k, :], in_=raw[:, k, w:w + TB], func=AF.Copy, scale=cw[:, k, w:w + 1]),
                        rd=[rk, cst], wr=[acks[k]])
                else:
                    P.dve(lambda e, k=k, w=w, raw=raw, acc=acc: e.scalar_tensor_tensor(
                        out=acc[:, k, :], in0=raw[:, k, w:w + TB], scalar=cw[:, k, w:w + 1], in1=acc[:, k, :],
                        op0=ALU.mult, op1=ALU.add), rd=[rk, cst, acks[k]], wr=[acks[k]])
        xT, xTk = xTr.next()
        bcT, bcTk = bcTr.next()
        for k in range(12):
            if k < 8:
                P.act(lambda e, k=k, xT=xT, acc=acc: e.activation(out=xT[:, k, :], in_=acc[:, k, :], func=AF.Silu,
                                                                  bias=cb[:, k:k + 1], scale=1.0),
                      rd=[acks[k], cst], wr=[xTk])
            else:
                P.act(lambda e, k=k, bcT=bcT, acc=acc: e.activation(out=bcT[:, k - 8, :], in_=acc[:, k, :],
                                                                    func=AF.Silu, bias=cb[:, k:k + 1], scale=1.0),
                      rd=[acks[k], cst], wr=[bcTk])
        oT, oTk = oTr.next()
        return dict(t0=t0, z_=z_, zk=zk, dt_=dt_, dtk=dtk, xT=xT, xTk=xTk, bcT=bcT, bcTk=bcTk, oT=oT, oTk=oTk)

    def front(B_, j, g):
        z_, zk, dt_, dtk, xT, xTk, bcT, bcTk = (B_[k_] for k_ in ('z_', 'zk', 'dt_', 'dtk', 'xT', 'xTk', 'bcT', 'bcTk'))
        cs = slice(j * 128, (j + 1) * 128)
        px, pxk = pm.next()
        for f in range(4):
            P.pe(lambda e, px=px, f=f, g=g, xT=xT, cs=cs: e.transpose(
                out=px[:, f * 128:(f + 1) * 128], in_=xT[:, g * 4 + f, cs], identity=idf[:]),
                rd=[xTk], wr=[pxk])
        xs, xsk = xsr.next()
        P.act(lambda e, xs=xs, px=px: e.copy(out=xs[:], in_=px[:]), rd=[pxk], wr=[xsk])
        pb, pbk = pm.next()
        pbb = pb[:].bitcast(BF16)
        P.pe(lambda e, pbb=pbb, bcT=bcT, g=g, cs=cs: e.transpose(out=pbb[:, 0:128], in_=bcT[:, g, cs],
                                                                 identity=idb[:]), rd=[bcTk], wr=[pbk])
        Bt, Btk = Btr.next()
        P.dve(lambda e, Bt=Bt, pbb=pbb: e.tensor_copy(out=Bt[:], in_=pbb[:, 0:128]), rd=[pbk], wr=[Btk])
        sm, smk = smr.next()
        kdA, kacol, keacol, keal, kdte = (Tok(n_) for n_ in ('dA', 'acol', 'eacol', 'eal', 'dte'))
        dtg = dt_[:, j, g * 8:(g + 1) * 8]
        P.dve(lambda e, sm=sm, dtg=dtg, g=g: e.tensor_tensor(out=sm[:, 0:8], in0=dtg,
                                                             in1=abc[:, g * 8:(g + 1) * 8], op=ALU.mult),
              rd=[dtk, cst], wr=[smk, kdA])
        pa, pak = pm.next()
        P.pe(lambda e, pa=pa, sm=sm: e.matmul(pa[:, 0:8], lhsT=tri[:], rhs=sm[:, 0:8], start=True, stop=True),
             rd=[kdA, cst], wr=[pak])
        P.act(lambda e, sm=sm, pa=pa: e.copy(out=sm[:, 8:16], in_=pa[:, 0:8]), rd=[pak, smk], wr=[kacol])
        P.act(lambda e, sm=sm, pa=pa: e.activation(out=sm[:, 16:24], in_=pa[:, 0:8], func=AF.Exp),
              rd=[pak, smk], wr=[keacol])
        rb, rbk = rbr.next()
        P.dve(lambda e, rb=rb, sm=sm: e.tensor_tensor(
            out=rb[:], in0=tri[:].unsqueeze(1).to_broadcast([128, 8, 128]),
            in1=sm[:, 0:8].unsqueeze(2).to_broadcast([128, 8, 128]), op=ALU.mult),
            rd=[kdA, cst], wr=[rbk])
        bc, bck = pbc.next()
        for hh in range(2):
            P.pe(lambda e, bc=bc, rb=rb, hh=hh: e.matmul(
                bc[:, hh * 512:(hh + 1) * 512], lhsT=ones_f[:],
                rhs=rb[:, hh * 4:(hh + 1) * 4, :].rearrange("p h l -> p (h l)"), start=True, stop=True),
                rd=[rbk, ck], wr=[bck])
        bc3 = bc[:].rearrange("p (h l) -> p h l", l=128)
        seg, segk = segr.next()
        for h in range(8):
            P.dve(lambda e, seg=seg, bc3=bc3, sm=sm, h=h: e.tensor_scalar(
                out=seg[:, h, :], in0=bc3[:, h, :], scalar1=sm[:, 8 + h:9 + h], scalar2=0.0,
                op0=ALU.subtract, op1=ALU.min), rd=[bck, kacol], wr=[segk])
        eb, ebk = ebr.next()
        P.act(lambda e, seg=seg, eb=eb: e.activation(out=eb[:], in_=seg[:], func=AF.Exp), rd=[segk], wr=[ebk])
        P.act(lambda e, sm=sm, bc3=bc3: e.activation(out=sm[:, 24:32], in_=bc3[:, :, 127], func=AF.Exp),
              rd=[bck, smk], wr=[keal])
        P.dve(lambda e, sm=sm, bc3=bc3: e.tensor_tensor(out=sm[:, 32:40], in0=bc3[:, :, 127],
                                                        in1=sm[:, 8:16], op=ALU.subtract),
              rd=[bck, kacol, smk], wr=[kdte])
        P.act(lambda e, sm=sm: e.activation(out=sm[:, 32:40], in_=sm[:, 32:40], func=AF.Exp),
              rd=[kdte], wr=[kdte])
        pc, pck = pm.next()
        P.pe(lambda e, pc=pc, bcT=bcT, g=g, cs=cs: e.matmul(
            pc[:, 0:128], lhsT=bcT[:, g, cs], rhs=bcT[:, 2 + g, cs], start=True, stop=True),
            rd=[bcTk], wr=[pck])
        cbm, cbmk = cbmr.next()
        P.dve(lambda e, cbm=cbm, pc=pc: e.tensor_tensor(out=cbm[:], in0=pc[:, 0:128], in1=tri[:], op=ALU.mult),
              rd=[pck, cst], wr=[cbmk])
        G, Gk = Gr.next()
        P.dve(lambda e, G=G, eb=eb, cbm=cbm: e.tensor_tensor(
            out=G[:], in0=eb[:], in1=cbm[:].unsqueeze(1).to_broadcast([128, 8, 128]), op=ALU.mult),
            rd=[ebk, cbmk], wr=[Gk])
        x32, x32k = x32r.next()
        xs3 = lambda t: t[:].rearrange("p (h d) -> p h d", d=64)
        P.pool(lambda e, x32=x32, xs=xs, dtg=dtg: e.tensor_tensor(
            out=xs3(x32), in0=xs3(xs), in1=dtg.unsqueeze(2).to_broadcast([128, 8, 64]), op=ALU.mult),
            rd=[xsk, dtk], wr=[x32k])
        xb, xbk = xbr.next()
        P.act(lambda e, xb=xb, x32=x32: e.copy(out=xb[:], in_=x32[:]), rd=[x32k], wr=[xbk])
        xe, xek = xer.next()
        P.dve(lambda e, xe=xe, x32=x32, sm=sm: e.tensor_tensor(
            out=xs3(xe), in0=xs3(x32), in1=sm[:, 32:40].unsqueeze(2).to_broadcast([128, 8, 64]),
            op=ALU.mult), rd=[x32k, kdte], wr=[xek])
        xd, xdk = xdr.next()
        P.pool(lambda e, xd=xd, xs=xs, g=g: e.tensor_tensor(
            out=xs3(xd), in0=xs3(xs), in1=Dbc[:, g * 8:(g + 1) * 8, :], op=ALU.mult),
            rd=[xsk, cst], wr=[xdk])
        return dict(B_=B_, j=j, g=g, cs=cs, sm=sm, smk=smk, keacol=keacol, keal=keal, Bt=Bt, Btk=Btk, G=G, Gk=Gk, xb=xb, xbk=xbk, xe=xe, xek=xek,
                    xd=xd, xdk=xdk, xs3=xs3)

    def back(F_):
        keacol, keal = F_['keacol'], F_['keal']
        B_, j, g, cs, sm, smk, Bt, Btk, G, Gk, xb, xbk, xe, xek, xd, xdk, xs3 = (F_[k_] for k_ in (
            'B_', 'j', 'g', 'cs', 'sm', 'smk', 'Bt', 'Btk', 'G', 'Gk', 'xb', 'xbk', 'xe', 'xek', 'xd', 'xdk', 'xs3'))
        z_, zk, bcT, bcTk, oT, oTk = (B_[k_] for k_ in ('z_', 'zk', 'bcT', 'bcTk', 'oT', 'oTk'))
        yd, ydk = pyd.next()
        for h in range(8):
            P.pe(lambda e, yd=yd, G=G, xb=xb, h=h: e.matmul(
                yd[:, h * 64:(h + 1) * 64], lhsT=G[:, h, :], rhs=xb[:, h * 64:(h + 1) * 64],
                start=True, stop=True), rd=[Gk, xbk], wr=[ydk])
        yo, yok = pyo.next()
        P.pe(lambda e, yo=yo, bcT=bcT, g=g, cs=cs: e.matmul(
            yo[:], lhsT=bcT[:, 2 + g, cs], rhs=Sbf[g][:], start=True, stop=True),
            rd=[bcTk, Sbk[g]], wr=[yok])
        y1, y1k = y1r.next()
        P.dve(lambda e, y1=y1, yo=yo, sm=sm: e.tensor_tensor(
            out=xs3(y1), in0=yo[:].rearrange("p (h d) -> p h d", d=64),
            in1=sm[:, 16:24].unsqueeze(2).to_broadcast([128, 8, 64]), op=ALU.mult),
            rd=[yok, keacol], wr=[y1k])
        P.dve(lambda e, y1=y1, yd=yd: e.tensor_tensor(out=y1[:], in0=y1[:], in1=yd[:], op=ALU.add),
              rd=[y1k, ydk], wr=[y1k])
        P.dve(lambda e, y1=y1, xd=xd: e.tensor_tensor(out=y1[:], in0=y1[:], in1=xd[:], op=ALU.add),
              rd=[y1k, xdk], wr=[y1k])
        st_, stk = pst.next()
        P.pe(lambda e, st_=st_, Bt=Bt, xe=xe: e.matmul(st_[:], lhsT=Bt[:], rhs=xe[:], start=True, stop=True),
             rd=[Btk, xek], wr=[stk])
        P.dve(lambda e, g=g, sm=sm: e.tensor_tensor(
            out=xs3(S32[g]), in0=xs3(S32[g]), in1=sm[:, 24:32].unsqueeze(2).to_broadcast([128, 8, 64]),
            op=ALU.mult), rd=[Sk[g], keal], wr=[Sk[g]])
        P.dve(lambda e, g=g, st_=st_: e.tensor_tensor(out=S32[g][:], in0=S32[g][:], in1=st_[:], op=ALU.add),
              rd=[Sk[g], stk], wr=[Sk[g]])
        P.act(lambda e, g=g: e.copy(out=Sbf[g][:], in_=S32[g][:]), rd=[Sk[g]], wr=[Sbk[g]])
        py, pyk = pm.next()
        for f in range(4):
            P.pe(lambda e, py=py, y1=y1, f=f: e.transpose(
                out=py[:, f * 128:(f + 1) * 128], in_=y1[:, f * 128:(f + 1) * 128], identity=idf[:]),
                rd=[y1k], wr=[pyk])
        gv, gvk = gvr.next()
        P.dve(lambda e, gv=gv, py=py, z_=z_, g=g, cs=cs: e.tensor_tensor(
            out=gv[:], in0=py[:].rearrange("p (f t) -> p f t", t=128), in1=z_[:, g * 4:(g + 1) * 4, cs],
            op=ALU.mult), rd=[pyk, zk], wr=[gvk])
        sq, sqk = sqr.next()
        P.act(lambda e, sq=sq, gv=gv: e.activation(out=sq[:], in_=gv[:], func=AF.Square), rd=[gvk], wr=[sqk])
        pq, pqk = pm.next()
        for f in range(4):
            P.pe(lambda e, pq=pq, sq=sq, f=f: e.matmul(pq[:, 0:128], lhsT=ones_f[:], rhs=sq[:, f, :],
                                                       start=(f == 0), stop=(f == 3)),
                 rd=[sqk, ck], wr=[pqk])
        rs, rsk = rsr.next()
        P.act(lambda e, rs=rs, pq=pq: e.activation(out=rs[:], in_=pq[:, 0:128], func=AF.Ln, bias=EPS,
                                                   scale=1.0 / 512), rd=[pqk], wr=[rsk])
        P.act(lambda e, rs=rs: e.activation(out=rs[:], in_=rs[:], func=AF.Exp, scale=-0.5), rd=[rsk], wr=[rsk])
        for f in range(4):
            P.dve(lambda e, oT=oT, gv=gv, rs=rs, f=f, g=g, cs=cs: e.scalar_tensor_tensor(
                out=oT[:, g * 4 + f, cs], in0=gv[:, f, :], scalar=nw[:, g * 4 + f:g * 4 + f + 1], in1=rs[:],
                op0=ALU.mult, op1=ALU.mult), rd=[gvk, rsk, cst], wr=[oTk])

    def block_epilogue(B_):
        oT, oTk, t0 = B_['oT'], B_['oTk'], B_['t0']
        P.dma(lambda e, oT=oT, t0=t0: e.dma_start(
            out=yT_out.rearrange("(k p) t -> p k t", p=128)[:, :, t0:t0 + TB], in_=oT[:]), rd=[oTk])

    passes = [(b, j, g) for b in range(NB) for j in range(TB // 128) for g in range(2)]
    blocks = {}
    pend = []
    DEPTH_F = 1

    def retire():
        F0 = pend.pop(0)
        back(F0)
        pb, pj, pg = F0["key"]
        if (pj, pg) == (TB // 128 - 1, 1):
            block_epilogue(blocks.pop(pb))

    for (b, j, g) in passes:
        if b not in blocks:
            blocks[b] = block_prologue(b)
        F_ = front(blocks[b], j, g)
        F_["key"] = (b, j, g)
        pend.append(F_)
        if len(pend) > DEPTH_F:
            retire()
    while pend:
        retire()


NCORES = 8
BATCH = 2
TC = BATCH * SEQ // NCORES
DEPTH = 4


def lay_colchunk(W):
    K, N = W.shape
    return np.ascontiguousarray(
        W.reshape(K // 128, 128, N // 128, 128).transpose(2, 1, 0, 3).reshape(N // 128, 128, K))


def lay_slab(W):
    K, N = W.shape
    return np.ascontiguousarray(W.reshape(K // 128, 128, N // 512, 512).transpose(2, 1, 0, 3))


def ssd_core_inputs(inp, j, gl):
    w_in = inp["ssd_w_in"][j]
    cols = np.concatenate([
        np.arange(gl * 1024, (gl + 1) * 1024),
        4096 + np.arange(gl * 1024, (gl + 1) * 1024),
        8192 + np.arange(gl * 256, (gl + 1) * 256),
        9216 + np.arange(gl * 256, (gl + 1) * 256)])
    ch = cols[1024:] - 4096
    dtc = 10240 + np.arange(gl * 16, (gl + 1) * 16)
    return {
        "s_w": lay_colchunk(w_in[:, cols]),
        "s_wdt": np.ascontiguousarray(w_in[:, dtc].reshape(KC_D, 128, 16).transpose(1, 0, 2)),
        "s_dtb": np.ascontiguousarray(inp["ssd_dt_bias"][j, gl * 16:(gl + 1) * 16]),
        "s_cw": np.ascontiguousarray(inp["ssd_conv_w"][j][:, ch].reshape(4, 12, 128).transpose(2, 1, 0)),
        "s_cb": np.ascontiguousarray(inp["ssd_conv_b"][j][ch].reshape(12, 128).T),
        "s_alog": np.ascontiguousarray(inp["ssd_a_log"][j, gl * 16:(gl + 1) * 16]),
        "s_dsk": np.ascontiguousarray(inp["ssd_d"][j, gl * 16:(gl + 1) * 16]),
        "s_nw": np.ascontiguousarray(inp["ssd_norm"][j, gl * 1024:(gl + 1) * 1024].reshape(8, 128).T),
    }


def attn_core_inputs(inp, j, gl):
    w = inp["da_w_qkv"][j]
    cols = np.concatenate([which * D + np.arange(gl * 512, (gl + 1) * 512) for which in range(3)])
    return {
        "a_w": lay_colchunk(w[:, cols]),
        "a_lam": np.ascontiguousarray(np.stack([inp["da_lambda_q1"][j], inp["da_lambda_k1"][j],
                                                inp["da_lambda_q2"][j], inp["da_lambda_k2"][j]])),
        "a_sub": np.ascontiguousarray(inp["da_subln"][j]),
    }


def lambda_init(i):
    return 0.8 - 0.6 * math.exp(-0.3 * i)


def _new_nc():
    _UID[0] = 0
    return bass.Bass("TRN2", target_bir_lowering=False)


def build_first():
    import contextlib
    nc = _new_nc()
    x = nc.dram_tensor("x", [TC, D], F32, kind="ExternalInput").ap()
    wpre = nc.dram_tensor("wpre", [D], F32, kind="ExternalInput").ap()
    xo = nc.dram_tensor("xo", [TC, D], F32, kind="ExternalOutput").ap()
    hT = nc.dram_tensor("hT", [D, TC], BF16, kind="ExternalOutput").ap()
    with contextlib.ExitStack() as st0:
        P = Prog(nc, st0)
        with contextlib.ExitStack() as st:
            idf, idb, _ = make_ident(nc, P, st)
            phase_norm(nc, P, st, TC, x, None, None, wpre, xo, hT, idb)
            P.emit()
    return nc


def hT_blk_fn(hTg):
    v = hTg.rearrange("(r k p) t -> p r k t", r=4, p=128)
    return lambda b: v[:, b // 4, :, (b % 4) * 512:(b % 4 + 1) * 512]


def hT_blk_fn_bm(hTg):
    v = hTg.rearrange("(b p) (k t) -> p b k t", p=128, t=512)
    return lambda b: v[:, b, :, :]


def build_ssd():
    import contextlib
    nc = _new_nc()
    hTg = nc.dram_tensor("hTg", [4 * D, TC], BF16, kind="ExternalInput").ap()
    w = nc.dram_tensor("s_w", [NZX, 128, D], F32, kind="ExternalInput").ap()
    wdt = nc.dram_tensor("s_wdt", [128, KC_D, 16], F32, kind="ExternalInput").ap()
    dtb = nc.dram_tensor("s_dtb", [16], F32, kind="ExternalInput").ap()
    cw = nc.dram_tensor("s_cw", [128, 12, 4], F32, kind="ExternalInput").ap()
    cb = nc.dram_tensor("s_cb", [128, 12], F32, kind="ExternalInput").ap()
    alog = nc.dram_tensor("s_alog", [16], F32, kind="ExternalInput").ap()
    dsk = nc.dram_tensor("s_dsk", [16], F32, kind="ExternalInput").ap()
    nw = nc.dram_tensor("s_nw", [128, 8], F32, kind="ExternalInput").ap()
    zx = nc.dram_tensor("zx", [NZX * 128, SEQ], F32).ap()
    dt = nc.dram_tensor("dt", [SEQ, 16], F32).ap()
    yT = nc.dram_tensor("yT", [1024, SEQ], BF16, kind="ExternalOutput").ap()
    with contextlib.ExitStack() as st0:
        P = Prog(nc, st0)
        with contextlib.ExitStack() as st:
            phase_ssd_in(nc, P, st, hT_blk_fn(hTg), w, wdt, dtb, zx, dt)
            P.emit()
        with contextlib.ExitStack() as st:
            idf, idb, _ = make_ident(nc, P, st)
            C = make_consts(nc, P, st)
            phase_ssd_scan(nc, P, st, zx, dt, cw, cb, alog, dsk, nw, yT, idf, idb, C)
            P.emit()
    return nc


def build_attn(lam_init):
    import contextlib
    nc = _new_nc()
    hTg = nc.dram_tensor("hTg", [4 * D, TC], BF16, kind="ExternalInput").ap()
    w = nc.dram_tensor("a_w", [12, 128, D], F32, kind="ExternalInput").ap()
    lam = nc.dram_tensor("a_lam", [4, 64], F32, kind="ExternalInput").ap()
    sub = nc.dram_tensor("a_sub", [128], F32, kind="ExternalInput").ap()
    oT = nc.dram_tensor("yT", [512, SEQ], BF16, kind="ExternalOutput").ap()
    with contextlib.ExitStack() as st0:
        P = Prog(nc, st0)
        with contextlib.ExitStack() as st:
            idf, idb, _ = make_ident(nc, P, st)
            C = make_consts(nc, P, st)
            qTd = nc.dram_tensor("sc_qT", [512, SEQ], BF16).ap()
            kTd = nc.dram_tensor("sc_kT", [512, SEQ], BF16).ap()
            Vd = nc.dram_tensor("sc_V", [4, 128, SEQ // 128, 128], BF16).ap()
            negMd = nc.dram_tensor("sc_negM", [128, 8], F32).ap()
            phase_attn_proj(nc, P, st, hT_blk_fn(hTg), w, qTd, kTd, Vd, negMd, idf, idb, C)
            P.emit()
        with contextlib.ExitStack() as st:
            idf, idb, _ = make_ident(nc, P, st)
            C = make_consts(nc, P, st)
            phase_attn_core(nc, P, st, qTd, kTd, Vd, negMd, lam, sub, lam_init, oT, idf, idb, C)
            P.emit()
    return nc


def emit_token_phases(nc, P, K, AT, wo, x, npost, nfpre, nfpost, npre_next, wg, wu, wd, xo, hTn, scr):
    import contextlib
    with contextlib.ExitStack() as st:
        phase_mmT(nc, P, st, TC, K, AT, wo, scr["m"], pfx="mo")
        P.emit()
    with contextlib.ExitStack() as st:
        idf, idb, _ = make_ident(nc, P, st)
        phase_norm(nc, P, st, TC, x, scr["m"], npost, nfpre, scr["x1"], scr["h2T"], idb)
        P.emit()
    with contextlib.ExitStack() as st:
        phase_ffn_gu(nc, P, st, TC, scr["h2T"], wg, wu, scr["aT"])
        P.emit()
    with contextlib.ExitStack() as st:
        phase_mmT(nc, P, st, TC, D_FF, scr["aT"], wd, scr["y2"], pfx="md")
        P.emit()
    with contextlib.ExitStack() as st:
        idf, idb, _ = make_ident(nc, P, st)
        phase_norm(nc, P, st, TC, scr["x1"], scr["y2"], nfpost, npre_next, xo, hTn, idb)
        P.emit()


def build_tok(K, last):
    import contextlib
    nc = _new_nc()
    AT = nc.dram_tensor("AT", [K, TC], BF16, kind="ExternalInput").ap()
    wo = nc.dram_tensor("wo", [4, 128, K // 128, 512], F32, kind="ExternalInput").ap()
    x = nc.dram_tensor("x", [TC, D], F32, kind="ExternalInput").ap()
    npost = nc.dram_tensor("npost", [D], F32, kind="ExternalInput").ap()
    nfpre = nc.dram_tensor("nfpre", [D], F32, kind="ExternalInput").ap()
    nfpost = nc.dram_tensor("nfpost", [D], F32, kind="ExternalInput").ap()
    npre_next = None if last else nc.dram_tensor("npre_next", [D], F32, kind="ExternalInput").ap()
    wg = nc.dram_tensor("wg", [KC_F, 128, D], F32, kind="ExternalInput").ap()
    wu = nc.dram_tensor("wu", [KC_F, 128, D], F32, kind="ExternalInput").ap()
    wd = nc.dram_tensor("wd", [4, 128, KC_F, 512], F32, kind="ExternalInput").ap()
    xo = nc.dram_tensor("xo", [TC, D], F32, kind="ExternalOutput").ap()
    hTn = None if last else nc.dram_tensor("hT", [D, TC], BF16, kind="ExternalOutput").ap()
    scr = {"m": nc.dram_tensor("sc_m", [TC, D], F32).ap(), "x1": nc.dram_tensor("sc_x1", [TC, D], F32).ap(),
           "h2T": nc.dram_tensor("sc_h2T", [D, TC], BF16).ap(), "aT": nc.dram_tensor("sc_aT", [D_FF, TC], BF16).ap(),
           "y2": nc.dram_tensor("sc_y2", [TC, D], F32).ap()}
    with contextlib.ExitStack() as st0:
        P = Prog(nc, st0)
        emit_token_phases(nc, P, K, AT, wo, x, npost, nfpre, nfpost, npre_next, wg, wu, wd, xo, hTn, scr)
    return nc


def kernel_multilaunch(**inp):
    inp = {k: np.asarray(v) for k, v in inp.items()}
    cores = list(range(NCORES))
    xs = np.ascontiguousarray(inp["x"].reshape(NCORES, TC, D))
    res = run_bass_kernel_spmd(build_first(), [{"x": xs[c], "wpre": inp["norm_mix_pre"][0]} for c in cores],
                               core_ids=cores)
    xcur = [r["xo"] for r in res.results]
    hT = [np.asarray(r["hT"]) for r in res.results]
    for i in range(DEPTH):
        j = i // 2
        hTg = [np.concatenate(hT[4 * b:4 * b + 4], axis=0) for b in range(BATCH)]
        if i % 2 == 0:
            ncm = build_ssd()
            maps = [dict(ssd_core_inputs(inp, j, c % 4), hTg=hTg[c // 4]) for c in cores]
            K = 4096
            wo = lay_slab(inp["ssd_w_out"][j])
        else:
            ncm = build_attn(lambda_init(i))
            maps = [dict(attn_core_inputs(inp, j, c % 4), hTg=hTg[c // 4]) for c in cores]
            K = 2048
            wo = lay_slab(inp["da_w_out"][j])
        res = run_bass_kernel_spmd(ncm, maps, core_ids=cores)
        yT = [np.asarray(r["yT"]) for r in res.results]
        yall = [np.concatenate(yT[4 * b:4 * b + 4], axis=0) for b in range(BATCH)]
        last = i == DEPTH - 1
        wg, wu, wd = lay_colchunk(inp["ffn_w_gate"][i]), lay_colchunk(inp["ffn_w_up"][i]), lay_slab(inp["ffn_w_down"][i])
        maps = []
        for c in cores:
            m = {"AT": np.ascontiguousarray(yall[c // 4][:, (c % 4) * TC:(c % 4 + 1) * TC]), "wo": wo, "x": xcur[c],
                 "npost": inp["norm_mix_post"][i], "nfpre": inp["norm_ffn_pre"][i], "nfpost": inp["norm_ffn_post"][i],
                 "wg": wg, "wu": wu, "wd": wd}
            if not last:
                m["npre_next"] = inp["norm_mix_pre"][i + 1]
            maps.append(m)
        res = run_bass_kernel_spmd(build_tok(K, last), maps, core_ids=cores)
        xcur = [r["xo"] for r in res.results]
        if not last:
            hT = [np.asarray(r["hT"]) for r in res.results]
    out = np.stack([np.asarray(a) for a in xcur]).reshape(BATCH, SEQ, D).astype(np.float32)
    return out


RG4 = [[0, 1, 2, 3], [4, 5, 6, 7]]
RG8 = [list(range(NCORES))]


def phase_select(nc, P, stack, g8, bsel_in, hTg):
    sb = nc.sbuf_tensor
    w = stack.enter_context(sb(U("sel_w"), [128, 2], F32))
    wk = Tok("sel_w")
    P.dma(lambda e: e.dma_start(out=w[:], in_=bsel_in.partition_broadcast(128)), wr=[wk])
    ar = Ring(stack, sb, "sel_a", [128, KC_D, 512], BF16, 2)
    br = Ring(stack, sb, "sel_b", [128, KC_D, 512], BF16, 2)
    orr = Ring(stack, sb, "sel_o", [128, KC_D, 512], BF16, 2)
    v8 = g8.rearrange("(r k p) t -> p r k t", r=8, p=128)
    vo = hTg.rearrange("(r k p) t -> p r k t", r=4, p=128)
    for r in range(4):
        for tb in range(TC // 512):
            ts = slice(tb * 512, (tb + 1) * 512)
            a, ak = ar.next()
            b, bk = br.next()
            P.dma(lambda e, a=a, r=r, ts=ts: e.dma_start(out=a[:], in_=v8[:, r, :, ts]), wr=[ak])
            P.dma(lambda e, b=b, r=r, ts=ts: e.dma_start(out=b[:], in_=v8[:, 4 + r, :, ts]), wr=[bk], q="q_pool")
            o, ok = orr.next()
            P.dve(lambda e, o=o, a=a: e.tensor_scalar(out=o[:], in0=a[:], scalar1=w[:, 0:1], scalar2=None,
                                                      op0=ALU.mult), rd=[ak, wk], wr=[ok])
            P.dve(lambda e, o=o, b=b: e.scalar_tensor_tensor(out=o[:], in0=b[:], scalar=w[:, 1:2], in1=o[:],
                                                             op0=ALU.mult, op1=ALU.add), rd=[bk, wk, ok], wr=[ok])
            P.dma(lambda e, o=o, r=r, ts=ts: e.dma_start(out=vo[:, r, :, ts], in_=o[:]), rd=[ok])


def build_fused():
    import contextlib
    nc = _new_nc()
    dt_ = nc.dram_tensor
    ext = lambda n, s, d=F32: dt_(n, s, d, kind="ExternalInput").ap()
    x = ext("x", [TC, D])
    sel4 = ext("sel4", [4])
    nmpre, nmpost = ext("nmpre", [DEPTH, D]), ext("nmpost", [DEPTH, D])
    nfpre, nfpost = ext("nfpre", [DEPTH, D]), ext("nfpost", [DEPTH, D])
    ffn = [(ext(f"wg{i}", [KC_F, 128, D]), ext(f"wu{i}", [KC_F, 128, D]), ext(f"wd{i}", [4, 128, KC_F, 512]))
           for i in range(DEPTH)]
    ssd = [dict(w=ext(f"s_w{j}", [NZX, 128, D]), wdt=ext(f"s_wdt{j}", [128, KC_D, 16]), dtb=ext(f"s_dtb{j}", [16]),
                cw=ext(f"s_cw{j}", [128, 12, 4]), cb=ext(f"s_cb{j}", [128, 12]), alog=ext(f"s_alog{j}", [16]),
                dsk=ext(f"s_dsk{j}", [16]), nw=ext(f"s_nw{j}", [128, 8]), wo=ext(f"s_wo{j}", [4, 128, 8, 512]))
           for j in range(2)]
    att = [dict(w=ext(f"a_w{j}", [12, 128, D]), lam=ext(f"a_lam{j}", [4, 64]), sub=ext(f"a_sub{j}", [128]),
                wo=ext(f"a_wo{j}", [4, 128, 4, 512])) for j in range(2)]
    xo = dt_("xo", [TC, D], F32, kind="ExternalOutput").ap()
    scr = lambda n, s, d=F32: dt_(n, s, d).ap()
    hT4 = scr("sc_hT4", [16 * 128, KC_D * 512], BF16)
    hTg = scr("sc_hTg", [16 * 128, KC_D * 512], BF16)
    zx = scr("sc_zx", [NZX * 128, SEQ])
    dtt = scr("sc_dt", [SEQ, 16])
    yT = scr("sc_yT", [1024, SEQ], BF16)
    mpA, mpB = scr("sc_mpA", [SEQ, 1024]), scr("sc_mpB", [SEQ, 1024])
    mA, mB = scr("sc_mA", [TC, 1024]), scr("sc_mB", [TC, 1024])
    qTd = scr("sc_qT", [512, SEQ], BF16)
    kTd = scr("sc_kT", [512, SEQ], BF16)
    Vd = scr("sc_V", [4, 128, SEQ // 128, 128], BF16)
    negMd = scr("sc_negM", [128, 8])
    xa = scr("sc_xa", [TC, D])
    x1 = scr("sc_x1", [TC, D])
    h2T = scr("sc_h2T", [D, TC], BF16)
    aT = scr("sc_aT", [D_FF, TC], BF16)
    y2 = scr("sc_y2", [TC, D])
    with contextlib.ExitStack() as st0:
        P = Prog(nc, st0)
        with contextlib.ExitStack() as st:
            idf, idb, _ = make_ident(nc, P, st)
            phase_norm(nc, P, st, TC, x, None, None, nmpre[0], None, hT4, idb, sel4=sel4)
            P.emit(reorder=True)
        xcur = x
        for i in range(DEPTH):
            j = i // 2
            last = i == DEPTH - 1
            ptoks = [Tok(f"hTg{pc}") for pc in range(8)]
            for pc in range(8):
                rs_ = slice(pc * 256, (pc + 1) * 256)
                P.cc(lambda e, rs_=rs_: e.collective_compute("AllReduce", ALU.add, replica_groups=RG4,
                                                             ins=[hT4[rs_, :]], outs=[hTg[rs_, :]]),
                     wr=[ptoks[pc]])
            hT_tok = lambda b, ptoks=ptoks: ptoks[b // 2]
            if i % 2 == 0:
                s = ssd[j]
                with contextlib.ExitStack() as st:
                    phase_ssd_in(nc, P, st, hT_blk_fn_bm(hTg), s["w"], s["wdt"], s["dtb"], zx, dtt, hT_tok=hT_tok)
                    P.emit(reorder=True)
                with contextlib.ExitStack() as st:
                    idf, idb, _ = make_ident(nc, P, st)
                    C = make_consts(nc, P, st)
                    phase_ssd_scan(nc, P, st, zx, dtt, s["cw"], s["cb"], s["alog"], s["dsk"], s["nw"], yT,
                                   idf, idb, C)
                    P.emit(reorder=True)
                K, yT_use, wo = 1024, yT, s["wo"]
            else:
                a = att[j]
                with contextlib.ExitStack() as st:
                    idf, idb, _ = make_ident(nc, P, st)
                    C = make_consts(nc, P, st)
                    phase_attn_proj(nc, P, st, hT_blk_fn_bm(hTg), a["w"], qTd, kTd, Vd, negMd, idf, idb, C,
                                    hT_tok=hT_tok)
                    P.emit(reorder=True)
                with contextlib.ExitStack() as st:
                    idf, idb, _ = make_ident(nc, P, st)
                    C = make_consts(nc, P, st)
                    phase_attn_core(nc, P, st, qTd, kTd, Vd, negMd, a["lam"], a["sub"], lambda_init(i),
                                    yT[0:512, :], idf, idb, C)
                    P.emit(reorder=True)
                K, yT_use, wo = 512, yT[0:512, :], a["wo"]
            with contextlib.ExitStack() as st:
                htoks = ([], [])
                def rs_half(hf, htoks=htoks):
                    src, dst = (mpA, mA) if hf == 0 else (mpB, mB)
                    P.cc(lambda e: e.collective_compute("ReduceScatter", ALU.add, replica_groups=RG4,
                                                        ins=[src], outs=[dst], dma_qos="P2"), rd=list(htoks[hf]))
                phase_mmT(nc, P, st, SEQ, K, yT_use, wo, None, Th=2048, pfx="mo", halves=(mpA, mpB), htoks=htoks,
                          half_done=rs_half)
                P.emit(reorder=True)
            with contextlib.ExitStack() as st:
                idf, idb, _ = make_ident(nc, P, st)
                phase_norm(nc, P, st, TC, xcur, (mA, mB), nmpost[i], nfpre[i], x1, h2T, idb)
                P.emit(reorder=True)
            wg, wu, wd = ffn[i]
            with contextlib.ExitStack() as st:
                phase_ffn_gu(nc, P, st, TC, h2T, wg, wu, aT)
                P.emit(reorder=True)
            with contextlib.ExitStack() as st:
                phase_mmT(nc, P, st, TC, D_FF, aT, wd, y2, pfx="md")
                P.emit(reorder=True)
            with contextlib.ExitStack() as st:
                idf, idb, _ = make_ident(nc, P, st)
                phase_norm(nc, P, st, TC, x1, y2, nfpost[i], None if last else nmpre[i + 1],
                           xo if last else xa, None if last else hT4, idb, sel4=None if last else sel4)
                P.emit(reorder=True)
            xcur = xa
    return nc


def fused_inputs(inp):
    inp = {k: np.asarray(v) for k, v in inp.items()}
    xs = np.ascontiguousarray(inp["x"].reshape(NCORES, TC, D))
    shared = {"nmpre": inp["norm_mix_pre"], "nmpost": inp["norm_mix_post"],
              "nfpre": inp["norm_ffn_pre"], "nfpost": inp["norm_ffn_post"]}
    for i in range(DEPTH):
        shared[f"wg{i}"] = lay_colchunk(inp["ffn_w_gate"][i])
        shared[f"wu{i}"] = lay_colchunk(inp["ffn_w_up"][i])
        shared[f"wd{i}"] = lay_slab(inp["ffn_w_down"][i])
    maps = []
    for c in range(NCORES):
        gl = c % 4
        m = dict(shared)
        m["x"] = xs[c]
        m["sel4"] = np.eye(4, dtype=np.float32)[c % 4]
        for j in range(2):
            for k, v in ssd_core_inputs(inp, j, gl).items():
                m[f"{k}{j}"] = v
            m[f"s_wo{j}"] = lay_slab(inp["ssd_w_out"][j][gl * 1024:(gl + 1) * 1024])
            for k, v in attn_core_inputs(inp, j, gl).items():
                m[f"{k}{j}"] = v
            m[f"a_wo{j}"] = lay_slab(inp["da_w_out"][j][gl * 512:(gl + 1) * 512])
        maps.append(m)
    return maps


def kernel_fused(**inp):
    maps = fused_inputs(inp)
    res = run_bass_kernel_spmd(build_fused(), maps, core_ids=list(range(NCORES)))
    return np.stack([np.asarray(r["xo"]) for r in res.results]).reshape(BATCH, SEQ, D).astype(np.float32)


def kernel(**inputs):
    return kernel_fused(**inputs)
```

```python
import math
import numpy as np
import ml_dtypes
import concourse.bass as bass
import concourse.mybir as mybir
from concourse.bass_utils import run_bass_kernel_spmd

F32 = mybir.dt.float32
BF16 = mybir.dt.bfloat16
AF = mybir.ActivationFunctionType
ALU = mybir.AluOpType
AX = mybir.AxisListType

NDMA_SLOTS = 6


class Tok:
    __slots__ = ("w", "r", "ra", "name")

    def __init__(self, name=""):
        self.w = {}
        self.r = {}
        self.ra = []
        self.name = name


class Prog:
    COMPUTE = ("pe", "act", "dve", "pool", "cc")
    QUEUES = ("q_sp", "q_pool", "q_act")
    ISSUE = {"pe": "pe", "act": "act", "dve": "dve", "pool": "pool", "cc": "pool",
             "q_sp": "sp", "q_pool": "pool", "q_act": "act"}

    def __init__(self, nc, stack):
        self.nc = nc
        self.ops = []
        self.base = 0
        self.cnt = {e: 0 for e in self.COMPUTE}
        self.dcnt = {q: 0 for q in self.QUEUES}
        self.waited = {e: {} for e in ("pe", "act", "dve", "pool", "sp")}
        self.sems = {}
        for e in self.COMPUTE:
            self.sems[e] = stack.enter_context(nc.semaphore(f"s_{e}"))
        for q in self.QUEUES:
            for s in range(NDMA_SLOTS):
                self.sems[(q, s)] = stack.enter_context(nc.semaphore(f"s_{q}{s}"))

    def op(self, eng, fn, rd=(), wr=()):
        self.ops.append((eng, fn, tuple(rd), tuple(wr)))

    def pe(self, fn, rd=(), wr=()):
        self.op("pe", fn, rd, wr)

    def act(self, fn, rd=(), wr=()):
        self.op("act", fn, rd, wr)

    def dve(self, fn, rd=(), wr=()):
        self.op("dve", fn, rd, wr)

    def pool(self, fn, rd=(), wr=()):
        self.op("pool", fn, rd, wr)

    def dma(self, fn, rd=(), wr=(), q="q_sp"):
        self.op(q, fn, rd, wr)

    def cc(self, fn, rd=(), wr=()):
        self.op("cc", fn, rd, wr)

    COST = {"pe": 0.27, "act": 0.5, "dve": 0.55, "pool": 1.2, "cc": 0.1, "q_sp": 0.06, "q_pool": 0.3, "q_act": 0.06}
    DONE_LAT = {"q_sp": 2.5, "q_pool": 3.0, "q_act": 2.5, "cc": 100.0}

    def _schedule(self, ops, order_deps, W=16):
        n = len(ops)
        issue = [self.ISSUE[o[0]] for o in ops]
        per = {e: [] for e in ("pe", "act", "dve", "pool", "sp")}
        for i in range(n):
            per[issue[i]].append(i)
        head = {e: 0 for e in per}
        done = [False] * n
        fin = [0.0] * n
        tfree = {e: 0.0 for e in per}
        order = []
        ready_t = [None] * n
        while len(order) < n:
            best = None
            for e, lst in per.items():
                h = head[e]
                while h < len(lst) and done[lst[h]]:
                    h += 1
                head[e] = h
                cnt = 0
                k = h
                while k < len(lst) and cnt < W:
                    i = lst[k]
                    k += 1
                    if done[i]:
                        continue
                    cnt += 1
                    rt = ready_t[i]
                    if rt is None:
                        ok = True
                        rt = 0.0
                        for d in order_deps[i]:
                            if not done[d]:
                                ok = False
                                break
                            f = fin[d] + (0.0 if issue[d] == e and not ops[d][0].startswith("q_") else 0.25)
                            if f > rt:
                                rt = f
                        if not ok:
                            continue
                        ready_t[i] = rt
                    st = rt if rt > tfree[e] else tfree[e]
                    key = (st, i)
                    if best is None or key < best[0]:
                        best = (key, i, e)
            (st, _), i, e = best
            c = self.COST[ops[i][0]]
            tfree[e] = st + c
            fin[i] = st + c + self.DONE_LAT.get(ops[i][0], 0.0)
            done[i] = True
            order.append(i)
        return order

    def emit(self, reorder=False):
        nc = self.nc
        ops = self.ops
        n = len(ops)
        base = self.base
        deps = [None] * n
        odeps = [None] * n
        signals = [False] * n
        for i, (eng, fn, rd, wr) in enumerate(ops):
            gi = base + i
            is_dma = eng.startswith("q_")
            key = ("dma", gi) if is_dma else eng
            d = set()
            od = set()
            for t in rd:
                for k, j in t.w.items():
                    if j >= base:
                        od.add(j - base)
                    if k == key and key == "pe":
                        continue
                    if j >= base:
                        d.add(j - base)
            for t in wr:
                for k, j in t.w.items():
                    if j >= base:
                        od.add(j - base)
                    if k == key:
                        continue
                    if j >= base:
                        d.add(j - base)
                for k, j in t.ra:
                    if j >= base:
                        od.add(j - base)
                    if k == key:
                        continue
                    if j >= base:
                        d.add(j - base)
            for t in rd:
                t.r[key] = gi
                t.ra.append((key, gi))
            for t in wr:
                t.w = {key: gi}
                t.r = {}
                t.ra = []
            od.discard(i)
            deps[i] = d
            odeps[i] = od
            for j in d:
                signals[j] = True
            if is_dma or eng == "cc":
                signals[i] = True
        if reorder and n > 2:
            order = self._schedule(ops, odeps)
            pos = [0] * n
            for p_, i in enumerate(order):
                pos[i] = p_
            ops = [ops[i] for i in order]
            deps = [{pos[j] for j in deps[i]} for i in order]
            signals = [signals[i] for i in order]
            self.ops = ops
        sig = [None] * n
        dma_idx = [None] * n
        for i, (eng, fn, rd, wr) in enumerate(ops):
            if eng.startswith("q_"):
                k = self.dcnt[eng]
                self.dcnt[eng] += 1
                dma_idx[i] = k
                sig[i] = ((eng, k % NDMA_SLOTS), 16 * (k // NDMA_SLOTS + 1))
            elif signals[i]:
                self.cnt[eng] += 1
                sig[i] = (eng, self.cnt[eng])
        waited = self.waited
        plan = {e: [] for e in ("pe", "act", "dve", "pool", "sp")}
        for i, (eng, fn, rd, wr) in enumerate(ops):
            ie = self.ISSUE[eng]
            ws = []
            need = {}
            for j in deps[i]:
                sk, v = sig[j]
                if need.get(sk, 0) < v:
                    need[sk] = v
            if eng.startswith("q_"):
                k = dma_idx[i]
                if k >= NDMA_SLOTS:
                    sk = (eng, k % NDMA_SLOTS)
                    v = 16 * (k // NDMA_SLOTS)
                    if need.get(sk, 0) < v:
                        need[sk] = v
            for sk, v in need.items():
                if waited[ie].get(sk, 0) < v:
                    waited[ie][sk] = v
                    ws.append((sk, v))
            plan[ie].append((i, ws))
        final = []
        for q, c in self.dcnt.items():
            for s in range(min(c, NDMA_SLOTS)):
                last = 16 * ((c - 1 - s) // NDMA_SLOTS + 1)
                final.append(((q, s), last))
        if self.cnt["cc"]:
            final.append(("cc", self.cnt["cc"]))
        sems = self.sems
        with nc.Block() as block:
            engs = {"pe": block.tensor, "act": block.scalar, "dve": block.vector,
                    "pool": block.gpsimd, "sp": block.sync}

            def mk(ie):
                def body(e):
                    for (i, ws) in plan[ie]:
                        for sk, v in ws:
                            e.wait_ge(sems[sk], v)
                        ins = ops[i][1](e)
                        if sig[i] is not None:
                            sk, v = sig[i]
                            ins.then_inc(sems[sk], 1 if isinstance(sk, str) else 16)
                    if ie == "sp":
                        for sk, v in final:
                            if waited["sp"].get(sk, 0) < v:
                                waited["sp"][sk] = v
                                e.wait_ge(sems[sk], v)
                return body

            for ie in ("sp", "pe", "act", "dve", "pool"):
                if plan[ie] or ie == "sp":
                    engs[ie](mk(ie))
        self.base += n
        self.ops = []


_UID = [0]


def U(name):
    _UID[0] += 1
    return f"{name}_{_UID[0]}"


class Ring:
    def __init__(self, stack, alloc, name, shape, dtype, n):
        self.bufs = []
        for i in range(n):
            t = stack.enter_context(alloc(U(f"{name}{i}"), list(shape), dtype))
            self.bufs.append((t, Tok(f"{name}{i}")))
        self.i = 0

    def next(self):
        b = self.bufs[self.i % len(self.bufs)]
        self.i += 1
        return b


D = 2048
KC_D = D // 128
EPS = 1e-6


def bcast_rows(ap_1d, nparts):
    return ap_1d.partition_broadcast(nparts)


def emit_rstd(P, ss, rstd, tok_ss, tok_rstd, n, width):
    P.act(lambda e: e.activation(out=rstd, in_=ss, func=AF.Ln, bias=EPS, scale=1.0 / width),
          rd=[tok_ss], wr=[tok_rstd])
    P.act(lambda e: e.activation(out=rstd, in_=rstd, func=AF.Exp, scale=-0.5),
          rd=[tok_rstd], wr=[tok_rstd])


def phase_norm(nc, P, stack, T, x_in, y_in, wpost, wpre, x_out, hT_out, ident_bf, sel4=None):
    nt = T // 128
    sb = nc.sbuf_tensor
    xr = Ring(stack, sb, "n_x", [128, D], F32, 2)
    yr = Ring(stack, sb, "n_y", [128, D], F32, 2) if y_in is not None else None
    tr = Ring(stack, sb, "n_t", [128, D], F32, 2)
    sqr = Ring(stack, sb, "n_sq", [128, D], BF16, 1)
    hr = Ring(stack, sb, "n_h", [128, D], BF16, 2)
    htr = Ring(stack, sb, "n_hT", [128, KC_D, 512], BF16, 2)
    str_ = Ring(stack, sb, "n_st", [128, 4], F32, 4)
    ptr = Ring(stack, nc.psum_tensor, "n_pt", [128, 1024], BF16, 4)
    if sel4 is not None:
        scr4 = Ring(stack, sb, "n_sc4", [128, KC_D, 512], BF16, 2)
        s4 = stack.enter_context(sb(U("n_sel4"), [128, 4], F32))
        s4k = Tok("n_sel4")
        P.dma(lambda e: e.dma_start(out=s4[:], in_=sel4.partition_broadcast(128)), wr=[s4k])
    consts = []
    for nm, w in (("n_wpost", wpost), ("n_wpre", wpre)):
        if w is None:
            consts.append((None, None))
            continue
        t = stack.enter_context(sb(U(nm), [128, D], F32))
        tk = Tok(nm)
        P.dma(lambda e, t=t, w=w: e.dma_start(out=t[:], in_=bcast_rows(w, 128)), wr=[tk])
        consts.append((t, tk))
    (wpost_t, wpost_k), (wpre_t, wpre_k) = consts
    for i in range(nt):
        rows = slice(i * 128, (i + 1) * 128)
        xt, xk = xr.next()
        P.dma(lambda e, xt=xt, rows=rows: e.dma_start(out=xt[:], in_=x_in[rows, :]), wr=[xk])
        cur, curk = xt, xk
        if y_in is not None:
            yt, yk = yr.next()
            if isinstance(y_in, tuple):
                P.dma(lambda e, yt=yt, rows=rows: e.dma_start(out=yt[:, 0:1024], in_=y_in[0][rows, :]), wr=[yk],
                      q="q_pool")
                P.dma(lambda e, yt=yt, rows=rows: e.dma_start(out=yt[:, 1024:2048], in_=y_in[1][rows, :]), wr=[yk],
                      q="q_pool")
            else:
                P.dma(lambda e, yt=yt, rows=rows: e.dma_start(out=yt[:], in_=y_in[rows, :]), wr=[yk], q="q_pool")
            sq, sqk = sqr.next()
            st, stk = str_.next()
            P.act(lambda e, sq=sq, yt=yt, st=st: e.activation(out=sq[:], in_=yt[:], func=AF.Square,
                                                               accum_out=st[:, 0:1]),
                  rd=[yk], wr=[sqk, stk])
            emit_rstd(P, st[:, 0:1], st[:, 1:2], stk, stk, 128, D)
            tt, tk = tr.next()
            P.dve(lambda e, tt=tt, yt=yt, st=st: e.scalar_tensor_tensor(
                out=tt[:], in0=yt[:], scalar=st[:, 1:2], in1=wpost_t[:], op0=ALU.mult, op1=ALU.mult),
                rd=[yk, stk, wpost_k], wr=[tk])
            P.dve(lambda e, tt=tt, xt=xt: e.tensor_tensor(out=tt[:], in0=tt[:], in1=xt[:], op=ALU.add),
                  rd=[tk, xk], wr=[tk])
            cur, curk = tt, tk
        if x_out is not None:
            P.dma(lambda e, cur=cur, rows=rows: e.dma_start(out=x_out[rows, :], in_=cur[:]), rd=[curk])
        if wpre is not None:
            sq, sqk = sqr.next()
            st, stk = str_.next()
            P.act(lambda e, sq=sq, cur=cur, st=st: e.activation(out=sq[:], in_=cur[:], func=AF.Square,
                                                                accum_out=st[:, 0:1]),
                  rd=[curk], wr=[sqk, stk])
            emit_rstd(P, st[:, 0:1], st[:, 1:2], stk, stk, 128, D)
            ht, hk = hr.next()
            P.dve(lambda e, ht=ht, cur=cur, st=st: e.scalar_tensor_tensor(
                out=ht[:], in0=cur[:], scalar=st[:, 1:2], in1=wpre_t[:], op0=ALU.mult, op1=ALU.mult),
                rd=[curk, stk, wpre_k], wr=[hk])
            if i % 4 == 0:
                hT, hTk = htr.next()
            tsl = slice((i % 4) * 128, (i % 4) * 128 + 128)
            for half in range(2):
                pt, ptk = ptr.next()
                for j in range(8):
                    kc = half * 8 + j
                    P.pe(lambda e, pt=pt, ht=ht, j=j, kc=kc: e.transpose(
                        out=pt[:, j * 128:(j + 1) * 128], in_=ht[:, kc * 128:(kc + 1) * 128],
                        identity=ident_bf[:]), rd=[hk], wr=[ptk])
                src = lambda pt: pt[:].rearrange("p (k t) -> p k t", t=128)
                if half == 0:
                    P.act(lambda e, hT=hT, pt=pt, tsl=tsl: e.copy(out=hT[:, 0:8, tsl], in_=src(pt)),
                          rd=[ptk], wr=[hTk])
                else:
                    P.dve(lambda e, hT=hT, pt=pt, tsl=tsl: e.tensor_copy(out=hT[:, 8:16, tsl], in_=src(pt)),
                          rd=[ptk], wr=[hTk])
            if i % 4 == 3 and sel4 is None:
                blk = i // 4
                P.dma(lambda e, hT=hT, blk=blk: e.dma_start(
                    out=hT_out.rearrange("(k p) t -> p k t", p=128)[:, :, blk * 512:(blk + 1) * 512],
                    in_=hT[:]), rd=[hTk], q="q_pool")
            elif i % 4 == 3:
                blk = i // 4
                for s_ in range(4):
                    sc, sck = scr4.next()
                    if s_ % 2 == 0:
                        P.dve(lambda e, sc=sc, hT=hT, s_=s_: e.tensor_scalar(
                            out=sc[:], in0=hT[:], scalar1=s4[:, s_:s_ + 1], scalar2=None, op0=ALU.mult),
                            rd=[hTk, s4k], wr=[sck])
                    else:
                        P.act(lambda e, sc=sc, hT=hT, s_=s_: e.activation(
                            out=sc[:], in_=hT[:], func=AF.Copy, scale=s4[:, s_:s_ + 1]),
                            rd=[hTk, s4k], wr=[sck])
                    P.dma(lambda e, sc=sc, blk=blk, s_=s_: e.dma_start(
                        out=hT_out.rearrange("(s b p) f -> p s b f", s=4, p=128)[:, s_, blk, :],
                        in_=sc[:].rearrange("p k t -> p (k t)")), rd=[sck], q="q_pool" if s_ % 2 else "q_sp")


def make_ident(nc, P, stack):
    idf = stack.enter_context(nc.sbuf_tensor(U("ident_f"), [128, 128], F32))
    idb = stack.enter_context(nc.sbuf_tensor(U("ident_b"), [128, 128], BF16))
    k = Tok("ident")
    P.pool(lambda e: e.memset(idf[:], 1.0), wr=[k])
    P.pool(lambda e: e.affine_select(out=idf[:], in_=idf[:], pattern=[[-1, 128]], compare_op=ALU.is_equal,
                                     fill=0.0, base=0, channel_multiplier=1), rd=[k], wr=[k])
    P.dve(lambda e: e.tensor_copy(out=idb[:], in_=idf[:]), rd=[k], wr=[k])
    return idf, idb, k


D_FF = 5632
KC_F = D_FF // 128


def cast_op(P, idx, out, in_, rd, wr):
    if idx % 2 == 0:
        P.dve(lambda e: e.tensor_copy(out=out, in_=in_), rd=rd, wr=wr)
    else:
        P.act(lambda e: e.copy(out=out, in_=in_), rd=rd, wr=wr)


def phase_ffn_gu(nc, P, stack, T, hT_in, wg_l, wu_l, actT_out):
    sb = nc.sbuf_tensor
    hT = stack.enter_context(sb(U("gu_hT"), [128, KC_D, T], BF16))
    hTk = [Tok(f"gu_hT{k}") for k in range(KC_D // 4)]
    hv = hT_in.rearrange("(k p) t -> p k t", p=128)
    for g in range(KC_D // 4):
        P.dma(lambda e, g=g: e.dma_start(out=hT[:, g * 4:(g + 1) * 4, :], in_=hv[:, g * 4:(g + 1) * 4, :]),
              wr=[hTk[g]], q="q_pool")
    wst = [Ring(stack, sb, f"gu_wst{m}", [128, D], F32, 2) for m in range(2)]
    wbf = [Ring(stack, sb, f"gu_wbf{m}", [128, KC_D, 128], BF16, 2) for m in range(2)]
    ps = [Ring(stack, nc.psum_tensor, f"gu_ps{m}", [128, 512], F32, 3) for m in range(2)]
    sgr = Ring(stack, sb, "gu_sg", [128, 512], F32, 2)
    ar = Ring(stack, sb, "gu_a", [128, 512], BF16, 4)
    nsl = T // 512
    ci = 0
    for fc in range(KC_F):
        wb = []
        for m, wl in enumerate((wg_l, wu_l)):
            st_, stk = wst[m].next()
            P.dma(lambda e, st_=st_, wl=wl, fc=fc: e.dma_start(out=st_[:], in_=wl[fc]), wr=[stk])
            b, bk = wbf[m].next()
            cast_op(P, ci, b[:].rearrange("p k j -> p (k j)"), st_[:], [stk], [bk])
            ci += 1
            wb.append((b, bk))
        for sl in range(nsl):
            tsl = slice(sl * 512, (sl + 1) * 512)
            pp = []
            for m in range(2):
                p_, pk = ps[m].next()
                b, bk = wb[m]
                for kc in range(KC_D):
                    P.pe(lambda e, p_=p_, b=b, kc=kc, tsl=tsl: e.matmul(
                        p_[:], lhsT=b[:, kc, :], rhs=hT[:, kc, tsl], start=(kc == 0), stop=(kc == KC_D - 1)),
                        rd=[bk, hTk[kc // 4]], wr=[pk])
                pp.append((p_, pk))
            sg, sgk = sgr.next()
            P.act(lambda e, sg=sg, p_=pp[0][0]: e.activation(out=sg[:], in_=p_[:], func=AF.Silu),
                  rd=[pp[0][1]], wr=[sgk])
            a, ak = ar.next()
            P.dve(lambda e, a=a, sg=sg, p_=pp[1][0]: e.tensor_tensor(out=a[:], in0=sg[:], in1=p_[:], op=ALU.mult),
                  rd=[sgk, pp[1][1]], wr=[ak])
            P.dma(lambda e, a=a, fc=fc, tsl=tsl: e.dma_start(out=actT_out[fc * 128:(fc + 1) * 128, tsl], in_=a[:]),
                  rd=[ak], q="q_pool")


def phase_mmT(nc, P, stack, T, K, AT_in, w_l, y_out, Th=1024, pfx="mt", halves=None, htoks=None, half_done=None):
    KC = K // 128
    G = 4
    NG = KC // G
    Th = min(Th, T, 1024)
    NT = Th // 128
    sb = nc.sbuf_tensor
    nAT = 1 if halves is None else 2
    ATs = [(stack.enter_context(sb(U(f"{pfx}_AT"), [128, KC, Th], BF16)), [Tok(f"{pfx}_AT{g}") for g in range(NG)])
           for _ in range(nAT)]
    ati = 0
    AT, ATk = ATs[0]
    wst = Ring(stack, sb, f"{pfx}_wst", [128, G, 512], F32, 3)
    wbf = Ring(stack, sb, f"{pfx}_wbf", [128, G, 512], BF16, 4)
    ps = Ring(stack, nc.psum_tensor, f"{pfx}_ps", [128, 512], F32, 8)
    ys = Ring(stack, sb, f"{pfx}_ys", [128, 512], F32, 4)
    av = AT_in.rearrange("(k p) t -> p k t", p=128)
    ci = 0
    ei = 0
    if halves is None:
        order = [(th, s) for th in range(T // Th) for s in range(4)]
    else:
        order = [(th, s) for hf in range(2) for th in range(T // Th) for s in (2 * hf, 2 * hf + 1)]
    last_th = None
    for (th, s) in order:
        if th != last_th:
            last_th = th
            AT, ATk = ATs[ati % nAT]
            ati += 1
            for g in range(NG):
                P.dma(lambda e, g=g, th=th, AT=AT: e.dma_start(out=AT[:, g * G:(g + 1) * G, :],
                                                        in_=av[:, g * G:(g + 1) * G, th * Th:(th + 1) * Th]),
                      wr=[ATk[g]], q="q_pool" if halves is None else "q_sp")
        if True:
            banks = [ps.next() for _ in range(NT)]
            for g in range(NG):
                st_, stk = wst.next()
                P.dma(lambda e, st_=st_, s=s, g=g: e.dma_start(out=st_[:], in_=w_l[s, :, g * G:(g + 1) * G, :]),
                      wr=[stk])
                wb, wbk = wbf.next()
                cast_op(P, ci, wb[:], st_[:], [stk], [wbk])
                ci += 1
                for tl in range(NT):
                    p_, pk = banks[tl]
                    for kk in range(G):
                        kc = g * G + kk
                        P.pe(lambda e, p_=p_, kc=kc, kk=kk, tl=tl, wb=wb, AT=AT: e.matmul(
                            p_[:], lhsT=AT[:, kc, tl * 128:(tl + 1) * 128], rhs=wb[:, kk, :],
                            start=(kc == 0), stop=(kc == KC - 1)),
                            rd=[ATk[g], wbk], wr=[pk])
            for tl in range(NT):
                p_, pk = banks[tl]
                y_, yk = ys.next()
                if ei % 2 == 0:
                    P.act(lambda e, y_=y_, p_=p_: e.copy(out=y_[:], in_=p_[:]), rd=[pk], wr=[yk])
                else:
                    P.dve(lambda e, y_=y_, p_=p_: e.tensor_copy(out=y_[:], in_=p_[:]), rd=[pk], wr=[yk])
                ei += 1
                r0 = th * Th + tl * 128
                if halves is None:
                    P.dma(lambda e, y_=y_, r0=r0, s=s: e.dma_start(out=y_out[r0:r0 + 128, s * 512:(s + 1) * 512],
                                                                   in_=y_[:]), rd=[yk], q="q_pool")
                else:
                    dst = halves[s // 2]
                    tk_ = Tok("yhalf")
                    htoks[s // 2].append(tk_)
                    P.dma(lambda e, y_=y_, r0=r0, s=s, dst=dst: e.dma_start(
                        out=dst[r0:r0 + 128, (s % 2) * 512:(s % 2 + 1) * 512], in_=y_[:]), rd=[yk], wr=[tk_],
                        q="q_act")
            if halves is not None and half_done is not None and s % 2 == 1 and th == T // Th - 1:
                half_done(s // 2)


SEQ = 8192
NBLK = SEQ // 512


def make_consts(nc, P, stack):
    sb = nc.sbuf_tensor
    c = {}
    c["ones_bf"] = stack.enter_context(sb(U("ones_bf"), [128, 128], BF16))
    c["ones_f"] = stack.enter_context(sb(U("ones_f"), [128, 128], F32))
    c["sel2"] = stack.enter_context(sb(U("sel2"), [128, 2], BF16))
    k = Tok("consts")
    c["tok"] = k
    P.pool(lambda e: e.memset(c["ones_bf"][:], 1.0), wr=[k])
    P.pool(lambda e: e.memset(c["ones_f"][:], 1.0), wr=[k])
    P.pool(lambda e: e.memset(c["sel2"][:], 0.0), wr=[k])
    P.pool(lambda e: e.memset(c["sel2"][0:64, 0:1], 1.0), wr=[k])
    P.pool(lambda e: e.memset(c["sel2"][64:128, 1:2], 1.0), wr=[k])
    return c


def phase_attn_proj(nc, P, stack, hT_blk, w_l, qTd, kTd, Vd, negMd, idf, idb, C, S=SEQ, hT_tok=None):
    sb = nc.sbuf_tensor
    ps = nc.psum_tensor
    NB = S // 512
    ones_f, sel2, ck = C["ones_f"], C["sel2"], C["tok"]
    W = stack.enter_context(sb(U("ap_w"), [128, KC_D, 12 * 128], BF16))
    Wk = [Tok(f"ap_w{f}") for f in range(12)]
    wst = Ring(stack, sb, "ap_wst", [128, D], F32, 2)
    for fc in range(12):
        st_, stk = wst.next()
        P.dma(lambda e, st_=st_, fc=fc: e.dma_start(out=st_[:], in_=w_l[fc]), wr=[stk], q="q_pool")
        cast_op(P, fc, W[:, :, fc * 128:(fc + 1) * 128], st_[:].rearrange("p (k j) -> p k j", j=128), [stk], [Wk[fc]])
    hr = Ring(stack, sb, "ap_h", [128, KC_D, 512], BF16, 2)
    pw = Ring(stack, ps, "ap_pw", [128, 512], F32, 5)
    pm = Ring(stack, ps, "ap_pm", [128, 512], F32, 2)
    osr = Ring(stack, sb, "ap_os", [128, 512], BF16, 6)
    sqr = Ring(stack, sb, "ap_sq", [128, 512], BF16, 3)
    vor = Ring(stack, sb, "ap_vo", [128, 4, 128], BF16, 3)
    nmx = stack.enter_context(sb(U("ap_nmx"), [2, 8, NB], F32))
    nmxk = Tok("ap_nmx")
    ei = 0
    for b in range(NB):
        h_, hk = hr.next()
        P.dma(lambda e, h_=h_, b=b: e.dma_start(out=h_[:], in_=hT_blk(b)), rd=([hT_tok(b)] if hT_tok else []),
              wr=[hk])
        tsl = slice(b * 512, (b + 1) * 512)
        for ch in range(12):
            which, hl = ch // 4, ch % 4
            p_, pk = pw.next()
            for kc in range(KC_D):
                P.pe(lambda e, p_=p_, ch=ch, kc=kc, h_=h_: e.matmul(
                    p_[:], lhsT=W[:, kc, ch * 128:(ch + 1) * 128], rhs=h_[:, kc, :],
                    start=(kc == 0), stop=(kc == KC_D - 1)), rd=[Wk[ch], hk], wr=[pk])
            o_, ok = osr.next()
            sc = 0.125 if which == 0 else 1.0
            if ei % 2 == 0:
                P.act(lambda e, o_=o_, p_=p_, sc=sc: e.activation(out=o_[:], in_=p_[:], func=AF.Copy, scale=sc),
                      rd=[pk], wr=[ok])
            else:
                P.dve(lambda e, o_=o_, p_=p_, sc=sc: e.tensor_scalar(out=o_[:], in0=p_[:], scalar1=sc, scalar2=None,
                                                                    op0=ALU.mult), rd=[pk], wr=[ok])
            ei += 1
            if which < 2:
                dst = qTd if which == 0 else kTd
                P.dma(lambda e, o_=o_, dst=dst, hl=hl, tsl=tsl: e.dma_start(
                    out=dst[hl * 128:(hl + 1) * 128, tsl], in_=o_[:]), rd=[ok], q="q_pool")
                sq, sqk = sqr.next()
                P.dve(lambda e, sq=sq, o_=o_: e.tensor_tensor(out=sq[:], in0=o_[:], in1=o_[:], op=ALU.mult),
                      rd=[ok], wr=[sqk])
                p2, p2k = pm.next()
                P.pe(lambda e, p2=p2, sq=sq: e.matmul(p2[0:2, :], lhsT=sel2[:], rhs=sq[:], start=True, stop=True),
                     rd=[sqk, ck], wr=[p2k])
                P.dve(lambda e, p2=p2, ch=ch, b=b: e.tensor_reduce(
                    out=nmx[:, ch, b:b + 1], in_=p2[0:2, :], axis=AX.X, op=ALU.max), rd=[p2k], wr=[nmxk])
            else:
                p2, p2k = pm.next()
                p2b = p2[:].bitcast(BF16)
                for j in range(4):
                    P.pe(lambda e, p2b=p2b, o_=o_, j=j: e.transpose(
                        out=p2b[:, j * 128:(j + 1) * 128], in_=o_[:, j * 128:(j + 1) * 128], identity=idb[:]),
                        rd=[ok], wr=[p2k])
                vo, vok = vor.next()
                P.act(lambda e, vo=vo, p2b=p2b: e.copy(out=vo[:], in_=p2b[:, 0:512].rearrange("p (j d) -> p j d", d=128)),
                      rd=[p2k], wr=[vok])
                P.dma(lambda e, vo=vo, hl=hl, b=b: e.dma_start(out=Vd[hl, :, b * 4:(b + 1) * 4, :], in_=vo[:]),
                      rd=[vok], q="q_pool")
    msc = stack.enter_context(sb(U("ap_msc"), [2, 32], F32))
    P.dve(lambda e: e.tensor_reduce(out=msc[:, 0:8], in_=nmx[:], axis=AX.X, op=ALU.max), rd=[nmxk], wr=[nmxk])
    P.dve(lambda e: e.tensor_tensor(out=msc[:, 8:12], in0=msc[:, 0:4], in1=msc[:, 4:8], op=ALU.mult),
          rd=[nmxk], wr=[nmxk])
    P.act(lambda e: e.activation(out=msc[:, 12:16], in_=msc[:, 8:12], func=AF.Ln), rd=[nmxk], wr=[nmxk])
    P.act(lambda e: e.activation(out=msc[:, 12:16], in_=msc[:, 12:16], func=AF.Exp, scale=0.5), rd=[nmxk], wr=[nmxk])
    for hl in range(4):
        P.dve(lambda e, hl=hl: e.tensor_scalar(out=msc[:, 16 + hl * 2:18 + hl * 2], in0=idf[0:2, 0:2],
                                               scalar1=msc[:, 12 + hl:13 + hl], scalar2=-1.02,
                                               op0=ALU.mult, op1=ALU.mult), rd=[nmxk], wr=[nmxk])
    p2, p2k = pm.next()
    P.pe(lambda e: e.matmul(p2[:, 0:8], lhsT=ones_f[0:2, :], rhs=msc[:, 16:24], start=True, stop=True),
         rd=[nmxk, ck], wr=[p2k])
    nm = stack.enter_context(sb(U("ap_nm"), [128, 8], F32))
    nmk = Tok("ap_nm")
    P.dve(lambda e: e.tensor_copy(out=nm[:], in_=p2[:, 0:8]), rd=[p2k], wr=[nmk])
    P.dma(lambda e: e.dma_start(out=negMd, in_=nm[:]), rd=[nmk])


def phase_attn_core(nc, P, stack, qTd, kTd, Vd, negMd, lam_in, subln_in, lambda_init, oT_out, idf, idb, C, S=SEQ):
    sb = nc.sbuf_tensor
    ps = nc.psum_tensor
    NB = S // 512
    ones_bf, ones_f, sel2, ck = C["ones_bf"], C["ones_f"], C["sel2"], C["tok"]
    lam4 = stack.enter_context(sb(U("at_lam4"), [128, 4, 64], F32))
    lamk = Tok("lam")
    P.dma(lambda e: e.dma_start(out=lam4[:].rearrange("p a d -> p (a d)"),
                                in_=lam_in.rearrange("a d -> (a d)").partition_broadcast(128)), wr=[lamk])
    lsc = stack.enter_context(sb(U("at_lsc"), [128, 8], F32))
    lpr = stack.enter_context(sb(U("at_lpr"), [128, 2, 64], F32))
    P.dve(lambda e: e.tensor_tensor(out=lpr[:, 0, :], in0=lam4[:, 0, :], in1=lam4[:, 1, :], op=ALU.mult),
          rd=[lamk], wr=[lamk])
    P.dve(lambda e: e.tensor_tensor(out=lpr[:, 1, :], in0=lam4[:, 2, :], in1=lam4[:, 3, :], op=ALU.mult),
          rd=[lamk], wr=[lamk])
    P.dve(lambda e: e.tensor_reduce(out=lsc[:, 0:2], in_=lpr[:], axis=AX.X, op=ALU.add), rd=[lamk], wr=[lamk])
    P.act(lambda e: e.activation(out=lsc[:, 2:4], in_=lsc[:, 0:2], func=AF.Exp), rd=[lamk], wr=[lamk])
    P.dve(lambda e: e.scalar_tensor_tensor(out=lsc[:, 4:5], in0=lsc[:, 3:4], scalar=-float(lambda_init),
                                           in1=lsc[:, 2:3], op0=ALU.add, op1=ALU.subtract), rd=[lamk], wr=[lamk])
    P.dma(lambda e: e.dma_start(out=lsc[:, 5:6], in_=subln_in.rearrange("(p o) -> p o", o=1)), wr=[lamk])
    P.dve(lambda e: e.tensor_scalar(out=lsc[:, 6:7], in0=lsc[:, 5:6], scalar1=1.0 - float(lambda_init),
                                    scalar2=None, op0=ALU.mult), rd=[lamk], wr=[lamk])
    neglam = lsc[:, 4:5]
    sublnw = lsc[:, 6:7]

    qTc = [stack.enter_context(sb(U(f"at_qT{c}"), [128, S], BF16)) for c in range(2)]
    qzk = Tok("at_qz")
    P.dve(lambda e: e.memset(qTc[0][64:128, :], 0.0), wr=[qzk])
    P.dve(lambda e: e.memset(qTc[1][0:64, :], 0.0), wr=[qzk])
    kT = stack.enter_context(sb(U("at_kT"), [128, S], BF16))
    V = stack.enter_context(sb(U("at_V"), [128, S // 128, 128], BF16))
    qk1, kk1, vk1 = Tok("at_q"), Tok("at_k"), Tok("at_v")
    qk_ = [qk1] * NB
    kk_ = [kk1] * NB
    vk_ = [vk1] * NB
    negM = stack.enter_context(sb(U("at_negM"), [128, 8], F32))
    negMk = Tok("negM")
    P.dma(lambda e: e.dma_start(out=negM[:], in_=negMd), wr=[negMk])
    pw = Ring(stack, ps, "at_pw", [128, 512], F32, 3)
    po = [Ring(stack, ps, f"at_po{c}", [128, 512], F32, 1) for c in range(2)]
    pl = [Ring(stack, ps, f"at_pl{c}", [128, 512], F32, 1) for c in range(2)]
    lacc = Ring(stack, sb, "at_la", [128, 512], F32, 4)
    pm = Ring(stack, ps, "at_pm", [128, 512], F32, 1)
    ptr = Ring(stack, sb, "at_pt", [128, 512], BF16, 6)
    e32 = Ring(stack, sb, "at_e32", [128, 512], F32, 8)
    obf = Ring(stack, sb, "at_obf", [128, 512], BF16, 2)
    for hl in range(4):
        hs = slice(hl * 128, (hl + 1) * 128)
        P.dma(lambda e, hl=hl: e.dma_start(out=qTc[0][0:64, :], in_=qTd[hl * 128:hl * 128 + 64, :]),
              rd=[qzk], wr=[qk1])
        P.dma(lambda e, hl=hl: e.dma_start(out=qTc[1][64:128, :], in_=qTd[hl * 128 + 64:hl * 128 + 128, :]),
              rd=[qzk], wr=[qk1], q="q_pool")
        P.dma(lambda e, hs=hs: e.dma_start(out=kT[:], in_=kTd[hs, :]), wr=[kk1])
        P.dma(lambda e, hl=hl: e.dma_start(out=V[:], in_=Vd[hl]), wr=[vk1], q="q_pool")
        steps = [(qt, c, sbk) for qt in range(NB) for c in range(2) for sbk in range(qt * 4 + 4)]
        LA = 2
        inflight = {}
        accs = {}
        tparts = {}
        deferred = []

        def stepA(k):
            qt, c, sbk = steps[k]
            q0 = qt * 512
            rows = slice(c * 64, (c + 1) * 64)
            d = max(0, sbk - qt * 4)
            cs = slice(d * 128, 512)
            w_, wk_ = pw.next()
            P.pe(lambda e: e.matmul(w_[:, cs], lhsT=kT[:, sbk * 128:(sbk + 1) * 128],
                                    rhs=qTc[c][:, q0 + cs.start:q0 + 512], start=True, stop=True),
                 rd=[kk_[sbk // 4], qk_[qt], qzk], wr=[wk_])
            pt, ptk = ptr.next()
            bcol = hl * 2 + c
            P.act(lambda e: e.activation(out=pt[:, cs], in_=w_[:, cs], func=AF.Exp, bias=negM[:, bcol:bcol + 1],
                                         scale=1.0), rd=[wk_, negMk], wr=[ptk])
            if sbk >= qt * 4:
                base = q0 + cs.start - sbk * 128
                P.pool(lambda e: e.affine_select(out=pt[:, cs], in_=pt[:, cs], pattern=[[1, 512 - cs.start]],
                                                 compare_op=ALU.is_ge, fill=0.0, base=base,
                                                 channel_multiplier=-1), rd=[ptk], wr=[ptk])
            inflight[k] = (pt, ptk, cs)

        def epi_c(qt, c, k):
            o_, ok, l_, lk, la, lak = accs.pop((qt, c))

            def part2():
                P.pe(lambda e: e.matmul(l_[:], lhsT=ones_f[:], rhs=la[:], start=False, stop=True),
                     rd=[lak, ck], wr=[lk])
                r_, rk = e32.next()
                P.dve(lambda e: e.reciprocal(out=r_[:], in_=l_[:]), rd=[lk], wr=[rk])
                t_, tk = e32.next()
                P.dve(lambda e: e.tensor_tensor(out=t_[:], in0=o_[:], in1=r_[:], op=ALU.mult), rd=[ok, rk], wr=[tk])
                tparts[(qt, c)] = (t_, tk)
                if c == 1:
                    epi_1(qt, k + 2)
            deferred.append((k + 2, part2))

        def epi_1(qt, k):
            t0, t0k = tparts.pop((qt, 0))
            t1, t1k = tparts.pop((qt, 1))
            of, ofk = e32.next()
            P.dve(lambda e: e.scalar_tensor_tensor(out=of[:], in0=t1[:], scalar=neglam, in1=t0[:],
                                                   op0=ALU.mult, op1=ALU.add), rd=[t0k, t1k, lamk], wr=[ofk])
            sq, sqk2 = e32.next()
            P.dve(lambda e: e.tensor_tensor(out=sq[:], in0=of[:], in1=of[:], op=ALU.mult), rd=[ofk], wr=[sqk2])

            def epi_2():
                m_, mk = pm.next()
                P.pe(lambda e: e.matmul(m_[:], lhsT=ones_f[:], rhs=sq[:], start=True, stop=True),
                     rd=[sqk2, ck], wr=[mk])
                rs, rsk = e32.next()
                P.act(lambda e: e.activation(out=rs[:], in_=m_[:], func=AF.Ln, bias=EPS, scale=1.0 / 128),
                      rd=[mk], wr=[rsk])
                P.act(lambda e: e.activation(out=rs[:], in_=rs[:], func=AF.Exp, scale=-0.5), rd=[rsk], wr=[rsk])
                ob, obk = obf.next()
                P.dve(lambda e: e.scalar_tensor_tensor(out=ob[:], in0=of[:], scalar=sublnw, in1=rs[:],
                                                       op0=ALU.mult, op1=ALU.mult), rd=[ofk, rsk, lamk], wr=[obk])
                P.dma(lambda e, hl=hl: e.dma_start(out=oT_out[hl * 128:(hl + 1) * 128, qt * 512:(qt + 1) * 512],
                                                   in_=ob[:]), rd=[obk])
            deferred.append((k + 6, epi_2))

        def stepB(k):
            qt, c, sbk = steps[k]
            nsb = qt * 4 + 4
            pt, ptk, cs = inflight.pop(k)
            if sbk == 0:
                o_, ok = po[c].next()
                l_, lk = pl[c].next()
                la, lak = lacc.next()
                accs[(qt, c)] = (o_, ok, l_, lk, la, lak)
            o_, ok, l_, lk, la, lak = accs[(qt, c)]
            P.pe(lambda e: e.matmul(o_[:, cs], lhsT=V[:, sbk, :], rhs=pt[:, cs], start=(sbk == 0),
                                    stop=(sbk == nsb - 1)), rd=[vk_[sbk // 4], ptk], wr=[ok])
            if sbk % 2 == 0:
                P.pe(lambda e: e.matmul(l_[:, cs], lhsT=ones_bf[:], rhs=pt[:, cs], start=(sbk == 0), stop=False),
                     rd=[ck, ptk], wr=[lk])
            elif sbk == 1:
                if cs.start > 0:
                    P.dve(lambda e: e.memset(la[:, 0:cs.start], 0.0), wr=[lak])
                P.dve(lambda e: e.tensor_copy(out=la[:, cs], in_=pt[:, cs]), rd=[ptk], wr=[lak])
            else:
                P.dve(lambda e: e.tensor_tensor(out=la[:, cs], in0=la[:, cs], in1=pt[:, cs], op=ALU.add),
                      rd=[ptk, lak], wr=[lak])
            if sbk == nsb - 1:
                epi_c(qt, c, k)

        ns = len(steps)
        for k in range(ns + LA):
            if k < ns:
                stepA(k)
            if k - LA >= 0:
                stepB(k - LA)
            for item in [d_ for d_ in deferred if d_[0] <= k]:
                deferred.remove(item)
                item[1]()
        for item in deferred:
            item[1]()


NZX = 20


def phase_ssd_in(nc, P, stack, hT_blk, w_l, wdt_l, dtb_in, zxT_out, dt_out, S=SEQ, hT_tok=None):
    sb = nc.sbuf_tensor
    NB = S // 512
    W = stack.enter_context(sb(U("si_w"), [128, KC_D, NZX * 128], BF16))
    Wk = [Tok(f"si_w{f}") for f in range(NZX)]
    wst = Ring(stack, sb, "si_wst", [128, D], F32, 2)
    for fc in range(NZX):
        st_, stk = wst.next()
        P.dma(lambda e, st_=st_, fc=fc: e.dma_start(out=st_[:], in_=w_l[fc]), wr=[stk])
        cast_op(P, fc, W[:, :, fc * 128:(fc + 1) * 128], st_[:].rearrange("p (k j) -> p k j", j=128), [stk], [Wk[fc]])
    wdtf = stack.enter_context(sb(U("si_wdtf"), [128, KC_D, 16], F32))
    wdt = stack.enter_context(sb(U("si_wdt"), [128, KC_D, 16], BF16))
    wdk = Tok("si_wdt")
    P.dma(lambda e: e.dma_start(out=wdtf[:], in_=wdt_l), wr=[wdk])
    P.dve(lambda e: e.tensor_copy(out=wdt[:], in_=wdtf[:]), rd=[wdk], wr=[wdk])
    dtb = stack.enter_context(sb(U("si_dtb"), [128, 16], F32))
    dtbk = Tok("si_dtb")
    P.dma(lambda e: e.dma_start(out=dtb[:], in_=dtb_in.partition_broadcast(128)), wr=[dtbk])
    hr = Ring(stack, sb, "si_h", [128, KC_D, 512], BF16, 2)
    pw = Ring(stack, nc.psum_tensor, "si_pw", [128, 512], F32, 4)
    pd = Ring(stack, nc.psum_tensor, "si_pd", [128, 512], F32, 2)
    osr = Ring(stack, sb, "si_os", [128, 512], F32, 4)
    dr = Ring(stack, sb, "si_d", [128, 4, 16], F32, 6)
    dto = Ring(stack, sb, "si_dto", [128, 4, 16], F32, 2)
    ei = 0
    for b in range(NB):
        h_, hk = hr.next()
        P.dma(lambda e, h_=h_, b=b: e.dma_start(out=h_[:], in_=hT_blk(b)), rd=([hT_tok(b)] if hT_tok else []),
              wr=[hk])
        tsl = slice(b * 512, (b + 1) * 512)
        for fc in range(NZX):
            p_, pk = pw.next()
            for kc in range(KC_D):
                P.pe(lambda e, p_=p_, fc=fc, kc=kc, h_=h_: e.matmul(
                    p_[:], lhsT=W[:, kc, fc * 128:(fc + 1) * 128], rhs=h_[:, kc, :],
                    start=(kc == 0), stop=(kc == KC_D - 1)), rd=[Wk[fc], hk], wr=[pk])
            o_, ok = osr.next()
            if ei % 2 == 0:
                P.act(lambda e, o_=o_, p_=p_: e.copy(out=o_[:], in_=p_[:]), rd=[pk], wr=[ok])
            else:
                P.dve(lambda e, o_=o_, p_=p_: e.tensor_copy(out=o_[:], in_=p_[:]), rd=[pk], wr=[ok])
            ei += 1
            P.dma(lambda e, o_=o_, fc=fc, tsl=tsl: e.dma_start(out=zxT_out[fc * 128:(fc + 1) * 128, tsl], in_=o_[:]),
                  rd=[ok])
        x_, xk = dr.next()
        for j in range(4):
            p_, pk = pd.next()
            for kc in range(KC_D):
                P.pe(lambda e, p_=p_, kc=kc, h_=h_, j=j: e.matmul(
                    p_[:, 0:16], lhsT=h_[:, kc, j * 128:(j + 1) * 128], rhs=wdt[:, kc, :],
                    start=(kc == 0), stop=(kc == KC_D - 1)), rd=[wdk, hk], wr=[pk])
            P.dve(lambda e, x_=x_, p_=p_, j=j: e.tensor_tensor(out=x_[:, j, :], in0=p_[:, 0:16], in1=dtb[:],
                                                              op=ALU.add), rd=[pk, dtbk], wr=[xk])
        a_, ak = dr.next()
        P.dve(lambda e, a_=a_, x_=x_: e.scalar_tensor_tensor(out=a_[:], in0=x_[:], scalar=-1.0, in1=x_[:],
                                                             op0=ALU.mult, op1=ALU.max), rd=[xk], wr=[ak])
        P.act(lambda e, a_=a_: e.activation(out=a_[:], in_=a_[:], func=AF.Exp, scale=-1.0), rd=[ak], wr=[ak])
        P.act(lambda e, a_=a_: e.activation(out=a_[:], in_=a_[:], func=AF.Ln, bias=1.0, scale=1.0), rd=[ak], wr=[ak])
        d_, dk = dto.next()
        P.dve(lambda e, d_=d_, x_=x_, a_=a_: e.scalar_tensor_tensor(
            out=d_[:], in0=x_[:], scalar=0.0, in1=a_[:], op0=ALU.max, op1=ALU.add), rd=[xk, ak], wr=[dk])
        P.dma(lambda e, d_=d_, b=b: e.dma_start(
            out=dt_out[b * 512:(b + 1) * 512, :].rearrange("(j p) h -> p j h", p=128), in_=d_[:]), rd=[dk])


def phase_ssd_scan(nc, P, stack, zxT_in, dt_in, convw_in, convb_in, alog_in, dsk_in, normw_in, yT_out,
                   idf, idb, C, S=SEQ):
    sb = nc.sbuf_tensor
    ps = nc.psum_tensor
    ones_f, ck = C["ones_f"], C["tok"]
    TB = 256
    NB = S // TB
    zv = zxT_in.rearrange("(k p) t -> p k t", p=128)
    tri = stack.enter_context(sb(U("ss_tri"), [128, 128], F32))
    cst = Tok("ss_const")
    P.pool(lambda e: e.memset(tri[:], 1.0), wr=[cst])
    P.pool(lambda e: e.affine_select(out=tri[:], in_=tri[:], pattern=[[1, 128]], compare_op=ALU.is_ge,
                                     fill=0.0, base=0, channel_multiplier=-1), rd=[cst], wr=[cst])
    cw = stack.enter_context(sb(U("ss_cw"), [128, 12, 4], F32))
    cb = stack.enter_context(sb(U("ss_cb"), [128, 12], F32))
    nw = stack.enter_context(sb(U("ss_nw"), [128, 8], F32))
    abc = stack.enter_context(sb(U("ss_abc"), [128, 16], F32))
    d16 = stack.enter_context(sb(U("ss_d16"), [128, 16], F32))
    Dbc = stack.enter_context(sb(U("ss_Dbc"), [128, 16, 64], F32))
    P.dma(lambda e: e.dma_start(out=cw[:], in_=convw_in), wr=[cst])
    P.dma(lambda e: e.dma_start(out=cb[:], in_=convb_in), wr=[cst])
    P.dma(lambda e: e.dma_start(out=nw[:], in_=normw_in), wr=[cst])
    P.dma(lambda e: e.dma_start(out=abc[:], in_=alog_in.partition_broadcast(128)), wr=[cst])
    P.dma(lambda e: e.dma_start(out=d16[:], in_=dsk_in.partition_broadcast(128)), wr=[cst])
    P.act(lambda e: e.activation(out=abc[:], in_=abc[:], func=AF.Exp), rd=[cst], wr=[cst])
    P.dve(lambda e: e.tensor_scalar(out=abc[:], in0=abc[:], scalar1=-1.0, scalar2=None, op0=ALU.mult),
          rd=[cst], wr=[cst])
    P.dve(lambda e: e.tensor_copy(out=Dbc[:], in_=d16[:].unsqueeze(2).to_broadcast([128, 16, 64])),
          rd=[cst], wr=[cst])
    S32 = [stack.enter_context(sb(U(f"ss_S32{g}"), [128, 512], F32)) for g in range(2)]
    Sbf = [stack.enter_context(sb(U(f"ss_Sbf{g}"), [128, 512], BF16)) for g in range(2)]
    Sk = [Tok(f"ss_S{g}") for g in range(2)]
    Sbk = [Tok(f"ss_Sb{g}") for g in range(2)]
    for g in range(2):
        P.pool(lambda e, g=g: e.memset(S32[g][:], 0.0), wr=[Sk[g]])
        P.pool(lambda e, g=g: e.memset(Sbf[g][:], 0.0), wr=[Sbk[g]])
    rawr = Ring(stack, sb, "ss_raw", [128, 12, TB + 3], F32, 2)
    zr = Ring(stack, sb, "ss_z", [128, 8, TB], F32, 3)
    accr = Ring(stack, sb, "ss_acc", [128, 12, TB], F32, 1)
    xTr = Ring(stack, sb, "ss_xT", [128, 8, TB], F32, 2)
    bcTr = Ring(stack, sb, "ss_bcT", [128, 4, TB], BF16, 3)
    dtr = Ring(stack, sb, "ss_dt", [128, TB // 128, 16], F32, 3)
    oTr = Ring(stack, sb, "ss_oT", [128, 8, TB], BF16, 3)
    xsr = Ring(stack, sb, "ss_xs", [128, 512], F32, 2)
    Btr = Ring(stack, sb, "ss_Bt", [128, 128], BF16, 4)
    smr = Ring(stack, sb, "ss_sm", [128, 48], F32, 5)
    rbr = Ring(stack, sb, "ss_rb", [128, 8, 128], F32, 2)
    segr = Ring(stack, sb, "ss_seg", [128, 8, 128], F32, 2)
    cbmr = Ring(stack, sb, "ss_cbm", [128, 128], BF16, 3)
    ebr = Ring(stack, sb, "ss_eb", [128, 8, 128], BF16, 3)
    Gr = Ring(stack, sb, "ss_G", [128, 8, 128], BF16, 4)
    x32r = Ring(stack, sb, "ss_x32", [128, 512], F32, 2)
    xbr = Ring(stack, sb, "ss_xb", [128, 512], BF16, 4)
    xer = Ring(stack, sb, "ss_xe", [128, 512], BF16, 4)
    y1r = Ring(stack, sb, "ss_y1", [128, 512], F32, 2)
    xdr = Ring(stack, sb, "ss_xd", [128, 512], F32, 4)
    gvr = Ring(stack, sb, "ss_gv", [128, 4, 128], F32, 2)
    sqr = Ring(stack, sb, "ss_sq", [128, 4, 128], F32, 2)
    rsr = Ring(stack, sb, "ss_rs", [128, 128], F32, 2)
    pbc = Ring(stack, ps, "ss_pbc", [128, 1024], F32, 1)
    pm = Ring(stack, ps, "ss_pm", [128, 512], F32, 3)
    pyd = Ring(stack, ps, "ss_pyd", [128, 512], F32, 1)
    pyo = Ring(stack, ps, "ss_pyo", [128, 512], F32, 1)
    pst = Ring(stack, ps, "ss_pst", [128, 512], F32, 1)

    def block_prologue(b):
        t0 = b * TB
        raw, rk = rawr.next()
        if b == 0:
            P.pool(lambda e, raw=raw: e.memset(raw[:, :, 0:3], 0.0), wr=[rk])
            P.dma(lambda e, raw=raw: e.dma_start(out=raw[:, :, 3:], in_=zv[:, 8:20, 0:TB]), wr=[rk])
        else:
            P.dma(lambda e, raw=raw, t0=t0: e.dma_start(out=raw[:], in_=zv[:, 8:20, t0 - 3:t0 + TB]), wr=[rk])
        z_, zk = zr.next()
        P.dma(lambda e, z_=z_, t0=t0: e.dma_start(out=z_[:], in_=zv[:, 0:8, t0:t0 + TB]), wr=[zk], q="q_pool")
        dt_, dtk = dtr.next()
        P.dma(lambda e, dt_=dt_, t0=t0: e.dma_start(
            out=dt_[:], in_=dt_in[t0:t0 + TB, :].rearrange("(j p) h -> p j h", p=128)), wr=[dtk], q="q_pool")
        P.act(lambda e, z_=z_: e.activation(out=z_[:], in_=z_[:], func=AF.Silu), rd=[zk], wr=[zk])
        acc, acck = accr.next()
        acks = [Tok(f"acc{k}") for k in range(12)]
        for w in range(4):
            for k in range(12):
                if w == 0:
                    P.act(lambda e, k=k, w=w, raw=raw, acc=acc: e.activation(
                        out=acc[:, k, :], in_=raw[:, k, w:w + TB], func=AF.Copy, scale=cw[:, k, w:w + 1]),
                        rd=[rk, cst], wr=[acks[k]])
                else:
                    P.dve(lambda e, k=k, w=w, raw=raw, acc=acc: e.scalar_tensor_tensor(
                        out=acc[:, k, :], in0=raw[:, k, w:w + TB], scalar=cw[:, k, w:w + 1], in1=acc[:, k, :],
                        op0=ALU.mult, op1=ALU.add), rd=[rk, cst, acks[k]], wr=[acks[k]])
        xT, xTk = xTr.next()
        bcT, bcTk = bcTr.next()
        for k in range(12):
            if k < 8:
                P.act(lambda e, k=k, xT=xT, acc=acc: e.activation(out=xT[:, k, :], in_=acc[:, k, :], func=AF.Silu,
                                                                  bias=cb[:, k:k + 1], scale=1.0),
                      rd=[acks[k], cst], wr=[xTk])
            else:
                P.act(lambda e, k=k, bcT=bcT, acc=acc: e.activation(out=bcT[:, k - 8, :], in_=acc[:, k, :],
                                                                    func=AF.Silu, bias=cb[:, k:k + 1], scale=1.0),
                      rd=[acks[k], cst], wr=[bcTk])
        oT, oTk = oTr.next()
        return dict(t0=t0, z_=z_, zk=zk, dt_=dt_, dtk=dtk, xT=xT, xTk=xTk, bcT=bcT, bcTk=bcTk, oT=oT, oTk=oTk)

    def front(B_, j, g):
        z_, zk, dt_, dtk, xT, xTk, bcT, bcTk = (B_[k_] for k_ in ('z_', 'zk', 'dt_', 'dtk', 'xT', 'xTk', 'bcT', 'bcTk'))
        cs = slice(j * 128, (j + 1) * 128)
        px, pxk = pm.next()
        for f in range(4):
            P.pe(lambda e, px=px, f=f, g=g, xT=xT, cs=cs: e.transpose(
                out=px[:, f * 128:(f + 1) * 128], in_=xT[:, g * 4 + f, cs], identity=idf[:]),
                rd=[xTk], wr=[pxk])
        xs, xsk = xsr.next()
        P.act(lambda e, xs=xs, px=px: e.copy(out=xs[:], in_=px[:]), rd=[pxk], wr=[xsk])
        pb, pbk = pm.next()
        pbb = pb[:].bitcast(BF16)
        P.pe(lambda e, pbb=pbb, bcT=bcT, g=g, cs=cs: e.transpose(out=pbb[:, 0:128], in_=bcT[:, g, cs],
                                                                 identity=idb[:]), rd=[bcTk], wr=[pbk])
        Bt, Btk = Btr.next()
        P.dve(lambda e, Bt=Bt, pbb=pbb: e.tensor_copy(out=Bt[:], in_=pbb[:, 0:128]), rd=[pbk], wr=[Btk])
        sm, smk = smr.next()
        kdA, kacol, keacol, keal, kdte = (Tok(n_) for n_ in ('dA', 'acol', 'eacol', 'eal', 'dte'))
        dtg = dt_[:, j, g * 8:(g + 1) * 8]
        P.dve(lambda e, sm=sm, dtg=dtg, g=g: e.tensor_tensor(out=sm[:, 0:8], in0=dtg,
                                                             in1=abc[:, g * 8:(g + 1) * 8], op=ALU.mult),
              rd=[dtk, cst], wr=[smk, kdA])
        pa, pak = pm.next()
        P.pe(lambda e, pa=pa, sm=sm: e.matmul(pa[:, 0:8], lhsT=tri[:], rhs=sm[:, 0:8], start=True, stop=True),
             rd=[kdA, cst], wr=[pak])
        P.act(lambda e, sm=sm, pa=pa: e.copy(out=sm[:, 8:16], in_=pa[:, 0:8]), rd=[pak, smk], wr=[kacol])
        P.act(lambda e, sm=sm, pa=pa: e.activation(out=sm[:, 16:24], in_=pa[:, 0:8], func=AF.Exp),
              rd=[pak, smk], wr=[keacol])
        rb, rbk = rbr.next()
        P.dve(lambda e, rb=rb, sm=sm: e.tensor_tensor(
            out=rb[:], in0=tri[:].unsqueeze(1).to_broadcast([128, 8, 128]),
            in1=sm[:, 0:8].unsqueeze(2).to_broadcast([128, 8, 128]), op=ALU.mult),
            rd=[kdA, cst], wr=[rbk])
        bc, bck = pbc.next()
        for hh in range(2):
            P.pe(lambda e, bc=bc, rb=rb, hh=hh: e.matmul(
                bc[:, hh * 512:(hh + 1) * 512], lhsT=ones_f[:],
                rhs=rb[:, hh * 4:(hh + 1) * 4, :].rearrange("p h l -> p (h l)"), start=True, stop=True),
                rd=[rbk, ck], wr=[bck])
        bc3 = bc[:].rearrange("p (h l) -> p h l", l=128)
        seg, segk = segr.next()
        for h in range(8):
            P.dve(lambda e, seg=seg, bc3=bc3, sm=sm, h=h: e.tensor_scalar(
                out=seg[:, h, :], in0=bc3[:, h, :], scalar1=sm[:, 8 + h:9 + h], scalar2=0.0,
                op0=ALU.subtract, op1=ALU.min), rd=[bck, kacol], wr=[segk])
        eb, ebk = ebr.next()
        P.act(lambda e, seg=seg, eb=eb: e.activation(out=eb[:], in_=seg[:], func=AF.Exp), rd=[segk], wr=[ebk])
        P.act(lambda e, sm=sm, bc3=bc3: e.activation(out=sm[:, 24:32], in_=bc3[:, :, 127], func=AF.Exp),
              rd=[bck, smk], wr=[keal])
        P.dve(lambda e, sm=sm, bc3=bc3: e.tensor_tensor(out=sm[:, 32:40], in0=bc3[:, :, 127],
                                                        in1=sm[:, 8:16], op=ALU.subtract),
              rd=[bck, kacol, smk], wr=[kdte])
        P.act(lambda e, sm=sm: e.activation(out=sm[:, 32:40], in_=sm[:, 32:40], func=AF.Exp),
              rd=[kdte], wr=[kdte])
        pc, pck = pm.next()
        P.pe(lambda e, pc=pc, bcT=bcT, g=g, cs=cs: e.matmul(
            pc[:, 0:128], lhsT=bcT[:, g, cs], rhs=bcT[:, 2 + g, cs], start=True, stop=True),
            rd=[bcTk], wr=[pck])
        cbm, cbmk = cbmr.next()
        P.dve(lambda e, cbm=cbm, pc=pc: e.tensor_tensor(out=cbm[:], in0=pc[:, 0:128], in1=tri[:], op=ALU.mult),
              rd=[pck, cst], wr=[cbmk])
        G, Gk = Gr.next()
        P.dve(lambda e, G=G, eb=eb, cbm=cbm: e.tensor_tensor(
            out=G[:], in0=eb[:], in1=cbm[:].unsqueeze(1).to_broadcast([128, 8, 128]), op=ALU.mult),
            rd=[ebk, cbmk], wr=[Gk])
        x32, x32k = x32r.next()
        xs3 = lambda t: t[:].rearrange("p (h d) -> p h d", d=64)
        P.pool(lambda e, x32=x32, xs=xs, dtg=dtg: e.tensor_tensor(
            out=xs3(x32), in0=xs3(xs), in1=dtg.unsqueeze(2).to_broadcast([128, 8, 64]), op=ALU.mult),
            rd=[xsk, dtk], wr=[x32k])
        xb, xbk = xbr.next()
        P.act(lambda e, xb=xb, x32=x32: e.copy(out=xb[:], in_=x32[:]), rd=[x32k], wr=[xbk])
        xe, xek = xer.next()
        P.pool(lambda e, xe=xe, x32=x32, sm=sm: e.tensor_tensor(
            out=xs3(xe), in0=xs3(x32), in1=sm[:, 32:40].unsqueeze(2).to_broadcast([128, 8, 64]),
            op=ALU.mult), rd=[x32k, kdte], wr=[xek])
        xd, xdk = xdr.next()
        P.pool(lambda e, xd=xd, xs=xs, g=g: e.tensor_tensor(
            out=xs3(xd), in0=xs3(xs), in1=Dbc[:, g * 8:(g + 1) * 8, :], op=ALU.mult),
            rd=[xsk, cst], wr=[xdk])
        return dict(B_=B_, j=j, g=g, cs=cs, sm=sm, smk=smk, keacol=keacol, keal=keal, Bt=Bt, Btk=Btk, G=G, Gk=Gk, xb=xb, xbk=xbk, xe=xe, xek=xek,
                    xd=xd, xdk=xdk, xs3=xs3)

    def back(F_):
        keacol, keal = F_['keacol'], F_['keal']
        B_, j, g, cs, sm, smk, Bt, Btk, G, Gk, xb, xbk, xe, xek, xd, xdk, xs3 = (F_[k_] for k_ in (
            'B_', 'j', 'g', 'cs', 'sm', 'smk', 'Bt', 'Btk', 'G', 'Gk', 'xb', 'xbk', 'xe', 'xek', 'xd', 'xdk', 'xs3'))
        z_, zk, bcT, bcTk, oT, oTk = (B_[k_] for k_ in ('z_', 'zk', 'bcT', 'bcTk', 'oT', 'oTk'))
        yd, ydk = pyd.next()
        for h in range(8):
            P.pe(lambda e, yd=yd, G=G, xb=xb, h=h: e.matmul(
                yd[:, h * 64:(h + 1) * 64], lhsT=G[:, h, :], rhs=xb[:, h * 64:(h + 1) * 64],
                start=True, stop=True), rd=[Gk, xbk], wr=[ydk])
        yo, yok = pyo.next()
        P.pe(lambda e, yo=yo, bcT=bcT, g=g, cs=cs: e.matmul(
            yo[:], lhsT=bcT[:, 2 + g, cs], rhs=Sbf[g][:], start=True, stop=True),
            rd=[bcTk, Sbk[g]], wr=[yok])
        y1, y1k = y1r.next()
        P.dve(lambda e, y1=y1, yo=yo, sm=sm: e.tensor_tensor(
            out=xs3(y1), in0=yo[:].rearrange("p (h d) -> p h d", d=64),
            in1=sm[:, 16:24].unsqueeze(2).to_broadcast([128, 8, 64]), op=ALU.mult),
            rd=[yok, keacol], wr=[y1k])
        P.dve(lambda e, y1=y1, yd=yd: e.tensor_tensor(out=y1[:], in0=y1[:], in1=yd[:], op=ALU.add),
              rd=[y1k, ydk], wr=[y1k])
        P.dve(lambda e, y1=y1, xd=xd: e.tensor_tensor(out=y1[:], in0=y1[:], in1=xd[:], op=ALU.add),
              rd=[y1k, xdk], wr=[y1k])
        st_, stk = pst.next()
        P.pe(lambda e, st_=st_, Bt=Bt, xe=xe: e.matmul(st_[:], lhsT=Bt[:], rhs=xe[:], start=True, stop=True),
             rd=[Btk, xek], wr=[stk])
        P.dve(lambda e, g=g, sm=sm: e.tensor_tensor(
            out=xs3(S32[g]), in0=xs3(S32[g]), in1=sm[:, 24:32].unsqueeze(2).to_broadcast([128, 8, 64]),
            op=ALU.mult), rd=[Sk[g], keal], wr=[Sk[g]])
        P.dve(lambda e, g=g, st_=st_: e.tensor_tensor(out=S32[g][:], in0=S32[g][:], in1=st_[:], op=ALU.add),
              rd=[Sk[g], stk], wr=[Sk[g]])
        P.act(lambda e, g=g: e.copy(out=Sbf[g][:], in_=S32[g][:]), rd=[Sk[g]], wr=[Sbk[g]])
        py, pyk = pm.next()
        for f in range(4):
            P.pe(lambda e, py=py, y1=y1, f=f: e.transpose(
                out=py[:, f * 128:(f + 1) * 128], in_=y1[:, f * 128:(f + 1) * 128], identity=idf[:]),
                rd=[y1k], wr=[pyk])
        gv, gvk = gvr.next()
        P.dve(lambda e, gv=gv, py=py, z_=z_, g=g, cs=cs: e.tensor_tensor(
            out=gv[:], in0=py[:].rearrange("p (f t) -> p f t", t=128), in1=z_[:, g * 4:(g + 1) * 4, cs],
            op=ALU.mult), rd=[pyk, zk], wr=[gvk])
        sq, sqk = sqr.next()
        P.act(lambda e, sq=sq, gv=gv: e.activation(out=sq[:], in_=gv[:], func=AF.Square), rd=[gvk], wr=[sqk])
        pq, pqk = pm.next()
        for f in range(4):
            P.pe(lambda e, pq=pq, sq=sq, f=f: e.matmul(pq[:, 0:128], lhsT=ones_f[:], rhs=sq[:, f, :],
                                                       start=(f == 0), stop=(f == 3)),
                 rd=[sqk, ck], wr=[pqk])
        rs, rsk = rsr.next()
        P.act(lambda e, rs=rs, pq=pq: e.activation(out=rs[:], in_=pq[:, 0:128], func=AF.Ln, bias=EPS,
                                                   scale=1.0 / 512), rd=[pqk], wr=[rsk])
        P.act(lambda e, rs=rs: e.activation(out=rs[:], in_=rs[:], func=AF.Exp, scale=-0.5), rd=[rsk], wr=[rsk])
        for f in range(4):
            P.dve(lambda e, oT=oT, gv=gv, rs=rs, f=f, g=g, cs=cs: e.scalar_tensor_tensor(
                out=oT[:, g * 4 + f, cs], in0=gv[:, f, :], scalar=nw[:, g * 4 + f:g * 4 + f + 1], in1=rs[:],
                op0=ALU.mult, op1=ALU.mult), rd=[gvk, rsk, cst], wr=[oTk])

    def block_epilogue(B_):
        oT, oTk, t0 = B_['oT'], B_['oTk'], B_['t0']
        P.dma(lambda e, oT=oT, t0=t0: e.dma_start(
            out=yT_out.rearrange("(k p) t -> p k t", p=128)[:, :, t0:t0 + TB], in_=oT[:]), rd=[oTk])

    passes = [(b, j, g) for b in range(NB) for j in range(TB // 128) for g in range(2)]
    blocks = {}
    pend = []
    DEPTH_F = 1

    def retire():
        F0 = pend.pop(0)
        back(F0)
        pb, pj, pg = F0["key"]
        if (pj, pg) == (TB // 128 - 1, 1):
            block_epilogue(blocks.pop(pb))

    for (b, j, g) in passes:
        if b not in blocks:
            blocks[b] = block_prologue(b)
        F_ = front(blocks[b], j, g)
        F_["key"] = (b, j, g)
        pend.append(F_)
        if len(pend) > DEPTH_F:
            retire()
    while pend:
        retire()


NCORES = 8
BATCH = 2
TC = BATCH * SEQ // NCORES
DEPTH = 4


def lay_colchunk(W):
    K, N = W.shape
    return np.ascontiguousarray(
        W.reshape(K // 128, 128, N // 128, 128).transpose(2, 1, 0, 3).reshape(N // 128, 128, K))


def lay_slab(W):
    K, N = W.shape
    return np.ascontiguousarray(W.reshape(K // 128, 128, N // 512, 512).transpose(2, 1, 0, 3))


def ssd_core_inputs(inp, j, gl):
    w_in = inp["ssd_w_in"][j]
    cols = np.concatenate([
        np.arange(gl * 1024, (gl + 1) * 1024),
        4096 + np.arange(gl * 1024, (gl + 1) * 1024),
        8192 + np.arange(gl * 256, (gl + 1) * 256),
        9216 + np.arange(gl * 256, (gl + 1) * 256)])
    ch = cols[1024:] - 4096
    dtc = 10240 + np.arange(gl * 16, (gl + 1) * 16)
    return {
        "s_w": lay_colchunk(w_in[:, cols]),
        "s_wdt": np.ascontiguousarray(w_in[:, dtc].reshape(KC_D, 128, 16).transpose(1, 0, 2)),
        "s_dtb": np.ascontiguousarray(inp["ssd_dt_bias"][j, gl * 16:(gl + 1) * 16]),
        "s_cw": np.ascontiguousarray(inp["ssd_conv_w"][j][:, ch].reshape(4, 12, 128).transpose(2, 1, 0)),
        "s_cb": np.ascontiguousarray(inp["ssd_conv_b"][j][ch].reshape(12, 128).T),
        "s_alog": np.ascontiguousarray(inp["ssd_a_log"][j, gl * 16:(gl + 1) * 16]),
        "s_dsk": np.ascontiguousarray(inp["ssd_d"][j, gl * 16:(gl + 1) * 16]),
        "s_nw": np.ascontiguousarray(inp["ssd_norm"][j, gl * 1024:(gl + 1) * 1024].reshape(8, 128).T),
    }


def attn_core_inputs(inp, j, gl):
    w = inp["da_w_qkv"][j]
    cols = np.concatenate([which * D + np.arange(gl * 512, (gl + 1) * 512) for which in range(3)])
    return {
        "a_w": lay_colchunk(w[:, cols]),
        "a_lam": np.ascontiguousarray(np.stack([inp["da_lambda_q1"][j], inp["da_lambda_k1"][j],
                                                inp["da_lambda_q2"][j], inp["da_lambda_k2"][j]])),
        "a_sub": np.ascontiguousarray(inp["da_subln"][j]),
    }


def lambda_init(i):
    return 0.8 - 0.6 * math.exp(-0.3 * i)


def _new_nc():
    _UID[0] = 0
    return bass.Bass("TRN2", target_bir_lowering=False)


def build_first():
    import contextlib
    nc = _new_nc()
    x = nc.dram_tensor("x", [TC, D], F32, kind="ExternalInput").ap()
    wpre = nc.dram_tensor("wpre", [D], F32, kind="ExternalInput").ap()
    xo = nc.dram_tensor("xo", [TC, D], F32, kind="ExternalOutput").ap()
    hT = nc.dram_tensor("hT", [D, TC], BF16, kind="ExternalOutput").ap()
    with contextlib.ExitStack() as st0:
        P = Prog(nc, st0)
        with contextlib.ExitStack() as st:
            idf, idb, _ = make_ident(nc, P, st)
            phase_norm(nc, P, st, TC, x, None, None, wpre, xo, hT, idb)
            P.emit()
    return nc


def hT_blk_fn(hTg):
    v = hTg.rearrange("(r k p) t -> p r k t", r=4, p=128)
    return lambda b: v[:, b // 4, :, (b % 4) * 512:(b % 4 + 1) * 512]


def hT_blk_fn_bm(hTg):
    v = hTg.rearrange("(b p) (k t) -> p b k t", p=128, t=512)
    return lambda b: v[:, b, :, :]


def build_ssd():
    import contextlib
    nc = _new_nc()
    hTg = nc.dram_tensor("hTg", [4 * D, TC], BF16, kind="ExternalInput").ap()
    w = nc.dram_tensor("s_w", [NZX, 128, D], F32, kind="ExternalInput").ap()
    wdt = nc.dram_tensor("s_wdt", [128, KC_D, 16], F32, kind="ExternalInput").ap()
    dtb = nc.dram_tensor("s_dtb", [16], F32, kind="ExternalInput").ap()
    cw = nc.dram_tensor("s_cw", [128, 12, 4], F32, kind="ExternalInput").ap()
    cb = nc.dram_tensor("s_cb", [128, 12], F32, kind="ExternalInput").ap()
    alog = nc.dram_tensor("s_alog", [16], F32, kind="ExternalInput").ap()
    dsk = nc.dram_tensor("s_dsk", [16], F32, kind="ExternalInput").ap()
    nw = nc.dram_tensor("s_nw", [128, 8], F32, kind="ExternalInput").ap()
    zx = nc.dram_tensor("zx", [NZX * 128, SEQ], F32).ap()
    dt = nc.dram_tensor("dt", [SEQ, 16], F32).ap()
    yT = nc.dram_tensor("yT", [1024, SEQ], BF16, kind="ExternalOutput").ap()
    with contextlib.ExitStack() as st0:
        P = Prog(nc, st0)
        with contextlib.ExitStack() as st:
            phase_ssd_in(nc, P, st, hT_blk_fn(hTg), w, wdt, dtb, zx, dt)
            P.emit()
        with contextlib.ExitStack() as st:
            idf, idb, _ = make_ident(nc, P, st)
            C = make_consts(nc, P, st)
            phase_ssd_scan(nc, P, st, zx, dt, cw, cb, alog, dsk, nw, yT, idf, idb, C)
            P.emit()
    return nc


def build_attn(lam_init):
    import contextlib
    nc = _new_nc()
    hTg = nc.dram_tensor("hTg", [4 * D, TC], BF16, kind="ExternalInput").ap()
    w = nc.dram_tensor("a_w", [12, 128, D], F32, kind="ExternalInput").ap()
    lam = nc.dram_tensor("a_lam", [4, 64], F32, kind="ExternalInput").ap()
    sub = nc.dram_tensor("a_sub", [128], F32, kind="ExternalInput").ap()
    oT = nc.dram_tensor("yT", [512, SEQ], BF16, kind="ExternalOutput").ap()
    with contextlib.ExitStack() as st0:
        P = Prog(nc, st0)
        with contextlib.ExitStack() as st:
            idf, idb, _ = make_ident(nc, P, st)
            C = make_consts(nc, P, st)
            qTd = nc.dram_tensor("sc_qT", [512, SEQ], BF16).ap()
            kTd = nc.dram_tensor("sc_kT", [512, SEQ], BF16).ap()
            Vd = nc.dram_tensor("sc_V", [4, 128, SEQ // 128, 128], BF16).ap()
            negMd = nc.dram_tensor("sc_negM", [128, 8], F32).ap()
            phase_attn_proj(nc, P, st, hT_blk_fn(hTg), w, qTd, kTd, Vd, negMd, idf, idb, C)
            P.emit()
        with contextlib.ExitStack() as st:
            idf, idb, _ = make_ident(nc, P, st)
            C = make_consts(nc, P, st)
            phase_attn_core(nc, P, st, qTd, kTd, Vd, negMd, lam, sub, lam_init, oT, idf, idb, C)
            P.emit()
    return nc


def emit_token_phases(nc, P, K, AT, wo, x, npost, nfpre, nfpost, npre_next, wg, wu, wd, xo, hTn, scr):
    import contextlib
    with contextlib.ExitStack() as st:
        phase_mmT(nc, P, st, TC, K, AT, wo, scr["m"], pfx="mo")
        P.emit()
    with contextlib.ExitStack() as st:
        idf, idb, _ = make_ident(nc, P, st)
        phase_norm(nc, P, st, TC, x, scr["m"], npost, nfpre, scr["x1"], scr["h2T"], idb)
        P.emit()
    with contextlib.ExitStack() as st:
        phase_ffn_gu(nc, P, st, TC, scr["h2T"], wg, wu, scr["aT"])
        P.emit()
    with contextlib.ExitStack() as st:
        phase_mmT(nc, P, st, TC, D_FF, scr["aT"], wd, scr["y2"], pfx="md")
        P.emit()
    with contextlib.ExitStack() as st:
        idf, idb, _ = make_ident(nc, P, st)
        phase_norm(nc, P, st, TC, scr["x1"], scr["y2"], nfpost, npre_next, xo, hTn, idb)
        P.emit()


def build_tok(K, last):
    import contextlib
    nc = _new_nc()
    AT = nc.dram_tensor("AT", [K, TC], BF16, kind="ExternalInput").ap()
    wo = nc.dram_tensor("wo", [4, 128, K // 128, 512], F32, kind="ExternalInput").ap()
    x = nc.dram_tensor("x", [TC, D], F32, kind="ExternalInput").ap()
    npost = nc.dram_tensor("npost", [D], F32, kind="ExternalInput").ap()
    nfpre = nc.dram_tensor("nfpre", [D], F32, kind="ExternalInput").ap()
    nfpost = nc.dram_tensor("nfpost", [D], F32, kind="ExternalInput").ap()
    npre_next = None if last else nc.dram_tensor("npre_next", [D], F32, kind="ExternalInput").ap()
    wg = nc.dram_tensor("wg", [KC_F, 128, D], F32, kind="ExternalInput").ap()
    wu = nc.dram_tensor("wu", [KC_F, 128, D], F32, kind="ExternalInput").ap()
    wd = nc.dram_tensor("wd", [4, 128, KC_F, 512], F32, kind="ExternalInput").ap()
    xo = nc.dram_tensor("xo", [TC, D], F32, kind="ExternalOutput").ap()
    hTn = None if last else nc.dram_tensor("hT", [D, TC], BF16, kind="ExternalOutput").ap()
    scr = {"m": nc.dram_tensor("sc_m", [TC, D], F32).ap(), "x1": nc.dram_tensor("sc_x1", [TC, D], F32).ap(),
           "h2T": nc.dram_tensor("sc_h2T", [D, TC], BF16).ap(), "aT": nc.dram_tensor("sc_aT", [D_FF, TC], BF16).ap(),
           "y2": nc.dram_tensor("sc_y2", [TC, D], F32).ap()}
    with contextlib.ExitStack() as st0:
        P = Prog(nc, st0)
        emit_token_phases(nc, P, K, AT, wo, x, npost, nfpre, nfpost, npre_next, wg, wu, wd, xo, hTn, scr)
    return nc


def kernel_multilaunch(**inp):
    inp = {k: np.asarray(v) for k, v in inp.items()}
    cores = list(range(NCORES))
    xs = np.ascontiguousarray(inp["x"].reshape(NCORES, TC, D))
    res = run_bass_kernel_spmd(build_first(), [{"x": xs[c], "wpre": inp["norm_mix_pre"][0]} for c in cores],
                               core_ids=cores)
    xcur = [r["xo"] for r in res.results]
    hT = [np.asarray(r["hT"]) for r in res.results]
    for i in range(DEPTH):
        j = i // 2
        hTg = [np.concatenate(hT[4 * b:4 * b + 4], axis=0) for b in range(BATCH)]
        if i % 2 == 0:
            ncm = build_ssd()
            maps = [dict(ssd_core_inputs(inp, j, c % 4), hTg=hTg[c // 4]) for c in cores]
            K = 4096
            wo = lay_slab(inp["ssd_w_out"][j])
        else:
            ncm = build_attn(lambda_init(i))
            maps = [dict(attn_core_inputs(inp, j, c % 4), hTg=hTg[c // 4]) for c in cores]
            K = 2048
            wo = lay_slab(inp["da_w_out"][j])
        res = run_bass_kernel_spmd(ncm, maps, core_ids=cores)
        yT = [np.asarray(r["yT"]) for r in res.results]
        yall = [np.concatenate(yT[4 * b:4 * b + 4], axis=0) for b in range(BATCH)]
        last = i == DEPTH - 1
        wg, wu, wd = lay_colchunk(inp["ffn_w_gate"][i]), lay_colchunk(inp["ffn_w_up"][i]), lay_slab(inp["ffn_w_down"][i])
        maps = []
        for c in cores:
            m = {"AT": np.ascontiguousarray(yall[c // 4][:, (c % 4) * TC:(c % 4 + 1) * TC]), "wo": wo, "x": xcur[c],
                 "npost": inp["norm_mix_post"][i], "nfpre": inp["norm_ffn_pre"][i], "nfpost": inp["norm_ffn_post"][i],
                 "wg": wg, "wu": wu, "wd": wd}
            if not last:
                m["npre_next"] = inp["norm_mix_pre"][i + 1]
            maps.append(m)
        res = run_bass_kernel_spmd(build_tok(K, last), maps, core_ids=cores)
        xcur = [r["xo"] for r in res.results]
        if not last:
            hT = [np.asarray(r["hT"]) for r in res.results]
    out = np.stack([np.asarray(a) for a in xcur]).reshape(BATCH, SEQ, D).astype(np.float32)
    return out


RG4 = [[0, 1, 2, 3], [4, 5, 6, 7]]
RG8 = [list(range(NCORES))]


def phase_select(nc, P, stack, g8, bsel_in, hTg):
    sb = nc.sbuf_tensor
    w = stack.enter_context(sb(U("sel_w"), [128, 2], F32))
    wk = Tok("sel_w")
    P.dma(lambda e: e.dma_start(out=w[:], in_=bsel_in.partition_broadcast(128)), wr=[wk])
    ar = Ring(stack, sb, "sel_a", [128, KC_D, 512], BF16, 2)
    br = Ring(stack, sb, "sel_b", [128, KC_D, 512], BF16, 2)
    orr = Ring(stack, sb, "sel_o", [128, KC_D, 512], BF16, 2)
    v8 = g8.rearrange("(r k p) t -> p r k t", r=8, p=128)
    vo = hTg.rearrange("(r k p) t -> p r k t", r=4, p=128)
    for r in range(4):
        for tb in range(TC // 512):
            ts = slice(tb * 512, (tb + 1) * 512)
            a, ak = ar.next()
            b, bk = br.next()
            P.dma(lambda e, a=a, r=r, ts=ts: e.dma_start(out=a[:], in_=v8[:, r, :, ts]), wr=[ak])
            P.dma(lambda e, b=b, r=r, ts=ts: e.dma_start(out=b[:], in_=v8[:, 4 + r, :, ts]), wr=[bk], q="q_pool")
            o, ok = orr.next()
            P.dve(lambda e, o=o, a=a: e.tensor_scalar(out=o[:], in0=a[:], scalar1=w[:, 0:1], scalar2=None,
                                                      op0=ALU.mult), rd=[ak, wk], wr=[ok])
            P.dve(lambda e, o=o, b=b: e.scalar_tensor_tensor(out=o[:], in0=b[:], scalar=w[:, 1:2], in1=o[:],
                                                             op0=ALU.mult, op1=ALU.add), rd=[bk, wk, ok], wr=[ok])
            P.dma(lambda e, o=o, r=r, ts=ts: e.dma_start(out=vo[:, r, :, ts], in_=o[:]), rd=[ok])


def build_fused():
    import contextlib
    nc = _new_nc()
    dt_ = nc.dram_tensor
    ext = lambda n, s, d=F32: dt_(n, s, d, kind="ExternalInput").ap()
    x = ext("x", [TC, D])
    sel4 = ext("sel4", [4])
    nmpre, nmpost = ext("nmpre", [DEPTH, D]), ext("nmpost", [DEPTH, D])
    nfpre, nfpost = ext("nfpre", [DEPTH, D]), ext("nfpost", [DEPTH, D])
    ffn = [(ext(f"wg{i}", [KC_F, 128, D]), ext(f"wu{i}", [KC_F, 128, D]), ext(f"wd{i}", [4, 128, KC_F, 512]))
           for i in range(DEPTH)]
    ssd = [dict(w=ext(f"s_w{j}", [NZX, 128, D]), wdt=ext(f"s_wdt{j}", [128, KC_D, 16]), dtb=ext(f"s_dtb{j}", [16]),
                cw=ext(f"s_cw{j}", [128, 12, 4]), cb=ext(f"s_cb{j}", [128, 12]), alog=ext(f"s_alog{j}", [16]),
                dsk=ext(f"s_dsk{j}", [16]), nw=ext(f"s_nw{j}", [128, 8]), wo=ext(f"s_wo{j}", [4, 128, 8, 512]))
           for j in range(2)]
    att = [dict(w=ext(f"a_w{j}", [12, 128, D]), lam=ext(f"a_lam{j}", [4, 64]), sub=ext(f"a_sub{j}", [128]),
                wo=ext(f"a_wo{j}", [4, 128, 4, 512])) for j in range(2)]
    xo = dt_("xo", [TC, D], F32, kind="ExternalOutput").ap()
    scr = lambda n, s, d=F32: dt_(n, s, d).ap()
    hT4 = scr("sc_hT4", [16 * 128, KC_D * 512], BF16)
    hTg = scr("sc_hTg", [16 * 128, KC_D * 512], BF16)
    zx = scr("sc_zx", [NZX * 128, SEQ])
    dtt = scr("sc_dt", [SEQ, 16])
    yT = scr("sc_yT", [1024, SEQ], BF16)
    mpA, mpB = scr("sc_mpA", [SEQ, 1024]), scr("sc_mpB", [SEQ, 1024])
    mA, mB = scr("sc_mA", [TC, 1024]), scr("sc_mB", [TC, 1024])
    qTd = scr("sc_qT", [512, SEQ], BF16)
    kTd = scr("sc_kT", [512, SEQ], BF16)
    Vd = scr("sc_V", [4, 128, SEQ // 128, 128], BF16)
    negMd = scr("sc_negM", [128, 8])
    xa = scr("sc_xa", [TC, D])
    x1 = scr("sc_x1", [TC, D])
    h2T = scr("sc_h2T", [D, TC], BF16)
    aT = scr("sc_aT", [D_FF, TC], BF16)
    y2 = scr("sc_y2", [TC, D])
    with contextlib.ExitStack() as st0:
        P = Prog(nc, st0)
        with contextlib.ExitStack() as st:
            idf, idb, _ = make_ident(nc, P, st)
            phase_norm(nc, P, st, TC, x, None, None, nmpre[0], None, hT4, idb, sel4=sel4)
            P.emit(reorder=True)
        xcur = x
        for i in range(DEPTH):
            j = i // 2
            last = i == DEPTH - 1
            ptoks = [Tok(f"hTg{pc}") for pc in range(8)]
            for pc in range(8):
                rs_ = slice(pc * 256, (pc + 1) * 256)
                P.cc(lambda e, rs_=rs_: e.collective_compute("AllReduce", ALU.add, replica_groups=RG4,
                                                             ins=[hT4[rs_, :]], outs=[hTg[rs_, :]]),
                     wr=[ptoks[pc]])
            hT_tok = lambda b, ptoks=ptoks: ptoks[b // 2]
            if i % 2 == 0:
                s = ssd[j]
                with contextlib.ExitStack() as st:
                    phase_ssd_in(nc, P, st, hT_blk_fn_bm(hTg), s["w"], s["wdt"], s["dtb"], zx, dtt, hT_tok=hT_tok)
                    P.emit(reorder=True)
                with contextlib.ExitStack() as st:
                    idf, idb, _ = make_ident(nc, P, st)
                    C = make_consts(nc, P, st)
                    phase_ssd_scan(nc, P, st, zx, dtt, s["cw"], s["cb"], s["alog"], s["dsk"], s["nw"], yT,
                                   idf, idb, C)
                    P.emit(reorder=True)
                K, yT_use, wo = 1024, yT, s["wo"]
            else:
                a = att[j]
                with contextlib.ExitStack() as st:
                    idf, idb, _ = make_ident(nc, P, st)
                    C = make_consts(nc, P, st)
                    phase_attn_proj(nc, P, st, hT_blk_fn_bm(hTg), a["w"], qTd, kTd, Vd, negMd, idf, idb, C,
                                    hT_tok=hT_tok)
                    P.emit(reorder=True)
                with contextlib.ExitStack() as st:
                    idf, idb, _ = make_ident(nc, P, st)
                    C = make_consts(nc, P, st)
                    phase_attn_core(nc, P, st, qTd, kTd, Vd, negMd, a["lam"], a["sub"], lambda_init(i),
                                    yT[0:512, :], idf, idb, C)
                    P.emit(reorder=True)
                K, yT_use, wo = 512, yT[0:512, :], a["wo"]
            with contextlib.ExitStack() as st:
                htoks = ([], [])
                def rs_half(hf, htoks=htoks):
                    src, dst = (mpA, mA) if hf == 0 else (mpB, mB)
                    P.cc(lambda e: e.collective_compute("ReduceScatter", ALU.add, replica_groups=RG4,
                                                        ins=[src], outs=[dst], dma_qos="P2"), rd=list(htoks[hf]))
                phase_mmT(nc, P, st, SEQ, K, yT_use, wo, None, Th=2048, pfx="mo", halves=(mpA, mpB), htoks=htoks,
                          half_done=rs_half)
                P.emit(reorder=True)
            with contextlib.ExitStack() as st:
                idf, idb, _ = make_ident(nc, P, st)
                phase_norm(nc, P, st, TC, xcur, (mA, mB), nmpost[i], nfpre[i], x1, h2T, idb)
                P.emit(reorder=True)
            wg, wu, wd = ffn[i]
            with contextlib.ExitStack() as st:
                phase_ffn_gu(nc, P, st, TC, h2T, wg, wu, aT)
                P.emit(reorder=True)
            with contextlib.ExitStack() as st:
                phase_mmT(nc, P, st, TC, D_FF, aT, wd, y2, pfx="md")
                P.emit(reorder=True)
            with contextlib.ExitStack() as st:
                idf, idb, _ = make_ident(nc, P, st)
                phase_norm(nc, P, st, TC, x1, y2, nfpost[i], None if last else nmpre[i + 1],
                           xo if last else xa, None if last else hT4, idb, sel4=None if last else sel4)
                P.emit(reorder=True)
            xcur = xa
    return nc


def fused_inputs(inp):
    inp = {k: np.asarray(v) for k, v in inp.items()}
    xs = np.ascontiguousarray(inp["x"].reshape(NCORES, TC, D))
    shared = {"nmpre": inp["norm_mix_pre"], "nmpost": inp["norm_mix_post"],
              "nfpre": inp["norm_ffn_pre"], "nfpost": inp["norm_ffn_post"]}
    for i in range(DEPTH):
        shared[f"wg{i}"] = lay_colchunk(inp["ffn_w_gate"][i])
        shared[f"wu{i}"] = lay_colchunk(inp["ffn_w_up"][i])
        shared[f"wd{i}"] = lay_slab(inp["ffn_w_down"][i])
    maps = []
    for c in range(NCORES):
        gl = c % 4
        m = dict(shared)
        m["x"] = xs[c]
        m["sel4"] = np.eye(4, dtype=np.float32)[c % 4]
        for j in range(2):
            for k, v in ssd_core_inputs(inp, j, gl).items():
                m[f"{k}{j}"] = v
            m[f"s_wo{j}"] = lay_slab(inp["ssd_w_out"][j][gl * 1024:(gl + 1) * 1024])
            for k, v in attn_core_inputs(inp, j, gl).items():
                m[f"{k}{j}"] = v
            m[f"a_wo{j}"] = lay_slab(inp["da_w_out"][j][gl * 512:(gl + 1) * 512])
        maps.append(m)
    return maps


def kernel_fused(**inp):
    maps = fused_inputs(inp)
    res = run_bass_kernel_spmd(build_fused(), maps, core_ids=list(range(NCORES)))
    return np.stack([np.asarray(r["xo"]) for r in res.results]).reshape(BATCH, SEQ, D).astype(np.float32)


def kernel(**inputs):
    return kernel_fused(**inputs)
```
